# Optimizing a Trainium2 kernel written in Bass

```python
import jax, jax.numpy as jnp
from jax import lax
import numpy as np

D_MODEL = 1024
BATCH = 4
SEQ = 4096
DEPTH = 4
DEC_BATCH = 32
DEC_SEQ = 1
PAST_LEN = 8192
PAGE_SIZE = 128

N_MIXERS = 4
N_REPEAT = DEPTH // N_MIXERS
P_DIM = 256
EPS = 1e-6

A_HEADS = 8
A_DK = 128
A_DV = 256
A_WIDTH = A_HEADS * A_DV
A_CHUNK = 128
A_IN = 2 * A_HEADS * A_DK + 3 * A_WIDTH + 2 * A_HEADS
F_BIAS = 3.0

B_WIDTH = 2 * D_MODEL
CONV_W = 3

C_HEADS = 8
C_DH = 128
C_WIDTH = C_HEADS * C_DH
C_GROUPS = ((128, 1), (512, 4), (2048, 16))
C_BLOCK = 128
ROPE_DIM = C_DH // 4
ROPE_THETA = 500000.0
C_IN = 3 * len(C_GROUPS) * C_WIDTH + C_WIDTH

D_WIDTH = 2 * D_MODEL
D_WINDOWS = (2, 4, 8, 16)
D_GROUP = D_WIDTH // len(D_WINDOWS)
POOL_STATE = max(D_WINDOWS) - 1

kernel_name = 'hybrid_mlstm_conv_dilated_pool_step'

ATTN_KEYS = tuple(kv + '_w' + str(win) for win, _ in C_GROUPS for kv in ('k', 'v'))
STATE_KEYS = ('mlstm_C', 'mlstm_n', 'mlstm_m', 'conv') + ATTN_KEYS + ('pool',)


def rms_norm(x, g):
    x32 = x.astype(jnp.float32)
    y = x32 * lax.rsqrt(jnp.mean(x32 * x32, axis=-1, keepdims=True) + EPS)
    return (y * g.astype(jnp.float32)).astype(x.dtype)


def rope_partial(x, pos):
    half = ROPE_DIM // 2
    inv = ROPE_THETA ** (-jnp.arange(half, dtype=jnp.float32) / half)
    ang = pos.astype(jnp.float32)[:, None] * inv[None, :]
    cos = jnp.cos(ang)[None, :, None, :]
    sin = jnp.sin(ang)[None, :, None, :]
    x32 = x.astype(jnp.float32)
    x1 = x32[..., :half]
    x2 = x32[..., half:ROPE_DIM]
    out = jnp.concatenate([x1 * cos - x2 * sin, x2 * cos + x1 * sin, x32[..., ROPE_DIM:]], axis=-1)
    return out.astype(x.dtype)


def mlstm_chunk(state, chunk):
    C, n, m = state
    q, k, v, ig, lf = chunk
    L = q.shape[2]
    b = jnp.cumsum(lf, axis=-1)
    causal = jnp.tril(jnp.ones((L, L), dtype=bool))
    log_d = jnp.where(causal, b[..., :, None] - b[..., None, :] + ig[..., None, :], -jnp.inf)
    inter = b + m[..., None]
    m_t = jnp.maximum(inter, jnp.max(log_d, axis=-1))
    dmat = jnp.exp(log_d - m_t[..., None])
    g = jnp.exp(inter - m_t)
    s = jnp.einsum('bhtk,bhsk->bhts', q, k) * dmat
    num = jnp.einsum('bhts,bhsv->bhtv', s, v) + g[..., None] * jnp.einsum('bhvk,bhtk->bhtv', C, q)
    den = jnp.sum(s, axis=-1) + g * jnp.einsum('bhk,bhtk->bht', n, q)
    h = num / jnp.maximum(jnp.abs(den), jnp.exp(-m_t))[..., None]
    b_last = b[..., -1]
    a = b_last[..., None] - b + ig
    m_new = jnp.maximum(b_last + m, jnp.max(a, axis=-1))
    decay = jnp.exp(b_last + m - m_new)
    w = jnp.exp(a - m_new[..., None])
    C_new = decay[..., None, None] * C + jnp.einsum('bhs,bhsv,bhsk->bhvk', w, v, k)
    n_new = decay[..., None] * n + jnp.einsum('bhs,bhsk->bhk', w, k)
    return (C_new, n_new, m_new), h


def mlstm_mixer(h, w_in, b_if, norm_g, w_out, state, chunk):
    f32 = jnp.float32
    Bn, T = h.shape[:2]
    u = h @ w_in
    qk = A_HEADS * A_DK
    q, k, v, o, z, gates = jnp.split(
        u, [qk, 2 * qk, 2 * qk + A_WIDTH, 2 * qk + 2 * A_WIDTH, 2 * qk + 3 * A_WIDTH], axis=-1)
    gates = gates.astype(f32) + b_if.astype(f32)
    ig = gates[..., :A_HEADS]
    lf = jax.nn.log_sigmoid(gates[..., A_HEADS:])
    nc = T // chunk

    def to_chunks(a, dh):
        return a.astype(f32).reshape(Bn, nc, chunk, A_HEADS, dh).transpose(1, 0, 3, 2, 4)

    def gate_chunks(a):
        return a.reshape(Bn, nc, chunk, A_HEADS).transpose(1, 0, 3, 2)

    qs = to_chunks(q, A_DK)
    ks = to_chunks(k, A_DK) * (A_DK ** -0.5)
    vs = to_chunks(v, A_DV)
    state = tuple(s_.astype(f32) for s_ in state)
    new_state, hs = lax.scan(mlstm_chunk, state, (qs, ks, vs, gate_chunks(ig), gate_chunks(lf)))
    hs = hs.transpose(1, 0, 3, 2, 4).reshape(Bn, T, A_HEADS, A_DV)
    hs = hs * lax.rsqrt(jnp.mean(hs * hs, axis=-1, keepdims=True) + EPS)
    hs = hs.reshape(Bn, T, A_WIDTH) * norm_g.astype(f32)
    y = (hs * jax.nn.sigmoid(o.astype(f32))).astype(h.dtype) * jax.nn.silu(z)
    return y @ w_out, new_state


def shortconv_mixer(h, w_in, conv_w, w_out, buf):
    u = h @ w_in
    bg, cg, xb, z = jnp.split(u, 4, axis=-1)
    cx = cg * xb
    T = cx.shape[1]
    cpad = jnp.concatenate([buf.astype(cx.dtype), cx], axis=1)
    y = conv_w[0] * cpad[:, 0:T]
    for j in range(1, CONV_W):
        y = y + conv_w[j] * cpad[:, j:j + T]
    out = (bg * y * jax.nn.silu(z)) @ w_out
    return out, cpad[:, -(CONV_W - 1):]


def masked_softmax_stats(s, mask):
    s = jnp.where(mask, s, -1e30)
    mx = jnp.max(s, axis=-1)
    pr = jnp.where(mask, jnp.exp(s - mx[..., None]), 0.0)
    l = jnp.sum(pr, axis=-1)
    return pr / l[..., None], mx, l


def dilated_prompt(q, k, v, dil, span):
    f32 = jnp.float32
    Bn, S, H, Dh = q.shape
    n = S // dil
    nb = -(-n // C_BLOCK)
    npad = nb * C_BLOCK

    def split(a):
        a = a.astype(f32).reshape(Bn, n, dil, H, Dh).transpose(0, 2, 1, 3, 4)
        a = jnp.pad(a, ((0, 0), (0, 0), (0, npad - n), (0, 0), (0, 0)))
        return a.reshape(Bn, dil, nb, C_BLOCK, H, Dh)

    def with_prev(a):
        prev = jnp.pad(a[:, :, :-1], ((0, 0), (0, 0), (1, 0), (0, 0), (0, 0), (0, 0)))
        return jnp.concatenate([prev, a], axis=3)

    qb = split(q)
    kk = with_prev(split(k))
    vv = with_prev(split(v))
    s = jnp.einsum('brnqhe,brnkhe->brnhqk', qb, kk) * (Dh ** -0.5)
    qi = jnp.arange(C_BLOCK)[:, None]
    kj = jnp.arange(2 * C_BLOCK)[None, :]
    dist = qi + C_BLOCK - kj
    band = (dist >= 0) & (dist <= span)
    first = (jnp.arange(nb) > 0)[:, None, None] | (kj >= C_BLOCK)[None]
    mask = (band[None] & first)[None, None, :, None]
    pr, mx, l = masked_softmax_stats(s, mask)
    o = jnp.einsum('brnhqk,brnkhe->brnqhe', pr, vv)
    o = o.reshape(Bn, dil, npad, H, Dh)[:, :, :n].transpose(0, 2, 1, 3, 4).reshape(Bn, S, H, Dh)

    def back(a):
        a = a.transpose(0, 1, 2, 4, 3).reshape(Bn, dil, npad, H)[:, :, :n]
        return a.transpose(0, 2, 1, 3).reshape(Bn, S, H)

    return o, back(mx), back(l)


def dilated_sample(q, kall, vall, dil, span, n_buf):
    f32 = jnp.float32
    T, Dh = q.shape[1], q.shape[-1]
    idx = n_buf + jnp.arange(T)[:, None] - dil * jnp.arange(span + 1)[None, :]
    valid = idx >= 0
    idx = jnp.maximum(idx, 0)
    kg = kall[:, idx].astype(f32)
    vg = vall[:, idx].astype(f32)
    s = jnp.einsum('bthe,btjhe->bhtj', q.astype(f32), kg) * (Dh ** -0.5)
    pr, mx, l = masked_softmax_stats(s, valid[None, None])
    o = jnp.einsum('bhtj,btjhe->bthe', pr, vg)
    return o, mx.transpose(0, 2, 1), l.transpose(0, 2, 1)


def merge_groups(outs):
    mx = jnp.max(jnp.stack([m for _, m, _ in outs]), axis=0)
    ws = [l * jnp.exp(m - mx) for _, m, l in outs]
    num = ws[0][..., None] * outs[0][0]
    den = ws[0]
    for (o, _, _), w in zip(outs[1:], ws[1:]):
        num = num + w[..., None] * o
        den = den + w
    return num / den[..., None]


def dilated_mixer(h, w_in, w_out, pos, caches):
    Bn, T = h.shape[:2]
    u = h @ w_in
    parts = jnp.split(u, 3 * len(C_GROUPS) + 1, axis=-1)
    z = parts[-1]
    outs, new = [], []
    for g, (win, dil) in enumerate(C_GROUPS):
        q, k, v = [a.reshape(Bn, T, C_HEADS, C_DH) for a in parts[3 * g:3 * g + 3]]
        q = rope_partial(q, pos)
        k = rope_partial(k, pos)
        span = win // dil
        if caches is None:
            outs.append(dilated_prompt(q, k, v, dil, span))
            nkeep = min(win, T)
            new += [k[:, T - nkeep:], v[:, T - nkeep:]]
        else:
            kc, vc = caches[2 * g], caches[2 * g + 1]
            kall = jnp.concatenate([kc.astype(k.dtype), k], axis=1)
            vall = jnp.concatenate([vc.astype(v.dtype), v], axis=1)
            outs.append(dilated_sample(q, kall, vall, dil, span, kc.shape[1]))
            new += [k, v]
    o = merge_groups(outs).reshape(Bn, T, C_WIDTH).astype(h.dtype)
    return (o * jax.nn.silu(z)) @ w_out, new


def pool_mixer(h, w_in, w_grp, scale, w_out, buf, pos):
    f32 = jnp.float32
    u = h @ w_in
    xp, z = jnp.split(u, 2, axis=-1)
    T = xp.shape[1]
    xpad = jnp.concatenate([buf.astype(xp.dtype), xp], axis=1).astype(f32)
    cs = jnp.pad(jnp.cumsum(xpad, axis=1), ((0, 0), (1, 0), (0, 0)))
    end = cs[:, POOL_STATE + 1:]
    xcur = xpad[:, POOL_STATE:]
    outs = []
    for g, w in enumerate(D_WINDOWS):
        sl = slice(g * D_GROUP, (g + 1) * D_GROUP)
        start = cs[:, POOL_STATE + 1 - w:POOL_STATE + 1 - w + T, sl]
        cnt = jnp.minimum(w, pos + 1).astype(f32)[None, :, None]
        r = (end[..., sl] - start) / cnt - xcur[..., sl]
        outs.append(r @ w_grp[g].astype(f32))
    y = jnp.concatenate(outs, axis=-1) * scale.astype(f32)
    out = (y.astype(h.dtype) * jax.nn.silu(z)) @ w_out
    return out, xpad[:, -POOL_STATE:].astype(xp.dtype)


def per_layer_embed(x, p, pe_w, pg_w):
    gate = jax.nn.sigmoid(x @ pg_w)
    return x + gate * (p.astype(x.dtype) @ pe_w)


def run_group(x, p, pos, W, st):
    f32 = jnp.float32
    Bn, T = x.shape[:2]
    new = {name: [] for name in STATE_KEYS}
    for i in range(DEPTH):
        kind, r = i % N_MIXERS, i // N_MIXERS
        h = rms_norm(x, W['norm_g'][i])
        if kind == 0:
            if st is None:
                s0 = (jnp.zeros((Bn, A_HEADS, A_DV, A_DK), f32), jnp.zeros((Bn, A_HEADS, A_DK), f32),
                      jnp.zeros((Bn, A_HEADS), f32))
                chunk = A_CHUNK
            else:
                s0 = (st['mlstm_C'][r], st['mlstm_n'][r], st['mlstm_m'][r])
                chunk = T
            y, (c_new, n_new, m_new) = mlstm_mixer(h, W['a_w_in'][r], W['a_b_if'][r], W['a_norm_g'][r],
                                                   W['a_w_out'][r], s0, chunk)
            new['mlstm_C'].append(c_new)
            new['mlstm_n'].append(n_new)
            new['mlstm_m'].append(m_new)
        elif kind == 1:
            buf = jnp.zeros((Bn, CONV_W - 1, B_WIDTH), x.dtype) if st is None else st['conv'][r]
            y, nbuf = shortconv_mixer(h, W['b_w_in'][r], W['b_conv_w'][r], W['b_w_out'][r], buf)
            new['conv'].append(nbuf)
        elif kind == 2:
            caches = None if st is None else [st[name][r] for name in ATTN_KEYS]
            y, nkv = dilated_mixer(h, W['c_w_in'][r], W['c_w_out'][r], pos, caches)
            for name, a in zip(ATTN_KEYS, nkv):
                new[name].append(a)
        else:
            buf = jnp.zeros((Bn, POOL_STATE, D_WIDTH), x.dtype) if st is None else st['pool'][r]
            y, nbuf = pool_mixer(h, W['d_w_in'][r], W['d_w_grp'][r], W['d_scale'][r], W['d_w_out'][r], buf, pos)
            new['pool'].append(nbuf)
        x = x + y.astype(x.dtype)
        x = per_layer_embed(x, p[i], W['pe_w'][i], W['pg_w'][i])
    return rms_norm(x, W['final_g']), {name: jnp.stack(v) for name, v in new.items()}


def setup_inputs(seed: int = 0) -> dict:
    key = jax.random.key(seed)
    ks = iter(jax.random.split(key, 48))
    f32 = jnp.float32

    def nrm(shape, scale=1.0):
        return jax.random.normal(next(ks), shape, f32) * scale

    R = N_REPEAT
    w128, w512, w2048 = [min(win, PAST_LEN) for win, _ in C_GROUPS]
    kv = lambda nbuf: nrm((R, DEC_BATCH, nbuf, C_HEADS, C_DH))
    return {
        'x_prompt': nrm((BATCH, SEQ, D_MODEL)),
        'x_sample': nrm((DEC_BATCH, DEC_SEQ, D_MODEL)),
        'state_mlstm_C': nrm((R, DEC_BATCH, A_HEADS, A_DV, A_DK), 0.1),
        'state_mlstm_n': nrm((R, DEC_BATCH, A_HEADS, A_DK), 0.1),
        'state_mlstm_m': nrm((R, DEC_BATCH, A_HEADS)),
        'state_conv': nrm((R, DEC_BATCH, CONV_W - 1, B_WIDTH)),
        'cache_k_w128': kv(w128),
        'cache_v_w128': kv(w128),
        'cache_k_w512': kv(w512),
        'cache_v_w512': kv(w512),
        'cache_k_w2048': kv(w2048),
        'cache_v_w2048': kv(w2048),
        'state_pool': nrm((R, DEC_BATCH, POOL_STATE, D_WIDTH)),
        'p_prompt': nrm((DEPTH, BATCH, SEQ, P_DIM)),
        'p_sample': nrm((DEPTH, DEC_BATCH, DEC_SEQ, P_DIM)),
        'norm_g': 1.0 + nrm((DEPTH, D_MODEL), 0.02),
        'pe_w': nrm((DEPTH, P_DIM, D_MODEL), P_DIM ** -0.5),
        'pg_w': nrm((DEPTH, D_MODEL, D_MODEL), D_MODEL ** -0.5),
        'final_g': 1.0 + nrm((D_MODEL,), 0.02),
        'a_w_in': nrm((R, D_MODEL, A_IN), D_MODEL ** -0.5),
        'a_b_if': jnp.concatenate([nrm((R, A_HEADS), 0.1), F_BIAS + nrm((R, A_HEADS), 0.1)], axis=-1),
        'a_norm_g': 1.0 + nrm((R, A_WIDTH), 0.02),
        'a_w_out': nrm((R, A_WIDTH, D_MODEL), A_WIDTH ** -0.5),
        'b_w_in': nrm((R, D_MODEL, 4 * B_WIDTH), D_MODEL ** -0.5),
        'b_conv_w': nrm((R, CONV_W, B_WIDTH), CONV_W ** -0.5),
        'b_w_out': nrm((R, B_WIDTH, D_MODEL), B_WIDTH ** -0.5),
        'c_w_in': nrm((R, D_MODEL, C_IN), D_MODEL ** -0.5),
        'c_w_out': nrm((R, C_WIDTH, D_MODEL), C_WIDTH ** -0.5),
        'd_w_in': nrm((R, D_MODEL, 2 * D_WIDTH), D_MODEL ** -0.5),
        'd_w_grp': nrm((R, len(D_WINDOWS), D_GROUP, D_GROUP), D_GROUP ** -0.5),
        'd_scale': 1.0 + nrm((R, D_WIDTH), 0.02),
        'd_w_out': nrm((R, D_WIDTH, D_MODEL), D_WIDTH ** -0.5),
    }


def reference(x_prompt, x_sample, state_mlstm_C, state_mlstm_n, state_mlstm_m, state_conv,
              cache_k_w128, cache_v_w128, cache_k_w512, cache_v_w512, cache_k_w2048, cache_v_w2048,
              state_pool, p_prompt, p_sample, norm_g, pe_w, pg_w, final_g,
              a_w_in, a_b_if, a_norm_g, a_w_out, b_w_in, b_conv_w, b_w_out,
              c_w_in, c_w_out, d_w_in, d_w_grp, d_scale, d_w_out):
    W = {'norm_g': norm_g, 'pe_w': pe_w, 'pg_w': pg_w, 'final_g': final_g,
         'a_w_in': a_w_in, 'a_b_if': a_b_if, 'a_norm_g': a_norm_g, 'a_w_out': a_w_out,
         'b_w_in': b_w_in, 'b_conv_w': b_conv_w, 'b_w_out': b_w_out,
         'c_w_in': c_w_in, 'c_w_out': c_w_out,
         'd_w_in': d_w_in, 'd_w_grp': d_w_grp, 'd_scale': d_scale, 'd_w_out': d_w_out}
    st = {'mlstm_C': state_mlstm_C, 'mlstm_n': state_mlstm_n, 'mlstm_m': state_mlstm_m,
          'conv': state_conv, 'k_w128': cache_k_w128, 'v_w128': cache_v_w128,
          'k_w512': cache_k_w512, 'v_w512': cache_v_w512,
          'k_w2048': cache_k_w2048, 'v_w2048': cache_v_w2048, 'pool': state_pool}
    pos_p = jnp.arange(x_prompt.shape[1])
    pos_s = PAST_LEN + jnp.arange(x_sample.shape[1])
    y_prompt, np_ = run_group(x_prompt, p_prompt, pos_p, W, None)
    y_sample, ns = run_group(x_sample, p_sample, pos_s, W, st)
    return (y_prompt, y_sample,
            np_['mlstm_C'], ns['mlstm_C'], np_['mlstm_n'], ns['mlstm_n'], np_['mlstm_m'], ns['mlstm_m'],
            np_['conv'], ns['conv'],
            np_['k_w128'], ns['k_w128'], np_['v_w128'], ns['v_w128'],
            np_['k_w512'], ns['k_w512'], np_['v_w512'], ns['v_w512'],
            np_['k_w2048'], ns['k_w2048'], np_['v_w2048'], ns['v_w2048'],
            np_['pool'], ns['pool'])
```

```python
import os
import numpy as np
from contextlib import ExitStack
import concourse.bass as bass
import concourse.mybir as mybir
from concourse.bass_utils import run_bass_kernel_spmd

F32 = mybir.dt.float32
BF16 = mybir.dt.bfloat16
AF = mybir.ActivationFunctionType
ALU = mybir.AluOpType
AX = mybir.AxisListType

ENGS = ('pe', 'act', 'dve', 'pool', 'sp')
NDMA = 32
D = 1024
NS = 4
EPS = 1e-6


def I(name, *a, **kw):
    return lambda e: getattr(e, name)(*a, **kw)


PHASE_LOG = []


class Pre:
    def __init__(self, items, load, depth):
        self.items, self.load, self.depth, self.n = items, load, depth, 0

    def need(self, i):
        while self.n <= min(i + self.depth, len(self.items) - 1):
            self.load(self.n, self.items[self.n])
            self.n += 1


def pipeline(gen_iter, width, ramp=None):
    active = []
    it = iter(gen_iter)
    done = False
    while True:
        started = 0
        while len(active) < width and not done and (ramp is None or started < ramp):
            try:
                active.append(next(it))
                started += 1
            except StopIteration:
                done = True
        if not active:
            break
        nxt = []
        for g in active:
            try:
                next(g)
                nxt.append(g)
            except StopIteration:
                pass
        active = nxt


def lockstep(gens):
    gens = list(gens)
    while gens:
        nxt = []
        for g in gens:
            try:
                next(g)
                nxt.append(g)
            except StopIteration:
                pass
        gens = nxt


class Trk:
    def __init__(self, nc, sems, dsems, nsw=8):
        self.nc = nc
        self.eng = {'pe': nc.tensor, 'act': nc.scalar, 'dve': nc.vector, 'pool': nc.gpsimd, 'sp': nc.sync}
        self.sems = dict(sems)
        for i, s in enumerate(dsems):
            self.sems['d%d' % i] = s
        self.cnt = {e: 0 for e in ENGS}
        self.dtot = [0] * len(dsems)
        self.nhw = len(dsems) - nsw
        self.rr = 0
        self.rr_sw = 0
        self.known = {e: {} for e in ENGS}
        self.streams = {e: [] for e in ENGS}
        self.last_w = {}
        self.readers = {}
        self.groups = {}

    def _exp(self, keys):
        out = []
        for k in keys:
            out.extend(self.groups.get(k, (k,)))
        return out

    def _deps(self, en, r, w):
        r = self._exp(r)
        w = self._exp(w)
        deps = []
        for k in r:
            t = self.last_w.get(k)
            if t:
                deps.append(t)
            if isinstance(k, tuple) and k[0] in ('ps', 'psb'):
                deps.extend(t2 for t2 in self.readers.get(k, ()) if t2[0] != en)
        for k in w:
            t = self.last_w.get(k)
            if t:
                deps.append(t)
            deps.extend(self.readers.get(k, ()))
        waits = []
        kn = self.known[en]
        for (s, v) in deps:
            if s == 'pe' and en == 'pe':
                continue
            if kn.get(s, 0) >= v:
                continue
            kn[s] = v
            waits.append((s, v))
        return waits

    def _commit(self, tok, r, w):
        r = self._exp(r)
        w = self._exp(w)
        for k in r:
            self.readers.setdefault(k, []).append(tok)
        for k in w:
            self.last_w[k] = tok
            self.readers[k] = []

    def op(self, en, fn, r=(), w=()):
        waits = self._deps(en, r, w)
        self.cnt[en] += 1
        tok = (en, self.cnt[en])
        self.streams[en].append((waits, fn, en, 1))
        self._commit(tok, r, w)

    def dma(self, out, in_, r=(), w=(), q='sp', **kw):
        waits = self._deps(q, r, w)
        if q == 'pool':
            i = self.nhw + self.rr_sw
            self.rr_sw = (self.rr_sw + 1) % (len(self.dtot) - self.nhw)
        else:
            i = self.rr
            self.rr = (self.rr + 1) % self.nhw
        s = 'd%d' % i
        if self.known[q].get(s, 0) < self.dtot[i]:
            self.known[q][s] = self.dtot[i]
            waits.append((s, self.dtot[i]))
        self.dtot[i] += 16
        tok = (s, self.dtot[i])
        self.streams[q].append((waits, I('dma_start', out=out, in_=in_, **kw), s, 16))
        self._commit(tok, r, w)

    def barrier(self):
        allt = [(e, self.cnt[e]) for e in ENGS if self.cnt[e] > 0]
        allt += [('d%d' % i, v) for i, v in enumerate(self.dtot) if v > 0]
        for en in ENGS:
            waits = []
            for (s, v) in allt:
                if self.known[en].get(s, 0) < v:
                    self.known[en][s] = v
                    waits.append((s, v))
            if waits:
                self.streams[en].append((waits, None, None, 0))
        self.last_w = {}
        self.readers = {}

    def flush(self):
        PHASE_LOG.append(dict(self.cnt))
        nc = self.nc
        streams = self.streams
        self.streams = {e: [] for e in ENGS}
        sems = self.sems

        def replay(en):
            def f(e):
                for (waits, fn, s, inc) in streams[en]:
                    for (ws, wv) in waits:
                        e.wait_ge(sems[ws], wv)
                    if fn is not None:
                        fn(e).then_inc(sems[s], inc)
            return f

        with nc.Block() as block:
            block.tensor(replay('pe'))
            block.scalar(replay('act'))
            block.vector(replay('dve'))
            block.gpsimd(replay('pool'))
            block.sync(replay('sp'))


def host_consts(T):
    c = {}
    c['ident'] = np.eye(128, dtype=np.float32)
    i = np.arange(128)
    c['triu'] = (i[:, None] <= i[None, :]).astype(np.float32)
    c['maskneg'] = np.where(i[None, :] <= i[:, None], 0.0, -1e30).astype(np.float32)
    sel = np.zeros((128, 128), np.float32)
    sel[127, :] = 1.0
    c['sel_last'] = sel
    qi = i[:, None]
    kj = np.arange(256)[None, :]
    dist = qi + 128 - kj
    band = (dist >= 0) & (dist <= 128)
    c['band'] = np.where(band, 0.0, -1e30).astype(np.float32)
    c['band0'] = np.where(band & (kj >= 128), 0.0, -1e30).astype(np.float32)
    half = 16
    inv = (500000.0 ** (-np.arange(half, dtype=np.float32) / half)).astype(np.float32)
    pos = np.concatenate([np.arange(T), np.array([8192])]).astype(np.float32)
    ang = (pos[None, :] * inv[:, None]).astype(np.float32)
    cosT = np.ones((128, T + 1), np.float32)
    sinT = np.zeros((128, T + 1), np.float32)
    cosT[0:16] = np.cos(ang)
    cosT[16:32] = np.cos(ang)
    sinT[0:16] = -np.sin(ang)
    sinT[16:32] = np.sin(ang)
    c['ropec'] = cosT
    c['ropes'] = sinT
    pm = np.zeros((128, 128), np.float32)
    for p in range(16):
        pm[p + 16, p] = 1.0
        pm[p, p + 16] = 1.0
    c['ropeperm'] = pm
    ic = np.zeros((128, 16, 16), np.float32)
    for ch in range(16):
        wdw = (2, 4, 8, 16)[ch // 4]
        for t in range(16):
            ic[:, ch, t] = 1.0 / min(wdw, t + 1)
    c['invcnt'] = ic.reshape(128, 256)
    return c


CONST_SHAPES = {'ident': [128, 128], 'triu': [128, 128], 'maskneg': [128, 128], 'sel_last': [128, 128],
                'band': [128, 256], 'band0': [128, 256], 'ropeperm': [128, 128], 'invcnt': [128, 256]}

W_SHAPES = {
    'norm_g': [4, 1024], 'pe_w': [4, 256, 1024], 'pg_w': [4, 1024, 1024], 'final_g': [1024],
    'a_w_in': [1024, 8208], 'a_b_if': [16], 'a_norm_g': [2048], 'a_w_out': [2048, 1024],
    'b_w_in': [1024, 8192], 'b_conv_w': [3, 2048], 'b_w_out': [2048, 1024],
    'c_w_in': [1024, 10240], 'c_w_out': [1024, 1024],
    'd_w_in': [1024, 4096], 'd_w_grp': [4, 512, 512], 'd_scale': [2048], 'd_w_out': [2048, 1024],
}


def build(T=4096, layers=(0, 1, 2, 3), dbg=False):
    TA = T + NS
    NCH = T // 128
    nc = bass.Bass("TRN2", target_bir_lowering=False)
    di = {}

    def din(name, shape, dt=F32):
        di[name] = nc.dram_tensor(name, list(shape), dt, kind="ExternalInput").ap()
        return di[name]

    def dout(name, shape, dt=F32):
        di[name] = nc.dram_tensor(name, list(shape), dt, kind="ExternalOutput").ap()
        return di[name]

    def dscr(name, shape, dt=F32):
        di[name] = nc.dram_tensor(name, list(shape), dt, kind="Internal").ap()
        return di[name]

    din('x_prompt', [T, D]); din('x_sample', [NS, D])
    din('p_prompt', [4, T, 256]); din('p_sample', [4, NS, 256])
    din('st_C', [NS, 8, 256, 128]); din('st_n', [NS, 8, 128]); din('st_m', [NS, 8])
    din('st_conv', [NS, 2, 2048]); din('st_pool', [NS, 15, 2048])
    for wn, nb in (('128', 128), ('512', 512), ('2048', 2048)):
        din('ck' + wn, [NS, nb, 1024]); din('cv' + wn, [NS, nb, 1024])
    for k_, s_ in W_SHAPES.items():
        din(k_, s_)
    for k_, s_ in CONST_SHAPES.items():
        din('c_' + k_, s_)
    din('c_ropec', [128, T + 1]); din('c_ropes', [128, T + 1])

    dout('y_prompt', [T, D]); dout('y_sample', [NS, D])
    dout('o_Cp', [8, 256, 128]); dout('o_Cs', [NS, 8, 256, 128])
    dout('o_np', [8, 128]); dout('o_ns', [NS, 8, 128])
    dout('o_mp', [1, 8]); dout('o_ms', [NS, 8])
    dout('o_convp', [2, 2048]); dout('o_convs', [NS, 2, 2048])
    for wn, nb in (('128', 128), ('512', 512), ('2048', 2048)):
        nk = min(nb, T)
        dout('o_kp' + wn, [nk, 1024]); dout('o_ks' + wn, [NS, 1024])
        dout('o_vp' + wn, [nk, 1024]); dout('o_vs' + wn, [NS, 1024])
    dout('o_poolp', [15, 2048]); dout('o_pools', [NS, 15, 2048])
    if dbg:
        dout('dbg_xT', [D, TA])

    xT = dscr('xT', [D, TA])
    yT = dscr('yT', [2048, TA], BF16)

    with ExitStack() as top:
        sems = {e: top.enter_context(nc.semaphore('s_' + e)) for e in ENGS}
        dsems = [top.enter_context(nc.semaphore('sd%d' % i)) for i in range(NDMA)]
        k = Trk(nc, sems, dsems)

        uid = [0]

        def sb(es, name, shape, dt=F32):
            uid[0] += 1
            return es.enter_context(nc.sbuf_tensor('%s_%d' % (name, uid[0]), list(shape), dt))

        ident = sb(top, 'ident', [128, 128]); identb = sb(top, 'identb', [128, 128], BF16)
        onesb = sb(top, 'onesb', [128, 128], BF16); onesf = sb(top, 'onesf', [128, 128])
        epsc = sb(top, 'epsc', [128, 1])
        psF = top.enter_context(nc.psum_tensor('psF', [128, 6 * 512], F32))
        psB = top.enter_context(nc.psum_tensor('psB', [128, 2 * 1024], BF16))

        def PS(i, n=512):
            return psF[:, i * 512:i * 512 + n]

        def PSB(i, n=128):
            return psB[:, i * 1024:i * 1024 + n]

        k.dma(ident[:], di['c_ident'], w=['ident'])
        k.op('dve', I('tensor_copy', identb[:], ident[:]), r=['ident'], w=['identb'])
        k.op('dve', I('memset', onesb[:], 1.0), w=['onesb'])
        k.op('dve', I('memset', onesf[:], 1.0), w=['onesf'])
        k.op('dve', I('memset', epsc[:], EPS), w=['epsc'])
        k.barrier()

        rrp = [0]

        psmod = [6]

        pslive = set()

        def nps():
            for _ in range(psmod[0]):
                rrp[0] = (rrp[0] + 1) % psmod[0]
                if rrp[0] not in pslive:
                    return rrp[0]
            raise RuntimeError('no free PSUM bank')

        def psalloc():
            p = nps()
            pslive.add(p)
            return p

        def psfree(p):
            pslive.discard(p)

        with ExitStack() as es:
            xin = [sb(es, 'xin%d' % i, [128, D]) for i in range(4)]
            xo = [sb(es, 'xo%d' % i, [128, 8, 128]) for i in range(4)]
            tiles = [(t * 128, 128, di['x_prompt'][t * 128:(t + 1) * 128, :]) for t in range(NCH)]
            tiles.append((T, NS, di['x_sample']))

            def p0_load(n):
                k.dma(xin[n % 4][:tiles[n][1], :], tiles[n][2], w=[('xin', n % 4)])

            p0_load(0)
            p0_load(1)
            for n, (t0, L, src) in enumerate(tiles):
                b = n % 4
                if n + 2 < len(tiles):
                    p0_load(n + 2)
                for c in range(8):
                    p = nps()
                    k.op('pe', I('transpose', PS(p, L), xin[b][:L, c * 128:(c + 1) * 128], ident[:L, :L]),
                         r=[('xin', b)], w=[('ps', p)])
                    k.op('act' if c % 2 else 'dve',
                         I('copy' if c % 2 else 'tensor_copy', xo[b][:, c, :L], PS(p, L)),
                         r=[('ps', p)], w=[('xo', b, c)])
                k.dma(xT.rearrange("(c p) t -> p c t", p=128)[:, :, t0:t0 + L], xo[b][:, :, :L],
                      r=[('xo', b, c) for c in range(8)])
            k.barrier()
            k.flush()

        def bcast_row(ap1d, n):
            return bass.AP(ap1d.tensor, ap1d.offset, [[0, 128], [1, n]])

        def norm_half(es, g_ap, t0, TW, hT, tag, dbuf=True):
            nb_ = 2 if dbuf else 1
            if '_norm' not in wst:
                gcol = sb(es, 'gcol' + tag, [128, 8])
                k.dma(gcol[:], g_ap.rearrange("(c p) -> p c", p=128), w=['gcol'], allow_slow_non_contiguous=True)
                xt = [sb(es, 'nx%s%d' % (tag, i), [128, 8, 512]) for i in range(nb_)]
                sq = [sb(es, 'nsq%s%d' % (tag, i), [128, 8, 512], BF16) for i in range(nb_)]
                rs = [sb(es, 'nrs%s%d' % (tag, i), [128, 512]) for i in range(nb_)]
                wst['_norm'] = (gcol, xt, sq, rs)
            gcol, xt, sq, rs = wst['_norm']
            ntl = [(s0, min(512, TW - s0)) for s0 in range(0, TW, 512)]

            def n_load(n):
                s0, W_ = ntl[n]
                b = n % nb_
                k.dma(xt[b][:, :, :W_], xT.rearrange("(c p) t -> p c t", p=128)[:, :, t0 + s0:t0 + s0 + W_],
                      w=[('nx', b)])

            n_load(0)
            for n, (s0, W_) in enumerate(ntl):
                b = n % nb_
                if nb_ == 2 and n + 1 < len(ntl):
                    n_load(n + 1)
                k.op('act', I('activation', sq[b][:, :, :W_], xt[b][:, :, :W_], AF.Square),
                     r=[('nx', b)], w=[('nsq', b)])
                p = nps()
                for c in range(8):
                    k.op('pe', I('matmul', PS(p, W_), onesb[:], sq[b][:, c, :W_], start=(c == 0), stop=(c == 7)),
                         r=[('nsq', b), 'onesb'], w=[('ps', p)])
                k.op('act', I('activation', rs[b][:, :W_], PS(p, W_), AF.Sqrt, bias=epsc[:], scale=1.0 / D),
                     r=[('ps', p), 'epsc'], w=[('nrs', b)])
                k.op('dve', I('reciprocal', rs[b][:, :W_], rs[b][:, :W_]), r=[('nrs', b)], w=[('nrs', b)])
                for c in range(8):
                    k.op('dve',
                         I('scalar_tensor_tensor', hT[:, c, s0:s0 + W_], xt[b][:, c, :W_], gcol[:, c:c + 1],
                           rs[b][:, :W_], ALU.mult, ALU.mult),
                         r=[('nx', b), ('nrs', b), 'gcol'], w=[('hT', c, s0 // 512)])
                if nb_ == 1 and n + 1 < len(ntl):
                    n_load(n + 1)

        fin_state = {}

        def norm_tile_f32(es, g_ap, tl_, n, hf):
            if 'g' not in fin_state:
                fin_state['g'] = sb(es, 'fgcol', [128, 8])
                k.dma(fin_state['g'][:], g_ap.rearrange("(c p) -> p c", p=128), w=['fgcol'], allow_slow_non_contiguous=True)
                fin_state['xt'] = [sb(es, 'fnx%d' % i, [128, 8, 512]) for i in range(2)]
                fin_state['sq'] = [sb(es, 'fnsq%d' % i, [128, 8, 512], BF16) for i in range(2)]
                fin_state['rs'] = [sb(es, 'fnrs%d' % i, [128, 512]) for i in range(2)]

            def f_load(m):
                t0_, Wm = tl_[m]
                k.dma(fin_state['xt'][m % 2][:, :, :Wm], xT.rearrange("(c p) t -> p c t", p=128)[:, :, t0_:t0_ + Wm],
                      w=[('fnx', m % 2)])

            if n == 0:
                f_load(0)
            if n + 1 < len(tl_):
                f_load(n + 1)
            b = n % 2
            t0, W_ = tl_[n]
            gcol, xt, sq, rs = fin_state['g'], fin_state['xt'][b], fin_state['sq'][b], fin_state['rs'][b]
            k.op('act', I('activation', sq[:, :, :W_], xt[:, :, :W_], AF.Square), r=[('fnx', b)], w=[('fnsq', b)])
            p = nps()
            for c in range(8):
                k.op('pe', I('matmul', PS(p, W_), onesb[:], sq[:, c, :W_], start=(c == 0), stop=(c == 7)),
                     r=[('fnsq', b), 'onesb'], w=[('ps', p)])
            k.op('act', I('activation', rs[:, :W_], PS(p, W_), AF.Sqrt, bias=epsc[:], scale=1.0 / D),
                 r=[('ps', p), 'epsc'], w=[('fnrs', b)])
            k.op('dve', I('reciprocal', rs[:, :W_], rs[:, :W_]), r=[('fnrs', b)], w=[('fnrs', b)])
            for c in range(8):
                k.op('dve',
                     I('scalar_tensor_tensor', hf[b][:, c, :W_], xt[:, c, :W_], gcol[:, c:c + 1],
                       rs[:, :W_], ALU.mult, ALU.mult),
                     r=[('fnx', b), ('fnrs', b), 'fgcol'], w=[('hf', b, c)])

        wst = {}

        def load_w(es, dst, dst_key, wsrc, kc, ncols, tag):
            parts = [(k0, min(8, kc - k0)) for k0 in range(0, kc, 8)]
            if len(parts) > 1:
                k.groups[dst_key] = [(dst_key, 'part', i) for i in range(len(parts))]
            for i, (k0, kw) in enumerate(parts):
                k.dma(dst[:, k0:k0 + kw, 0:ncols],
                      wsrc[k0 * 128:(k0 + kw) * 128, :].rearrange("(c p) n -> p c n", p=128),
                      w=[(dst_key, 'part', i) if len(parts) > 1 else dst_key], q='pool')

        def out_phase(layer, w_out_ap, KY):
            with ExitStack() as es:
                wst.clear()
                wo = sb(es, 'wo', [128, KY, D], BF16)
                wg = sb(es, 'wg', [128, 8, D], BF16)
                wp = sb(es, 'wp', [128, 2, D], BF16)
                load_w(es, wo, 'wo', w_out_ap, KY, D, 'o')
                load_w(es, wg, 'wg', di['pg_w'][layer], 8, D, 'o')
                load_w(es, wp, 'wp', di['pe_w'][layer], 2, D, 'o')
                yt = [sb(es, 'oy%d' % i, [128, KY, 512], BF16) for i in range(2)]
                xt = [sb(es, 'ox%d' % i, [128, 8, 512]) for i in range(2)]
                xb = [sb(es, 'oxb%d' % i, [128, 8, 512], BF16) for i in range(2)]
                pt = [sb(es, 'op%d' % i, [128, 4, 256]) for i in range(2)]
                pT = [sb(es, 'opT%d' % i, [128, 2, 512], BF16) for i in range(2)]
                gt = [sb(es, 'og%d' % i, [128, 512]) for i in range(2)]
                tl = [(s0, 512) for s0 in range(0, T, 512)] + [(T, NS)]
                def o_loads(n):
                    s0, W_ = tl[n]
                    b = n % 2
                    k.dma(yt[b][:, :, :W_], yT.rearrange("(c p) t -> p c t", p=128)[:, 0:KY, s0:s0 + W_],
                          w=[('oy', b)])
                    k.dma(xt[b][:, :, :W_], xT.rearrange("(c p) t -> p c t", p=128)[:, :, s0:s0 + W_],
                          w=[('ox', b)])
                    if W_ == 512:
                        k.dma(pt[b][:], di['p_prompt'][layer, s0:s0 + 512, :].rearrange("(j p) n -> p j n", p=128),
                              w=[('op', b)])
                    else:
                        k.dma(pt[b][:NS, 0, :], di['p_sample'][layer], w=[('op', b)])

                o_loads(0)
                for n, (s0, W_) in enumerate(tl):
                    b = n % 2
                    if n + 1 < len(tl):
                        o_loads(n + 1)
                    if W_ == 512:
                        subs = [(j, 128) for j in range(4)]
                    else:
                        subs = [(0, NS)]
                    for (j, L) in subs:
                        for c in range(2):
                            p = nps()
                            k.op('pe', I('transpose', PS(p, L), pt[b][:L, j, c * 128:(c + 1) * 128], ident[:L, :L]),
                                 r=[('op', b)], w=[('ps', p)])
                            k.op('act', I('copy', pT[b][:, c, j * 128:j * 128 + L], PS(p, L)),
                                 r=[('ps', p)], w=[('opT', b, j, c)])
                    pTr = [('opT', b, j, c) for (j, L) in subs for c in range(2)]
                    for dc in range(8):
                        p = nps()
                        for c in range(KY):
                            k.op('pe', I('matmul', PS(p, W_), wo[:, c, dc * 128:(dc + 1) * 128], yt[b][:, c, :W_],
                                         start=(c == 0), stop=(c == KY - 1)),
                                 r=['wo', ('oy', b)], w=[('ps', p)])
                        k.op('dve', I('tensor_tensor', xt[b][:, dc, :W_], xt[b][:, dc, :W_], PS(p, W_), ALU.add),
                             r=[('ps', p), ('ox', b)], w=[('ox', b, dc)])
                        k.op('act', I('copy', xb[b][:, dc, :W_], xt[b][:, dc, :W_]),
                             r=[('ox', b, dc)], w=[('oxb', b, dc)])
                    for dc in range(8):
                        p = nps()
                        for c in range(8):
                            k.op('pe', I('matmul', PS(p, W_), wg[:, c, dc * 128:(dc + 1) * 128], xb[b][:, c, :W_],
                                         start=(c == 0), stop=(c == 7)),
                                 r=['wg'] + [('oxb', b, cc) for cc in range(8)], w=[('ps', p)])
                        k.op('act', I('activation', gt[b][:, :W_], PS(p, W_), AF.Sigmoid),
                             r=[('ps', p)], w=[('og', b)])
                        p2 = nps()
                        for c in range(2):
                            k.op('pe', I('matmul', PS(p2, W_), wp[:, c, dc * 128:(dc + 1) * 128], pT[b][:, c, :W_],
                                         start=(c == 0), stop=(c == 1)),
                                 r=['wp'] + pTr, w=[('ps', p2)])
                        k.op('dve', I('tensor_tensor', gt[b][:, :W_], gt[b][:, :W_], PS(p2, W_), ALU.mult),
                             r=[('ps', p2), ('og', b)], w=[('og', b)])
                        k.op('pool', I('tensor_tensor', xt[b][:, dc, :W_], xt[b][:, dc, :W_], gt[b][:, :W_], ALU.add),
                             r=[('og', b), ('ox', b, dc), ('oxb', b, dc)], w=[('ox', b, dc)])
                    k.dma(xT.rearrange("(c p) t -> p c t", p=128)[:, :, s0:s0 + W_], xt[b][:, :, :W_],
                          r=[('ox', b, dc) for dc in range(8)] + [('ox', b)])
                k.barrier()
                k.flush()


        xTv = xT.rearrange("(c p) t -> p c t", p=128)
        yTv = yT.rearrange("(c p) t -> p c t", p=128)
        HALVES = [(0, T // 2), (T // 2, T // 2 + NS)]

        def hkeys(s0):
            return [('hT', c, s0 // 512) for c in range(8)]

        def gemm_fm(p, wt, wkey, col0, hT, s0, W_, kc=8, M=128):
            for c in range(kc):
                k.op('pe', I('matmul', PS(p, W_)[:M, :], wt[:, c, col0:col0 + M], hT[:, c, s0:s0 + W_],
                             start=(c == 0), stop=(c == kc - 1)),
                     r=[wkey] + hkeys(s0), w=[('ps', p)])

        def rows_to_fm(es, src, R, ncols, dst, dkey, tag):
            if '_rowtmp' not in wst:
                wst['_rowtmp'] = sb(es, 'rowtmp' + tag, [64, 2048])
            tmp = wst['_rowtmp']
            k.dma(tmp[:R, :ncols], src, w=['rowtmp'])
            for c in range(ncols // 128):
                p = nps()
                k.op('pe', I('transpose', PS(p, R), tmp[:R, c * 128:(c + 1) * 128], ident[:R, :R]),
                     r=['rowtmp'], w=[('ps', p)])
                k.op('dve', I('tensor_copy', dst[:, c, 0:R], PS(p, R)), r=[('ps', p)], w=[dkey])

        def fm_to_rows(es, srcs, skeys, R, dst, tag):
            if '_rowtmp' not in wst:
                wst['_rowtmp'] = sb(es, 'rowtmp' + tag, [64, 2048])
            tmp = wst['_rowtmp']
            for c, a in enumerate(srcs):
                p = nps()
                k.op('pe', I('transpose', PS(p, 128)[:R, :], a, ident[:]), r=skeys, w=[('ps', p)])
                k.op('dve', I('tensor_copy', tmp[:R, c * 128:(c + 1) * 128], PS(p, 128)[:R, :]),
                     r=[('ps', p)], w=['rowtmp'])
            k.dma(dst, tmp[:R, :len(srcs) * 128], r=['rowtmp'])

        def conv_layer(layer=1):
            w_in = di['b_w_in']
            with ExitStack() as es:
                wst.clear()
                wc = sb(es, 'cv_wc', [128, 16, 3])
                rows_to_fm(es, di['b_conv_w'], 3, 2048, wc, 'cv_wc', 'cw')
                stT = sb(es, 'cv_st', [128, 16, NS * 2])
                rows_to_fm(es, di['st_conv'].rearrange("b j n -> (b j) n"), NS * 2, 2048, stT, 'cv_st', 'cs')
                halo = sb(es, 'cv_halo', [128, 16, 2])
                k.op('dve', I('memset', halo[:], 0.0), w=['cv_halo'])
                so = sb(es, 'cv_so', [128, 16, NS * 2])
                wf = [sb(es, 'cv_wf%d' % i, [128, 8, 512], BF16) for i in range(2)]
                cx = sb(es, 'cv_cx', [128, 2 + T // 2 + NS])
                yo = [sb(es, 'cv_yo%d' % i, [128, T // 2 + NS], BF16) for i in range(2)]
                cgs = [sb(es, 'cv_cg%d' % i, [128, 512]) for i in range(2)]
                zs = [sb(es, 'cv_zs%d' % i, [128, 512]) for i in range(2)]
                acc = [sb(es, 'cv_acc%d' % i, [128, 512]) for i in range(2)]
                ys = sb(es, 'cv_ys', [128, NS])
                hT = sb(es, 'cv_hT', [128, 8, T // 2 + NS], BF16)
                nw = 0

                def cv_load(n_, it):
                    for q_ in range(4):
                        load_w(es, wf[n_ % 2][:, :, q_ * 128:(q_ + 1) * 128], ('cv_wf', n_ % 2, q_),
                               w_in[:, q_ * 2048 + it[1] * 128:q_ * 2048 + (it[1] + 1) * 128], 8, 128, 'cv')

                cvpre = Pre([(hi_, f_) for hi_ in range(2) for f_ in range(16)], cv_load, 1)
                for hi, (t0, TW) in enumerate(HALVES):
                    norm_half(es, di['norm_g'][layer], t0, TW, hT, 'cv%d' % hi)
                    TP = T // 2
                    for f in range(16):
                        wb = nw % 2
                        cvpre.need(nw)
                        nw += 1
                        wkeys = [('cv_wf', wb, q_) for q_ in range(4)]
                        k.op('act', I('copy', cx[:, 0:2], halo[:, f, :]), r=['cv_halo'], w=[('cv_cx', 'h')])
                        tl = [(s0, 512) for s0 in range(0, TP, 512)]
                        if TW > TP:
                            tl.append((TP, NS))
                        for n, (s0, W_) in enumerate(tl):
                            b = n % 2
                            pc, px, pz, pb = nps(), nps(), nps(), nps()
                            for q_, p in ((1, pc), (2, px), (3, pz), (0, pb)):
                                for c in range(8):
                                    k.op('pe', I('matmul', PS(p, W_), wf[wb][:, c, q_ * 128:(q_ + 1) * 128],
                                                 hT[:, c, s0:s0 + W_], start=(c == 0), stop=(c == 7)),
                                         r=[('cv_wf', wb, q_)] + hkeys(s0), w=[('ps', p)])
                            k.op('act', I('copy', cgs[b][:, :W_], PS(pc, W_)), r=[('ps', pc)], w=[('cv_cg', b)])
                            k.op('dve', I('tensor_tensor', cx[:, 2 + s0:2 + s0 + W_], cgs[b][:, :W_], PS(px, W_), ALU.mult),
                                 r=[('cv_cg', b), ('ps', px)], w=[('cv_cx', n)])
                            k.op('act', I('activation', zs[b][:, :W_], PS(pz, W_), AF.Silu), r=[('ps', pz)], w=[('cv_zs', b)])
                            if W_ == 512:
                                rk = [('cv_cx', n), ('cv_cx', n - 1) if n > 0 else ('cv_cx', 'h')]
                                k.op('act', I('mul', acc[b][:, :W_], cx[:, s0:s0 + W_], wc[:, f, 0:1]),
                                     r=rk + ['cv_wc'], w=[('cv_acc', b)])
                                k.op('dve', I('scalar_tensor_tensor', acc[b][:, :W_], cx[:, 1 + s0:1 + s0 + W_], wc[:, f, 1:2],
                                               acc[b][:, :W_], ALU.mult, ALU.add), r=rk + [('cv_acc', b)], w=[('cv_acc', b)])
                                k.op('dve', I('scalar_tensor_tensor', acc[b][:, :W_], cx[:, 2 + s0:2 + s0 + W_], wc[:, f, 2:3],
                                               acc[b][:, :W_], ALU.mult, ALU.add), r=rk + [('cv_acc', b)], w=[('cv_acc', b)])
                                accv = acc[b][:, :W_]
                                ak = ('cv_acc', b)
                            else:
                                stv = stT[:, f, :].rearrange("p (b j) -> p b j", j=2)
                                k.op('act', I('mul', ys[:, :], stv[:, :, 0], wc[:, f, 0:1]),
                                     r=['cv_st', 'cv_wc'], w=['cv_ys'])
                                k.op('dve', I('scalar_tensor_tensor', ys[:, :], stv[:, :, 1], wc[:, f, 1:2], ys[:, :],
                                               ALU.mult, ALU.add), r=['cv_st', 'cv_ys'], w=['cv_ys'])
                                k.op('dve', I('scalar_tensor_tensor', ys[:, :], cx[:, 2 + s0:2 + s0 + W_], wc[:, f, 2:3], ys[:, :],
                                               ALU.mult, ALU.add), r=[('cv_cx', n), 'cv_ys'], w=['cv_ys'])
                                sov = so[:, f, :].rearrange("p (b j) -> p b j", j=2)
                                k.op('act', I('copy', sov[:, :, 0], stv[:, :, 1]), r=['cv_st'], w=[('cv_so', f, 0)])
                                k.op('act', I('copy', sov[:, :, 1], cx[:, 2 + s0:2 + s0 + W_]), r=[('cv_cx', n)], w=[('cv_so', f, 1)])
                                accv = ys[:, :]
                                ak = 'cv_ys'
                            k.op('dve', I('tensor_tensor', zs[b][:, :W_], zs[b][:, :W_], accv, ALU.mult),
                                 r=[ak, ('cv_zs', b)], w=[('cv_zs', b)])
                            k.op('dve', I('tensor_tensor', yo[wb][:, s0:s0 + W_], zs[b][:, :W_], PS(pb, W_), ALU.mult),
                                 r=[('cv_zs', b), ('ps', pb)], w=[('cv_yo', wb, n)])
                        k.op('act', I('copy', halo[:, f, :], cx[:, TP:TP + 2]),
                             r=[('cv_cx', len(tl) - 1 - (1 if TW > TP else 0))], w=['cv_halo'])
                        k.dma(yTv[:, f, t0:t0 + TW], yo[wb][:, :TW], r=[('cv_yo', wb, n) for n in range(len(tl))])
                k.barrier()
                fm_to_rows(es, [halo[:, f, :] for f in range(16)], ['cv_halo'], 2, di['o_convp'], 'cp')
                fm_to_rows(es, [so[:, f, :] for f in range(16)], [('cv_so', f, j) for f in range(16) for j in range(2)],
                           NS * 2, di['o_convs'].rearrange("b j n -> (b j) n"), 'cq')
                k.barrier()
                k.flush()
            out_phase(layer, di['b_w_out'], 16)


        def pool_layer(layer=3):
            w_in = di['d_w_in']
            TP = T // 2
            with ExitStack() as es:
                wst.clear()
                scT = sb(es, 'pl_sc', [128, 16, 1])
                rows_to_fm(es, di['d_scale'].rearrange("(o n) -> o n", o=1), 1, 2048, scT, 'pl_sc', 'ps')
                stT = sb(es, 'pl_st', [128, 16, NS * 15])
                rows_to_fm(es, di['st_pool'].rearrange("b j n -> (b j) n"), NS * 15, 2048, stT, 'pl_st', 'pt')
                invc = sb(es, 'pl_invc', [128, 256])
                k.dma(invc[:], di['c_invcnt'], w=['pl_invc'])
                halo = sb(es, 'pl_halo', [128, 16, 15])
                k.op('dve', I('memset', halo[:], 0.0), w=['pl_halo'])
                so = sb(es, 'pl_so', [128, 16, NS * 15])
                wf = [sb(es, 'pl_wf%d' % i, [128, 8, 256], BF16) for i in range(2)]
                wg = sb(es, 'pl_wg', [128, 4, 512], BF16)
                xpb = sb(es, 'pl_xp', [128, 15 + TP + NS])
                sA = sb(es, 'pl_sA', [128, 15 + TP])
                sB = sb(es, 'pl_sB', [128, 15 + TP])
                rb = sb(es, 'pl_rb', [128, 4, TP + NS], BF16)
                zsb = sb(es, 'pl_zs', [128, 4, TP + NS], BF16)
                yo = [sb(es, 'pl_yo%d' % i, [128, TP + NS], BF16) for i in range(2)]
                t16 = sb(es, 'pl_t16', [128, 16])
                red = sb(es, 'pl_red', [128, NS])
                ytmp = [sb(es, 'pl_yt%d' % i, [128, 512]) for i in range(2)]
                hT = sb(es, 'pl_hT', [128, 8, TP + NS], BF16)
                nw = 0
                ny = 0

                def pl_load(n_, it):
                    for q_ in range(2):
                        load_w(es, wf[n_ % 2][:, :, q_ * 128:(q_ + 1) * 128], ('pl_wf', n_ % 2, q_),
                               w_in[:, q_ * 2048 + it * 128:q_ * 2048 + (it + 1) * 128], 8, 128, 'pl')

                plpre = Pre([f_ for hi_ in range(2) for f_ in range(16)], pl_load, 1)
                for hi, (t0, TW) in enumerate(HALVES):
                    norm_half(es, di['norm_g'][layer], t0, TW, hT, 'pl%d' % hi)
                    tl = [(s0, 512) for s0 in range(0, TP, 512)]
                    if TW > TP:
                        tl.append((TP, NS))
                    for g in range(4):
                        wdw = (2, 4, 8, 16)[g]
                        load_w(es, wg, 'pl_wg', di['d_w_grp'][g], 4, 512, 'pl')
                        for fi in range(4):
                            f = 4 * g + fi
                            wb = nw % 2
                            plpre.need(nw)
                            nw += 1
                            k.op('act', I('copy', xpb[:, 0:15], halo[:, f, :]), r=['pl_halo'], w=['pl_xp'])
                            for n, (s0, W_) in enumerate(tl):
                                px, pz = nps(), nps()
                                for q_, p in enumerate((px, pz)):
                                    for c in range(8):
                                        k.op('pe', I('matmul', PS(p, W_), wf[wb][:, c, q_ * 128:(q_ + 1) * 128],
                                                     hT[:, c, s0:s0 + W_], start=(c == 0), stop=(c == 7)),
                                             r=[('pl_wf', wb, q_)] + hkeys(s0), w=[('ps', p)])
                                k.op('act', I('copy', xpb[:, 15 + s0:15 + s0 + W_], PS(px, W_)), r=[('ps', px)], w=['pl_xp'])
                                k.op('act', I('activation', zsb[:, fi, s0:s0 + W_], PS(pz, W_), AF.Silu),
                                     r=[('ps', pz)], w=[('pl_zs', fi)])
                            cur, ck = xpb, 'pl_xp'
                            step, lo = 1, 0
                            pp = [(sA, 'pl_sA'), (sB, 'pl_sB')]
                            ip = 0
                            while step < wdw:
                                nxt, nk = pp[ip % 2]
                                ip += 1
                                lo2 = lo + step
                                k.op('dve', I('tensor_tensor', nxt[:, lo2:15 + TP], cur[:, lo2:15 + TP],
                                              cur[:, lo2 - step:15 + TP - step], ALU.add), r=[ck], w=[nk])
                                cur, ck, lo, step = nxt, nk, lo2, step * 2
                            k.op('dve', I('scalar_tensor_tensor', rb[:, fi, 0:TP], cur[:, 15:15 + TP], 1.0 / wdw,
                                          xpb[:, 15:15 + TP], ALU.mult, ALU.subtract), r=[ck, 'pl_xp'], w=[('pl_rb', fi)])
                            if hi == 0:
                                k.op('dve', I('tensor_tensor', t16[:], cur[:, 15:31], invc[:, f * 16:(f + 1) * 16], ALU.mult),
                                     r=[ck, 'pl_invc'], w=['pl_t16'])
                                k.op('dve', I('tensor_tensor', rb[:, fi, 0:16], t16[:], xpb[:, 15:31], ALU.subtract),
                                     r=['pl_t16', 'pl_xp'], w=[('pl_rb', fi)])
                            if TW > TP:
                                stv = stT[:, f, :].rearrange("p (b j) -> p b j", j=15)
                                xs_ = xpb[:, 15 + TP:15 + TP + NS]
                                k.op('dve', I('tensor_reduce', red[:], stv[:, :, 15 - (wdw - 1):15], AX.X, ALU.add),
                                     r=['pl_st'], w=['pl_red'])
                                k.op('dve', I('tensor_tensor', red[:], red[:], xs_, ALU.add), r=['pl_red', 'pl_xp'], w=['pl_red'])
                                k.op('dve', I('scalar_tensor_tensor', rb[:, fi, TP:TP + NS], red[:], 1.0 / wdw, xs_,
                                              ALU.mult, ALU.subtract), r=['pl_red', 'pl_xp'], w=[('pl_rb', fi)])
                                sov = so[:, f, :].rearrange("p (b j) -> p b j", j=15)
                                k.op('act', I('copy', sov[:, :, 0:14], stv[:, :, 1:15]), r=['pl_st'], w=[('pl_so', f, 0)])
                                k.op('act', I('copy', sov[:, :, 14], xs_), r=['pl_xp'], w=[('pl_so', f, 1)])
                            k.op('act', I('copy', halo[:, f, :], xpb[:, TP:TP + 15]), r=['pl_xp'], w=['pl_halo'])
                        for fo in range(4):
                            f = 4 * g + fo
                            yb = ny % 2
                            ny += 1
                            for n, (s0, W_) in enumerate(tl):
                                p = nps()
                                for c in range(4):
                                    k.op('pe', I('matmul', PS(p, W_), wg[:, c, fo * 128:(fo + 1) * 128], rb[:, c, s0:s0 + W_],
                                                 start=(c == 0), stop=(c == 3)),
                                         r=['pl_wg'] + [('pl_rb', c) for c in range(4)], w=[('ps', p)])
                                b = n % 2
                                k.op('act', I('mul', ytmp[b][:, :W_], PS(p, W_), scT[:, f, 0:1]), r=[('ps', p), 'pl_sc'], w=[('pl_yt', b)])
                                k.op('dve', I('tensor_tensor', yo[yb][:, s0:s0 + W_], ytmp[b][:, :W_], zsb[:, fo, s0:s0 + W_], ALU.mult),
                                     r=[('pl_yt', b), ('pl_zs', fo)], w=[('pl_yo', yb, n)])
                            k.dma(yTv[:, f, t0:t0 + TW], yo[yb][:, :TW], r=[('pl_yo', yb, n) for n in range(len(tl))])
                k.barrier()
                fm_to_rows(es, [halo[:, f, :] for f in range(16)], ['pl_halo'], 15, di['o_poolp'], 'pp')
                fm_to_rows(es, [so[:, f, :] for f in range(16)], [('pl_so', f, j) for f in range(16) for j in range(2)],
                           NS * 15, di['o_pools'].rearrange("b j n -> (b j) n"), 'pq')
                k.barrier()
                k.flush()
            out_phase(layer, di['d_w_out'], 16)


        def mlstm_layer(layer=0):
            w_in = di['a_w_in']
            TP = T // 2
            NU = TP // 128 + NS
            with ExitStack() as es:
                wst.clear()
                triu = sb(es, 'ml_triu', [128, 128]); k.dma(triu[:], di['c_triu'], w=['ml_triu'])
                mneg = sb(es, 'ml_mneg', [128, 128]); k.dma(mneg[:], di['c_maskneg'], w=['ml_mneg'])
                sell = sb(es, 'ml_sell', [128, 128]); k.dma(sell[:], di['c_sel_last'], w=['ml_sell'])
                bif = sb(es, 'ml_bif', [128, 16]); k.dma(bif[:], bcast_row(di['a_b_if'], 16), w=['ml_bif'])
                ng = sb(es, 'ml_ng', [128, 2048]); k.dma(ng[:], bcast_row(di['a_norm_g'], 2048), w=['ml_ng'])
                triub = sb(es, 'ml_triub', [128, 128], BF16); sellb = sb(es, 'ml_sellb', [128, 128], BF16)
                k.op('dve', I('tensor_copy', triub[:], triu[:]), r=['ml_triu'], w=['ml_triu'])
                k.op('dve', I('tensor_copy', sellb[:], sell[:]), r=['ml_sell'], w=['ml_sell'])
                hlA = sb(es, 'ml_hlA', [128, 128], BF16); hlB = sb(es, 'ml_hlB', [128, 128], BF16)

                def mm_hl(p, pv, lhsT, src, L, n, rkeys):
                    k.op('dve', I('tensor_copy', hlA[:L, :n], src), r=rkeys, w=['ml_hlA'])
                    k.op('dve', I('tensor_tensor', hlB[:L, :n], src, hlA[:L, :n], ALU.subtract), r=rkeys + ['ml_hlA'], w=['ml_hlB'])
                    k.op('pe', I('matmul', pv, lhsT, hlA[:L, :n], start=True, stop=False), r=['ml_hlA', 'ml_triu', 'ml_sell'], w=[('ps', p)])
                    k.op('pe', I('matmul', pv, lhsT, hlB[:L, :n], start=False, stop=True), r=['ml_hlB'], w=[('ps', p)])

                wgate = sb(es, 'ml_wgate', [128, 8, 16], BF16)
                load_w(es, wgate, 'ml_wgate', w_in[:, 8192:8208], 8, 16, 'ml')
                CT = sb(es, 'ml_CT', [128, 8, 257])
                CTb = sb(es, 'ml_CTb', [128, 257], BF16)
                mprev = sb(es, 'ml_mprev', [128, 8])
                k.op('dve', I('memset', CT[:], 0.0), w=[('ml_CT', h) for h in range(8)])
                k.op('dve', I('memset', mprev[:], 0.0), w=['ml_mprev'])
                col3 = sb(es, 'ml_col3', [128, NU, 24])
                ccol = sb(es, 'ml_ccol', [128, NU, 8])
                negm = sb(es, 'ml_negm', [128, NU, 8])
                expnegm = sb(es, 'ml_enm', [128, NU, 8])
                wcol = sb(es, 'ml_wcol', [128, NU, 8])
                bcs = sb(es, 'ml_bcs', [128, NU, 24])
                gs = sb(es, 'ml_gs', [128, 16]); lp = sb(es, 'ml_lp', [128, 8]); mxa = sb(es, 'ml_mx', [128, 8])
                inter = sb(es, 'ml_inter', [128, 8]); tmp8 = sb(es, 'ml_tmp8', [128, 8])
                from types import SimpleNamespace
                BS = []
                NBU = 4
                for par in range(NBU):
                    B = SimpleNamespace()
                    B.par = par
                    B.diagc = sb(es, 'ml_diagc', [128, 128]); B.logd = sb(es, 'ml_logd', [128, 128]); B.Dm = sb(es, 'ml_Dm', [128, 128])
                    B.Pm = sb(es, 'ml_P', [128, 128], BF16); B.PTs = sb(es, 'ml_PT', [128, 128], BF16)
                    B.ktok = sb(es, 'ml_ktok', [128, 128], BF16)
                    B.v1 = sb(es, 'ml_v1', [128, 257], BF16); B.wv = sb(es, 'ml_wv', [128, 257], BF16)
                    k.op('dve', I('memset', B.v1[:, 256:257], 1.0), w=[('ml_v1', par)])
                    B.og = sb(es, 'ml_og', [128, 256]); B.ez = sb(es, 'ml_ez', [128, 512]); B.zs = sb(es, 'ml_zs', [128, 256])
                    B.intra = sb(es, 'ml_intra', [128, 257]); B.nd = sb(es, 'ml_nd', [128, 257])
                    B.den = sb(es, 'ml_den', [128, 1]); B.ssq = sb(es, 'ml_ssq', [128, 1]); B.junk = BS[0].junk if BS else sb(es, 'ml_junk', [128, 256])
                    B.hs = sb(es, 'ml_hs', [128, 256]); B.yb = sb(es, 'ml_yb', [128, 256], BF16)
                    B.hlA = sb(es, 'ml_hlA2', [128, 128], BF16); B.hlB = sb(es, 'ml_hlB2', [128, 128], BF16)
                    BS.append(B)
                diagc = BS[0].diagc; logd = BS[0].logd
                qTb = sb(es, 'ml_qTb', [128, 512], BF16); kTb = sb(es, 'ml_kTb', [128, 512], BF16)
                yTs = sb(es, 'ml_yTs', [128, 2, TP + NS], BF16)
                ctmp = sb(es, 'ml_ctmp', [128, 2, 128])
                CTs = sb(es, 'ml_CTs', [128, NS, 257]); CTbs = [sb(es, 'ml_CTbs', [128, 257], BF16) for _ in range(NS)]
                ctmps = [sb(es, 'ml_ctmps', [128, 2, 128]) for _ in range(NS)]
                wh = [sb(es, 'ml_wh%d' % i, [128, 8, 1024], BF16) for i in range(2)]
                hT = sb(es, 'ml_hT', [128, 8, TP + NS], BF16)
                SC = 128.0 ** -0.5

                def logd_unit(u, h, L, B=None):
                    dg, ld, kd, kl = (diagc, logd, 'ml_diagc', 'ml_logd') if B is None else (B.diagc, B.logd, ('ml_diagc', B.par), ('ml_logd', B.par))
                    k.op('dve', I('tensor_scalar', dg[:L, :L], ident[:L, :L], ccol[:L, u, h:h + 1], None, ALU.mult),
                         r=[('col', u)], w=[kd])
                    p = nps()
                    if B is None:
                        mm_hl(p, PS(p, L)[:L, :], onesb[:L, :L], dg[:L, :L], L, L, [kd])
                    else:
                        k.op('dve', I('tensor_copy', B.hlA[:L, :L], dg[:L, :L]), r=[kd], w=[('ml_hlA', B.par)])
                        k.op('dve', I('tensor_tensor', B.hlB[:L, :L], dg[:L, :L], B.hlA[:L, :L], ALU.subtract), r=[kd, ('ml_hlA', B.par)], w=[('ml_hlB', B.par)])
                        k.op('pe', I('matmul', PS(p, L)[:L, :], onesb[:L, :L], B.hlA[:L, :L], start=True, stop=False), r=[('ml_hlA', B.par)], w=[('ps', p)])
                        k.op('pe', I('matmul', PS(p, L)[:L, :], onesb[:L, :L], B.hlB[:L, :L], start=False, stop=True), r=[('ml_hlB', B.par)], w=[('ps', p)])
                    k.op('dve', I('scalar_tensor_tensor', ld[:L, :L], PS(p, L)[:L, :], col3[:L, u, h:h + 1], mneg[:L, :L],
                                  ALU.add, ALU.add), r=[('ps', p), ('col', u), 'ml_mneg'], w=[kl])

                def gate_unit(u, c0, L):
                    p = nps()
                    for c in range(8):
                        k.op('pe', I('matmul', PS(p, 16)[:L, :], hT[:, c, c0:c0 + L], wgate[:, c, :], start=(c == 0), stop=(c == 7)),
                             r=['ml_wgate'] + hkeys(c0), w=[('ps', p)])
                    k.op('dve', I('tensor_tensor', gs[:L, :], PS(p, 16)[:L, :], bif[:L, :], ALU.add), r=[('ps', p), 'ml_bif'], w=['ml_gs'])
                    k.op('act', I('activation', lp[:L, :], gs[:L, 8:16], AF.Exp, scale=-1.0), r=['ml_gs'], w=['ml_lp'])
                    k.op('act', I('activation', lp[:L, :], lp[:L, :], AF.Ln, bias=onesf[:L, 0:1]), r=['ml_lp'], w=['ml_lp'])
                    p2 = nps()
                    mm_hl(p2, PS(p2, 8)[:L, :], triub[:L, :L], lp[:L, :], L, 8, ['ml_lp'])
                    k.op('act', I('mul', col3[:L, u, 0:8], PS(p2, 8)[:L, :], -1.0), r=[('ps', p2)], w=[('col', u)])
                    k.op('dve', I('tensor_tensor', ccol[:L, u, :], gs[:L, 0:8], PS(p2, 8)[:L, :], ALU.add),
                         r=[('ps', p2), 'ml_gs'], w=[('col', u)])
                    def gate_head(h, B):
                        P_ = B.par
                        K_ = lambda nm: (nm, P_)
                        k.op('act', I('mul', B.diagc[:L, :L], ident[:L, :L], ccol[:L, u, h:h + 1]), r=[('col', u)], w=[K_('ml_diagc')])
                        yield
                        k.op('act', I('copy', B.hlA[:L, :L], B.diagc[:L, :L]), r=[K_('ml_diagc')], w=[K_('ml_hlA')])
                        yield
                        k.op('pool', I('tensor_tensor', B.hlB[:L, :L], B.diagc[:L, :L], B.hlA[:L, :L], ALU.subtract), r=[K_('ml_diagc'), K_('ml_hlA')], w=[K_('ml_hlB')])
                        yield
                        p = psalloc()
                        k.op('pe', I('matmul', PS(p, L)[:L, :], onesb[:L, :L], B.hlA[:L, :L], start=True, stop=False), r=[K_('ml_hlA')], w=[('ps', p)])
                        k.op('pe', I('matmul', PS(p, L)[:L, :], onesb[:L, :L], B.hlB[:L, :L], start=False, stop=True), r=[K_('ml_hlB')], w=[('ps', p)])
                        yield
                        k.op('dve', I('scalar_tensor_tensor', B.logd[:L, :L], PS(p, L)[:L, :], col3[:L, u, h:h + 1], mneg[:L, :L],
                                      ALU.add, ALU.add), r=[('ps', p), ('col', u), 'ml_mneg'], w=[K_('ml_logd')])
                        psfree(p)
                        yield
                        k.op('dve', I('tensor_reduce', mxa[:L, h:h + 1], B.logd[:L, :L], AX.X, ALU.max), r=[K_('ml_logd')], w=[('ml_mx', h)])

                    for h0 in range(0, 8, NBU):
                        lockstep([gate_head(h, BS[h - h0]) for h in range(h0, min(h0 + NBU, 8))])
                    k.op('dve', I('tensor_tensor', inter[:L, :], col3[:L, u, 0:8], mprev[:L, :], ALU.add),
                         r=[('col', u), 'ml_mprev'], w=['ml_inter'])
                    k.op('dve', I('tensor_tensor', col3[:L, u, 8:16], inter[:L, :], mxa[:L, :], ALU.max),
                         r=['ml_inter'] + [('ml_mx', hh) for hh in range(8)], w=[('col', u)])
                    k.op('dve', I('tensor_tensor', tmp8[:L, :], inter[:L, :], col3[:L, u, 8:16], ALU.subtract),
                         r=['ml_inter', ('col', u)], w=['ml_tmp8'])
                    k.op('act', I('activation', col3[:L, u, 16:24], tmp8[:L, :], AF.Exp), r=['ml_tmp8'], w=[('col', u)])
                    k.op('act', I('mul', negm[:L, u, :], col3[:L, u, 8:16], -1.0), r=[('col', u)], w=[('col', u)])
                    k.op('act', I('activation', expnegm[:L, u, :], col3[:L, u, 8:16], AF.Exp, scale=-1.0), r=[('col', u)], w=[('col', u)])
                    p3 = nps()
                    lsel = sellb[:, :] if L == 128 else onesb[0:1, :]
                    mm_hl(p3, PS(p3, 24), lsel, col3[:L, u, :], L, 24, [('col', u)])
                    k.op('act', I('copy', bcs[:, u, :], PS(p3, 24)), r=[('ps', p3)], w=[('bcs', u)])
                    k.op('dve', I('tensor_tensor', tmp8[:L, :], ccol[:L, u, :], bcs[:L, u, 0:8], ALU.add),
                         r=[('col', u), ('bcs', u)], w=['ml_tmp8'])
                    k.op('dve', I('tensor_tensor', tmp8[:L, :], tmp8[:L, :], bcs[:L, u, 8:16], ALU.subtract),
                         r=['ml_tmp8', ('bcs', u)], w=['ml_tmp8'])
                    k.op('act', I('activation', wcol[:L, u, :], tmp8[:L, :], AF.Exp), r=['ml_tmp8'], w=[('col', u)])
                    k.op('dve', I('tensor_copy', mprev[:, :], bcs[:, u, 8:16]), r=[('bcs', u)], w=['ml_mprev'])

                def stage_a(h, u, c0, L, w_, wk, B):
                    P_ = B.par
                    K_ = lambda nm: (nm, P_)
                    dg, ld = B.diagc, B.logd
                    p1 = psalloc()
                    for c in range(8):
                        k.op('pe', I('matmul', PS(p1, 384)[:L, :], hT[:, c, c0:c0 + L], w_[:, c, 128:512], start=(c == 0), stop=(c == 7)),
                             r=[wk] + hkeys(c0), w=[('ps', p1)])
                    k.op('act', I('mul', dg[:L, :L], ident[:L, :L], ccol[:L, u, h:h + 1]), r=[('col', u)], w=[K_('ml_diagc')])
                    yield
                    k.op('act', I('copy', B.hlA[:L, :L], dg[:L, :L]), r=[K_('ml_diagc')], w=[K_('ml_hlA')])
                    k.op('act', I('mul', B.ktok[:L, :], PS(p1, 384)[:L, 0:128], SC), r=[('ps', p1)], w=[K_('ml_ktok')])
                    k.op('act', I('copy', B.v1[:L, 0:256], PS(p1, 384)[:L, 128:384]), r=[('ps', p1)], w=[K_('ml_v1')])
                    psfree(p1)
                    p2 = psalloc()
                    for c in range(8):
                        k.op('pe', I('matmul', PS(p2, 512)[:L, :], hT[:, c, c0:c0 + L], w_[:, c, 512:1024], start=(c == 0), stop=(c == 7)),
                             r=[wk] + hkeys(c0), w=[('ps', p2)])
                    yield
                    k.op('pool', I('tensor_tensor', B.hlB[:L, :L], dg[:L, :L], B.hlA[:L, :L], ALU.subtract), r=[K_('ml_diagc'), K_('ml_hlA')], w=[K_('ml_hlB')])
                    k.op('act', I('activation', B.ez[:L, :], PS(p2, 512)[:L, :], AF.Exp, scale=-1.0), r=[('ps', p2)], w=[K_('ml_ez')])
                    k.op('act', I('copy', B.zs[:L, :], PS(p2, 512)[:L, 256:512]), r=[('ps', p2)], w=[K_('ml_zs')])
                    psfree(p2)
                    k.op('dve', I('tensor_scalar', B.wv[:L, :], B.v1[:L, :], wcol[:L, u, h:h + 1], None, ALU.mult), r=[K_('ml_v1'), ('col', u)], w=[K_('ml_wv')])
                    yield
                    p = psalloc()
                    k.op('pe', I('matmul', PS(p, L)[:L, :], onesb[:L, :L], B.hlA[:L, :L], start=True, stop=False), r=[K_('ml_hlA')], w=[('ps', p)])
                    k.op('pe', I('matmul', PS(p, L)[:L, :], onesb[:L, :L], B.hlB[:L, :L], start=False, stop=True), r=[K_('ml_hlB')], w=[('ps', p)])
                    k.op('dve', I('tensor_scalar', B.ez[:L, :], B.ez[:L, :], 1.0, None, ALU.add), r=[K_('ml_ez')], w=[K_('ml_ez')])
                    yield
                    k.op('dve', I('scalar_tensor_tensor', ld[:L, :L], PS(p, L)[:L, :], col3[:L, u, h:h + 1], mneg[:L, :L],
                                  ALU.add, ALU.add), r=[('ps', p), ('col', u), 'ml_mneg'], w=[K_('ml_logd')])
                    psfree(p)
                    k.op('pool', I('tensor_tensor', B.ez[:L, 0:256], B.ez[:L, 0:256], B.ez[:L, 256:512], ALU.mult), r=[K_('ml_ez')], w=[K_('ml_ez')])
                    yield
                    k.op('act', I('activation', B.Dm[:L, :L], ld[:L, :L], AF.Exp, bias=negm[:L, u, h:h + 1]),
                         r=[K_('ml_logd'), ('col', u)], w=[K_('ml_Dm')])
                    ps_ = psalloc()
                    k.op('pe', I('matmul', PS(ps_, L)[:L, :], B.qT[:, :L], B.kT[:, :L], start=True, stop=True),
                         r=[('ml_qTb', B.qk), ('ml_kTb', B.qk)], w=[('ps', ps_)])
                    k.op('dve', I('reciprocal', B.ez[:L, 0:256], B.ez[:L, 0:256]), r=[K_('ml_ez')], w=[K_('ml_ez')])
                    yield
                    k.op('dve', I('tensor_tensor', B.Pm[:L, :L], PS(ps_, L)[:L, :], B.Dm[:L, :L], ALU.mult), r=[('ps', ps_), K_('ml_Dm')], w=[K_('ml_P')])
                    psfree(ps_)
                    k.op('dve', I('tensor_tensor', B.og[:L, :], B.zs[:L, :], B.ez[:L, 0:256], ALU.mult), r=[K_('ml_zs'), K_('ml_ez')], w=[K_('ml_og')])
                    yield
                    pb_ = P_ % 2
                    k.op('pe', I('transpose', PSB(pb_, L)[:L, :], B.Pm[:L, :L], identb[:L, :L]), r=[K_('ml_P')], w=[('psb', pb_)])
                    k.op('act', I('copy', B.PTs[:L, :L], PSB(pb_, L)[:L, :]), r=[('psb', pb_)], w=[K_('ml_PT')])
                    yield
                    pi = psalloc()
                    k.op('pe', I('matmul', PS(pi, 257)[:L, :], B.PTs[:L, :L], B.v1[:L, :], start=True, stop=True),
                         r=[K_('ml_PT'), K_('ml_v1')], w=[('ps', pi)])
                    yield
                    k.op('act', I('copy', B.intra[:L, :], PS(pi, 257)[:L, :]), r=[('ps', pi)], w=[K_('ml_intra')])
                    psfree(pi)

                def stage_c(h, u, c0, L, B, st=None):
                    P_ = B.par
                    K_ = lambda nm: (nm, P_)
                    CTv, CTbv, ck, bk = (CT[:, h, :], CTb, ('ml_CT', h), 'ml_CTb') if st is None else st
                    pj = nps()
                    k.op('pe', I('matmul', PS(pj, 257)[:L, :], B.qT[:, :L], CTbv[:, :], start=True, stop=True),
                         r=[('ml_qTb', B.qk), bk], w=[('ps', pj)])
                    pu = nps()
                    k.op('pe', I('matmul', PS(pu, 257), B.ktok[:L, :], B.wv[:L, :], start=True, stop=True), r=[K_('ml_ktok'), K_('ml_wv')], w=[('ps', pu)])
                    k.op('dve', I('scalar_tensor_tensor', CTv, CTv, bcs[:, u, 16 + h:17 + h], PS(pu, 257), ALU.mult, ALU.add),
                         r=[('ps', pu), ('bcs', u), ck], w=[ck])
                    if st is None:
                        k.op('act', I('copy', CTbv[:, :], CTv), r=[ck], w=[bk])
                    k.op('dve', I('scalar_tensor_tensor', B.nd[:L, :], PS(pj, 257)[:L, :], col3[:L, u, 16 + h:17 + h], B.intra[:L, :],
                                  ALU.mult, ALU.add), r=[('ps', pj), K_('ml_intra'), ('col', u)], w=[K_('ml_nd')])

                def stage_d(h, u, c0, L, B):
                    P_ = B.par
                    K_ = lambda nm: (nm, P_)
                    k.op('dve', I('scalar_tensor_tensor', B.den[:L, :], B.nd[:L, 256:257], -1.0, B.nd[:L, 256:257], ALU.mult, ALU.max),
                         r=[K_('ml_nd')], w=[K_('ml_den')])
                    k.op('dve', I('tensor_tensor', B.den[:L, :], B.den[:L, :], expnegm[:L, u, h:h + 1], ALU.max),
                         r=[K_('ml_den'), ('col', u)], w=[K_('ml_den')])
                    k.op('dve', I('reciprocal', B.den[:L, :], B.den[:L, :]), r=[K_('ml_den')], w=[K_('ml_den')])
                    k.op('dve', I('tensor_scalar', B.hs[:L, :], B.nd[:L, 0:256], B.den[:L, 0:1], None, ALU.mult), r=[K_('ml_nd'), K_('ml_den')], w=[K_('ml_hs')])
                    yield
                    k.op('act', I('activation', B.junk[:L, :], B.hs[:L, :], AF.Square, accum_out=B.ssq[:L, :]), r=[K_('ml_hs')], w=[K_('ml_ssq'), 'ml_junk'])
                    k.op('act', I('activation', B.ssq[:L, :], B.ssq[:L, :], AF.Ln, bias=epsc[:L, :], scale=1.0 / 256), r=[K_('ml_ssq')], w=[K_('ml_ssq')])
                    k.op('act', I('activation', B.ssq[:L, :], B.ssq[:L, :], AF.Exp, scale=-0.5), r=[K_('ml_ssq')], w=[K_('ml_ssq')])
                    yield
                    k.op('dve', I('scalar_tensor_tensor', B.hs[:L, :], B.hs[:L, :], B.ssq[:L, 0:1], ng[:L, h * 256:(h + 1) * 256],
                                  ALU.mult, ALU.mult), r=[K_('ml_hs'), K_('ml_ssq'), 'ml_ng'], w=[K_('ml_hs')])
                    yield
                    k.op('pool', I('tensor_tensor', B.yb[:L, :], B.hs[:L, :], B.og[:L, :], ALU.mult), r=[K_('ml_hs'), K_('ml_og')], w=[K_('ml_yb')])
                    yield
                    pb_ = P_ % 2
                    for vc in range(2):
                        k.op('pe', I('transpose', PSB(pb_, 256)[:, vc * 128:vc * 128 + L], B.yb[:L, vc * 128:(vc + 1) * 128], identb[:L, :L]),
                             r=[K_('ml_yb')], w=[('psb', pb_)])
                    for vc in range(2):
                        k.op('act' if vc else 'dve', I('copy' if vc else 'tensor_copy', yTs[:, vc, c0:c0 + L], PSB(pb_, 256)[:, vc * 128:vc * 128 + L]),
                             r=[('psb', pb_)], w=[('ml_yTs', u)])

                def head_setup(h, units, w_, wk, half):
                    bs = [(u, c0, L, BS[half * 2 + i]) for i, (u, c0, L) in enumerate(units)]
                    cb0 = units[0][1]
                    WB = sum(L for (_, _, L) in units)
                    o0 = half * 256
                    for (cc, dst, sc, nm) in ((0, qTb, None, ('ml_qTb', half)), (128, kTb, SC, ('ml_kTb', half))):
                        pq = nps()
                        for c in range(8):
                            k.op('pe', I('matmul', PS(pq, WB), w_[:, c, cc:cc + 128], hT[:, c, cb0:cb0 + WB], start=(c == 0), stop=(c == 7)),
                                 r=[wk] + hkeys(cb0) + hkeys(cb0 + WB - 1), w=[('ps', pq)])
                        if sc is None:
                            k.op('act', I('copy', dst[:, o0:o0 + WB], PS(pq, WB)), r=[('ps', pq)], w=[nm])
                        else:
                            k.op('act', I('mul', dst[:, o0:o0 + WB], PS(pq, WB), sc), r=[('ps', pq)], w=[nm])
                    off = o0
                    for (u, c0, L, B) in bs:
                        B.qT = qTb[:, off:off + 128]
                        B.kT = kTb[:, off:off + 128]
                        B.qk = half
                        off += L
                    return bs

                def batch_tail(h, bs):
                    for (u, c0, L, B) in bs:
                        stage_c(h, u, c0, L, B)
                        yield
                    gens = [stage_d(h, u, c0, L, B) for (u, c0, L, B) in bs]
                    while gens:
                        nxt = []
                        for g_ in gens:
                            try:
                                next(g_)
                                nxt.append(g_)
                            except StopIteration:
                                pass
                        gens = nxt
                        yield

                def head_run(h, unit_batches, w_, wk):
                    prev = None
                    for bi, units in enumerate(unit_batches):
                        bs = head_setup(h, units, w_, wk, bi % 2)
                        gens = [stage_a(h, u, c0, L, w_, wk, B) for (u, c0, L, B) in bs]
                        if prev is not None:
                            gens.append(prev)
                        lockstep(gens)
                        prev = batch_tail(h, bs)
                    if prev is not None:
                        lockstep([prev])

                def state_out(h, dC, dn):
                    for vc in range(2):
                        p = nps()
                        k.op('pe', I('transpose', PS(p, 128), CT[:, h, vc * 128:(vc + 1) * 128], ident[:]), r=[('ml_CT', h)], w=[('ps', p)])
                        k.op('act', I('copy', ctmp[:, vc, :], PS(p, 128)), r=[('ps', p)], w=[('ml_ctmp', vc)])
                    k.dma(dC.rearrange("(vc p) kk -> p vc kk", p=128), ctmp[:], r=[('ml_ctmp', 0), ('ml_ctmp', 1)])
                    k.dma(dn.rearrange("(p o) -> p o", o=1), CT[:, h, 256:257], r=[('ml_CT', h)])

                def state_in(h, sC, sn):
                    k.dma(ctmp[:], sC.rearrange("(vc p) kk -> p vc kk", p=128), w=[('ml_ctmp', 0), ('ml_ctmp', 1)])
                    for vc in range(2):
                        p = nps()
                        k.op('pe', I('transpose', PS(p, 128), ctmp[:, vc, :], ident[:]), r=[('ml_ctmp', vc)], w=[('ps', p)])
                        k.op('act', I('copy', CT[:, h, vc * 128:(vc + 1) * 128], PS(p, 128)), r=[('ps', p)], w=[('ml_CT', h)])
                    k.dma(CT[:, h, 256:257], sn.rearrange("(p o) -> p o", o=1), w=[('ml_CT', h)])
                    k.op('act', I('copy', CTb[:, :], CT[:, h, :]), r=[('ml_CT', h)], w=['ml_CTb'])

                def sample_run(h, w_, wk, npu_):
                    for j in range(NS):
                        k.groups[('ml_CTs', j)] = [(('ml_CTs', j), 0), (('ml_CTs', j), 1), (('ml_CTs', j), 'n')]
                    for j in range(NS):
                        ck, bk = ('ml_CTs', j), ('ml_CTbs', j)
                        k.dma(ctmps[j][:], di['st_C'][j, h].rearrange("(vc p) kk -> p vc kk", p=128), w=[('ml_ctmps', j)])
                        k.dma(CTs[:, j, 256:257], di['st_n'][j, h].rearrange("(p o) -> p o", o=1), w=[(ck, 'n')])
                    for j in range(NS):
                        ck, bk = ('ml_CTs', j), ('ml_CTbs', j)
                        for vc in range(2):
                            p = nps()
                            k.op('pe', I('transpose', PS(p, 128), ctmps[j][:, vc, :], ident[:]), r=[('ml_ctmps', j)], w=[('ps', p)])
                            k.op('act', I('copy', CTs[:, j, vc * 128:(vc + 1) * 128], PS(p, 128)), r=[('ps', p)], w=[(ck, vc)])
                        k.op('act', I('copy', CTbs[j][:, :], CTs[:, j, :]), r=[ck], w=[bk])
                    units = [(npu_ + j, TP + j, 1) for j in range(NS)]
                    bs = [(u, c0, L, BS[i]) for i, (u, c0, L) in enumerate(units)]
                    for (cc, dst, sc, nm) in ((0, qTb, None, ('ml_qTb', 0)), (128, kTb, SC, ('ml_kTb', 0))):
                        pq = nps()
                        for c in range(8):
                            k.op('pe', I('matmul', PS(pq, NS), w_[:, c, cc:cc + 128], hT[:, c, TP:TP + NS], start=(c == 0), stop=(c == 7)),
                                 r=[wk] + hkeys(TP), w=[('ps', pq)])
                        if sc is None:
                            k.op('act', I('copy', dst[:, 0:NS], PS(pq, NS)), r=[('ps', pq)], w=[nm])
                        else:
                            k.op('act', I('mul', dst[:, 0:NS], PS(pq, NS), sc), r=[('ps', pq)], w=[nm])
                    for i, (u, c0, L, B) in enumerate(bs):
                        B.qT = qTb[:, i:i + 128]
                        B.kT = kTb[:, i:i + 128]
                        B.qk = 0
                    lockstep([stage_a(h, u, c0, L, w_, wk, B) for (u, c0, L, B) in bs])
                    for j, (u, c0, L, B) in enumerate(bs):
                        stage_c(h, u, c0, L, B, st=(CTs[:, j, :], CTbs[j], ('ml_CTs', j), ('ml_CTbs', j)))
                    lockstep([stage_d(h, u, c0, L, B) for (u, c0, L, B) in bs])
                    for j in range(NS):
                        ck = ('ml_CTs', j)
                        for vc in range(2):
                            p = nps()
                            k.op('pe', I('transpose', PS(p, 128), CTs[:, j, vc * 128:(vc + 1) * 128], ident[:]), r=[ck], w=[('ps', p)])
                            k.op('act', I('copy', ctmps[j][:, vc, :], PS(p, 128)), r=[('ps', p)], w=[('ml_ctmps', j, vc)])
                        k.dma(di['o_Cs'][j, h].rearrange("(vc p) kk -> p vc kk", p=128), ctmps[j][:], r=[('ml_ctmps', j, 0), ('ml_ctmps', j, 1), ('ml_ctmps', j)])
                        k.dma(di['o_ns'][j, h].rearrange("(p o) -> p o", o=1), CTs[:, j, 256:257], r=[ck])

                nw = 0

                for wb_ in range(2):
                    k.groups[('ml_wh', wb_)] = [('ml_wh', wb_, i_) for i_ in range(5)]

                def ml_load(n_, h_):
                    for i_, (d0, s0_, nn_) in enumerate(((0, h_ * 128, 128), (128, 1024 + h_ * 128, 128), (256, 2048 + h_ * 256, 256),
                                                         (512, 4096 + h_ * 256, 256), (768, 6144 + h_ * 256, 256))):
                        load_w(es, wh[n_ % 2][:, :, d0:d0 + nn_], ('ml_wh', n_ % 2, i_), w_in[:, s0_:s0_ + nn_], 8, nn_, 'ml')

                mlpre = Pre([h_ for hi_ in range(2) for h_ in range(8)], ml_load, 1)
                for hi, (t0, TW) in enumerate(HALVES):
                    norm_half(es, di['norm_g'][layer], t0, TW, hT, 'ml%d' % hi, dbuf=False)
                    npu = TP // 128
                    for u in range(npu):
                        gate_unit(u, u * 128, 128)
                    NSS = 0 if os.environ.get('ML_NOSAMPLE') else NS
                    if hi == 1:
                        k.dma(di['o_mp'], mprev[0:1, :], r=['ml_mprev'])
                        for j in range(NSS):
                            k.dma(mprev[:, :], bass.AP(di['st_m'].tensor, di['st_m'][j].offset, [[0, 128], [1, 8]]), w=['ml_mprev'])
                            gate_unit(npu + j, TP + j, 1)
                            k.dma(di['o_ms'][j:j + 1, :], mprev[0:1, :], r=['ml_mprev'])
                    for h in range(8):
                        wb = nw % 2
                        mlpre.need(nw)
                        nw += 1
                        wk = ('ml_wh', wb)
                        k.op('act', I('copy', CTb[:, :], CT[:, h, :]), r=[('ml_CT', h)], w=['ml_CTb'])
                        head_run(h, [[(u, u * 128, 128) for u in range(u0, min(u0 + 2, npu))] for u0 in range(0, npu, 2)], wh[wb], wk)
                        if hi == 1:
                            state_out(h, di['o_Cp'][h], di['o_np'][h])
                            if NSS:
                                sample_run(h, wh[wb], wk, npu)
                        psmod[0] = 6
                        nuu = npu + (NSS if hi == 1 else 0)
                        for vc in range(2):
                            k.dma(yTv[:, h * 2 + vc, t0:t0 + TW], yTs[:, vc, :TW], r=[('ml_yTs', u) for u in range(nuu)])
                k.barrier()
                k.flush()
            out_phase(layer, di['a_w_out'], 16)


        def attn_layer(layer=2):
            w_in = di['c_w_in']
            TP = T // 2
            GR = ((128, 1), (512, 4), (2048, 16))
            qkT = dscr('qkT', [6, 1024, TA], BF16)
            vtok = dscr('vtok', [3, TA, 1024], BF16)
            zT = dscr('zT', [1024, TA], BF16)
            numS = dscr('numS', [3, T, 8 * 130])
            qs_tok = dscr('qs_tok', [3, 2, NS, 1024])
            vs_tok = dscr('vs_tok', [3, NS, 1024])
            os_tok = dscr('os_tok', [NS, 1024])
            ISQ = 128.0 ** -0.5
            with ExitStack() as es:
                wst.clear()
                permf = sb(es, 'at_permf', [128, 128]); k.dma(permf[:], di['c_ropeperm'], w=['at_perm'])
                permb = sb(es, 'at_permb', [128, 128], BF16)
                k.op('dve', I('tensor_copy', permb[:], permf[:]), r=['at_perm'], w=['at_permb'])
                rc = sb(es, 'at_rc', [128, TP + NS]); rs_ = sb(es, 'at_rs', [128, TP + NS])
                wq = [sb(es, 'at_wq%d' % i, [128, 8, 128], BF16) for i in range(3)]
                wv_ = sb(es, 'at_wv', [128, 8, 512], BF16)
                xb16 = [sb(es, 'at_xb%d' % i, [128, 512], BF16) for i in range(5)]
                t1 = [sb(es, 'at_t1%d' % i, [128, 512]) for i in range(5)]
                res = [sb(es, 'at_res%d' % i, [128, 512]) for i in range(5)]
                stage = [sb(es, 'at_stage%d' % i, [128, TP + NS], BF16) for i in range(3)]
                ktile = [sb(es, 'at_ktile%d' % i, [128, 128]) for i in range(2)]
                vt32 = [sb(es, 'at_vt32%d' % i, [128, 512]) for i in range(3)]
                vt16 = [sb(es, 'at_vt16%d' % i, [128, 512], BF16) for i in range(3)]
                hT = sb(es, 'at_hT', [128, 8, TP + NS], BF16)
                nw = 0; nk = 0; nv = 0; nt3 = [0]
                for hi, (t0, TW) in enumerate(HALVES):
                    norm_half(es, di['norm_g'][layer], t0, TW, hT, 'at%d' % hi)
                    k.dma(rc[:, :TP], di['c_ropec'][:, t0:t0 + TP], w=['at_rc'])
                    k.dma(rs_[:, :TP], di['c_ropes'][:, t0:t0 + TP], w=['at_rs'])
                    if TW > TP:
                        for (dst_, src_, kk_) in ((rc, di['c_ropec'], 'at_rc'), (rs_, di['c_ropes'], 'at_rs')):
                            for jj in range(NS):
                                k.dma(dst_[:, TP + jj:TP + jj + 1], src_[:, T:T + 1], w=[kk_], allow_slow_non_contiguous=True)
                    tl = [(s0, 512) for s0 in range(0, TP, 512)]
                    if TW > TP:
                        tl.append((TP, NS))
                    for g, (win, dil) in enumerate(GR):
                        wn = str(win)
                        def qk_tile(n, s0, W_, b, wb, g, qk, h, win, wn, t0, last):
                            nonlocal nk
                            p = psalloc()
                            gemm_fm(p, wq[wb], ('at_wq', wb), 0, hT, s0, W_)
                            yield
                            k.op('act', I('copy', xb16[b][:, :W_], PS(p, W_)), r=[('ps', p)], w=[('at_xb', b)])
                            k.op('dve', I('tensor_tensor', t1[b][:, :W_], PS(p, W_), rc[:, s0:s0 + W_], ALU.mult),
                                 r=[('ps', p), 'at_rc'], w=[('at_t1', b)])
                            psfree(p)
                            yield
                            p2 = psalloc()
                            k.op('pe', I('matmul', PS(p2, W_), permb[:], xb16[b][:, :W_], start=True, stop=True),
                                 r=[('at_xb', b), 'at_permb'], w=[('ps', p2)])
                            yield
                            k.op('dve', I('tensor_tensor', res[b][:, :W_], PS(p2, W_), rs_[:, s0:s0 + W_], ALU.mult),
                                 r=[('ps', p2), 'at_rs'], w=[('at_res', b)])
                            psfree(p2)
                            yield
                            k.op('dve', I('tensor_tensor', res[b][:, :W_], res[b][:, :W_], t1[b][:, :W_], ALU.add),
                                 r=[('at_res', b), ('at_t1', b)], w=[('at_res', b)])
                            yield
                            k.op('act', I('copy', stage[wb][:, s0:s0 + W_], res[b][:, :W_]), r=[('at_res', b)], w=[('at_stage', wb, n)])
                            if W_ == 512 and qk == 1:
                                for j in range(4):
                                    tok0 = t0 + s0 + j * 128
                                    if tok0 >= T - min(win, T):
                                        kb_ = nk % 2; nk += 1
                                        p3 = nps()
                                        k.op('pe', I('transpose', PS(p3, 128), res[b][:, j * 128:(j + 1) * 128], ident[:]),
                                             r=[('at_res', b)], w=[('ps', p3)])
                                        k.op('act', I('copy', ktile[kb_][:, :], PS(p3, 128)), r=[('ps', p3)], w=[('at_ktile', kb_)])
                                        o0 = tok0 - (T - min(win, T))
                                        k.dma(di['o_kp' + wn][o0:o0 + 128, h * 128:(h + 1) * 128], ktile[kb_][:, :], r=[('at_ktile', kb_)])
                            if W_ == NS:
                                kb_ = nk % 2; nk += 1
                                p3 = nps()
                                k.op('pe', I('transpose', PS(p3, 128)[:NS, :], res[b][:, :NS], ident[:]), r=[('at_res', b)], w=[('ps', p3)])
                                k.op('act', I('copy', ktile[kb_][:NS, :], PS(p3, 128)[:NS, :]), r=[('ps', p3)], w=[('at_ktile', kb_)])
                                k.dma(qs_tok[g, qk, :, h * 128:(h + 1) * 128], ktile[kb_][:NS, :], r=[('at_ktile', kb_)])
                                if qk == 1:
                                    k.dma(di['o_ks' + wn][:, h * 128:(h + 1) * 128], ktile[kb_][:NS, :], r=[('at_ktile', kb_)])
                            if last:
                                k.dma(qkT[2 * g + qk, h * 128:(h + 1) * 128, t0:t0 + TW], stage[wb][:, :TW],
                                      r=[('at_stage', wb, n_) for n_ in range(len(tl))])

                        def qk_load(n_, it, g=g):
                            c0q = (3 * g + it[0]) * 1024 + it[1] * 128
                            load_w(es, wq[(nwb[0] + n_) % 3], ('at_wq', (nwb[0] + n_) % 3), w_in[:, c0q:c0q + 128], 8, 128, 'at')

                        nwb = [nw]
                        qkpre = Pre([(qk_, h_) for qk_ in range(2) for h_ in range(8)], qk_load, 1)

                        def qk_tiles(g=g, win=win, wn=wn, t0=t0):
                            nonlocal nw
                            gi = 0
                            for qk in range(2):
                                for h in range(8):
                                    wb = nw % 3; nw += 1
                                    qkpre.need(gi)
                                    gi += 1
                                    for n, (s0, W_) in enumerate(tl):
                                        b = nt3[0] % 5; nt3[0] += 1
                                        yield qk_tile(n, s0, W_, b, wb, g, qk, h, win, wn, t0, n == len(tl) - 1)

                        pipeline(qk_tiles(), 4)
                        for hb in range(2):
                            c0 = (3 * g + 2) * 1024 + hb * 512
                            load_w(es, wv_, 'at_wv', w_in[:, c0:c0 + 512], 8, 512, 'at')
                            subs = [(j * 128, 128) for j in range(TP // 128)] + ([(TP, NS)] if TW > TP else [])
                            for (c0_, L) in subs:
                                b = nv % 3; nv += 1
                                p = nps()
                                for c in range(8):
                                    k.op('pe', I('matmul', PS(p, 512)[:L, :], hT[:, c, c0_:c0_ + L], wv_[:, c, :], start=(c == 0), stop=(c == 7)),
                                         r=['at_wv'] + hkeys(c0_), w=[('ps', p)])
                                k.op('act', I('copy', vt32[b][:L, :], PS(p, 512)[:L, :]), r=[('ps', p)], w=[('at_vt32', b)])
                                k.op('dve', I('tensor_copy', vt16[b][:L, :], PS(p, 512)[:L, :]), r=[('ps', p)], w=[('at_vt16', b)])
                                if L == 128:
                                    tok0 = t0 + c0_
                                    k.dma(vtok[g, tok0:tok0 + 128, hb * 512:(hb + 1) * 512], vt16[b][:, :], r=[('at_vt16', b)])
                                    if tok0 >= T - min(win, T):
                                        o0 = tok0 - (T - min(win, T))
                                        k.dma(di['o_vp' + wn][o0:o0 + 128, hb * 512:(hb + 1) * 512], vt32[b][:, :], r=[('at_vt32', b)])
                                else:
                                    k.dma(di['o_vs' + wn][:, hb * 512:(hb + 1) * 512], vt32[b][:NS, :], r=[('at_vt32', b)])
                                    k.dma(vs_tok[g, :, hb * 512:(hb + 1) * 512], vt32[b][:NS, :], r=[('at_vt32', b)])
                    for f in range(8):
                        wb = nw % 2; nw += 1
                        load_w(es, wq[wb], ('at_wq', wb), w_in[:, 9216 + f * 128:9216 + (f + 1) * 128], 8, 128, 'at')
                        for n, (s0, W_) in enumerate(tl):
                            p = nps()
                            gemm_fm(p, wq[wb], ('at_wq', wb), 0, hT, s0, W_)
                            k.op('act', I('activation', stage[wb][:, s0:s0 + W_], PS(p, W_), AF.Silu), r=[('ps', p)], w=[('at_stage', wb, n)])
                        k.dma(zT[f * 128:(f + 1) * 128, t0:t0 + TW], stage[wb][:, :TW], r=[('at_stage', wb, n) for n in range(len(tl))])
                k.barrier()
                k.flush()
            with ExitStack() as es:
                wst.clear()
                band = sb(es, 'ab_band', [128, 256]); k.dma(band[:], di['c_band'], w=['ab_band'])
                NB = 8
                qh = [sb(es, 'ab_qh%d' % i, [128, T], BF16) for i in range(2)]
                kh = [sb(es, 'ab_kh%d' % i, [128, T], BF16) for i in range(2)]
                sm = [sb(es, 'ab_sm%d' % i, [128, 256]) for i in range(NB)]
                Pm = [sb(es, 'ab_P%d' % i, [128, 256], BF16) for i in range(NB)]
                PT = [sb(es, 'ab_PT%d' % i, [128, 2, 128], BF16) for i in range(NB)]
                vt = [sb(es, 'ab_vt%d' % i, [128, 2, 128], BF16) for i in range(NB)]
                osb = [sb(es, 'ab_o%d' % i, [128, 130]) for i in range(NB)]
                nmx = [sb(es, 'ab_nmx%d' % i, [128, 1]) for i in range(NB)]
                nu = 0
                ABW = int(os.environ.get('ABW', '7'))
                heads_ = [(g, h) for g in range(len(GR)) for h in range(8)]

                def ab_load(i):
                    g, h = heads_[i]
                    hb = i % 2
                    k.dma(qh[hb][:, :], qkT[2 * g, h * 128:(h + 1) * 128, 0:T], w=[('ab_qh', hb)])
                    k.dma(kh[hb][:, :], qkT[2 * g + 1, h * 128:(h + 1) * 128, 0:T], w=[('ab_kh', hb)])

                def attn_unit(r, n, b, g, h, hb, dil, qv, kv, vg):
                    k0 = max(n - 1, 0) * 128
                    NK = 256 if n > 0 else 128
                    m0 = 0 if n > 0 else 128
                    k.dma(vt[b][:, 0:NK // 128, :], vg[r, k0:k0 + NK, :].rearrange("(j p) e -> p j e", p=128), w=[('ab_vt', b)])
                    p = psalloc()
                    k.op('pe', I('matmul', PS(p, NK), qv[:, r, n * 128:(n + 1) * 128], kv[:, r, k0:k0 + NK], start=True, stop=True),
                         r=[('ab_qh', hb), ('ab_kh', hb)], w=[('ps', p)])
                    yield
                    k.op('dve', I('scalar_tensor_tensor', sm[b][:, :NK], PS(p, NK), ISQ, band[:, m0:m0 + NK], ALU.mult, ALU.add),
                         r=[('ps', p), 'ab_band'], w=[('ab_sm', b)])
                    psfree(p)
                    yield
                    k.op('dve', I('tensor_reduce', osb[b][:, 128:129], sm[b][:, :NK], AX.X, ALU.max), r=[('ab_sm', b)], w=[('ab_o', b, 1)])
                    yield
                    k.op('pool', I('tensor_scalar', nmx[b][:, :], osb[b][:, 128:129], -1.0, None, ALU.mult), r=[('ab_o', b, 1)], w=[('ab_nmx', b)])
                    yield
                    k.op('act', I('activation', Pm[b][:, :NK], sm[b][:, :NK], AF.Exp, bias=nmx[b][:, :], accum_out=osb[b][:, 129:130]),
                         r=[('ab_sm', b), ('ab_nmx', b)], w=[('ab_P', b), ('ab_o', b, 2)])
                    yield
                    for j in range(NK // 128):
                        k.op('pe', I('transpose', PSB(j, 128), Pm[b][:, j * 128:(j + 1) * 128], identb[:]), r=[('ab_P', b)], w=[('psb', j)])
                        k.op('act' if j else 'dve', I('copy' if j else 'tensor_copy', PT[b][:, j, :], PSB(j, 128)), r=[('psb', j)], w=[('ab_PT', b, j)])
                    yield
                    po = psalloc()
                    for j in range(NK // 128):
                        k.op('pe', I('matmul', PS(po, 128), PT[b][:, j, :], vt[b][:, j, :], start=(j == 0), stop=(j == NK // 128 - 1)),
                             r=[('ab_PT', b, j), ('ab_vt', b)], w=[('ps', po)])
                    yield
                    k.op('act', I('copy', osb[b][:, 0:128], PS(po, 128)), r=[('ps', po)], w=[('ab_o', b, 0)])
                    psfree(po)
                    dst = numS[g].rearrange("(u d) f -> d u f", d=dil)[r, n * 128:(n + 1) * 128, h * 130:(h + 1) * 130]
                    k.dma(dst, osb[b][:, :], r=[('ab_o', b, 0), ('ab_o', b, 1), ('ab_o', b, 2)])

                def all_units():
                    nonlocal nu
                    ab_load(0)
                    for i, (g, h) in enumerate(heads_):
                        if i + 1 < len(heads_):
                            ab_load(i + 1)
                        win, dil = GR[g]
                        nb_ = (T // dil) // 128
                        hb = i % 2
                        qv = qh[hb][:, :].rearrange("p (u d) -> p d u", d=dil)
                        kv = kh[hb][:, :].rearrange("p (u d) -> p d u", d=dil)
                        vg = vtok[g, 0:T, h * 128:(h + 1) * 128].rearrange("(u d) e -> d u e", d=dil)
                        for r in range(dil):
                            for n in range(nb_):
                                b = nu % NB
                                nu += 1
                                yield attn_unit(r, n, b, g, h, hb, dil, qv, kv, vg)

                pipeline(all_units(), ABW, ramp=1)
                k.barrier()
                k.flush()
            with ExitStack() as es:
                wst.clear()
                NBC = 4
                A = [sb(es, 'ac_A%d' % i, [128, 3, 8, 130]) for i in range(NBC)]
                zt = [sb(es, 'ac_z%d' % i, [128, 8, 128], BF16) for i in range(NBC)]
                from types import SimpleNamespace as _NS
                MB = []
                for par in range(NBC):
                    m_ = _NS(par=par)
                    m_.M = sb(es, 'ac_M', [128, 8]); m_.wg = sb(es, 'ac_w', [128, 3, 8]); m_.den = sb(es, 'ac_den', [128, 8])
                    m_.tmp = sb(es, 'ac_tmp', [128, 3, 8]); m_.acc = sb(es, 'ac_acc', [128, 8, 128]); m_.ob = sb(es, 'ac_ob', [128, 8, 128], BF16)
                    MB.append(m_)
                yo = [sb(es, 'ac_yo%d' % i, [128, 8, 128], BF16) for i in range(NBC)]
                acc = MB[0].acc; ob = MB[0].ob

                def merge_g(Av, L, akeys, m_):
                    P_ = m_.par
                    K_ = lambda nm, *a: (nm, P_) + a
                    M, wg_, den, tmp, acc_, ob_ = m_.M, m_.wg, m_.den, m_.tmp, m_.acc, m_.ob
                    k.op('dve', I('tensor_tensor', M[:L, :], Av[:L, 0, :, 128], Av[:L, 1, :, 128], ALU.max), r=akeys, w=[K_('ac_M')])
                    yield
                    k.op('dve', I('tensor_tensor', M[:L, :], M[:L, :], Av[:L, 2, :, 128], ALU.max), r=akeys + [K_('ac_M')], w=[K_('ac_M')])
                    yield
                    for g in range(3):
                        k.op('dve', I('tensor_tensor', wg_[:L, g, :], Av[:L, g, :, 128], M[:L, :], ALU.subtract), r=akeys + [K_('ac_M')], w=[K_('ac_w', g)])
                    yield
                    k.op('act', I('activation', wg_[:L, :, :], wg_[:L, :, :], AF.Exp), r=[K_('ac_w', g) for g in range(3)], w=[K_('ac_w', g) for g in range(3)])
                    yield
                    for g in range(3):
                        k.op('dve', I('tensor_tensor', tmp[:L, g, :], wg_[:L, g, :], Av[:L, g, :, 129], ALU.mult), r=akeys + [K_('ac_w', g)], w=[K_('ac_tmp', g)])
                    yield
                    k.op('dve', I('tensor_tensor', den[:L, :], tmp[:L, 0, :], tmp[:L, 1, :], ALU.add), r=[K_('ac_tmp', 0), K_('ac_tmp', 1)], w=[K_('ac_den')])
                    yield
                    k.op('dve', I('tensor_tensor', den[:L, :], den[:L, :], tmp[:L, 2, :], ALU.add), r=[K_('ac_tmp', 2), K_('ac_den')], w=[K_('ac_den')])
                    yield
                    k.op('dve', I('reciprocal', den[:L, :], den[:L, :]), r=[K_('ac_den')], w=[K_('ac_den')])
                    yield
                    for h in range(8):
                        k.op('dve', I('tensor_scalar', acc_[:L, h, :], Av[:L, 0, h, 0:128], wg_[:L, 0, h:h + 1], None, ALU.mult),
                             r=akeys + [K_('ac_w', 0)], w=[K_('ac_acc', h)])
                    yield
                    for g in (1, 2):
                        for h in range(8):
                            k.op('dve', I('scalar_tensor_tensor', acc_[:L, h, :], Av[:L, g, h, 0:128], wg_[:L, g, h:h + 1], acc_[:L, h, :], ALU.mult, ALU.add),
                                 r=akeys + [K_('ac_w', g), K_('ac_acc', h)], w=[K_('ac_acc', h)])
                        yield
                    for h in range(8):
                        k.op('act', I('mul', ob_[:L, h, :], acc_[:L, h, :], den[:L, h:h + 1]),
                             r=[K_('ac_acc', h), K_('ac_den')], w=[K_('ac_ob', h)])

                def tile_g(tt, b):
                    m_ = MB[b]
                    for g in range(3):
                        k.dma(A[b][:, g, :, :], numS[g, tt * 128:(tt + 1) * 128, :].rearrange("t (h f) -> t h f", f=130), w=[('ac_A', b, g)])
                    k.dma(zt[b][:, :, :128], zT.rearrange("(c p) t -> p c t", p=128)[:, :, tt * 128:(tt + 1) * 128], w=[('ac_z', b)])
                    yield
                    yield from merge_g(A[b], 128, [('ac_A', b, g) for g in range(3)], m_)
                    yield
                    for h in range(8):
                        k.op('pe', I('transpose', PSB(h % 2, 128), m_.ob[:, h, :], identb[:, :]), r=[('ac_ob', b, h)], w=[('psb', h % 2)])
                        k.op('dve', I('tensor_tensor', yo[b][:, h, :], PSB(h % 2, 128), zt[b][:, h, :], ALU.mult),
                             r=[('psb', h % 2), ('ac_z', b)], w=[('ac_yo', b, h)])
                    k.dma(yTv[:, 0:8, tt * 128:(tt + 1) * 128], yo[b][:, :, :], r=[('ac_yo', b, h) for h in range(8)])

                pipeline((tile_g(tt, tt % NBC) for tt in range(T // 128)), NBC, ramp=1)

                def merge(Av, L, akeys):
                    for _ in merge_g(Av, L, akeys, MB[0]):
                        pass

                As = sb(es, 'as_A', [128, 3, 8, 130])
                Kc = sb(es, 'as_K', [128, 1024]); Vc = sb(es, 'as_V', [128, 1024])
                qb = sb(es, 'as_qb', [128, 1024]); kn = sb(es, 'as_kn', [128, 1024]); vn = sb(es, 'as_vn', [1, 1024])
                prod = sb(es, 'as_prod', [128, 1024])
                sc2 = sb(es, 'as_sc2', [128, 16])
                sall = sb(es, 'as_sall', [8, 130]); Ps = sb(es, 'as_P', [8, 130]); mxs = sb(es, 'as_mx', [8, 2]); nmxs = sb(es, 'as_nmx', [8, 1])
                PTk = sb(es, 'as_PTk', [128, 8]); PTn = sb(es, 'as_PTn', [1, 24])
                k.op('dve', I('memset', As[:], 0.0), w=[('as_A', j) for j in range(NS)])
                for j in range(NS):
                    for g, (win, dil) in enumerate(GR):
                        wn = str(win)
                        k.dma(Kc[:, :], di['ck' + wn][j].rearrange("(u d) f -> d u f", d=dil)[0, :, :], w=['as_K'])
                        k.dma(Vc[:, :], di['cv' + wn][j].rearrange("(u d) f -> d u f", d=dil)[0, :, :], w=['as_V'])
                        k.dma(qb[:, :], bass.AP(qs_tok.tensor, qs_tok[g, 0, j].offset, [[0, 128], [1, 1024]]), w=['as_qb'])
                        k.dma(kn[:, :], bass.AP(qs_tok.tensor, qs_tok[g, 1, j].offset, [[0, 128], [1, 1024]]), w=['as_kn'])
                        k.dma(vn[:, :], vs_tok[g, j:j + 1, :], w=['as_vn'])
                        k.op('dve', I('tensor_tensor', prod[:, :], Kc[:, :], qb[:, :], ALU.mult), r=['as_K', 'as_qb'], w=['as_prod'])
                        k.op('dve', I('tensor_reduce', sc2[:, 0:8], prod[:, :].rearrange("p (h e) -> p h e", e=128), AX.X, ALU.add), r=['as_prod'], w=['as_sc2'])
                        k.op('dve', I('tensor_tensor', prod[:, :], kn[:, :], qb[:, :], ALU.mult), r=['as_kn', 'as_qb', 'as_sc2'], w=['as_prod'])
                        k.op('dve', I('tensor_reduce', sc2[:, 8:16], prod[:, :].rearrange("p (h e) -> p h e", e=128), AX.X, ALU.add), r=['as_prod'], w=['as_sc2'])
                        p = nps()
                        k.op('pe', I('transpose', PS(p, 128)[:8, :], sc2[:, 0:8], ident[:]), r=['as_sc2'], w=[('ps', p)])
                        k.op('act', I('mul', sall[:, 0:128], PS(p, 128)[:8, :], ISQ), r=[('ps', p)], w=['as_sall'])
                        p = nps()
                        k.op('pe', I('transpose', PS(p, 128)[:8, :], sc2[:, 8:16], ident[:]), r=['as_sc2'], w=[('ps', p)])
                        k.op('act', I('mul', sall[:, 128:129], PS(p, 128)[:8, 0:1], ISQ), r=[('ps', p)], w=['as_sall'])
                        k.op('dve', I('tensor_reduce', mxs[:, 0:1], sall[:, 0:129], AX.X, ALU.max), r=['as_sall'], w=['as_mx'])
                        k.op('act', I('mul', nmxs[:, :], mxs[:, 0:1], -1.0), r=['as_mx'], w=['as_nmx'])
                        k.op('act', I('activation', Ps[:, 0:129], sall[:, 0:129], AF.Exp, bias=nmxs[:, :], accum_out=mxs[:, 1:2]),
                             r=['as_sall', 'as_nmx'], w=['as_P', 'as_mx'])
                        p = nps()
                        k.op('pe', I('transpose', PS(p, 8), Ps[:, 0:128], ident[:8, :8]), r=['as_P'], w=[('ps', p)])
                        k.op('act', I('copy', PTk[:, :], PS(p, 8)), r=[('ps', p)], w=['as_PTk'])
                        p = nps()
                        k.op('pe', I('transpose', PS(p, 8)[:1, :], Ps[:, 128:129], ident[:8, :8]), r=['as_P'], w=[('ps', p)])
                        k.op('act', I('copy', PTn[:, 0:8], PS(p, 8)[:1, :]), r=[('ps', p)], w=['as_PTn'])
                        p = nps()
                        k.op('pe', I('transpose', PS(p, 8)[:1, :], mxs[:, 0:1], ident[:8, :8]), r=['as_mx'], w=[('ps', p)])
                        k.op('act', I('copy', PTn[:, 8:16], PS(p, 8)[:1, :]), r=[('ps', p)], w=['as_PTn'])
                        p = nps()
                        k.op('pe', I('transpose', PS(p, 8)[:1, :], mxs[:, 1:2], ident[:8, :8]), r=['as_mx'], w=[('ps', p)])
                        k.op('act', I('copy', PTn[:, 16:24], PS(p, 8)[:1, :]), r=[('ps', p)], w=['as_PTn'])
                        k.op('dve', I('tensor_copy', As[j:j + 1, g, :, 128] if False else As[0:1, g, :, 128], PTn[:, 8:16]), r=['as_PTn'], w=[('as_A', j)])
                        k.op('dve', I('tensor_copy', As[0:1, g, :, 129], PTn[:, 16:24]), r=['as_PTn'], w=[('as_A', j)])
                        for h in range(8):
                            p = nps()
                            k.op('pe', I('matmul', PS(p, 128)[:1, :], PTk[:, h:h + 1], Vc[:, h * 128:(h + 1) * 128], start=True, stop=False),
                                 r=['as_PTk', 'as_V'], w=[('ps', p)])
                            k.op('pe', I('matmul', PS(p, 128)[:1, :], PTn[:, h:h + 1], vn[:, h * 128:(h + 1) * 128], start=False, stop=True),
                                 r=['as_PTn', 'as_vn'], w=[('ps', p)])
                            k.op('act', I('copy', As[0:1, g, h, 0:128], PS(p, 128)[:1, :]), r=[('ps', p)], w=[('as_A', j)])
                    merge(As, 1, [('as_A', j)])
                    k.op('act', I('copy', acc[0:1, :, :], ob[0:1, :, :]), r=[('ac_ob', 0, h) for h in range(8)], w=[('ac_acc', 0, h) for h in range(8)])
                    k.dma(os_tok[j:j + 1, :], acc[0:1, :, :].rearrange("p h e -> p (h e)"), r=[('ac_acc', 0, h) for h in range(8)])
                k.barrier()
                osT = sb(es, 'as_osT', [128, 8, NS])
                rows_to_fm(es, os_tok, NS, 1024, osT, 'as_osT', 'ao')
                k.dma(zt[0][:, :, :NS], zT.rearrange("(c p) t -> p c t", p=128)[:, :, T:T + NS], w=[('ac_z', 0)])
                k.op('dve', I('tensor_tensor', yo[0][:, :, :NS], osT[:, :, :], zt[0][:, :, :NS], ALU.mult), r=['as_osT', ('ac_z', 0)], w=[('ac_yo', 0, 0)])
                k.dma(yTv[:, 0:8, T:T + NS], yo[0][:, :, :NS], r=[('ac_yo', 0, 0)])
                k.barrier()
                k.flush()
            out_phase(layer, di['c_w_out'], 8)

        MIXERS = {0: mlstm_layer, 1: conv_layer, 2: attn_layer, 3: pool_layer}
        for li in layers:
            MIXERS[li]()

        with ExitStack() as es:
            wst.clear()
            hf = [sb(es, 'hf%d' % i, [128, 8, 512]) for i in range(2)]
            yo = [sb(es, 'yo%d' % i, [128, D]) for i in range(2)]
            n = 0
            ftl = [(s0, min(512, TA - s0)) for s0 in list(range(0, T, 512)) + [T]]
            for fn_, (s0, W_) in enumerate(ftl):
                norm_tile_f32(es, di['final_g'], ftl, fn_, hf)
                for j0 in range(0, W_, 128):
                    L = min(128, W_ - j0)
                    b = n % 2
                    n += 1
                    for c in range(8):
                        p = nps()
                        k.op('pe', I('transpose', PS(p, 128)[:L, :], hf[fn_ % 2][:, c, j0:j0 + L], ident[:]),
                             r=[('hf', fn_ % 2, c)], w=[('ps', p)])
                        k.op('act' if c % 2 else 'dve',
                             I('copy' if c % 2 else 'tensor_copy', yo[b][:L, c * 128:(c + 1) * 128], PS(p, 128)[:L, :]),
                             r=[('ps', p)], w=[('yo', b, c)])
                    dst = di['y_prompt'][s0 + j0:s0 + j0 + L, :] if s0 < T else di['y_sample']
                    k.dma(dst, yo[b][:L, :], r=[('yo', b, c) for c in range(8)])
            if dbg:
                k.barrier()
                for c in range(8):
                    k.dma(di['dbg_xT'][c * 128:(c + 1) * 128, :], xT[c * 128:(c + 1) * 128, :])
            k.barrier()
            k.flush()
    return nc


_CACHE = {}


def _prep_inputs(inp, T):
    cst = host_consts(T)
    shared = {}
    for k_ in W_SHAPES:
        a = np.asarray(inp[k_], dtype=np.float32)
        if k_ in ('norm_g', 'pe_w', 'pg_w', 'final_g'):
            shared[k_] = np.ascontiguousarray(a)
        else:
            shared[k_] = np.ascontiguousarray(a[0])
    for k_ in CONST_SHAPES:
        shared['c_' + k_] = cst[k_]
    shared['c_ropec'] = cst['ropec']
    shared['c_ropes'] = cst['ropes']
    maps = []
    for c in range(8):
        b = c % 4
        sl = slice(NS * c, NS * c + NS)
        m = dict(shared)
        m['x_prompt'] = np.ascontiguousarray(inp['x_prompt'][b])
        m['x_sample'] = np.ascontiguousarray(inp['x_sample'][sl, 0])
        m['p_prompt'] = np.ascontiguousarray(inp['p_prompt'][:, b])
        m['p_sample'] = np.ascontiguousarray(inp['p_sample'][:, sl, 0])
        m['st_C'] = np.ascontiguousarray(inp['state_mlstm_C'][0, sl])
        m['st_n'] = np.ascontiguousarray(inp['state_mlstm_n'][0, sl])
        m['st_m'] = np.ascontiguousarray(inp['state_mlstm_m'][0, sl])
        m['st_conv'] = np.ascontiguousarray(inp['state_conv'][0, sl])
        m['st_pool'] = np.ascontiguousarray(inp['state_pool'][0, sl])
        for wn in ('128', '512', '2048'):
            m['ck' + wn] = np.ascontiguousarray(inp['cache_k_w' + wn][0, sl]).reshape(NS, -1, 1024)
            m['cv' + wn] = np.ascontiguousarray(inp['cache_v_w' + wn][0, sl]).reshape(NS, -1, 1024)
        maps.append(m)
    return maps


def _gather(res, T):
    R = res.results
    B = 4

    def P(name):
        return np.stack([np.asarray(R[c][name]) for c in range(B)])

    def S(name):
        return np.concatenate([np.asarray(R[c][name]) for c in range(8)], axis=0)

    outs = [P('y_prompt'), S('y_sample')[:, None, :],
            P('o_Cp')[None], S('o_Cs')[None],
            P('o_np')[None], S('o_ns')[None],
            P('o_mp')[:, 0][None], S('o_ms')[None],
            P('o_convp')[None], S('o_convs')[None]]
    for wn in ('128', '512', '2048'):
        nk = min(int(wn), T)
        outs += [P('o_kp' + wn).reshape(1, B, nk, 8, 128), S('o_ks' + wn).reshape(1, 32, 1, 8, 128),
                 P('o_vp' + wn).reshape(1, B, nk, 8, 128), S('o_vs' + wn).reshape(1, 32, 1, 8, 128)]
    outs += [P('o_poolp')[None], S('o_pools')[None]]
    return tuple(np.ascontiguousarray(o, dtype=np.float32) for o in outs)


def kernel(**inp):
    T = int(np.asarray(inp['x_prompt']).shape[1])
    if T not in _CACHE:
        _CACHE[T] = build(T)
    nc = _CACHE[T]
    inp = {k_: np.asarray(v) for k_, v in inp.items()}
    maps = _prep_inputs(inp, T)
    res = run_bass_kernel_spmd(nc, maps, core_ids=list(range(8)))
    return _gather(res, T)
```

```python
import os
import numpy as np
from contextlib import ExitStack
import concourse.bass as bass
import concourse.mybir as mybir
from concourse.bass_utils import run_bass_kernel_spmd

F32 = mybir.dt.float32
BF16 = mybir.dt.bfloat16
AF = mybir.ActivationFunctionType
ALU = mybir.AluOpType
AX = mybir.AxisListType

ENGS = ('pe', 'act', 'dve', 'pool', 'sp')
NDMA = 32
D = 1024
NS = 4
EPS = 1e-6


def I(name, *a, **kw):
    return lambda e: getattr(e, name)(*a, **kw)


PHASE_LOG = []


class Pre:
    def __init__(self, items, load, depth):
        self.items, self.load, self.depth, self.n = items, load, depth, 0

    def need(self, i):
        while self.n <= min(i + self.depth, len(self.items) - 1):
            self.load(self.n, self.items[self.n])
            self.n += 1


def pipeline(gen_iter, width, ramp=None):
    active = []
    it = iter(gen_iter)
    done = False
    while True:
        started = 0
        while len(active) < width and not done and (ramp is None or started < ramp):
            try:
                active.append(next(it))
                started += 1
            except StopIteration:
                done = True
        if not active:
            break
        nxt = []
        for g in active:
            try:
                next(g)
                nxt.append(g)
            except StopIteration:
                pass
        active = nxt


def lockstep(gens):
    gens = list(gens)
    while gens:
        nxt = []
        for g in gens:
            try:
                next(g)
                nxt.append(g)
            except StopIteration:
                pass
        gens = nxt


class Trk:
    def __init__(self, nc, sems, dsems, nsw=8):
        self.nc = nc
        self.eng = {'pe': nc.tensor, 'act': nc.scalar, 'dve': nc.vector, 'pool': nc.gpsimd, 'sp': nc.sync}
        self.sems = dict(sems)
        for i, s in enumerate(dsems):
            self.sems['d%d' % i] = s
        self.cnt = {e: 0 for e in ENGS}
        self.dtot = [0] * len(dsems)
        self.nhw = len(dsems) - nsw
        self.rr = 0
        self.rr_sw = 0
        self.known = {e: {} for e in ENGS}
        self.streams = {e: [] for e in ENGS}
        self.last_w = {}
        self.readers = {}
        self.groups = {}

    def _exp(self, keys):
        out = []
        for k in keys:
            out.extend(self.groups.get(k, (k,)))
        return out

    def _deps(self, en, r, w):
        r = self._exp(r)
        w = self._exp(w)
        deps = []
        for k in r:
            t = self.last_w.get(k)
            if t:
                deps.append(t)
            if isinstance(k, tuple) and k[0] in ('ps', 'psb'):
                deps.extend(t2 for t2 in self.readers.get(k, ()) if t2[0] != en)
        for k in w:
            t = self.last_w.get(k)
            if t:
                deps.append(t)
            deps.extend(self.readers.get(k, ()))
        waits = []
        kn = self.known[en]
        for (s, v) in deps:
            if s == 'pe' and en == 'pe':
                continue
            if kn.get(s, 0) >= v:
                continue
            kn[s] = v
            waits.append((s, v))
        return waits

    def _commit(self, tok, r, w):
        r = self._exp(r)
        w = self._exp(w)
        for k in r:
            self.readers.setdefault(k, []).append(tok)
        for k in w:
            self.last_w[k] = tok
            self.readers[k] = []

    def op(self, en, fn, r=(), w=()):
        waits = self._deps(en, r, w)
        self.cnt[en] += 1
        tok = (en, self.cnt[en])
        self.streams[en].append((waits, fn, en, 1))
        self._commit(tok, r, w)

    def dma(self, out, in_, r=(), w=(), q='sp', **kw):
        waits = self._deps(q, r, w)
        if q == 'pool':
            i = self.nhw + self.rr_sw
            self.rr_sw = (self.rr_sw + 1) % (len(self.dtot) - self.nhw)
        else:
            i = self.rr
            self.rr = (self.rr + 1) % self.nhw
        s = 'd%d' % i
        if self.known[q].get(s, 0) < self.dtot[i]:
            self.known[q][s] = self.dtot[i]
            waits.append((s, self.dtot[i]))
        self.dtot[i] += 16
        tok = (s, self.dtot[i])
        self.streams[q].append((waits, I('dma_start', out=out, in_=in_, **kw), s, 16))
        self._commit(tok, r, w)

    def barrier(self):
        allt = [(e, self.cnt[e]) for e in ENGS if self.cnt[e] > 0]
        allt += [('d%d' % i, v) for i, v in enumerate(self.dtot) if v > 0]
        for en in ENGS:
            waits = []
            for (s, v) in allt:
                if self.known[en].get(s, 0) < v:
                    self.known[en][s] = v
                    waits.append((s, v))
            if waits:
                self.streams[en].append((waits, None, None, 0))
        self.last_w = {}
        self.readers = {}

    def flush(self):
        PHASE_LOG.append(dict(self.cnt))
        nc = self.nc
        streams = self.streams
        self.streams = {e: [] for e in ENGS}
        sems = self.sems

        def replay(en):
            def f(e):
                for (waits, fn, s, inc) in streams[en]:
                    for (ws, wv) in waits:
                        e.wait_ge(sems[ws], wv)
                    if fn is not None:
                        fn(e).then_inc(sems[s], inc)
            return f

        with nc.Block() as block:
            block.tensor(replay('pe'))
            block.scalar(replay('act'))
            block.vector(replay('dve'))
            block.gpsimd(replay('pool'))
            block.sync(replay('sp'))


def host_consts(T):
    c = {}
    c['ident'] = np.eye(128, dtype=np.float32)
    i = np.arange(128)
    c['triu'] = (i[:, None] <= i[None, :]).astype(np.float32)
    c['maskneg'] = np.where(i[None, :] <= i[:, None], 0.0, -1e30).astype(np.float32)
    sel = np.zeros((128, 128), np.float32)
    sel[127, :] = 1.0
    c['sel_last'] = sel
    qi = i[:, None]
    kj = np.arange(256)[None, :]
    dist = qi + 128 - kj
    band = (dist >= 0) & (dist <= 128)
    c['band'] = np.where(band, 0.0, -1e30).astype(np.float32)
    c['band0'] = np.where(band & (kj >= 128), 0.0, -1e30).astype(np.float32)
    half = 16
    inv = (500000.0 ** (-np.arange(half, dtype=np.float32) / half)).astype(np.float32)
    pos = np.concatenate([np.arange(T), np.array([8192])]).astype(np.float32)
    ang = (pos[None, :] * inv[:, None]).astype(np.float32)
    cosT = np.ones((128, T + 1), np.float32)
    sinT = np.zeros((128, T + 1), np.float32)
    cosT[0:16] = np.cos(ang)
    cosT[16:32] = np.cos(ang)
    sinT[0:16] = -np.sin(ang)
    sinT[16:32] = np.sin(ang)
    c['ropec'] = cosT
    c['ropes'] = sinT
    pm = np.zeros((128, 128), np.float32)
    for p in range(16):
        pm[p + 16, p] = 1.0
        pm[p, p + 16] = 1.0
    c['ropeperm'] = pm
    ic = np.zeros((128, 16, 16), np.float32)
    for ch in range(16):
        wdw = (2, 4, 8, 16)[ch // 4]
        for t in range(16):
            ic[:, ch, t] = 1.0 / min(wdw, t + 1)
    c['invcnt'] = ic.reshape(128, 256)
    return c


CONST_SHAPES = {'ident': [128, 128], 'triu': [128, 128], 'maskneg': [128, 128], 'sel_last': [128, 128],
                'band': [128, 256], 'band0': [128, 256], 'ropeperm': [128, 128], 'invcnt': [128, 256]}

W_SHAPES = {
    'norm_g': [4, 1024], 'pe_w': [4, 256, 1024], 'pg_w': [4, 1024, 1024], 'final_g': [1024],
    'a_w_in': [1024, 8208], 'a_b_if': [16], 'a_norm_g': [2048], 'a_w_out': [2048, 1024],
    'b_w_in': [1024, 8192], 'b_conv_w': [3, 2048], 'b_w_out': [2048, 1024],
    'c_w_in': [1024, 10240], 'c_w_out': [1024, 1024],
    'd_w_in': [1024, 4096], 'd_w_grp': [4, 512, 512], 'd_scale': [2048], 'd_w_out': [2048, 1024],
}


def build(T=4096, layers=(0, 1, 2, 3), dbg=False):
    TA = T + NS
    NCH = T // 128
    nc = bass.Bass("TRN2", target_bir_lowering=False)
    di = {}

    def din(name, shape, dt=F32):
        di[name] = nc.dram_tensor(name, list(shape), dt, kind="ExternalInput").ap()
        return di[name]

    def dout(name, shape, dt=F32):
        di[name] = nc.dram_tensor(name, list(shape), dt, kind="ExternalOutput").ap()
        return di[name]

    def dscr(name, shape, dt=F32):
        di[name] = nc.dram_tensor(name, list(shape), dt, kind="Internal").ap()
        return di[name]

    din('x_prompt', [T, D]); din('x_sample', [NS, D])
    din('p_prompt', [4, T, 256]); din('p_sample', [4, NS, 256])
    din('st_C', [NS, 8, 256, 128]); din('st_n', [NS, 8, 128]); din('st_m', [NS, 8])
    din('st_conv', [NS, 2, 2048]); din('st_pool', [NS, 15, 2048])
    for wn, nb in (('128', 128), ('512', 512), ('2048', 2048)):
        din('ck' + wn, [NS, nb, 1024]); din('cv' + wn, [NS, nb, 1024])
    for k_, s_ in W_SHAPES.items():
        din(k_, s_)
    for k_, s_ in CONST_SHAPES.items():
        din('c_' + k_, s_)
    din('c_ropec', [128, T + 1]); din('c_ropes', [128, T + 1])

    dout('y_prompt', [T, D]); dout('y_sample', [NS, D])
    dout('o_Cp', [8, 256, 128]); dout('o_Cs', [NS, 8, 256, 128])
    dout('o_np', [8, 128]); dout('o_ns', [NS, 8, 128])
    dout('o_mp', [1, 8]); dout('o_ms', [NS, 8])
    dout('o_convp', [2, 2048]); dout('o_convs', [NS, 2, 2048])
    for wn, nb in (('128', 128), ('512', 512), ('2048', 2048)):
        nk = min(nb, T)
        dout('o_kp' + wn, [nk, 1024]); dout('o_ks' + wn, [NS, 1024])
        dout('o_vp' + wn, [nk, 1024]); dout('o_vs' + wn, [NS, 1024])
    dout('o_poolp', [15, 2048]); dout('o_pools', [NS, 15, 2048])
    if dbg:
        dout('dbg_xT', [D, TA])

    xT = dscr('xT', [D, TA])
    yT = dscr('yT', [2048, TA], BF16)

    with ExitStack() as top:
        sems = {e: top.enter_context(nc.semaphore('s_' + e)) for e in ENGS}
        dsems = [top.enter_context(nc.semaphore('sd%d' % i)) for i in range(NDMA)]
        k = Trk(nc, sems, dsems)

        uid = [0]

        def sb(es, name, shape, dt=F32):
            uid[0] += 1
            return es.enter_context(nc.sbuf_tensor('%s_%d' % (name, uid[0]), list(shape), dt))

        ident = sb(top, 'ident', [128, 128]); identb = sb(top, 'identb', [128, 128], BF16)
        onesb = sb(top, 'onesb', [128, 128], BF16); onesf = sb(top, 'onesf', [128, 128])
        epsc = sb(top, 'epsc', [128, 1])
        psF = top.enter_context(nc.psum_tensor('psF', [128, 6 * 512], F32))
        psB = top.enter_context(nc.psum_tensor('psB', [128, 2 * 1024], BF16))

        def PS(i, n=512):
            return psF[:, i * 512:i * 512 + n]

        def PSB(i, n=128):
            return psB[:, i * 1024:i * 1024 + n]

        k.dma(ident[:], di['c_ident'], w=['ident'])
        k.op('dve', I('tensor_copy', identb[:], ident[:]), r=['ident'], w=['identb'])
        k.op('dve', I('memset', onesb[:], 1.0), w=['onesb'])
        k.op('dve', I('memset', onesf[:], 1.0), w=['onesf'])
        k.op('dve', I('memset', epsc[:], EPS), w=['epsc'])
        k.barrier()

        rrp = [0]

        psmod = [6]

        pslive = set()

        def nps():
            for _ in range(psmod[0]):
                rrp[0] = (rrp[0] + 1) % psmod[0]
                if rrp[0] not in pslive:
                    return rrp[0]
            raise RuntimeError('no free PSUM bank')

        def psalloc():
            p = nps()
            pslive.add(p)
            return p

        def psfree(p):
            pslive.discard(p)

        with ExitStack() as es:
            xin = [sb(es, 'xin%d' % i, [128, D]) for i in range(4)]
            xo = [sb(es, 'xo%d' % i, [128, 8, 128]) for i in range(4)]
            tiles = [(t * 128, 128, di['x_prompt'][t * 128:(t + 1) * 128, :]) for t in range(NCH)]
            tiles.append((T, NS, di['x_sample']))

            def p0_load(n):
                k.dma(xin[n % 4][:tiles[n][1], :], tiles[n][2], w=[('xin', n % 4)])

            p0_load(0)
            p0_load(1)
            for n, (t0, L, src) in enumerate(tiles):
                b = n % 4
                if n + 2 < len(tiles):
                    p0_load(n + 2)
                for c in range(8):
                    p = nps()
                    k.op('pe', I('transpose', PS(p, L), xin[b][:L, c * 128:(c + 1) * 128], ident[:L, :L]),
                         r=[('xin', b)], w=[('ps', p)])
                    k.op('act' if c % 2 else 'dve',
                         I('copy' if c % 2 else 'tensor_copy', xo[b][:, c, :L], PS(p, L)),
                         r=[('ps', p)], w=[('xo', b, c)])
                k.dma(xT.rearrange("(c p) t -> p c t", p=128)[:, :, t0:t0 + L], xo[b][:, :, :L],
                      r=[('xo', b, c) for c in range(8)])
            k.barrier()
            k.flush()

        def bcast_row(ap1d, n):
            return bass.AP(ap1d.tensor, ap1d.offset, [[0, 128], [1, n]])

        def norm_half(es, g_ap, t0, TW, hT, tag, dbuf=True):
            nb_ = 2 if dbuf else 1
            if '_norm' not in wst:
                gcol = sb(es, 'gcol' + tag, [128, 8])
                k.dma(gcol[:], g_ap.rearrange("(c p) -> p c", p=128), w=['gcol'], allow_slow_non_contiguous=True)
                xt = [sb(es, 'nx%s%d' % (tag, i), [128, 8, 512]) for i in range(nb_)]
                sq = [sb(es, 'nsq%s%d' % (tag, i), [128, 8, 512], BF16) for i in range(nb_)]
                rs = [sb(es, 'nrs%s%d' % (tag, i), [128, 512]) for i in range(nb_)]
                wst['_norm'] = (gcol, xt, sq, rs)
            gcol, xt, sq, rs = wst['_norm']
            ntl = [(s0, min(512, TW - s0)) for s0 in range(0, TW, 512)]

            def n_load(n):
                s0, W_ = ntl[n]
                b = n % nb_
                k.dma(xt[b][:, :, :W_], xT.rearrange("(c p) t -> p c t", p=128)[:, :, t0 + s0:t0 + s0 + W_],
                      w=[('nx', b)])

            n_load(0)
            for n, (s0, W_) in enumerate(ntl):
                b = n % nb_
                if nb_ == 2 and n + 1 < len(ntl):
                    n_load(n + 1)
                k.op('act', I('activation', sq[b][:, :, :W_], xt[b][:, :, :W_], AF.Square),
                     r=[('nx', b)], w=[('nsq', b)])
                p = nps()
                for c in range(8):
                    k.op('pe', I('matmul', PS(p, W_), onesb[:], sq[b][:, c, :W_], start=(c == 0), stop=(c == 7)),
                         r=[('nsq', b), 'onesb'], w=[('ps', p)])
                k.op('act', I('activation', rs[b][:, :W_], PS(p, W_), AF.Sqrt, bias=epsc[:], scale=1.0 / D),
                     r=[('ps', p), 'epsc'], w=[('nrs', b)])
                k.op('dve', I('reciprocal', rs[b][:, :W_], rs[b][:, :W_]), r=[('nrs', b)], w=[('nrs', b)])
                for c in range(8):
                    k.op('dve',
                         I('scalar_tensor_tensor', hT[:, c, s0:s0 + W_], xt[b][:, c, :W_], gcol[:, c:c + 1],
                           rs[b][:, :W_], ALU.mult, ALU.mult),
                         r=[('nx', b), ('nrs', b), 'gcol'], w=[('hT', c, s0 // 512)])
                if nb_ == 1 and n + 1 < len(ntl):
                    n_load(n + 1)

        fin_state = {}

        def norm_tile_f32(es, g_ap, tl_, n, hf):
            if 'g' not in fin_state:
                fin_state['g'] = sb(es, 'fgcol', [128, 8])
                k.dma(fin_state['g'][:], g_ap.rearrange("(c p) -> p c", p=128), w=['fgcol'], allow_slow_non_contiguous=True)
                fin_state['xt'] = [sb(es, 'fnx%d' % i, [128, 8, 512]) for i in range(2)]
                fin_state['sq'] = [sb(es, 'fnsq%d' % i, [128, 8, 512], BF16) for i in range(2)]
                fin_state['rs'] = [sb(es, 'fnrs%d' % i, [128, 512]) for i in range(2)]

            def f_load(m):
                t0_, Wm = tl_[m]
                k.dma(fin_state['xt'][m % 2][:, :, :Wm], xT.rearrange("(c p) t -> p c t", p=128)[:, :, t0_:t0_ + Wm],
                      w=[('fnx', m % 2)])

            if n == 0:
                f_load(0)
            if n + 1 < len(tl_):
                f_load(n + 1)
            b = n % 2
            t0, W_ = tl_[n]
            gcol, xt, sq, rs = fin_state['g'], fin_state['xt'][b], fin_state['sq'][b], fin_state['rs'][b]
            k.op('act', I('activation', sq[:, :, :W_], xt[:, :, :W_], AF.Square), r=[('fnx', b)], w=[('fnsq', b)])
            p = nps()
            for c in range(8):
                k.op('pe', I('matmul', PS(p, W_), onesb[:], sq[:, c, :W_], start=(c == 0), stop=(c == 7)),
                     r=[('fnsq', b), 'onesb'], w=[('ps', p)])
            k.op('act', I('activation', rs[:, :W_], PS(p, W_), AF.Sqrt, bias=epsc[:], scale=1.0 / D),
                 r=[('ps', p), 'epsc'], w=[('fnrs', b)])
            k.op('dve', I('reciprocal', rs[:, :W_], rs[:, :W_]), r=[('fnrs', b)], w=[('fnrs', b)])
            for c in range(8):
                k.op('dve',
                     I('scalar_tensor_tensor', hf[b][:, c, :W_], xt[:, c, :W_], gcol[:, c:c + 1],
                       rs[:, :W_], ALU.mult, ALU.mult),
                     r=[('fnx', b), ('fnrs', b), 'fgcol'], w=[('hf', b, c)])

        wst = {}

        def load_w(es, dst, dst_key, wsrc, kc, ncols, tag):
            parts = [(k0, min(8, kc - k0)) for k0 in range(0, kc, 8)]
            if len(parts) > 1:
                k.groups[dst_key] = [(dst_key, 'part', i) for i in range(len(parts))]
            for i, (k0, kw) in enumerate(parts):
                k.dma(dst[:, k0:k0 + kw, 0:ncols],
                      wsrc[k0 * 128:(k0 + kw) * 128, :].rearrange("(c p) n -> p c n", p=128),
                      w=[(dst_key, 'part', i) if len(parts) > 1 else dst_key], q='pool')

        def out_phase(layer, w_out_ap, KY):
            with ExitStack() as es:
                wst.clear()
                wo = sb(es, 'wo', [128, KY, D], BF16)
                wg = sb(es, 'wg', [128, 8, D], BF16)
                wp = sb(es, 'wp', [128, 2, D], BF16)
                load_w(es, wo, 'wo', w_out_ap, KY, D, 'o')
                load_w(es, wg, 'wg', di['pg_w'][layer], 8, D, 'o')
                load_w(es, wp, 'wp', di['pe_w'][layer], 2, D, 'o')
                yt = [sb(es, 'oy%d' % i, [128, KY, 512], BF16) for i in range(2)]
                xt = [sb(es, 'ox%d' % i, [128, 8, 512]) for i in range(2)]
                xb = [sb(es, 'oxb%d' % i, [128, 8, 512], BF16) for i in range(2)]
                pt = [sb(es, 'op%d' % i, [128, 4, 256]) for i in range(2)]
                pT = [sb(es, 'opT%d' % i, [128, 2, 512], BF16) for i in range(2)]
                gt = [sb(es, 'og%d' % i, [128, 512]) for i in range(2)]
                tl = [(s0, 512) for s0 in range(0, T, 512)] + [(T, NS)]
                def o_loads(n):
                    s0, W_ = tl[n]
                    b = n % 2
                    k.dma(yt[b][:, :, :W_], yT.rearrange("(c p) t -> p c t", p=128)[:, 0:KY, s0:s0 + W_],
                          w=[('oy', b)])
                    k.dma(xt[b][:, :, :W_], xT.rearrange("(c p) t -> p c t", p=128)[:, :, s0:s0 + W_],
                          w=[('ox', b)])
                    if W_ == 512:
                        k.dma(pt[b][:], di['p_prompt'][layer, s0:s0 + 512, :].rearrange("(j p) n -> p j n", p=128),
                              w=[('op', b)])
                    else:
                        k.dma(pt[b][:NS, 0, :], di['p_sample'][layer], w=[('op', b)])

                o_loads(0)
                for n, (s0, W_) in enumerate(tl):
                    b = n % 2
                    if n + 1 < len(tl):
                        o_loads(n + 1)
                    if W_ == 512:
                        subs = [(j, 128) for j in range(4)]
                    else:
                        subs = [(0, NS)]
                    for (j, L) in subs:
                        for c in range(2):
                            p = nps()
                            k.op('pe', I('transpose', PS(p, L), pt[b][:L, j, c * 128:(c + 1) * 128], ident[:L, :L]),
                                 r=[('op', b)], w=[('ps', p)])
                            k.op('act', I('copy', pT[b][:, c, j * 128:j * 128 + L], PS(p, L)),
                                 r=[('ps', p)], w=[('opT', b, j, c)])
                    pTr = [('opT', b, j, c) for (j, L) in subs for c in range(2)]
                    for dc in range(8):
                        p = nps()
                        for c in range(KY):
                            k.op('pe', I('matmul', PS(p, W_), wo[:, c, dc * 128:(dc + 1) * 128], yt[b][:, c, :W_],
                                         start=(c == 0), stop=(c == KY - 1)),
                                 r=['wo', ('oy', b)], w=[('ps', p)])
                        k.op('dve', I('tensor_tensor', xt[b][:, dc, :W_], xt[b][:, dc, :W_], PS(p, W_), ALU.add),
                             r=[('ps', p), ('ox', b)], w=[('ox', b, dc)])
                        k.op('act', I('copy', xb[b][:, dc, :W_], xt[b][:, dc, :W_]),
                             r=[('ox', b, dc)], w=[('oxb', b, dc)])
                    for dc in range(8):
                        p = nps()
                        for c in range(8):
                            k.op('pe', I('matmul', PS(p, W_), wg[:, c, dc * 128:(dc + 1) * 128], xb[b][:, c, :W_],
                                         start=(c == 0), stop=(c == 7)),
                                 r=['wg'] + [('oxb', b, cc) for cc in range(8)], w=[('ps', p)])
                        k.op('act', I('activation', gt[b][:, :W_], PS(p, W_), AF.Sigmoid),
                             r=[('ps', p)], w=[('og', b)])
                        p2 = nps()
                        for c in range(2):
                            k.op('pe', I('matmul', PS(p2, W_), wp[:, c, dc * 128:(dc + 1) * 128], pT[b][:, c, :W_],
                                         start=(c == 0), stop=(c == 1)),
                                 r=['wp'] + pTr, w=[('ps', p2)])
                        k.op('dve', I('tensor_tensor', gt[b][:, :W_], gt[b][:, :W_], PS(p2, W_), ALU.mult),
                             r=[('ps', p2), ('og', b)], w=[('og', b)])
                        k.op('pool', I('tensor_tensor', xt[b][:, dc, :W_], xt[b][:, dc, :W_], gt[b][:, :W_], ALU.add),
                             r=[('og', b), ('ox', b, dc), ('oxb', b, dc)], w=[('ox', b, dc)])
                    k.dma(xT.rearrange("(c p) t -> p c t", p=128)[:, :, s0:s0 + W_], xt[b][:, :, :W_],
                          r=[('ox', b, dc) for dc in range(8)] + [('ox', b)])
                k.barrier()
                k.flush()


        xTv = xT.rearrange("(c p) t -> p c t", p=128)
        yTv = yT.rearrange("(c p) t -> p c t", p=128)
        HALVES = [(0, T // 2), (T // 2, T // 2 + NS)]

        def hkeys(s0):
            return [('hT', c, s0 // 512) for c in range(8)]

        def gemm_fm(p, wt, wkey, col0, hT, s0, W_, kc=8, M=128):
            for c in range(kc):
                k.op('pe', I('matmul', PS(p, W_)[:M, :], wt[:, c, col0:col0 + M], hT[:, c, s0:s0 + W_],
                             start=(c == 0), stop=(c == kc - 1)),
                     r=[wkey] + hkeys(s0), w=[('ps', p)])

        def rows_to_fm(es, src, R, ncols, dst, dkey, tag):
            if '_rowtmp' not in wst:
                wst['_rowtmp'] = sb(es, 'rowtmp' + tag, [64, 2048])
            tmp = wst['_rowtmp']
            k.dma(tmp[:R, :ncols], src, w=['rowtmp'])
            for c in range(ncols // 128):
                p = nps()
                k.op('pe', I('transpose', PS(p, R), tmp[:R, c * 128:(c + 1) * 128], ident[:R, :R]),
                     r=['rowtmp'], w=[('ps', p)])
                k.op('dve', I('tensor_copy', dst[:, c, 0:R], PS(p, R)), r=[('ps', p)], w=[dkey])

        def fm_to_rows(es, srcs, skeys, R, dst, tag):
            if '_rowtmp' not in wst:
                wst['_rowtmp'] = sb(es, 'rowtmp' + tag, [64, 2048])
            tmp = wst['_rowtmp']
            for c, a in enumerate(srcs):
                p = nps()
                k.op('pe', I('transpose', PS(p, 128)[:R, :], a, ident[:]), r=skeys, w=[('ps', p)])
                k.op('dve', I('tensor_copy', tmp[:R, c * 128:(c + 1) * 128], PS(p, 128)[:R, :]),
                     r=[('ps', p)], w=['rowtmp'])
            k.dma(dst, tmp[:R, :len(srcs) * 128], r=['rowtmp'])

        def conv_layer(layer=1):
            w_in = di['b_w_in']
            with ExitStack() as es:
                wst.clear()
                wc = sb(es, 'cv_wc', [128, 16, 3])
                rows_to_fm(es, di['b_conv_w'], 3, 2048, wc, 'cv_wc', 'cw')
                stT = sb(es, 'cv_st', [128, 16, NS * 2])
                rows_to_fm(es, di['st_conv'].rearrange("b j n -> (b j) n"), NS * 2, 2048, stT, 'cv_st', 'cs')
                halo = sb(es, 'cv_halo', [128, 16, 2])
                k.op('dve', I('memset', halo[:], 0.0), w=['cv_halo'])
                so = sb(es, 'cv_so', [128, 16, NS * 2])
                wf = [sb(es, 'cv_wf%d' % i, [128, 8, 512], BF16) for i in range(2)]
                cx = sb(es, 'cv_cx', [128, 2 + T // 2 + NS])
                yo = [sb(es, 'cv_yo%d' % i, [128, T // 2 + NS], BF16) for i in range(2)]
                cgs = [sb(es, 'cv_cg%d' % i, [128, 512]) for i in range(2)]
                zs = [sb(es, 'cv_zs%d' % i, [128, 512]) for i in range(2)]
                acc = [sb(es, 'cv_acc%d' % i, [128, 512]) for i in range(2)]
                ys = sb(es, 'cv_ys', [128, NS])
                hT = sb(es, 'cv_hT', [128, 8, T // 2 + NS], BF16)
                nw = 0

                def cv_load(n_, it):
                    for q_ in range(4):
                        load_w(es, wf[n_ % 2][:, :, q_ * 128:(q_ + 1) * 128], ('cv_wf', n_ % 2, q_),
                               w_in[:, q_ * 2048 + it[1] * 128:q_ * 2048 + (it[1] + 1) * 128], 8, 128, 'cv')

                cvpre = Pre([(hi_, f_) for hi_ in range(2) for f_ in range(16)], cv_load, 1)
                for hi, (t0, TW) in enumerate(HALVES):
                    norm_half(es, di['norm_g'][layer], t0, TW, hT, 'cv%d' % hi)
                    TP = T // 2
                    for f in range(16):
                        wb = nw % 2
                        cvpre.need(nw)
                        nw += 1
                        wkeys = [('cv_wf', wb, q_) for q_ in range(4)]
                        k.op('act', I('copy', cx[:, 0:2], halo[:, f, :]), r=['cv_halo'], w=[('cv_cx', 'h')])
                        tl = [(s0, 512) for s0 in range(0, TP, 512)]
                        if TW > TP:
                            tl.append((TP, NS))
                        for n, (s0, W_) in enumerate(tl):
                            b = n % 2
                            pc, px, pz, pb = nps(), nps(), nps(), nps()
                            for q_, p in ((1, pc), (2, px), (3, pz), (0, pb)):
                                for c in range(8):
                                    k.op('pe', I('matmul', PS(p, W_), wf[wb][:, c, q_ * 128:(q_ + 1) * 128],
                                                 hT[:, c, s0:s0 + W_], start=(c == 0), stop=(c == 7)),
                                         r=[('cv_wf', wb, q_)] + hkeys(s0), w=[('ps', p)])
                            k.op('act', I('copy', cgs[b][:, :W_], PS(pc, W_)), r=[('ps', pc)], w=[('cv_cg', b)])
                            k.op('dve', I('tensor_tensor', cx[:, 2 + s0:2 + s0 + W_], cgs[b][:, :W_], PS(px, W_), ALU.mult),
                                 r=[('cv_cg', b), ('ps', px)], w=[('cv_cx', n)])
                            k.op('act', I('activation', zs[b][:, :W_], PS(pz, W_), AF.Silu), r=[('ps', pz)], w=[('cv_zs', b)])
                            if W_ == 512:
                                rk = [('cv_cx', n), ('cv_cx', n - 1) if n > 0 else ('cv_cx', 'h')]
                                k.op('act', I('mul', acc[b][:, :W_], cx[:, s0:s0 + W_], wc[:, f, 0:1]),
                                     r=rk + ['cv_wc'], w=[('cv_acc', b)])
                                k.op('dve', I('scalar_tensor_tensor', acc[b][:, :W_], cx[:, 1 + s0:1 + s0 + W_], wc[:, f, 1:2],
                                               acc[b][:, :W_], ALU.mult, ALU.add), r=rk + [('cv_acc', b)], w=[('cv_acc', b)])
                                k.op('dve', I('scalar_tensor_tensor', acc[b][:, :W_], cx[:, 2 + s0:2 + s0 + W_], wc[:, f, 2:3],
                                               acc[b][:, :W_], ALU.mult, ALU.add), r=rk + [('cv_acc', b)], w=[('cv_acc', b)])
                                accv = acc[b][:, :W_]
                                ak = ('cv_acc', b)
                            else:
                                stv = stT[:, f, :].rearrange("p (b j) -> p b j", j=2)
                                k.op('act', I('mul', ys[:, :], stv[:, :, 0], wc[:, f, 0:1]),
                                     r=['cv_st', 'cv_wc'], w=['cv_ys'])
                                k.op('dve', I('scalar_tensor_tensor', ys[:, :], stv[:, :, 1], wc[:, f, 1:2], ys[:, :],
                                               ALU.mult, ALU.add), r=['cv_st', 'cv_ys'], w=['cv_ys'])
                                k.op('dve', I('scalar_tensor_tensor', ys[:, :], cx[:, 2 + s0:2 + s0 + W_], wc[:, f, 2:3], ys[:, :],
                                               ALU.mult, ALU.add), r=[('cv_cx', n), 'cv_ys'], w=['cv_ys'])
                                sov = so[:, f, :].rearrange("p (b j) -> p b j", j=2)
                                k.op('act', I('copy', sov[:, :, 0], stv[:, :, 1]), r=['cv_st'], w=[('cv_so', f, 0)])
                                k.op('act', I('copy', sov[:, :, 1], cx[:, 2 + s0:2 + s0 + W_]), r=[('cv_cx', n)], w=[('cv_so', f, 1)])
                                accv = ys[:, :]
                                ak = 'cv_ys'
                            k.op('dve', I('tensor_tensor', zs[b][:, :W_], zs[b][:, :W_], accv, ALU.mult),
                                 r=[ak, ('cv_zs', b)], w=[('cv_zs', b)])
                            k.op('dve', I('tensor_tensor', yo[wb][:, s0:s0 + W_], zs[b][:, :W_], PS(pb, W_), ALU.mult),
                                 r=[('cv_zs', b), ('ps', pb)], w=[('cv_yo', wb, n)])
                        k.op('act', I('copy', halo[:, f, :], cx[:, TP:TP + 2]),
                             r=[('cv_cx', len(tl) - 1 - (1 if TW > TP else 0))], w=['cv_halo'])
                        k.dma(yTv[:, f, t0:t0 + TW], yo[wb][:, :TW], r=[('cv_yo', wb, n) for n in range(len(tl))])
                k.barrier()
                fm_to_rows(es, [halo[:, f, :] for f in range(16)], ['cv_halo'], 2, di['o_convp'], 'cp')
                fm_to_rows(es, [so[:, f, :] for f in range(16)], [('cv_so', f, j) for f in range(16) for j in range(2)],
                           NS * 2, di['o_convs'].rearrange("b j n -> (b j) n"), 'cq')
                k.barrier()
                k.flush()
            out_phase(layer, di['b_w_out'], 16)


        def pool_layer(layer=3):
            w_in = di['d_w_in']
            TP = T // 2
            with ExitStack() as es:
                wst.clear()
                scT = sb(es, 'pl_sc', [128, 16, 1])
                rows_to_fm(es, di['d_scale'].rearrange("(o n) -> o n", o=1), 1, 2048, scT, 'pl_sc', 'ps')
                stT = sb(es, 'pl_st', [128, 16, NS * 15])
                rows_to_fm(es, di['st_pool'].rearrange("b j n -> (b j) n"), NS * 15, 2048, stT, 'pl_st', 'pt')
                invc = sb(es, 'pl_invc', [128, 256])
                k.dma(invc[:], di['c_invcnt'], w=['pl_invc'])
                halo = sb(es, 'pl_halo', [128, 16, 15])
                k.op('dve', I('memset', halo[:], 0.0), w=['pl_halo'])
                so = sb(es, 'pl_so', [128, 16, NS * 15])
                wf = [sb(es, 'pl_wf%d' % i, [128, 8, 256], BF16) for i in range(2)]
                wg = sb(es, 'pl_wg', [128, 4, 512], BF16)
                xpb = sb(es, 'pl_xp', [128, 15 + TP + NS])
                sA = sb(es, 'pl_sA', [128, 15 + TP])
                sB = sb(es, 'pl_sB', [128, 15 + TP])
                rb = sb(es, 'pl_rb', [128, 4, TP + NS], BF16)
                zsb = sb(es, 'pl_zs', [128, 4, TP + NS], BF16)
                yo = [sb(es, 'pl_yo%d' % i, [128, TP + NS], BF16) for i in range(2)]
                t16 = sb(es, 'pl_t16', [128, 16])
                red = sb(es, 'pl_red', [128, NS])
                ytmp = [sb(es, 'pl_yt%d' % i, [128, 512]) for i in range(2)]
                hT = sb(es, 'pl_hT', [128, 8, TP + NS], BF16)
                nw = 0
                ny = 0

                def pl_load(n_, it):
                    for q_ in range(2):
                        load_w(es, wf[n_ % 2][:, :, q_ * 128:(q_ + 1) * 128], ('pl_wf', n_ % 2, q_),
                               w_in[:, q_ * 2048 + it * 128:q_ * 2048 + (it + 1) * 128], 8, 128, 'pl')

                plpre = Pre([f_ for hi_ in range(2) for f_ in range(16)], pl_load, 1)
                for hi, (t0, TW) in enumerate(HALVES):
                    norm_half(es, di['norm_g'][layer], t0, TW, hT, 'pl%d' % hi)
                    tl = [(s0, 512) for s0 in range(0, TP, 512)]
                    if TW > TP:
                        tl.append((TP, NS))
                    for g in range(4):
                        wdw = (2, 4, 8, 16)[g]
                        load_w(es, wg, 'pl_wg', di['d_w_grp'][g], 4, 512, 'pl')
                        for fi in range(4):
                            f = 4 * g + fi
                            wb = nw % 2
                            plpre.need(nw)
                            nw += 1
                            k.op('act', I('copy', xpb[:, 0:15], halo[:, f, :]), r=['pl_halo'], w=['pl_xp'])
                            for n, (s0, W_) in enumerate(tl):
                                px, pz = nps(), nps()
                                for q_, p in enumerate((px, pz)):
                                    for c in range(8):
                                        k.op('pe', I('matmul', PS(p, W_), wf[wb][:, c, q_ * 128:(q_ + 1) * 128],
                                                     hT[:, c, s0:s0 + W_], start=(c == 0), stop=(c == 7)),
                                             r=[('pl_wf', wb, q_)] + hkeys(s0), w=[('ps', p)])
                                k.op('act', I('copy', xpb[:, 15 + s0:15 + s0 + W_], PS(px, W_)), r=[('ps', px)], w=['pl_xp'])
                                k.op('act', I('activation', zsb[:, fi, s0:s0 + W_], PS(pz, W_), AF.Silu),
                                     r=[('ps', pz)], w=[('pl_zs', fi)])
                            cur, ck = xpb, 'pl_xp'
                            step, lo = 1, 0
                            pp = [(sA, 'pl_sA'), (sB, 'pl_sB')]
                            ip = 0
                            while step < wdw:
                                nxt, nk = pp[ip % 2]
                                ip += 1
                                lo2 = lo + step
                                k.op('dve', I('tensor_tensor', nxt[:, lo2:15 + TP], cur[:, lo2:15 + TP],
                                              cur[:, lo2 - step:15 + TP - step], ALU.add), r=[ck], w=[nk])
                                cur, ck, lo, step = nxt, nk, lo2, step * 2
                            k.op('dve', I('scalar_tensor_tensor', rb[:, fi, 0:TP], cur[:, 15:15 + TP], 1.0 / wdw,
                                          xpb[:, 15:15 + TP], ALU.mult, ALU.subtract), r=[ck, 'pl_xp'], w=[('pl_rb', fi)])
                            if hi == 0:
                                k.op('dve', I('tensor_tensor', t16[:], cur[:, 15:31], invc[:, f * 16:(f + 1) * 16], ALU.mult),
                                     r=[ck, 'pl_invc'], w=['pl_t16'])
                                k.op('dve', I('tensor_tensor', rb[:, fi, 0:16], t16[:], xpb[:, 15:31], ALU.subtract),
                                     r=['pl_t16', 'pl_xp'], w=[('pl_rb', fi)])
                            if TW > TP:
                                stv = stT[:, f, :].rearrange("p (b j) -> p b j", j=15)
                                xs_ = xpb[:, 15 + TP:15 + TP + NS]
                                k.op('dve', I('tensor_reduce', red[:], stv[:, :, 15 - (wdw - 1):15], AX.X, ALU.add),
                                     r=['pl_st'], w=['pl_red'])
                                k.op('dve', I('tensor_tensor', red[:], red[:], xs_, ALU.add), r=['pl_red', 'pl_xp'], w=['pl_red'])
                                k.op('dve', I('scalar_tensor_tensor', rb[:, fi, TP:TP + NS], red[:], 1.0 / wdw, xs_,
                                              ALU.mult, ALU.subtract), r=['pl_red', 'pl_xp'], w=[('pl_rb', fi)])
                                sov = so[:, f, :].rearrange("p (b j) -> p b j", j=15)
                                k.op('act', I('copy', sov[:, :, 0:14], stv[:, :, 1:15]), r=['pl_st'], w=[('pl_so', f, 0)])
                                k.op('act', I('copy', sov[:, :, 14], xs_), r=['pl_xp'], w=[('pl_so', f, 1)])
                            k.op('act', I('copy', halo[:, f, :], xpb[:, TP:TP + 15]), r=['pl_xp'], w=['pl_halo'])
                        for fo in range(4):
                            f = 4 * g + fo
                            yb = ny % 2
                            ny += 1
                            for n, (s0, W_) in enumerate(tl):
                                p = nps()
                                for c in range(4):
                                    k.op('pe', I('matmul', PS(p, W_), wg[:, c, fo * 128:(fo + 1) * 128], rb[:, c, s0:s0 + W_],
                                                 start=(c == 0), stop=(c == 3)),
                                         r=['pl_wg'] + [('pl_rb', c) for c in range(4)], w=[('ps', p)])
                                b = n % 2
                                k.op('act', I('mul', ytmp[b][:, :W_], PS(p, W_), scT[:, f, 0:1]), r=[('ps', p), 'pl_sc'], w=[('pl_yt', b)])
                                k.op('dve', I('tensor_tensor', yo[yb][:, s0:s0 + W_], ytmp[b][:, :W_], zsb[:, fo, s0:s0 + W_], ALU.mult),
                                     r=[('pl_yt', b), ('pl_zs', fo)], w=[('pl_yo', yb, n)])
                            k.dma(yTv[:, f, t0:t0 + TW], yo[yb][:, :TW], r=[('pl_yo', yb, n) for n in range(len(tl))])
                k.barrier()
                fm_to_rows(es, [halo[:, f, :] for f in range(16)], ['pl_halo'], 15, di['o_poolp'], 'pp')
                fm_to_rows(es, [so[:, f, :] for f in range(16)], [('pl_so', f, j) for f in range(16) for j in range(2)],
                           NS * 15, di['o_pools'].rearrange("b j n -> (b j) n"), 'pq')
                k.barrier()
                k.flush()
            out_phase(layer, di['d_w_out'], 16)


        def mlstm_layer(layer=0):
            w_in = di['a_w_in']
            TP = T // 2
            NU = TP // 128 + NS
            with ExitStack() as es:
                wst.clear()
                triu = sb(es, 'ml_triu', [128, 128]); k.dma(triu[:], di['c_triu'], w=['ml_triu'])
                mneg = sb(es, 'ml_mneg', [128, 128]); k.dma(mneg[:], di['c_maskneg'], w=['ml_mneg'])
                sell = sb(es, 'ml_sell', [128, 128]); k.dma(sell[:], di['c_sel_last'], w=['ml_sell'])
                bif = sb(es, 'ml_bif', [128, 16]); k.dma(bif[:], bcast_row(di['a_b_if'], 16), w=['ml_bif'])
                ng = sb(es, 'ml_ng', [128, 2048]); k.dma(ng[:], bcast_row(di['a_norm_g'], 2048), w=['ml_ng'])
                triub = sb(es, 'ml_triub', [128, 128], BF16); sellb = sb(es, 'ml_sellb', [128, 128], BF16)
                k.op('dve', I('tensor_copy', triub[:], triu[:]), r=['ml_triu'], w=['ml_triu'])
                k.op('dve', I('tensor_copy', sellb[:], sell[:]), r=['ml_sell'], w=['ml_sell'])
                hlA = sb(es, 'ml_hlA', [128, 128], BF16); hlB = sb(es, 'ml_hlB', [128, 128], BF16)

                def mm_hl(p, pv, lhsT, src, L, n, rkeys):
                    k.op('dve', I('tensor_copy', hlA[:L, :n], src), r=rkeys, w=['ml_hlA'])
                    k.op('dve', I('tensor_tensor', hlB[:L, :n], src, hlA[:L, :n], ALU.subtract), r=rkeys + ['ml_hlA'], w=['ml_hlB'])
                    k.op('pe', I('matmul', pv, lhsT, hlA[:L, :n], start=True, stop=False), r=['ml_hlA', 'ml_triu', 'ml_sell'], w=[('ps', p)])
                    k.op('pe', I('matmul', pv, lhsT, hlB[:L, :n], start=False, stop=True), r=['ml_hlB'], w=[('ps', p)])

                wgate = sb(es, 'ml_wgate', [128, 8, 16], BF16)
                load_w(es, wgate, 'ml_wgate', w_in[:, 8192:8208], 8, 16, 'ml')
                CT = sb(es, 'ml_CT', [128, 8, 257])
                CTb = sb(es, 'ml_CTb', [128, 257], BF16)
                mprev = sb(es, 'ml_mprev', [128, 8])
                k.op('dve', I('memset', CT[:], 0.0), w=[('ml_CT', h) for h in range(8)])
                k.op('dve', I('memset', mprev[:], 0.0), w=['ml_mprev'])
                col3 = sb(es, 'ml_col3', [128, NU, 24])
                ccol = sb(es, 'ml_ccol', [128, NU, 8])
                negm = sb(es, 'ml_negm', [128, NU, 8])
                expnegm = sb(es, 'ml_enm', [128, NU, 8])
                wcol = sb(es, 'ml_wcol', [128, NU, 8])
                bcs = sb(es, 'ml_bcs', [128, NU, 24])
                gs = sb(es, 'ml_gs', [128, 16]); lp = sb(es, 'ml_lp', [128, 8]); mxa = sb(es, 'ml_mx', [128, 8])
                inter = sb(es, 'ml_inter', [128, 8]); tmp8 = sb(es, 'ml_tmp8', [128, 8])
                from types import SimpleNamespace
                BS = []
                NBU = 4
                for par in range(NBU):
                    B = SimpleNamespace()
                    B.par = par
                    B.diagc = sb(es, 'ml_diagc', [128, 128]); B.logd = sb(es, 'ml_logd', [128, 128]); B.Dm = sb(es, 'ml_Dm', [128, 128])
                    B.Pm = sb(es, 'ml_P', [128, 128], BF16); B.PTs = sb(es, 'ml_PT', [128, 128], BF16)
                    B.ktok = sb(es, 'ml_ktok', [128, 128], BF16)
                    B.v1 = sb(es, 'ml_v1', [128, 257], BF16); B.wv = sb(es, 'ml_wv', [128, 257], BF16)
                    k.op('dve', I('memset', B.v1[:, 256:257], 1.0), w=[('ml_v1', par)])
                    B.og = sb(es, 'ml_og', [128, 256]); B.ez = sb(es, 'ml_ez', [128, 512]); B.zs = sb(es, 'ml_zs', [128, 256])
                    B.intra = sb(es, 'ml_intra', [128, 257]); B.nd = sb(es, 'ml_nd', [128, 257])
                    B.den = sb(es, 'ml_den', [128, 1]); B.ssq = sb(es, 'ml_ssq', [128, 1]); B.junk = BS[0].junk if BS else sb(es, 'ml_junk', [128, 256])
                    B.hs = sb(es, 'ml_hs', [128, 256]); B.yb = sb(es, 'ml_yb', [128, 256], BF16)
                    B.hlA = sb(es, 'ml_hlA2', [128, 128], BF16); B.hlB = sb(es, 'ml_hlB2', [128, 128], BF16)
                    BS.append(B)
                diagc = BS[0].diagc; logd = BS[0].logd
                qTb = sb(es, 'ml_qTb', [128, 512], BF16); kTb = sb(es, 'ml_kTb', [128, 512], BF16)
                yTs = sb(es, 'ml_yTs', [128, 2, TP + NS], BF16)
                ctmp = sb(es, 'ml_ctmp', [128, 2, 128])
                CTs = sb(es, 'ml_CTs', [128, NS, 257]); CTbs = [sb(es, 'ml_CTbs', [128, 257], BF16) for _ in range(NS)]
                ctmps = [sb(es, 'ml_ctmps', [128, 2, 128]) for _ in range(NS)]
                wh = [sb(es, 'ml_wh%d' % i, [128, 8, 1024], BF16) for i in range(2)]
                hT = sb(es, 'ml_hT', [128, 8, TP + NS], BF16)
                SC = 128.0 ** -0.5

                def logd_unit(u, h, L, B=None):
                    dg, ld, kd, kl = (diagc, logd, 'ml_diagc', 'ml_logd') if B is None else (B.diagc, B.logd, ('ml_diagc', B.par), ('ml_logd', B.par))
                    k.op('dve', I('tensor_scalar', dg[:L, :L], ident[:L, :L], ccol[:L, u, h:h + 1], None, ALU.mult),
                         r=[('col', u)], w=[kd])
                    p = nps()
                    if B is None:
                        mm_hl(p, PS(p, L)[:L, :], onesb[:L, :L], dg[:L, :L], L, L, [kd])
                    else:
                        k.op('dve', I('tensor_copy', B.hlA[:L, :L], dg[:L, :L]), r=[kd], w=[('ml_hlA', B.par)])
                        k.op('dve', I('tensor_tensor', B.hlB[:L, :L], dg[:L, :L], B.hlA[:L, :L], ALU.subtract), r=[kd, ('ml_hlA', B.par)], w=[('ml_hlB', B.par)])
                        k.op('pe', I('matmul', PS(p, L)[:L, :], onesb[:L, :L], B.hlA[:L, :L], start=True, stop=False), r=[('ml_hlA', B.par)], w=[('ps', p)])
                        k.op('pe', I('matmul', PS(p, L)[:L, :], onesb[:L, :L], B.hlB[:L, :L], start=False, stop=True), r=[('ml_hlB', B.par)], w=[('ps', p)])
                    k.op('dve', I('scalar_tensor_tensor', ld[:L, :L], PS(p, L)[:L, :], col3[:L, u, h:h + 1], mneg[:L, :L],
                                  ALU.add, ALU.add), r=[('ps', p), ('col', u), 'ml_mneg'], w=[kl])

                def gate_unit(u, c0, L):
                    p = nps()
                    for c in range(8):
                        k.op('pe', I('matmul', PS(p, 16)[:L, :], hT[:, c, c0:c0 + L], wgate[:, c, :], start=(c == 0), stop=(c == 7)),
                             r=['ml_wgate'] + hkeys(c0), w=[('ps', p)])
                    k.op('dve', I('tensor_tensor', gs[:L, :], PS(p, 16)[:L, :], bif[:L, :], ALU.add), r=[('ps', p), 'ml_bif'], w=['ml_gs'])
                    k.op('act', I('activation', lp[:L, :], gs[:L, 8:16], AF.Exp, scale=-1.0), r=['ml_gs'], w=['ml_lp'])
                    k.op('act', I('activation', lp[:L, :], lp[:L, :], AF.Ln, bias=onesf[:L, 0:1]), r=['ml_lp'], w=['ml_lp'])
                    p2 = nps()
                    mm_hl(p2, PS(p2, 8)[:L, :], triub[:L, :L], lp[:L, :], L, 8, ['ml_lp'])
                    k.op('act', I('mul', col3[:L, u, 0:8], PS(p2, 8)[:L, :], -1.0), r=[('ps', p2)], w=[('col', u)])
                    k.op('dve', I('tensor_tensor', ccol[:L, u, :], gs[:L, 0:8], PS(p2, 8)[:L, :], ALU.add),
                         r=[('ps', p2), 'ml_gs'], w=[('col', u)])
                    def gate_head(h, B):
                        P_ = B.par
                        K_ = lambda nm: (nm, P_)
                        k.op('act', I('mul', B.diagc[:L, :L], ident[:L, :L], ccol[:L, u, h:h + 1]), r=[('col', u)], w=[K_('ml_diagc')])
                        yield
                        k.op('act', I('copy', B.hlA[:L, :L], B.diagc[:L, :L]), r=[K_('ml_diagc')], w=[K_('ml_hlA')])
                        yield
                        k.op('pool', I('tensor_tensor', B.hlB[:L, :L], B.diagc[:L, :L], B.hlA[:L, :L], ALU.subtract), r=[K_('ml_diagc'), K_('ml_hlA')], w=[K_('ml_hlB')])
                        yield
                        p = psalloc()
                        k.op('pe', I('matmul', PS(p, L)[:L, :], onesb[:L, :L], B.hlA[:L, :L], start=True, stop=False), r=[K_('ml_hlA')], w=[('ps', p)])
                        k.op('pe', I('matmul', PS(p, L)[:L, :], onesb[:L, :L], B.hlB[:L, :L], start=False, stop=True), r=[K_('ml_hlB')], w=[('ps', p)])
                        yield
                        k.op('dve', I('scalar_tensor_tensor', B.logd[:L, :L], PS(p, L)[:L, :], col3[:L, u, h:h + 1], mneg[:L, :L],
                                      ALU.add, ALU.add), r=[('ps', p), ('col', u), 'ml_mneg'], w=[K_('ml_logd')])
                        psfree(p)
                        yield
                        k.op('dve', I('tensor_reduce', mxa[:L, h:h + 1], B.logd[:L, :L], AX.X, ALU.max), r=[K_('ml_logd')], w=[('ml_mx', h)])

                    for h0 in range(0, 8, NBU):
                        lockstep([gate_head(h, BS[h - h0]) for h in range(h0, min(h0 + NBU, 8))])
                    k.op('dve', I('tensor_tensor', inter[:L, :], col3[:L, u, 0:8], mprev[:L, :], ALU.add),
                         r=[('col', u), 'ml_mprev'], w=['ml_inter'])
                    k.op('dve', I('tensor_tensor', col3[:L, u, 8:16], inter[:L, :], mxa[:L, :], ALU.max),
                         r=['ml_inter'] + [('ml_mx', hh) for hh in range(8)], w=[('col', u)])
                    k.op('dve', I('tensor_tensor', tmp8[:L, :], inter[:L, :], col3[:L, u, 8:16], ALU.subtract),
                         r=['ml_inter', ('col', u)], w=['ml_tmp8'])
                    k.op('act', I('activation', col3[:L, u, 16:24], tmp8[:L, :], AF.Exp), r=['ml_tmp8'], w=[('col', u)])
                    k.op('act', I('mul', negm[:L, u, :], col3[:L, u, 8:16], -1.0), r=[('col', u)], w=[('col', u)])
                    k.op('act', I('activation', expnegm[:L, u, :], col3[:L, u, 8:16], AF.Exp, scale=-1.0), r=[('col', u)], w=[('col', u)])
                    p3 = nps()
                    lsel = sellb[:, :] if L == 128 else onesb[0:1, :]
                    mm_hl(p3, PS(p3, 24), lsel, col3[:L, u, :], L, 24, [('col', u)])
                    k.op('act', I('copy', bcs[:, u, :], PS(p3, 24)), r=[('ps', p3)], w=[('bcs', u)])
                    k.op('dve', I('tensor_tensor', tmp8[:L, :], ccol[:L, u, :], bcs[:L, u, 0:8], ALU.add),
                         r=[('col', u), ('bcs', u)], w=['ml_tmp8'])
                    k.op('dve', I('tensor_tensor', tmp8[:L, :], tmp8[:L, :], bcs[:L, u, 8:16], ALU.subtract),
                         r=['ml_tmp8', ('bcs', u)], w=['ml_tmp8'])
                    k.op('act', I('activation', wcol[:L, u, :], tmp8[:L, :], AF.Exp), r=['ml_tmp8'], w=[('col', u)])
                    k.op('dve', I('tensor_copy', mprev[:, :], bcs[:, u, 8:16]), r=[('bcs', u)], w=['ml_mprev'])

                def stage_a(h, u, c0, L, w_, wk, B):
                    P_ = B.par
                    K_ = lambda nm: (nm, P_)
                    dg, ld = B.diagc, B.logd
                    p1 = psalloc()
                    for c in range(8):
                        k.op('pe', I('matmul', PS(p1, 384)[:L, :], hT[:, c, c0:c0 + L], w_[:, c, 128:512], start=(c == 0), stop=(c == 7)),
                             r=[wk] + hkeys(c0), w=[('ps', p1)])
                    k.op('act', I('mul', dg[:L, :L], ident[:L, :L], ccol[:L, u, h:h + 1]), r=[('col', u)], w=[K_('ml_diagc')])
                    yield
                    k.op('act', I('copy', B.hlA[:L, :L], dg[:L, :L]), r=[K_('ml_diagc')], w=[K_('ml_hlA')])
                    k.op('act', I('mul', B.ktok[:L, :], PS(p1, 384)[:L, 0:128], SC), r=[('ps', p1)], w=[K_('ml_ktok')])
                    k.op('act', I('copy', B.v1[:L, 0:256], PS(p1, 384)[:L, 128:384]), r=[('ps', p1)], w=[K_('ml_v1')])
                    psfree(p1)
                    p2 = psalloc()
                    for c in range(8):
                        k.op('pe', I('matmul', PS(p2, 512)[:L, :], hT[:, c, c0:c0 + L], w_[:, c, 512:1024], start=(c == 0), stop=(c == 7)),
                             r=[wk] + hkeys(c0), w=[('ps', p2)])
                    yield
                    k.op('pool', I('tensor_tensor', B.hlB[:L, :L], dg[:L, :L], B.hlA[:L, :L], ALU.subtract), r=[K_('ml_diagc'), K_('ml_hlA')], w=[K_('ml_hlB')])
                    k.op('act', I('activation', B.ez[:L, :], PS(p2, 512)[:L, :], AF.Exp, scale=-1.0), r=[('ps', p2)], w=[K_('ml_ez')])
                    k.op('act', I('copy', B.zs[:L, :], PS(p2, 512)[:L, 256:512]), r=[('ps', p2)], w=[K_('ml_zs')])
                    psfree(p2)
                    k.op('dve', I('tensor_scalar', B.wv[:L, :], B.v1[:L, :], wcol[:L, u, h:h + 1], None, ALU.mult), r=[K_('ml_v1'), ('col', u)], w=[K_('ml_wv')])
                    yield
                    p = psalloc()
                    k.op('pe', I('matmul', PS(p, L)[:L, :], onesb[:L, :L], B.hlA[:L, :L], start=True, stop=False), r=[K_('ml_hlA')], w=[('ps', p)])
                    k.op('pe', I('matmul', PS(p, L)[:L, :], onesb[:L, :L], B.hlB[:L, :L], start=False, stop=True), r=[K_('ml_hlB')], w=[('ps', p)])
                    k.op('dve', I('tensor_scalar', B.ez[:L, :], B.ez[:L, :], 1.0, None, ALU.add), r=[K_('ml_ez')], w=[K_('ml_ez')])
                    yield
                    k.op('dve', I('scalar_tensor_tensor', ld[:L, :L], PS(p, L)[:L, :], col3[:L, u, h:h + 1], mneg[:L, :L],
                                  ALU.add, ALU.add), r=[('ps', p), ('col', u), 'ml_mneg'], w=[K_('ml_logd')])
                    psfree(p)
                    k.op('pool', I('tensor_tensor', B.ez[:L, 0:256], B.ez[:L, 0:256], B.ez[:L, 256:512], ALU.mult), r=[K_('ml_ez')], w=[K_('ml_ez')])
                    yield
                    k.op('act', I('activation', B.Dm[:L, :L], ld[:L, :L], AF.Exp, bias=negm[:L, u, h:h + 1]),
                         r=[K_('ml_logd'), ('col', u)], w=[K_('ml_Dm')])
                    ps_ = psalloc()
                    k.op('pe', I('matmul', PS(ps_, L)[:L, :], B.qT[:, :L], B.kT[:, :L], start=True, stop=True),
                         r=[('ml_qTb', B.qk), ('ml_kTb', B.qk)], w=[('ps', ps_)])
                    k.op('dve', I('reciprocal', B.ez[:L, 0:256], B.ez[:L, 0:256]), r=[K_('ml_ez')], w=[K_('ml_ez')])
                    yield
                    k.op('dve', I('tensor_tensor', B.Pm[:L, :L], PS(ps_, L)[:L, :], B.Dm[:L, :L], ALU.mult), r=[('ps', ps_), K_('ml_Dm')], w=[K_('ml_P')])
                    psfree(ps_)
                    k.op('dve', I('tensor_tensor', B.og[:L, :], B.zs[:L, :], B.ez[:L, 0:256], ALU.mult), r=[K_('ml_zs'), K_('ml_ez')], w=[K_('ml_og')])
                    yield
                    pb_ = P_ % 2
                    k.op('pe', I('transpose', PSB(pb_, L)[:L, :], B.Pm[:L, :L], identb[:L, :L]), r=[K_('ml_P')], w=[('psb', pb_)])
                    k.op('act', I('copy', B.PTs[:L, :L], PSB(pb_, L)[:L, :]), r=[('psb', pb_)], w=[K_('ml_PT')])
                    yield
                    pi = psalloc()
                    k.op('pe', I('matmul', PS(pi, 257)[:L, :], B.PTs[:L, :L], B.v1[:L, :], start=True, stop=True),
                         r=[K_('ml_PT'), K_('ml_v1')], w=[('ps', pi)])
                    yield
                    k.op('act', I('copy', B.intra[:L, :], PS(pi, 257)[:L, :]), r=[('ps', pi)], w=[K_('ml_intra')])
                    psfree(pi)

                def stage_c(h, u, c0, L, B, st=None):
                    P_ = B.par
                    K_ = lambda nm: (nm, P_)
                    CTv, CTbv, ck, bk = (CT[:, h, :], CTb, ('ml_CT', h), 'ml_CTb') if st is None else st
                    pj = nps()
                    k.op('pe', I('matmul', PS(pj, 257)[:L, :], B.qT[:, :L], CTbv[:, :], start=True, stop=True),
                         r=[('ml_qTb', B.qk), bk], w=[('ps', pj)])
                    pu = nps()
                    k.op('pe', I('matmul', PS(pu, 257), B.ktok[:L, :], B.wv[:L, :], start=True, stop=True), r=[K_('ml_ktok'), K_('ml_wv')], w=[('ps', pu)])
                    k.op('dve', I('scalar_tensor_tensor', CTv, CTv, bcs[:, u, 16 + h:17 + h], PS(pu, 257), ALU.mult, ALU.add),
                         r=[('ps', pu), ('bcs', u), ck], w=[ck])
                    if st is None:
                        k.op('act', I('copy', CTbv[:, :], CTv), r=[ck], w=[bk])
                    k.op('dve', I('scalar_tensor_tensor', B.nd[:L, :], PS(pj, 257)[:L, :], col3[:L, u, 16 + h:17 + h], B.intra[:L, :],
                                  ALU.mult, ALU.add), r=[('ps', pj), K_('ml_intra'), ('col', u)], w=[K_('ml_nd')])

                def stage_d(h, u, c0, L, B):
                    P_ = B.par
                    K_ = lambda nm: (nm, P_)
                    k.op('dve', I('scalar_tensor_tensor', B.den[:L, :], B.nd[:L, 256:257], -1.0, B.nd[:L, 256:257], ALU.mult, ALU.max),
                         r=[K_('ml_nd')], w=[K_('ml_den')])
                    k.op('dve', I('tensor_tensor', B.den[:L, :], B.den[:L, :], expnegm[:L, u, h:h + 1], ALU.max),
                         r=[K_('ml_den'), ('col', u)], w=[K_('ml_den')])
                    k.op('dve', I('reciprocal', B.den[:L, :], B.den[:L, :]), r=[K_('ml_den')], w=[K_('ml_den')])
                    k.op('dve', I('tensor_scalar', B.hs[:L, :], B.nd[:L, 0:256], B.den[:L, 0:1], None, ALU.mult), r=[K_('ml_nd'), K_('ml_den')], w=[K_('ml_hs')])
                    yield
                    k.op('act', I('activation', B.junk[:L, :], B.hs[:L, :], AF.Square, accum_out=B.ssq[:L, :]), r=[K_('ml_hs')], w=[K_('ml_ssq'), 'ml_junk'])
                    k.op('act', I('activation', B.ssq[:L, :], B.ssq[:L, :], AF.Ln, bias=epsc[:L, :], scale=1.0 / 256), r=[K_('ml_ssq')], w=[K_('ml_ssq')])
                    k.op('act', I('activation', B.ssq[:L, :], B.ssq[:L, :], AF.Exp, scale=-0.5), r=[K_('ml_ssq')], w=[K_('ml_ssq')])
                    yield
                    k.op('dve', I('scalar_tensor_tensor', B.hs[:L, :], B.hs[:L, :], B.ssq[:L, 0:1], ng[:L, h * 256:(h + 1) * 256],
                                  ALU.mult, ALU.mult), r=[K_('ml_hs'), K_('ml_ssq'), 'ml_ng'], w=[K_('ml_hs')])
                    yield
                    k.op('pool', I('tensor_tensor', B.yb[:L, :], B.hs[:L, :], B.og[:L, :], ALU.mult), r=[K_('ml_hs'), K_('ml_og')], w=[K_('ml_yb')])
                    yield
                    pb_ = P_ % 2
                    for vc in range(2):
                        k.op('pe', I('transpose', PSB(pb_, 256)[:, vc * 128:vc * 128 + L], B.yb[:L, vc * 128:(vc + 1) * 128], identb[:L, :L]),
                             r=[K_('ml_yb')], w=[('psb', pb_)])
                    for vc in range(2):
                        k.op('act' if vc else 'dve', I('copy' if vc else 'tensor_copy', yTs[:, vc, c0:c0 + L], PSB(pb_, 256)[:, vc * 128:vc * 128 + L]),
                             r=[('psb', pb_)], w=[('ml_yTs', u)])

                def head_setup(h, units, w_, wk, half):
                    bs = [(u, c0, L, BS[half * 2 + i]) for i, (u, c0, L) in enumerate(units)]
                    cb0 = units[0][1]
                    WB = sum(L for (_, _, L) in units)
                    o0 = half * 256
                    for (cc, dst, sc, nm) in ((0, qTb, None, ('ml_qTb', half)), (128, kTb, SC, ('ml_kTb', half))):
                        pq = nps()
                        for c in range(8):
                            k.op('pe', I('matmul', PS(pq, WB), w_[:, c, cc:cc + 128], hT[:, c, cb0:cb0 + WB], start=(c == 0), stop=(c == 7)),
                                 r=[wk] + hkeys(cb0) + hkeys(cb0 + WB - 1), w=[('ps', pq)])
                        if sc is None:
                            k.op('act', I('copy', dst[:, o0:o0 + WB], PS(pq, WB)), r=[('ps', pq)], w=[nm])
                        else:
                            k.op('act', I('mul', dst[:, o0:o0 + WB], PS(pq, WB), sc), r=[('ps', pq)], w=[nm])
                    off = o0
                    for (u, c0, L, B) in bs:
                        B.qT = qTb[:, off:off + 128]
                        B.kT = kTb[:, off:off + 128]
                        B.qk = half
                        off += L
                    return bs

                def batch_tail(h, bs):
                    for (u, c0, L, B) in bs:
                        stage_c(h, u, c0, L, B)
                        yield
                    gens = [stage_d(h, u, c0, L, B) for (u, c0, L, B) in bs]
                    while gens:
                        nxt = []
                        for g_ in gens:
                            try:
                                next(g_)
                                nxt.append(g_)
                            except StopIteration:
                                pass
                        gens = nxt
                        yield

                def head_run(h, unit_batches, w_, wk):
                    prev = None
                    for bi, units in enumerate(unit_batches):
                        bs = head_setup(h, units, w_, wk, bi % 2)
                        gens = [stage_a(h, u, c0, L, w_, wk, B) for (u, c0, L, B) in bs]
                        if prev is not None:
                            gens.append(prev)
                        lockstep(gens)
                        prev = batch_tail(h, bs)
                    if prev is not None:
                        lockstep([prev])

                def state_out(h, dC, dn):
                    for vc in range(2):
                        p = nps()
                        k.op('pe', I('transpose', PS(p, 128), CT[:, h, vc * 128:(vc + 1) * 128], ident[:]), r=[('ml_CT', h)], w=[('ps', p)])
                        k.op('act', I('copy', ctmp[:, vc, :], PS(p, 128)), r=[('ps', p)], w=[('ml_ctmp', vc)])
                    k.dma(dC.rearrange("(vc p) kk -> p vc kk", p=128), ctmp[:], r=[('ml_ctmp', 0), ('ml_ctmp', 1)])
                    k.dma(dn.rearrange("(p o) -> p o", o=1), CT[:, h, 256:257], r=[('ml_CT', h)])

                def state_in(h, sC, sn):
                    k.dma(ctmp[:], sC.rearrange("(vc p) kk -> p vc kk", p=128), w=[('ml_ctmp', 0), ('ml_ctmp', 1)])
                    for vc in range(2):
                        p = nps()
                        k.op('pe', I('transpose', PS(p, 128), ctmp[:, vc, :], ident[:]), r=[('ml_ctmp', vc)], w=[('ps', p)])
                        k.op('act', I('copy', CT[:, h, vc * 128:(vc + 1) * 128], PS(p, 128)), r=[('ps', p)], w=[('ml_CT', h)])
                    k.dma(CT[:, h, 256:257], sn.rearrange("(p o) -> p o", o=1), w=[('ml_CT', h)])
                    k.op('act', I('copy', CTb[:, :], CT[:, h, :]), r=[('ml_CT', h)], w=['ml_CTb'])

                def sample_run(h, w_, wk, npu_):
                    for j in range(NS):
                        k.groups[('ml_CTs', j)] = [(('ml_CTs', j), 0), (('ml_CTs', j), 1), (('ml_CTs', j), 'n')]
                    for j in range(NS):
                        ck, bk = ('ml_CTs', j), ('ml_CTbs', j)
                        k.dma(ctmps[j][:], di['st_C'][j, h].rearrange("(vc p) kk -> p vc kk", p=128), w=[('ml_ctmps', j)])
                        k.dma(CTs[:, j, 256:257], di['st_n'][j, h].rearrange("(p o) -> p o", o=1), w=[(ck, 'n')])
                    for j in range(NS):
                        ck, bk = ('ml_CTs', j), ('ml_CTbs', j)
                        for vc in range(2):
                            p = nps()
                            k.op('pe', I('transpose', PS(p, 128), ctmps[j][:, vc, :], ident[:]), r=[('ml_ctmps', j)], w=[('ps', p)])
                            k.op('act', I('copy', CTs[:, j, vc * 128:(vc + 1) * 128], PS(p, 128)), r=[('ps', p)], w=[(ck, vc)])
                        k.op('act', I('copy', CTbs[j][:, :], CTs[:, j, :]), r=[ck], w=[bk])
                    units = [(npu_ + j, TP + j, 1) for j in range(NS)]
                    bs = [(u, c0, L, BS[i]) for i, (u, c0, L) in enumerate(units)]
                    for (cc, dst, sc, nm) in ((0, qTb, None, ('ml_qTb', 0)), (128, kTb, SC, ('ml_kTb', 0))):
                        pq = nps()
                        for c in range(8):
                            k.op('pe', I('matmul', PS(pq, NS), w_[:, c, cc:cc + 128], hT[:, c, TP:TP + NS], start=(c == 0), stop=(c == 7)),
                                 r=[wk] + hkeys(TP), w=[('ps', pq)])
                        if sc is None:
                            k.op('act', I('copy', dst[:, 0:NS], PS(pq, NS)), r=[('ps', pq)], w=[nm])
                        else:
                            k.op('act', I('mul', dst[:, 0:NS], PS(pq, NS), sc), r=[('ps', pq)], w=[nm])
                    for i, (u, c0, L, B) in enumerate(bs):
                        B.qT = qTb[:, i:i + 128]
                        B.kT = kTb[:, i:i + 128]
                        B.qk = 0
                    lockstep([stage_a(h, u, c0, L, w_, wk, B) for (u, c0, L, B) in bs])
                    for j, (u, c0, L, B) in enumerate(bs):
                        stage_c(h, u, c0, L, B, st=(CTs[:, j, :], CTbs[j], ('ml_CTs', j), ('ml_CTbs', j)))
                    lockstep([stage_d(h, u, c0, L, B) for (u, c0, L, B) in bs])
                    for j in range(NS):
                        ck = ('ml_CTs', j)
                        for vc in range(2):
                            p = nps()
                            k.op('pe', I('transpose', PS(p, 128), CTs[:, j, vc * 128:(vc + 1) * 128], ident[:]), r=[ck], w=[('ps', p)])
                            k.op('act', I('copy', ctmps[j][:, vc, :], PS(p, 128)), r=[('ps', p)], w=[('ml_ctmps', j, vc)])
                        k.dma(di['o_Cs'][j, h].rearrange("(vc p) kk -> p vc kk", p=128), ctmps[j][:], r=[('ml_ctmps', j, 0), ('ml_ctmps', j, 1), ('ml_ctmps', j)])
                        k.dma(di['o_ns'][j, h].rearrange("(p o) -> p o", o=1), CTs[:, j, 256:257], r=[ck])

                nw = 0

                for wb_ in range(2):
                    k.groups[('ml_wh', wb_)] = [('ml_wh', wb_, i_) for i_ in range(5)]

                def ml_load(n_, h_):
                    for i_, (d0, s0_, nn_) in enumerate(((0, h_ * 128, 128), (128, 1024 + h_ * 128, 128), (256, 2048 + h_ * 256, 256),
                                                         (512, 4096 + h_ * 256, 256), (768, 6144 + h_ * 256, 256))):
                        load_w(es, wh[n_ % 2][:, :, d0:d0 + nn_], ('ml_wh', n_ % 2, i_), w_in[:, s0_:s0_ + nn_], 8, nn_, 'ml')

                mlpre = Pre([h_ for hi_ in range(2) for h_ in range(8)], ml_load, 1)
                for hi, (t0, TW) in enumerate(HALVES):
                    norm_half(es, di['norm_g'][layer], t0, TW, hT, 'ml%d' % hi, dbuf=False)
                    npu = TP // 128
                    for u in range(npu):
                        gate_unit(u, u * 128, 128)
                    NSS = 0 if os.environ.get('ML_NOSAMPLE') else NS
                    if hi == 1:
                        k.dma(di['o_mp'], mprev[0:1, :], r=['ml_mprev'])
                        for j in range(NSS):
                            k.dma(mprev[:, :], bass.AP(di['st_m'].tensor, di['st_m'][j].offset, [[0, 128], [1, 8]]), w=['ml_mprev'])
                            gate_unit(npu + j, TP + j, 1)
                            k.dma(di['o_ms'][j:j + 1, :], mprev[0:1, :], r=['ml_mprev'])
                    for h in range(8):
                        wb = nw % 2
                        mlpre.need(nw)
                        nw += 1
                        wk = ('ml_wh', wb)
                        k.op('act', I('copy', CTb[:, :], CT[:, h, :]), r=[('ml_CT', h)], w=['ml_CTb'])
                        head_run(h, [[(u, u * 128, 128) for u in range(u0, min(u0 + 2, npu))] for u0 in range(0, npu, 2)], wh[wb], wk)
                        if hi == 1:
                            state_out(h, di['o_Cp'][h], di['o_np'][h])
                            if NSS:
                                sample_run(h, wh[wb], wk, npu)
                        psmod[0] = 6
                        nuu = npu + (NSS if hi == 1 else 0)
                        for vc in range(2):
                            k.dma(yTv[:, h * 2 + vc, t0:t0 + TW], yTs[:, vc, :TW], r=[('ml_yTs', u) for u in range(nuu)])
                k.barrier()
                k.flush()
            out_phase(layer, di['a_w_out'], 16)


        def attn_layer(layer=2):
            w_in = di['c_w_in']
            TP = T // 2
            GR = ((128, 1), (512, 4), (2048, 16))
            qkT = dscr('qkT', [6, 1024, TA], BF16)
            vtok = dscr('vtok', [3, TA, 1024], BF16)
            zT = dscr('zT', [1024, TA], BF16)
            numS = dscr('numS', [3, T, 8 * 130])
            qs_tok = dscr('qs_tok', [3, 2, NS, 1024])
            vs_tok = dscr('vs_tok', [3, NS, 1024])
            os_tok = dscr('os_tok', [NS, 1024])
            ISQ = 128.0 ** -0.5
            with ExitStack() as es:
                wst.clear()
                permf = sb(es, 'at_permf', [128, 128]); k.dma(permf[:], di['c_ropeperm'], w=['at_perm'])
                permb = sb(es, 'at_permb', [128, 128], BF16)
                k.op('dve', I('tensor_copy', permb[:], permf[:]), r=['at_perm'], w=['at_permb'])
                rc = sb(es, 'at_rc', [128, TP + NS]); rs_ = sb(es, 'at_rs', [128, TP + NS])
                wq = [sb(es, 'at_wq%d' % i, [128, 8, 128], BF16) for i in range(3)]
                wvs = [sb(es, 'at_wv%d' % i, [128, 8, 512], BF16) for i in range(2)]
                xb16 = [sb(es, 'at_xb%d' % i, [128, 512], BF16) for i in range(5)]
                t1 = [sb(es, 'at_t1%d' % i, [128, 512]) for i in range(5)]
                res = [sb(es, 'at_res%d' % i, [128, 512]) for i in range(5)]
                stage = [sb(es, 'at_stage%d' % i, [128, TP + NS], BF16) for i in range(3)]
                ktile = [sb(es, 'at_ktile%d' % i, [128, 128]) for i in range(2)]
                vt32 = [sb(es, 'at_vt32%d' % i, [128, 512]) for i in range(3)]
                vt16 = [sb(es, 'at_vt16%d' % i, [128, 512], BF16) for i in range(3)]
                hT = sb(es, 'at_hT', [128, 8, TP + NS], BF16)
                nw = 0; nk = 0; nv = 0; nt3 = [0]

                def v_load(n_, it):
                    c0v = (3 * it[0] + 2) * 1024 + it[1] * 512
                    load_w(es, wvs[n_ % 2], ('at_wv', n_ % 2), w_in[:, c0v:c0v + 512], 8, 512, 'at')

                vpre = Pre([(g_, hb_) for hi_ in range(2) for g_ in range(len(GR)) for hb_ in range(2)], v_load, 1)
                nvw = 0
                for hi, (t0, TW) in enumerate(HALVES):
                    norm_half(es, di['norm_g'][layer], t0, TW, hT, 'at%d' % hi)
                    k.dma(rc[:, :TP], di['c_ropec'][:, t0:t0 + TP], w=['at_rc'])
                    k.dma(rs_[:, :TP], di['c_ropes'][:, t0:t0 + TP], w=['at_rs'])
                    if TW > TP:
                        for (dst_, src_, kk_) in ((rc, di['c_ropec'], 'at_rc'), (rs_, di['c_ropes'], 'at_rs')):
                            for jj in range(NS):
                                k.dma(dst_[:, TP + jj:TP + jj + 1], src_[:, T:T + 1], w=[kk_], allow_slow_non_contiguous=True)
                    tl = [(s0, 512) for s0 in range(0, TP, 512)]
                    if TW > TP:
                        tl.append((TP, NS))
                    for g, (win, dil) in enumerate(GR):
                        wn = str(win)
                        def qk_tile(n, s0, W_, b, wb, g, qk, h, win, wn, t0, last):
                            nonlocal nk
                            p = psalloc()
                            gemm_fm(p, wq[wb], ('at_wq', wb), 0, hT, s0, W_)
                            yield
                            k.op('act', I('copy', xb16[b][:, :W_], PS(p, W_)), r=[('ps', p)], w=[('at_xb', b)])
                            k.op('dve', I('tensor_tensor', t1[b][:, :W_], PS(p, W_), rc[:, s0:s0 + W_], ALU.mult),
                                 r=[('ps', p), 'at_rc'], w=[('at_t1', b)])
                            psfree(p)
                            yield
                            p2 = psalloc()
                            k.op('pe', I('matmul', PS(p2, W_), permb[:], xb16[b][:, :W_], start=True, stop=True),
                                 r=[('at_xb', b), 'at_permb'], w=[('ps', p2)])
                            yield
                            k.op('dve', I('tensor_tensor', res[b][:, :W_], PS(p2, W_), rs_[:, s0:s0 + W_], ALU.mult),
                                 r=[('ps', p2), 'at_rs'], w=[('at_res', b)])
                            psfree(p2)
                            yield
                            k.op('dve', I('tensor_tensor', res[b][:, :W_], res[b][:, :W_], t1[b][:, :W_], ALU.add),
                                 r=[('at_res', b), ('at_t1', b)], w=[('at_res', b)])
                            yield
                            k.op('act', I('copy', stage[wb][:, s0:s0 + W_], res[b][:, :W_]), r=[('at_res', b)], w=[('at_stage', wb, n)])
                            if W_ == 512 and qk == 1:
                                for j in range(4):
                                    tok0 = t0 + s0 + j * 128
                                    if tok0 >= T - min(win, T):
                                        kb_ = nk % 2; nk += 1
                                        p3 = nps()
                                        k.op('pe', I('transpose', PS(p3, 128), res[b][:, j * 128:(j + 1) * 128], ident[:]),
                                             r=[('at_res', b)], w=[('ps', p3)])
                                        k.op('act', I('copy', ktile[kb_][:, :], PS(p3, 128)), r=[('ps', p3)], w=[('at_ktile', kb_)])
                                        o0 = tok0 - (T - min(win, T))
                                        k.dma(di['o_kp' + wn][o0:o0 + 128, h * 128:(h + 1) * 128], ktile[kb_][:, :], r=[('at_ktile', kb_)])
                            if W_ == NS:
                                kb_ = nk % 2; nk += 1
                                p3 = nps()
                                k.op('pe', I('transpose', PS(p3, 128)[:NS, :], res[b][:, :NS], ident[:]), r=[('at_res', b)], w=[('ps', p3)])
                                k.op('act', I('copy', ktile[kb_][:NS, :], PS(p3, 128)[:NS, :]), r=[('ps', p3)], w=[('at_ktile', kb_)])
                                k.dma(qs_tok[g, qk, :, h * 128:(h + 1) * 128], ktile[kb_][:NS, :], r=[('at_ktile', kb_)])
                                if qk == 1:
                                    k.dma(di['o_ks' + wn][:, h * 128:(h + 1) * 128], ktile[kb_][:NS, :], r=[('at_ktile', kb_)])
                            if last:
                                k.dma(qkT[2 * g + qk, h * 128:(h + 1) * 128, t0:t0 + TW], stage[wb][:, :TW],
                                      r=[('at_stage', wb, n_) for n_ in range(len(tl))])

                        def qk_load(n_, it, g=g):
                            c0q = (3 * g + it[0]) * 1024 + it[1] * 128
                            load_w(es, wq[(nwb[0] + n_) % 3], ('at_wq', (nwb[0] + n_) % 3), w_in[:, c0q:c0q + 128], 8, 128, 'at')

                        vpre.need(nvw - 1)
                        nwb = [nw]
                        qkpre = Pre([(qk_, h_) for qk_ in range(2) for h_ in range(8)], qk_load, 1)

                        def qk_tiles(g=g, win=win, wn=wn, t0=t0):
                            nonlocal nw
                            gi = 0
                            for qk in range(2):
                                for h in range(8):
                                    wb = nw % 3; nw += 1
                                    qkpre.need(gi)
                                    gi += 1
                                    for n, (s0, W_) in enumerate(tl):
                                        b = nt3[0] % 5; nt3[0] += 1
                                        yield qk_tile(n, s0, W_, b, wb, g, qk, h, win, wn, t0, n == len(tl) - 1)

                        pipeline(qk_tiles(), 4)
                        for hb in range(2):
                            vpre.need(nvw)
                            wv_, wvk = wvs[nvw % 2], ('at_wv', nvw % 2)
                            nvw += 1
                            subs = [(j * 128, 128) for j in range(TP // 128)] + ([(TP, NS)] if TW > TP else [])
                            for (c0_, L) in subs:
                                b = nv % 3; nv += 1
                                p = nps()
                                for c in range(8):
                                    k.op('pe', I('matmul', PS(p, 512)[:L, :], hT[:, c, c0_:c0_ + L], wv_[:, c, :], start=(c == 0), stop=(c == 7)),
                                         r=[wvk] + hkeys(c0_), w=[('ps', p)])
                                k.op('act', I('copy', vt32[b][:L, :], PS(p, 512)[:L, :]), r=[('ps', p)], w=[('at_vt32', b)])
                                k.op('dve', I('tensor_copy', vt16[b][:L, :], PS(p, 512)[:L, :]), r=[('ps', p)], w=[('at_vt16', b)])
                                if L == 128:
                                    tok0 = t0 + c0_
                                    k.dma(vtok[g, tok0:tok0 + 128, hb * 512:(hb + 1) * 512], vt16[b][:, :], r=[('at_vt16', b)])
                                    if tok0 >= T - min(win, T):
                                        o0 = tok0 - (T - min(win, T))
                                        k.dma(di['o_vp' + wn][o0:o0 + 128, hb * 512:(hb + 1) * 512], vt32[b][:, :], r=[('at_vt32', b)])
                                else:
                                    k.dma(di['o_vs' + wn][:, hb * 512:(hb + 1) * 512], vt32[b][:NS, :], r=[('at_vt32', b)])
                                    k.dma(vs_tok[g, :, hb * 512:(hb + 1) * 512], vt32[b][:NS, :], r=[('at_vt32', b)])
                    for f in range(8):
                        wb = nw % 2; nw += 1
                        load_w(es, wq[wb], ('at_wq', wb), w_in[:, 9216 + f * 128:9216 + (f + 1) * 128], 8, 128, 'at')
                        for n, (s0, W_) in enumerate(tl):
                            p = nps()
                            gemm_fm(p, wq[wb], ('at_wq', wb), 0, hT, s0, W_)
                            k.op('act', I('activation', stage[wb][:, s0:s0 + W_], PS(p, W_), AF.Silu), r=[('ps', p)], w=[('at_stage', wb, n)])
                        k.dma(zT[f * 128:(f + 1) * 128, t0:t0 + TW], stage[wb][:, :TW], r=[('at_stage', wb, n) for n in range(len(tl))])
                k.barrier()
                k.flush()
            with ExitStack() as es:
                wst.clear()
                band = sb(es, 'ab_band', [128, 256]); k.dma(band[:], di['c_band'], w=['ab_band'])
                NB = 8
                qh = [sb(es, 'ab_qh%d' % i, [128, T], BF16) for i in range(2)]
                kh = [sb(es, 'ab_kh%d' % i, [128, T], BF16) for i in range(2)]
                sm = [sb(es, 'ab_sm%d' % i, [128, 256]) for i in range(NB)]
                Pm = [sb(es, 'ab_P%d' % i, [128, 256], BF16) for i in range(NB)]
                PT = [sb(es, 'ab_PT%d' % i, [128, 2, 128], BF16) for i in range(NB)]
                vt = [sb(es, 'ab_vt%d' % i, [128, 2, 128], BF16) for i in range(NB)]
                osb = [sb(es, 'ab_o%d' % i, [128, 130]) for i in range(NB)]
                nmx = [sb(es, 'ab_nmx%d' % i, [128, 1]) for i in range(NB)]
                nu = 0
                ABW = int(os.environ.get('ABW', '7'))
                heads_ = [(g, h) for g in range(len(GR)) for h in range(8)]

                def ab_load(i):
                    g, h = heads_[i]
                    hb = i % 2
                    k.dma(qh[hb][:, :], qkT[2 * g, h * 128:(h + 1) * 128, 0:T], w=[('ab_qh', hb)])
                    k.dma(kh[hb][:, :], qkT[2 * g + 1, h * 128:(h + 1) * 128, 0:T], w=[('ab_kh', hb)])

                def attn_unit(r, n, b, g, h, hb, dil, qv, kv, vg):
                    k0 = max(n - 1, 0) * 128
                    NK = 256 if n > 0 else 128
                    m0 = 0 if n > 0 else 128
                    k.dma(vt[b][:, 0:NK // 128, :], vg[r, k0:k0 + NK, :].rearrange("(j p) e -> p j e", p=128), w=[('ab_vt', b)])
                    p = psalloc()
                    k.op('pe', I('matmul', PS(p, NK), qv[:, r, n * 128:(n + 1) * 128], kv[:, r, k0:k0 + NK], start=True, stop=True),
                         r=[('ab_qh', hb), ('ab_kh', hb)], w=[('ps', p)])
                    yield
                    k.op('dve', I('scalar_tensor_tensor', sm[b][:, :NK], PS(p, NK), ISQ, band[:, m0:m0 + NK], ALU.mult, ALU.add),
                         r=[('ps', p), 'ab_band'], w=[('ab_sm', b)])
                    psfree(p)
                    yield
                    k.op('dve', I('tensor_reduce', osb[b][:, 128:129], sm[b][:, :NK], AX.X, ALU.max), r=[('ab_sm', b)], w=[('ab_o', b, 1)])
                    yield
                    k.op('pool', I('tensor_scalar', nmx[b][:, :], osb[b][:, 128:129], -1.0, None, ALU.mult), r=[('ab_o', b, 1)], w=[('ab_nmx', b)])
                    yield
                    k.op('act', I('activation', Pm[b][:, :NK], sm[b][:, :NK], AF.Exp, bias=nmx[b][:, :], accum_out=osb[b][:, 129:130]),
                         r=[('ab_sm', b), ('ab_nmx', b)], w=[('ab_P', b), ('ab_o', b, 2)])
                    yield
                    for j in range(NK // 128):
                        k.op('pe', I('transpose', PSB(j, 128), Pm[b][:, j * 128:(j + 1) * 128], identb[:]), r=[('ab_P', b)], w=[('psb', j)])
                        k.op('act' if j else 'dve', I('copy' if j else 'tensor_copy', PT[b][:, j, :], PSB(j, 128)), r=[('psb', j)], w=[('ab_PT', b, j)])
                    yield
                    po = psalloc()
                    for j in range(NK // 128):
                        k.op('pe', I('matmul', PS(po, 128), PT[b][:, j, :], vt[b][:, j, :], start=(j == 0), stop=(j == NK // 128 - 1)),
                             r=[('ab_PT', b, j), ('ab_vt', b)], w=[('ps', po)])
                    yield
                    k.op('act', I('copy', osb[b][:, 0:128], PS(po, 128)), r=[('ps', po)], w=[('ab_o', b, 0)])
                    psfree(po)
                    dst = numS[g].rearrange("(u d) f -> d u f", d=dil)[r, n * 128:(n + 1) * 128, h * 130:(h + 1) * 130]
                    k.dma(dst, osb[b][:, :], r=[('ab_o', b, 0), ('ab_o', b, 1), ('ab_o', b, 2)])

                def all_units():
                    nonlocal nu
                    ab_load(0)
                    for i, (g, h) in enumerate(heads_):
                        if i + 1 < len(heads_):
                            ab_load(i + 1)
                        win, dil = GR[g]
                        nb_ = (T // dil) // 128
                        hb = i % 2
                        qv = qh[hb][:, :].rearrange("p (u d) -> p d u", d=dil)
                        kv = kh[hb][:, :].rearrange("p (u d) -> p d u", d=dil)
                        vg = vtok[g, 0:T, h * 128:(h + 1) * 128].rearrange("(u d) e -> d u e", d=dil)
                        for r in range(dil):
                            for n in range(nb_):
                                b = nu % NB
                                nu += 1
                                yield attn_unit(r, n, b, g, h, hb, dil, qv, kv, vg)

                pipeline(all_units(), ABW, ramp=1)
                k.barrier()
                k.flush()
            with ExitStack() as es:
                wst.clear()
                NBC = 4
                A = [sb(es, 'ac_A%d' % i, [128, 3, 8, 130]) for i in range(NBC)]
                zt = [sb(es, 'ac_z%d' % i, [128, 8, 128], BF16) for i in range(NBC)]
                from types import SimpleNamespace as _NS
                MB = []
                for par in range(NBC):
                    m_ = _NS(par=par)
                    m_.M = sb(es, 'ac_M', [128, 8]); m_.wg = sb(es, 'ac_w', [128, 3, 8]); m_.den = sb(es, 'ac_den', [128, 8])
                    m_.tmp = sb(es, 'ac_tmp', [128, 3, 8]); m_.acc = sb(es, 'ac_acc', [128, 8, 128]); m_.ob = sb(es, 'ac_ob', [128, 8, 128], BF16)
                    MB.append(m_)
                yo = [sb(es, 'ac_yo%d' % i, [128, 8, 128], BF16) for i in range(NBC)]
                acc = MB[0].acc; ob = MB[0].ob

                def merge_g(Av, L, akeys, m_):
                    P_ = m_.par
                    K_ = lambda nm, *a: (nm, P_) + a
                    M, wg_, den, tmp, acc_, ob_ = m_.M, m_.wg, m_.den, m_.tmp, m_.acc, m_.ob
                    k.op('dve', I('tensor_tensor', M[:L, :], Av[:L, 0, :, 128], Av[:L, 1, :, 128], ALU.max), r=akeys, w=[K_('ac_M')])
                    yield
                    k.op('dve', I('tensor_tensor', M[:L, :], M[:L, :], Av[:L, 2, :, 128], ALU.max), r=akeys + [K_('ac_M')], w=[K_('ac_M')])
                    yield
                    for g in range(3):
                        k.op('dve', I('tensor_tensor', wg_[:L, g, :], Av[:L, g, :, 128], M[:L, :], ALU.subtract), r=akeys + [K_('ac_M')], w=[K_('ac_w', g)])
                    yield
                    k.op('act', I('activation', wg_[:L, :, :], wg_[:L, :, :], AF.Exp), r=[K_('ac_w', g) for g in range(3)], w=[K_('ac_w', g) for g in range(3)])
                    yield
                    for g in range(3):
                        k.op('dve', I('tensor_tensor', tmp[:L, g, :], wg_[:L, g, :], Av[:L, g, :, 129], ALU.mult), r=akeys + [K_('ac_w', g)], w=[K_('ac_tmp', g)])
                    yield
                    k.op('dve', I('tensor_tensor', den[:L, :], tmp[:L, 0, :], tmp[:L, 1, :], ALU.add), r=[K_('ac_tmp', 0), K_('ac_tmp', 1)], w=[K_('ac_den')])
                    yield
                    k.op('dve', I('tensor_tensor', den[:L, :], den[:L, :], tmp[:L, 2, :], ALU.add), r=[K_('ac_tmp', 2), K_('ac_den')], w=[K_('ac_den')])
                    yield
                    k.op('dve', I('reciprocal', den[:L, :], den[:L, :]), r=[K_('ac_den')], w=[K_('ac_den')])
                    yield
                    for h in range(8):
                        k.op('dve', I('tensor_scalar', acc_[:L, h, :], Av[:L, 0, h, 0:128], wg_[:L, 0, h:h + 1], None, ALU.mult),
                             r=akeys + [K_('ac_w', 0)], w=[K_('ac_acc', h)])
                    yield
                    for g in (1, 2):
                        for h in range(8):
                            k.op('dve', I('scalar_tensor_tensor', acc_[:L, h, :], Av[:L, g, h, 0:128], wg_[:L, g, h:h + 1], acc_[:L, h, :], ALU.mult, ALU.add),
                                 r=akeys + [K_('ac_w', g), K_('ac_acc', h)], w=[K_('ac_acc', h)])
                        yield
                    for h in range(8):
                        k.op('act', I('mul', ob_[:L, h, :], acc_[:L, h, :], den[:L, h:h + 1]),
                             r=[K_('ac_acc', h), K_('ac_den')], w=[K_('ac_ob', h)])

                def tile_g(tt, b):
                    m_ = MB[b]
                    for g in range(3):
                        k.dma(A[b][:, g, :, :], numS[g, tt * 128:(tt + 1) * 128, :].rearrange("t (h f) -> t h f", f=130), w=[('ac_A', b, g)])
                    k.dma(zt[b][:, :, :128], zT.rearrange("(c p) t -> p c t", p=128)[:, :, tt * 128:(tt + 1) * 128], w=[('ac_z', b)])
                    yield
                    yield from merge_g(A[b], 128, [('ac_A', b, g) for g in range(3)], m_)
                    yield
                    for h in range(8):
                        k.op('pe', I('transpose', PSB(h % 2, 128), m_.ob[:, h, :], identb[:, :]), r=[('ac_ob', b, h)], w=[('psb', h % 2)])
                        k.op('dve', I('tensor_tensor', yo[b][:, h, :], PSB(h % 2, 128), zt[b][:, h, :], ALU.mult),
                             r=[('psb', h % 2), ('ac_z', b)], w=[('ac_yo', b, h)])
                    k.dma(yTv[:, 0:8, tt * 128:(tt + 1) * 128], yo[b][:, :, :], r=[('ac_yo', b, h) for h in range(8)])

                pipeline((tile_g(tt, tt % NBC) for tt in range(T // 128)), NBC, ramp=1)

                def merge(Av, L, akeys):
                    for _ in merge_g(Av, L, akeys, MB[0]):
                        pass

                As = sb(es, 'as_A', [128, 3, 8, 130])
                Kc = sb(es, 'as_K', [128, 1024]); Vc = sb(es, 'as_V', [128, 1024])
                qb = sb(es, 'as_qb', [128, 1024]); kn = sb(es, 'as_kn', [128, 1024]); vn = sb(es, 'as_vn', [1, 1024])
                prod = sb(es, 'as_prod', [128, 1024])
                sc2 = sb(es, 'as_sc2', [128, 16])
                sall = sb(es, 'as_sall', [8, 130]); Ps = sb(es, 'as_P', [8, 130]); mxs = sb(es, 'as_mx', [8, 2]); nmxs = sb(es, 'as_nmx', [8, 1])
                PTk = sb(es, 'as_PTk', [128, 8]); PTn = sb(es, 'as_PTn', [1, 24])
                k.op('dve', I('memset', As[:], 0.0), w=[('as_A', j) for j in range(NS)])
                for j in range(NS):
                    for g, (win, dil) in enumerate(GR):
                        wn = str(win)
                        k.dma(Kc[:, :], di['ck' + wn][j].rearrange("(u d) f -> d u f", d=dil)[0, :, :], w=['as_K'])
                        k.dma(Vc[:, :], di['cv' + wn][j].rearrange("(u d) f -> d u f", d=dil)[0, :, :], w=['as_V'])
                        k.dma(qb[:, :], bass.AP(qs_tok.tensor, qs_tok[g, 0, j].offset, [[0, 128], [1, 1024]]), w=['as_qb'])
                        k.dma(kn[:, :], bass.AP(qs_tok.tensor, qs_tok[g, 1, j].offset, [[0, 128], [1, 1024]]), w=['as_kn'])
                        k.dma(vn[:, :], vs_tok[g, j:j + 1, :], w=['as_vn'])
                        k.op('dve', I('tensor_tensor', prod[:, :], Kc[:, :], qb[:, :], ALU.mult), r=['as_K', 'as_qb'], w=['as_prod'])
                        k.op('dve', I('tensor_reduce', sc2[:, 0:8], prod[:, :].rearrange("p (h e) -> p h e", e=128), AX.X, ALU.add), r=['as_prod'], w=['as_sc2'])
                        k.op('dve', I('tensor_tensor', prod[:, :], kn[:, :], qb[:, :], ALU.mult), r=['as_kn', 'as_qb', 'as_sc2'], w=['as_prod'])
                        k.op('dve', I('tensor_reduce', sc2[:, 8:16], prod[:, :].rearrange("p (h e) -> p h e", e=128), AX.X, ALU.add), r=['as_prod'], w=['as_sc2'])
                        p = nps()
                        k.op('pe', I('transpose', PS(p, 128)[:8, :], sc2[:, 0:8], ident[:]), r=['as_sc2'], w=[('ps', p)])
                        k.op('act', I('mul', sall[:, 0:128], PS(p, 128)[:8, :], ISQ), r=[('ps', p)], w=['as_sall'])
                        p = nps()
                        k.op('pe', I('transpose', PS(p, 128)[:8, :], sc2[:, 8:16], ident[:]), r=['as_sc2'], w=[('ps', p)])
                        k.op('act', I('mul', sall[:, 128:129], PS(p, 128)[:8, 0:1], ISQ), r=[('ps', p)], w=['as_sall'])
                        k.op('dve', I('tensor_reduce', mxs[:, 0:1], sall[:, 0:129], AX.X, ALU.max), r=['as_sall'], w=['as_mx'])
                        k.op('act', I('mul', nmxs[:, :], mxs[:, 0:1], -1.0), r=['as_mx'], w=['as_nmx'])
                        k.op('act', I('activation', Ps[:, 0:129], sall[:, 0:129], AF.Exp, bias=nmxs[:, :], accum_out=mxs[:, 1:2]),
                             r=['as_sall', 'as_nmx'], w=['as_P', 'as_mx'])
                        p = nps()
                        k.op('pe', I('transpose', PS(p, 8), Ps[:, 0:128], ident[:8, :8]), r=['as_P'], w=[('ps', p)])
                        k.op('act', I('copy', PTk[:, :], PS(p, 8)), r=[('ps', p)], w=['as_PTk'])
                        p = nps()
                        k.op('pe', I('transpose', PS(p, 8)[:1, :], Ps[:, 128:129], ident[:8, :8]), r=['as_P'], w=[('ps', p)])
                        k.op('act', I('copy', PTn[:, 0:8], PS(p, 8)[:1, :]), r=[('ps', p)], w=['as_PTn'])
                        p = nps()
                        k.op('pe', I('transpose', PS(p, 8)[:1, :], mxs[:, 0:1], ident[:8, :8]), r=['as_mx'], w=[('ps', p)])
                        k.op('act', I('copy', PTn[:, 8:16], PS(p, 8)[:1, :]), r=[('ps', p)], w=['as_PTn'])
                        p = nps()
                        k.op('pe', I('transpose', PS(p, 8)[:1, :], mxs[:, 1:2], ident[:8, :8]), r=['as_mx'], w=[('ps', p)])
                        k.op('act', I('copy', PTn[:, 16:24], PS(p, 8)[:1, :]), r=[('ps', p)], w=['as_PTn'])
                        k.op('dve', I('tensor_copy', As[j:j + 1, g, :, 128] if False else As[0:1, g, :, 128], PTn[:, 8:16]), r=['as_PTn'], w=[('as_A', j)])
                        k.op('dve', I('tensor_copy', As[0:1, g, :, 129], PTn[:, 16:24]), r=['as_PTn'], w=[('as_A', j)])
                        for h in range(8):
                            p = nps()
                            k.op('pe', I('matmul', PS(p, 128)[:1, :], PTk[:, h:h + 1], Vc[:, h * 128:(h + 1) * 128], start=True, stop=False),
                                 r=['as_PTk', 'as_V'], w=[('ps', p)])
                            k.op('pe', I('matmul', PS(p, 128)[:1, :], PTn[:, h:h + 1], vn[:, h * 128:(h + 1) * 128], start=False, stop=True),
                                 r=['as_PTn', 'as_vn'], w=[('ps', p)])
                            k.op('act', I('copy', As[0:1, g, h, 0:128], PS(p, 128)[:1, :]), r=[('ps', p)], w=[('as_A', j)])
                    merge(As, 1, [('as_A', j)])
                    k.op('act', I('copy', acc[0:1, :, :], ob[0:1, :, :]), r=[('ac_ob', 0, h) for h in range(8)], w=[('ac_acc', 0, h) for h in range(8)])
                    k.dma(os_tok[j:j + 1, :], acc[0:1, :, :].rearrange("p h e -> p (h e)"), r=[('ac_acc', 0, h) for h in range(8)])
                k.barrier()
                osT = sb(es, 'as_osT', [128, 8, NS])
                rows_to_fm(es, os_tok, NS, 1024, osT, 'as_osT', 'ao')
                k.dma(zt[0][:, :, :NS], zT.rearrange("(c p) t -> p c t", p=128)[:, :, T:T + NS], w=[('ac_z', 0)])
                k.op('dve', I('tensor_tensor', yo[0][:, :, :NS], osT[:, :, :], zt[0][:, :, :NS], ALU.mult), r=['as_osT', ('ac_z', 0)], w=[('ac_yo', 0, 0)])
                k.dma(yTv[:, 0:8, T:T + NS], yo[0][:, :, :NS], r=[('ac_yo', 0, 0)])
                k.barrier()
                k.flush()
            out_phase(layer, di['c_w_out'], 8)

        MIXERS = {0: mlstm_layer, 1: conv_layer, 2: attn_layer, 3: pool_layer}
        for li in layers:
            MIXERS[li]()

        with ExitStack() as es:
            wst.clear()
            hf = [sb(es, 'hf%d' % i, [128, 8, 512]) for i in range(2)]
            yo = [sb(es, 'yo%d' % i, [128, D]) for i in range(2)]
            n = 0
            ftl = [(s0, min(512, TA - s0)) for s0 in list(range(0, T, 512)) + [T]]
            for fn_, (s0, W_) in enumerate(ftl):
                norm_tile_f32(es, di['final_g'], ftl, fn_, hf)
                for j0 in range(0, W_, 128):
                    L = min(128, W_ - j0)
                    b = n % 2
                    n += 1
                    for c in range(8):
                        p = nps()
                        k.op('pe', I('transpose', PS(p, 128)[:L, :], hf[fn_ % 2][:, c, j0:j0 + L], ident[:]),
                             r=[('hf', fn_ % 2, c)], w=[('ps', p)])
                        k.op('act' if c % 2 else 'dve',
                             I('copy' if c % 2 else 'tensor_copy', yo[b][:L, c * 128:(c + 1) * 128], PS(p, 128)[:L, :]),
                             r=[('ps', p)], w=[('yo', b, c)])
                    dst = di['y_prompt'][s0 + j0:s0 + j0 + L, :] if s0 < T else di['y_sample']
                    k.dma(dst, yo[b][:L, :], r=[('yo', b, c) for c in range(8)])
            if dbg:
                k.barrier()
                for c in range(8):
                    k.dma(di['dbg_xT'][c * 128:(c + 1) * 128, :], xT[c * 128:(c + 1) * 128, :])
            k.barrier()
            k.flush()
    return nc


_CACHE = {}


def _prep_inputs(inp, T):
    cst = host_consts(T)
    shared = {}
    for k_ in W_SHAPES:
        a = np.asarray(inp[k_], dtype=np.float32)
        if k_ in ('norm_g', 'pe_w', 'pg_w', 'final_g'):
            shared[k_] = np.ascontiguousarray(a)
        else:
            shared[k_] = np.ascontiguousarray(a[0])
    for k_ in CONST_SHAPES:
        shared['c_' + k_] = cst[k_]
    shared['c_ropec'] = cst['ropec']
    shared['c_ropes'] = cst['ropes']
    maps = []
    for c in range(8):
        b = c % 4
        sl = slice(NS * c, NS * c + NS)
        m = dict(shared)
        m['x_prompt'] = np.ascontiguousarray(inp['x_prompt'][b])
        m['x_sample'] = np.ascontiguousarray(inp['x_sample'][sl, 0])
        m['p_prompt'] = np.ascontiguousarray(inp['p_prompt'][:, b])
        m['p_sample'] = np.ascontiguousarray(inp['p_sample'][:, sl, 0])
        m['st_C'] = np.ascontiguousarray(inp['state_mlstm_C'][0, sl])
        m['st_n'] = np.ascontiguousarray(inp['state_mlstm_n'][0, sl])
        m['st_m'] = np.ascontiguousarray(inp['state_mlstm_m'][0, sl])
        m['st_conv'] = np.ascontiguousarray(inp['state_conv'][0, sl])
        m['st_pool'] = np.ascontiguousarray(inp['state_pool'][0, sl])
        for wn in ('128', '512', '2048'):
            m['ck' + wn] = np.ascontiguousarray(inp['cache_k_w' + wn][0, sl]).reshape(NS, -1, 1024)
            m['cv' + wn] = np.ascontiguousarray(inp['cache_v_w' + wn][0, sl]).reshape(NS, -1, 1024)
        maps.append(m)
    return maps


def _gather(res, T):
    R = res.results
    B = 4

    def P(name):
        return np.stack([np.asarray(R[c][name]) for c in range(B)])

    def S(name):
        return np.concatenate([np.asarray(R[c][name]) for c in range(8)], axis=0)

    outs = [P('y_prompt'), S('y_sample')[:, None, :],
            P('o_Cp')[None], S('o_Cs')[None],
            P('o_np')[None], S('o_ns')[None],
            P('o_mp')[:, 0][None], S('o_ms')[None],
            P('o_convp')[None], S('o_convs')[None]]
    for wn in ('128', '512', '2048'):
        nk = min(int(wn), T)
        outs += [P('o_kp' + wn).reshape(1, B, nk, 8, 128), S('o_ks' + wn).reshape(1, 32, 1, 8, 128),
                 P('o_vp' + wn).reshape(1, B, nk, 8, 128), S('o_vs' + wn).reshape(1, 32, 1, 8, 128)]
    outs += [P('o_poolp')[None], S('o_pools')[None]]
    return tuple(np.ascontiguousarray(o, dtype=np.float32) for o in outs)


def kernel(**inp):
    T = int(np.asarray(inp['x_prompt']).shape[1])
    if T not in _CACHE:
        _CACHE[T] = build(T)
    nc = _CACHE[T]
    inp = {k_: np.asarray(v) for k_, v in inp.items()}
    maps = _prep_inputs(inp, T)
    res = run_bass_kernel_spmd(nc, maps, core_ids=list(range(8)))
    return _gather(res, T)
```

```python
import os
import numpy as np
from contextlib import ExitStack
import concourse.bass as bass
import concourse.mybir as mybir
from concourse.bass_utils import run_bass_kernel_spmd

F32 = mybir.dt.float32
BF16 = mybir.dt.bfloat16
AF = mybir.ActivationFunctionType
ALU = mybir.AluOpType
AX = mybir.AxisListType

ENGS = ('pe', 'act', 'dve', 'pool', 'sp')
NDMA = 32
D = 1024
NS = 4
EPS = 1e-6


def I(name, *a, **kw):
    return lambda e: getattr(e, name)(*a, **kw)


PHASE_LOG = []


class Pre:
    def __init__(self, items, load, depth):
        self.items, self.load, self.depth, self.n = items, load, depth, 0

    def need(self, i):
        while self.n <= min(i + self.depth, len(self.items) - 1):
            self.load(self.n, self.items[self.n])
            self.n += 1


def pipeline(gen_iter, width, ramp=None):
    active = []
    it = iter(gen_iter)
    done = False
    while True:
        started = 0
        while len(active) < width and not done and (ramp is None or started < ramp):
            try:
                active.append(next(it))
                started += 1
            except StopIteration:
                done = True
        if not active:
            break
        nxt = []
        for g in active:
            try:
                next(g)
                nxt.append(g)
            except StopIteration:
                pass
        active = nxt


def lockstep(gens):
    gens = list(gens)
    while gens:
        nxt = []
        for g in gens:
            try:
                next(g)
                nxt.append(g)
            except StopIteration:
                pass
        gens = nxt


class Trk:
    def __init__(self, nc, sems, dsems, nsw=8):
        self.nc = nc
        self.eng = {'pe': nc.tensor, 'act': nc.scalar, 'dve': nc.vector, 'pool': nc.gpsimd, 'sp': nc.sync}
        self.sems = dict(sems)
        for i, s in enumerate(dsems):
            self.sems['d%d' % i] = s
        self.cnt = {e: 0 for e in ENGS}
        self.dtot = [0] * len(dsems)
        self.nhw = len(dsems) - nsw
        self.rr = 0
        self.rr_sw = 0
        self.known = {e: {} for e in ENGS}
        self.streams = {e: [] for e in ENGS}
        self.last_w = {}
        self.readers = {}
        self.groups = {}

    def _exp(self, keys):
        out = []
        for k in keys:
            out.extend(self.groups.get(k, (k,)))
        return out

    def _deps(self, en, r, w):
        r = self._exp(r)
        w = self._exp(w)
        deps = []
        for k in r:
            t = self.last_w.get(k)
            if t:
                deps.append(t)
            if isinstance(k, tuple) and k[0] in ('ps', 'psb'):
                deps.extend(t2 for t2 in self.readers.get(k, ()) if t2[0] != en)
        for k in w:
            t = self.last_w.get(k)
            if t:
                deps.append(t)
            deps.extend(self.readers.get(k, ()))
        waits = []
        kn = self.known[en]
        for (s, v) in deps:
            if s == 'pe' and en == 'pe':
                continue
            if kn.get(s, 0) >= v:
                continue
            kn[s] = v
            waits.append((s, v))
        return waits

    def _commit(self, tok, r, w):
        r = self._exp(r)
        w = self._exp(w)
        for k in r:
            self.readers.setdefault(k, []).append(tok)
        for k in w:
            self.last_w[k] = tok
            self.readers[k] = []

    def op(self, en, fn, r=(), w=()):
        waits = self._deps(en, r, w)
        self.cnt[en] += 1
        tok = (en, self.cnt[en])
        self.streams[en].append((waits, fn, en, 1))
        self._commit(tok, r, w)

    def dma(self, out, in_, r=(), w=(), q='sp', **kw):
        waits = self._deps(q, r, w)
        if q == 'pool':
            i = self.nhw + self.rr_sw
            self.rr_sw = (self.rr_sw + 1) % (len(self.dtot) - self.nhw)
        else:
            i = self.rr
            self.rr = (self.rr + 1) % self.nhw
        s = 'd%d' % i
        if self.known[q].get(s, 0) < self.dtot[i]:
            self.known[q][s] = self.dtot[i]
            waits.append((s, self.dtot[i]))
        self.dtot[i] += 16
        tok = (s, self.dtot[i])
        self.streams[q].append((waits, I('dma_start', out=out, in_=in_, **kw), s, 16))
        self._commit(tok, r, w)

    def barrier(self):
        allt = [(e, self.cnt[e]) for e in ENGS if self.cnt[e] > 0]
        allt += [('d%d' % i, v) for i, v in enumerate(self.dtot) if v > 0]
        for en in ENGS:
            waits = []
            for (s, v) in allt:
                if self.known[en].get(s, 0) < v:
                    self.known[en][s] = v
                    waits.append((s, v))
            if waits:
                self.streams[en].append((waits, None, None, 0))
        self.last_w = {}
        self.readers = {}

    def flush(self):
        PHASE_LOG.append(dict(self.cnt))
        nc = self.nc
        streams = self.streams
        self.streams = {e: [] for e in ENGS}
        sems = self.sems

        def replay(en):
            def f(e):
                for (waits, fn, s, inc) in streams[en]:
                    for (ws, wv) in waits:
                        e.wait_ge(sems[ws], wv)
                    if fn is not None:
                        fn(e).then_inc(sems[s], inc)
            return f

        with nc.Block() as block:
            block.tensor(replay('pe'))
            block.scalar(replay('act'))
            block.vector(replay('dve'))
            block.gpsimd(replay('pool'))
            block.sync(replay('sp'))


def host_consts(T):
    c = {}
    c['ident'] = np.eye(128, dtype=np.float32)
    i = np.arange(128)
    c['triu'] = (i[:, None] <= i[None, :]).astype(np.float32)
    c['maskneg'] = np.where(i[None, :] <= i[:, None], 0.0, -1e30).astype(np.float32)
    sel = np.zeros((128, 128), np.float32)
    sel[127, :] = 1.0
    c['sel_last'] = sel
    qi = i[:, None]
    kj = np.arange(256)[None, :]
    dist = qi + 128 - kj
    band = (dist >= 0) & (dist <= 128)
    c['band'] = np.where(band, 0.0, -1e30).astype(np.float32)
    c['band0'] = np.where(band & (kj >= 128), 0.0, -1e30).astype(np.float32)
    half = 16
    inv = (500000.0 ** (-np.arange(half, dtype=np.float32) / half)).astype(np.float32)
    pos = np.concatenate([np.arange(T), np.array([8192])]).astype(np.float32)
    ang = (pos[None, :] * inv[:, None]).astype(np.float32)
    cosT = np.ones((128, T + 1), np.float32)
    sinT = np.zeros((128, T + 1), np.float32)
    cosT[0:16] = np.cos(ang)
    cosT[16:32] = np.cos(ang)
    sinT[0:16] = -np.sin(ang)
    sinT[16:32] = np.sin(ang)
    c['ropec'] = cosT
    c['ropes'] = sinT
    pm = np.zeros((128, 128), np.float32)
    for p in range(16):
        pm[p + 16, p] = 1.0
        pm[p, p + 16] = 1.0
    c['ropeperm'] = pm
    ic = np.zeros((128, 16, 16), np.float32)
    for ch in range(16):
        wdw = (2, 4, 8, 16)[ch // 4]
        for t in range(16):
            ic[:, ch, t] = 1.0 / min(wdw, t + 1)
    c['invcnt'] = ic.reshape(128, 256)
    return c


CONST_SHAPES = {'ident': [128, 128], 'triu': [128, 128], 'maskneg': [128, 128], 'sel_last': [128, 128],
                'band': [128, 256], 'band0': [128, 256], 'ropeperm': [128, 128], 'invcnt': [128, 256]}

W_SHAPES = {
    'norm_g': [4, 1024], 'pe_w': [4, 256, 1024], 'pg_w': [4, 1024, 1024], 'final_g': [1024],
    'a_w_in': [1024, 8208], 'a_b_if': [16], 'a_norm_g': [2048], 'a_w_out': [2048, 1024],
    'b_w_in': [1024, 8192], 'b_conv_w': [3, 2048], 'b_w_out': [2048, 1024],
    'c_w_in': [1024, 10240], 'c_w_out': [1024, 1024],
    'd_w_in': [1024, 4096], 'd_w_grp': [4, 512, 512], 'd_scale': [2048], 'd_w_out': [2048, 1024],
}


def build(T=4096, layers=(0, 1, 2, 3), dbg=False):
    TA = T + NS
    NCH = T // 128
    nc = bass.Bass("TRN2", target_bir_lowering=False)
    di = {}

    def din(name, shape, dt=F32):
        di[name] = nc.dram_tensor(name, list(shape), dt, kind="ExternalInput").ap()
        return di[name]

    def dout(name, shape, dt=F32):
        di[name] = nc.dram_tensor(name, list(shape), dt, kind="ExternalOutput").ap()
        return di[name]

    def dscr(name, shape, dt=F32):
        di[name] = nc.dram_tensor(name, list(shape), dt, kind="Internal").ap()
        return di[name]

    din('x_prompt', [T, D]); din('x_sample', [NS, D])
    din('p_prompt', [4, T, 256]); din('p_sample', [4, NS, 256])
    din('st_C', [NS, 8, 256, 128]); din('st_n', [NS, 8, 128]); din('st_m', [NS, 8])
    din('st_conv', [NS, 2, 2048]); din('st_pool', [NS, 15, 2048])
    for wn, nb in (('128', 128), ('512', 512), ('2048', 2048)):
        din('ck' + wn, [NS, nb, 1024]); din('cv' + wn, [NS, nb, 1024])
    for k_, s_ in W_SHAPES.items():
        din(k_, s_)
    for k_, s_ in CONST_SHAPES.items():
        din('c_' + k_, s_)
    din('c_ropec', [128, T + 1]); din('c_ropes', [128, T + 1])

    dout('y_prompt', [T, D]); dout('y_sample', [NS, D])
    dout('o_Cp', [8, 256, 128]); dout('o_Cs', [NS, 8, 256, 128])
    dout('o_np', [8, 128]); dout('o_ns', [NS, 8, 128])
    dout('o_mp', [1, 8]); dout('o_ms', [NS, 8])
    dout('o_convp', [2, 2048]); dout('o_convs', [NS, 2, 2048])
    for wn, nb in (('128', 128), ('512', 512), ('2048', 2048)):
        nk = min(nb, T)
        dout('o_kp' + wn, [nk, 1024]); dout('o_ks' + wn, [NS, 1024])
        dout('o_vp' + wn, [nk, 1024]); dout('o_vs' + wn, [NS, 1024])
    dout('o_poolp', [15, 2048]); dout('o_pools', [NS, 15, 2048])
    if dbg:
        dout('dbg_xT', [D, TA])

    xT = dscr('xT', [D, TA])
    yT = dscr('yT', [2048, TA], BF16)

    with ExitStack() as top:
        sems = {e: top.enter_context(nc.semaphore('s_' + e)) for e in ENGS}
        dsems = [top.enter_context(nc.semaphore('sd%d' % i)) for i in range(NDMA)]
        k = Trk(nc, sems, dsems)

        uid = [0]

        def sb(es, name, shape, dt=F32):
            uid[0] += 1
            return es.enter_context(nc.sbuf_tensor('%s_%d' % (name, uid[0]), list(shape), dt))

        ident = sb(top, 'ident', [128, 128]); identb = sb(top, 'identb', [128, 128], BF16)
        onesb = sb(top, 'onesb', [128, 128], BF16); onesf = sb(top, 'onesf', [128, 128])
        epsc = sb(top, 'epsc', [128, 1])
        psF = top.enter_context(nc.psum_tensor('psF', [128, 6 * 512], F32))
        psB = top.enter_context(nc.psum_tensor('psB', [128, 2 * 1024], BF16))

        def PS(i, n=512):
            return psF[:, i * 512:i * 512 + n]

        def PSB(i, n=128):
            return psB[:, i * 1024:i * 1024 + n]

        k.dma(ident[:], di['c_ident'], w=['ident'])
        k.op('dve', I('tensor_copy', identb[:], ident[:]), r=['ident'], w=['identb'])
        k.op('dve', I('memset', onesb[:], 1.0), w=['onesb'])
        k.op('dve', I('memset', onesf[:], 1.0), w=['onesf'])
        k.op('dve', I('memset', epsc[:], EPS), w=['epsc'])
        k.barrier()

        rrp = [0]

        psmod = [6]

        pslive = set()

        def nps():
            for _ in range(psmod[0]):
                rrp[0] = (rrp[0] + 1) % psmod[0]
                if rrp[0] not in pslive:
                    return rrp[0]
            raise RuntimeError('no free PSUM bank')

        def psalloc():
            p = nps()
            pslive.add(p)
            return p

        def psfree(p):
            pslive.discard(p)

        with ExitStack() as es:
            xin = [sb(es, 'xin%d' % i, [128, D]) for i in range(4)]
            xo = [sb(es, 'xo%d' % i, [128, 8, 128]) for i in range(4)]
            tiles = [(t * 128, 128, di['x_prompt'][t * 128:(t + 1) * 128, :]) for t in range(NCH)]
            tiles.append((T, NS, di['x_sample']))

            def p0_load(n):
                k.dma(xin[n % 4][:tiles[n][1], :], tiles[n][2], w=[('xin', n % 4)])

            p0_load(0)
            p0_load(1)
            for n, (t0, L, src) in enumerate(tiles):
                b = n % 4
                if n + 2 < len(tiles):
                    p0_load(n + 2)
                for c in range(8):
                    p = nps()
                    k.op('pe', I('transpose', PS(p, L), xin[b][:L, c * 128:(c + 1) * 128], ident[:L, :L]),
                         r=[('xin', b)], w=[('ps', p)])
                    k.op('act' if c % 2 else 'dve',
                         I('copy' if c % 2 else 'tensor_copy', xo[b][:, c, :L], PS(p, L)),
                         r=[('ps', p)], w=[('xo', b, c)])
                k.dma(xT.rearrange("(c p) t -> p c t", p=128)[:, :, t0:t0 + L], xo[b][:, :, :L],
                      r=[('xo', b, c) for c in range(8)])
            k.barrier()
            k.flush()

        def bcast_row(ap1d, n):
            return bass.AP(ap1d.tensor, ap1d.offset, [[0, 128], [1, n]])

        def norm_half(es, g_ap, t0, TW, hT, tag, dbuf=True):
            nb_ = 2 if dbuf else 1
            if '_norm' not in wst:
                gcol = sb(es, 'gcol' + tag, [128, 8])
                k.dma(gcol[:], g_ap.rearrange("(c p) -> p c", p=128), w=['gcol'], allow_slow_non_contiguous=True)
                xt = [sb(es, 'nx%s%d' % (tag, i), [128, 8, 512]) for i in range(nb_)]
                sq = [sb(es, 'nsq%s%d' % (tag, i), [128, 8, 512], BF16) for i in range(nb_)]
                rs = [sb(es, 'nrs%s%d' % (tag, i), [128, 512]) for i in range(nb_)]
                wst['_norm'] = (gcol, xt, sq, rs)
            gcol, xt, sq, rs = wst['_norm']
            ntl = [(s0, min(512, TW - s0)) for s0 in range(0, TW, 512)]

            def n_load(n):
                s0, W_ = ntl[n]
                b = n % nb_
                k.dma(xt[b][:, :, :W_], xT.rearrange("(c p) t -> p c t", p=128)[:, :, t0 + s0:t0 + s0 + W_],
                      w=[('nx', b)])

            n_load(0)
            for n, (s0, W_) in enumerate(ntl):
                b = n % nb_
                if nb_ == 2 and n + 1 < len(ntl):
                    n_load(n + 1)
                k.op('act', I('activation', sq[b][:, :, :W_], xt[b][:, :, :W_], AF.Square),
                     r=[('nx', b)], w=[('nsq', b)])
                p = nps()
                for c in range(8):
                    k.op('pe', I('matmul', PS(p, W_), onesb[:], sq[b][:, c, :W_], start=(c == 0), stop=(c == 7)),
                         r=[('nsq', b), 'onesb'], w=[('ps', p)])
                k.op('act', I('activation', rs[b][:, :W_], PS(p, W_), AF.Sqrt, bias=epsc[:], scale=1.0 / D),
                     r=[('ps', p), 'epsc'], w=[('nrs', b)])
                k.op('dve', I('reciprocal', rs[b][:, :W_], rs[b][:, :W_]), r=[('nrs', b)], w=[('nrs', b)])
                for c in range(8):
                    k.op('dve',
                         I('scalar_tensor_tensor', hT[:, c, s0:s0 + W_], xt[b][:, c, :W_], gcol[:, c:c + 1],
                           rs[b][:, :W_], ALU.mult, ALU.mult),
                         r=[('nx', b), ('nrs', b), 'gcol'], w=[('hT', c, s0 // 512)])
                if nb_ == 1 and n + 1 < len(ntl):
                    n_load(n + 1)

        fin_state = {}

        def norm_tile_f32(es, g_ap, tl_, n, hf):
            if 'g' not in fin_state:
                fin_state['g'] = sb(es, 'fgcol', [128, 8])
                k.dma(fin_state['g'][:], g_ap.rearrange("(c p) -> p c", p=128), w=['fgcol'], allow_slow_non_contiguous=True)
                fin_state['xt'] = [sb(es, 'fnx%d' % i, [128, 8, 512]) for i in range(2)]
                fin_state['sq'] = [sb(es, 'fnsq%d' % i, [128, 8, 512], BF16) for i in range(2)]
                fin_state['rs'] = [sb(es, 'fnrs%d' % i, [128, 512]) for i in range(2)]

            def f_load(m):
                t0_, Wm = tl_[m]
                k.dma(fin_state['xt'][m % 2][:, :, :Wm], xT.rearrange("(c p) t -> p c t", p=128)[:, :, t0_:t0_ + Wm],
                      w=[('fnx', m % 2)])

            if n == 0:
                f_load(0)
            if n + 1 < len(tl_):
                f_load(n + 1)
            b = n % 2
            t0, W_ = tl_[n]
            gcol, xt, sq, rs = fin_state['g'], fin_state['xt'][b], fin_state['sq'][b], fin_state['rs'][b]
            k.op('act', I('activation', sq[:, :, :W_], xt[:, :, :W_], AF.Square), r=[('fnx', b)], w=[('fnsq', b)])
            p = nps()
            for c in range(8):
                k.op('pe', I('matmul', PS(p, W_), onesb[:], sq[:, c, :W_], start=(c == 0), stop=(c == 7)),
                     r=[('fnsq', b), 'onesb'], w=[('ps', p)])
            k.op('act', I('activation', rs[:, :W_], PS(p, W_), AF.Sqrt, bias=epsc[:], scale=1.0 / D),
                 r=[('ps', p), 'epsc'], w=[('fnrs', b)])
            k.op('dve', I('reciprocal', rs[:, :W_], rs[:, :W_]), r=[('fnrs', b)], w=[('fnrs', b)])
            for c in range(8):
                k.op('dve',
                     I('scalar_tensor_tensor', hf[b][:, c, :W_], xt[:, c, :W_], gcol[:, c:c + 1],
                       rs[:, :W_], ALU.mult, ALU.mult),
                     r=[('fnx', b), ('fnrs', b), 'fgcol'], w=[('hf', b, c)])

        wst = {}

        def load_w(es, dst, dst_key, wsrc, kc, ncols, tag):
            parts = [(k0, min(8, kc - k0)) for k0 in range(0, kc, 8)]
            if len(parts) > 1:
                k.groups[dst_key] = [(dst_key, 'part', i) for i in range(len(parts))]
            for i, (k0, kw) in enumerate(parts):
                k.dma(dst[:, k0:k0 + kw, 0:ncols],
                      wsrc[k0 * 128:(k0 + kw) * 128, :].rearrange("(c p) n -> p c n", p=128),
                      w=[(dst_key, 'part', i) if len(parts) > 1 else dst_key], q='pool')

        def out_phase(layer, w_out_ap, KY):
            with ExitStack() as es:
                wst.clear()
                wo = sb(es, 'wo', [128, KY, D], BF16)
                wg = sb(es, 'wg', [128, 8, D], BF16)
                wp = sb(es, 'wp', [128, 2, D], BF16)
                load_w(es, wo, 'wo', w_out_ap, KY, D, 'o')
                load_w(es, wg, 'wg', di['pg_w'][layer], 8, D, 'o')
                load_w(es, wp, 'wp', di['pe_w'][layer], 2, D, 'o')
                yt = [sb(es, 'oy%d' % i, [128, KY, 512], BF16) for i in range(2)]
                xt = [sb(es, 'ox%d' % i, [128, 8, 512]) for i in range(2)]
                xb = [sb(es, 'oxb%d' % i, [128, 8, 512], BF16) for i in range(2)]
                pt = [sb(es, 'op%d' % i, [128, 4, 256]) for i in range(2)]
                pT = [sb(es, 'opT%d' % i, [128, 2, 512], BF16) for i in range(2)]
                gt = [sb(es, 'og%d' % i, [128, 512]) for i in range(2)]
                tl = [(s0, 512) for s0 in range(0, T, 512)] + [(T, NS)]
                def o_loads(n):
                    s0, W_ = tl[n]
                    b = n % 2
                    k.dma(yt[b][:, :, :W_], yT.rearrange("(c p) t -> p c t", p=128)[:, 0:KY, s0:s0 + W_],
                          w=[('oy', b)])
                    k.dma(xt[b][:, :, :W_], xT.rearrange("(c p) t -> p c t", p=128)[:, :, s0:s0 + W_],
                          w=[('ox', b)])
                    if W_ == 512:
                        k.dma(pt[b][:], di['p_prompt'][layer, s0:s0 + 512, :].rearrange("(j p) n -> p j n", p=128),
                              w=[('op', b)])
                    else:
                        k.dma(pt[b][:NS, 0, :], di['p_sample'][layer], w=[('op', b)])

                o_loads(0)
                for n, (s0, W_) in enumerate(tl):
                    b = n % 2
                    if n + 1 < len(tl):
                        o_loads(n + 1)
                    if W_ == 512:
                        subs = [(j, 128) for j in range(4)]
                    else:
                        subs = [(0, NS)]
                    for (j, L) in subs:
                        for c in range(2):
                            p = nps()
                            k.op('pe', I('transpose', PS(p, L), pt[b][:L, j, c * 128:(c + 1) * 128], ident[:L, :L]),
                                 r=[('op', b)], w=[('ps', p)])
                            k.op('act', I('copy', pT[b][:, c, j * 128:j * 128 + L], PS(p, L)),
                                 r=[('ps', p)], w=[('opT', b, j, c)])
                    pTr = [('opT', b, j, c) for (j, L) in subs for c in range(2)]
                    for dc in range(8):
                        p = nps()
                        for c in range(KY):
                            k.op('pe', I('matmul', PS(p, W_), wo[:, c, dc * 128:(dc + 1) * 128], yt[b][:, c, :W_],
                                         start=(c == 0), stop=(c == KY - 1)),
                                 r=['wo', ('oy', b)], w=[('ps', p)])
                        k.op('dve', I('tensor_tensor', xt[b][:, dc, :W_], xt[b][:, dc, :W_], PS(p, W_), ALU.add),
                             r=[('ps', p), ('ox', b)], w=[('ox', b, dc)])
                        k.op('act', I('copy', xb[b][:, dc, :W_], xt[b][:, dc, :W_]),
                             r=[('ox', b, dc)], w=[('oxb', b, dc)])
                    for dc in range(8):
                        p = nps()
                        for c in range(8):
                            k.op('pe', I('matmul', PS(p, W_), wg[:, c, dc * 128:(dc + 1) * 128], xb[b][:, c, :W_],
                                         start=(c == 0), stop=(c == 7)),
                                 r=['wg'] + [('oxb', b, cc) for cc in range(8)], w=[('ps', p)])
                        k.op('act', I('activation', gt[b][:, :W_], PS(p, W_), AF.Sigmoid),
                             r=[('ps', p)], w=[('og', b)])
                        p2 = nps()
                        for c in range(2):
                            k.op('pe', I('matmul', PS(p2, W_), wp[:, c, dc * 128:(dc + 1) * 128], pT[b][:, c, :W_],
                                         start=(c == 0), stop=(c == 1)),
                                 r=['wp'] + pTr, w=[('ps', p2)])
                        k.op('dve', I('tensor_tensor', gt[b][:, :W_], gt[b][:, :W_], PS(p2, W_), ALU.mult),
                             r=[('ps', p2), ('og', b)], w=[('og', b)])
                        k.op('pool', I('tensor_tensor', xt[b][:, dc, :W_], xt[b][:, dc, :W_], gt[b][:, :W_], ALU.add),
                             r=[('og', b), ('ox', b, dc), ('oxb', b, dc)], w=[('ox', b, dc)])
                    k.dma(xT.rearrange("(c p) t -> p c t", p=128)[:, :, s0:s0 + W_], xt[b][:, :, :W_],
                          r=[('ox', b, dc) for dc in range(8)] + [('ox', b)])
                k.barrier()
                k.flush()


        xTv = xT.rearrange("(c p) t -> p c t", p=128)
        yTv = yT.rearrange("(c p) t -> p c t", p=128)
        HALVES = [(0, T // 2), (T // 2, T // 2 + NS)]

        def hkeys(s0):
            return [('hT', c, s0 // 512) for c in range(8)]

        def gemm_fm(p, wt, wkey, col0, hT, s0, W_, kc=8, M=128):
            for c in range(kc):
                k.op('pe', I('matmul', PS(p, W_)[:M, :], wt[:, c, col0:col0 + M], hT[:, c, s0:s0 + W_],
                             start=(c == 0), stop=(c == kc - 1)),
                     r=[wkey] + hkeys(s0), w=[('ps', p)])

        def rows_to_fm(es, src, R, ncols, dst, dkey, tag):
            if '_rowtmp' not in wst:
                wst['_rowtmp'] = sb(es, 'rowtmp' + tag, [64, 2048])
            tmp = wst['_rowtmp']
            k.dma(tmp[:R, :ncols], src, w=['rowtmp'])
            for c in range(ncols // 128):
                p = nps()
                k.op('pe', I('transpose', PS(p, R), tmp[:R, c * 128:(c + 1) * 128], ident[:R, :R]),
                     r=['rowtmp'], w=[('ps', p)])
                k.op('dve', I('tensor_copy', dst[:, c, 0:R], PS(p, R)), r=[('ps', p)], w=[dkey])

        def fm_to_rows(es, srcs, skeys, R, dst, tag):
            if '_rowtmp' not in wst:
                wst['_rowtmp'] = sb(es, 'rowtmp' + tag, [64, 2048])
            tmp = wst['_rowtmp']
            for c, a in enumerate(srcs):
                p = nps()
                k.op('pe', I('transpose', PS(p, 128)[:R, :], a, ident[:]), r=skeys, w=[('ps', p)])
                k.op('dve', I('tensor_copy', tmp[:R, c * 128:(c + 1) * 128], PS(p, 128)[:R, :]),
                     r=[('ps', p)], w=['rowtmp'])
            k.dma(dst, tmp[:R, :len(srcs) * 128], r=['rowtmp'])

        def conv_layer(layer=1):
            w_in = di['b_w_in']
            with ExitStack() as es:
                wst.clear()
                wc = sb(es, 'cv_wc', [128, 16, 3])
                rows_to_fm(es, di['b_conv_w'], 3, 2048, wc, 'cv_wc', 'cw')
                stT = sb(es, 'cv_st', [128, 16, NS * 2])
                rows_to_fm(es, di['st_conv'].rearrange("b j n -> (b j) n"), NS * 2, 2048, stT, 'cv_st', 'cs')
                halo = sb(es, 'cv_halo', [128, 16, 2])
                k.op('dve', I('memset', halo[:], 0.0), w=['cv_halo'])
                so = sb(es, 'cv_so', [128, 16, NS * 2])
                wf = [sb(es, 'cv_wf%d' % i, [128, 8, 512], BF16) for i in range(2)]
                cx = sb(es, 'cv_cx', [128, 2 + T // 2 + NS])
                yo = [sb(es, 'cv_yo%d' % i, [128, T // 2 + NS], BF16) for i in range(2)]
                cgs = [sb(es, 'cv_cg%d' % i, [128, 512]) for i in range(2)]
                zs = [sb(es, 'cv_zs%d' % i, [128, 512]) for i in range(2)]
                acc = [sb(es, 'cv_acc%d' % i, [128, 512]) for i in range(2)]
                ys = sb(es, 'cv_ys', [128, NS])
                hT = sb(es, 'cv_hT', [128, 8, T // 2 + NS], BF16)
                nw = 0

                def cv_load(n_, it):
                    for q_ in range(4):
                        load_w(es, wf[n_ % 2][:, :, q_ * 128:(q_ + 1) * 128], ('cv_wf', n_ % 2, q_),
                               w_in[:, q_ * 2048 + it[1] * 128:q_ * 2048 + (it[1] + 1) * 128], 8, 128, 'cv')

                cvpre = Pre([(hi_, f_) for hi_ in range(2) for f_ in range(16)], cv_load, 1)
                for hi, (t0, TW) in enumerate(HALVES):
                    norm_half(es, di['norm_g'][layer], t0, TW, hT, 'cv%d' % hi)
                    TP = T // 2
                    for f in range(16):
                        wb = nw % 2
                        cvpre.need(nw)
                        nw += 1
                        wkeys = [('cv_wf', wb, q_) for q_ in range(4)]
                        k.op('act', I('copy', cx[:, 0:2], halo[:, f, :]), r=['cv_halo'], w=[('cv_cx', 'h')])
                        tl = [(s0, 512) for s0 in range(0, TP, 512)]
                        if TW > TP:
                            tl.append((TP, NS))
                        for n, (s0, W_) in enumerate(tl):
                            b = n % 2
                            pc, px, pz, pb = nps(), nps(), nps(), nps()
                            for q_, p in ((1, pc), (2, px), (3, pz), (0, pb)):
                                for c in range(8):
                                    k.op('pe', I('matmul', PS(p, W_), wf[wb][:, c, q_ * 128:(q_ + 1) * 128],
                                                 hT[:, c, s0:s0 + W_], start=(c == 0), stop=(c == 7)),
                                         r=[('cv_wf', wb, q_)] + hkeys(s0), w=[('ps', p)])
                            k.op('act', I('copy', cgs[b][:, :W_], PS(pc, W_)), r=[('ps', pc)], w=[('cv_cg', b)])
                            k.op('dve', I('tensor_tensor', cx[:, 2 + s0:2 + s0 + W_], cgs[b][:, :W_], PS(px, W_), ALU.mult),
                                 r=[('cv_cg', b), ('ps', px)], w=[('cv_cx', n)])
                            k.op('act', I('activation', zs[b][:, :W_], PS(pz, W_), AF.Silu), r=[('ps', pz)], w=[('cv_zs', b)])
                            if W_ == 512:
                                rk = [('cv_cx', n), ('cv_cx', n - 1) if n > 0 else ('cv_cx', 'h')]
                                k.op('act', I('mul', acc[b][:, :W_], cx[:, s0:s0 + W_], wc[:, f, 0:1]),
                                     r=rk + ['cv_wc'], w=[('cv_acc', b)])
                                k.op('dve', I('scalar_tensor_tensor', acc[b][:, :W_], cx[:, 1 + s0:1 + s0 + W_], wc[:, f, 1:2],
                                               acc[b][:, :W_], ALU.mult, ALU.add), r=rk + [('cv_acc', b)], w=[('cv_acc', b)])
                                k.op('dve', I('scalar_tensor_tensor', acc[b][:, :W_], cx[:, 2 + s0:2 + s0 + W_], wc[:, f, 2:3],
                                               acc[b][:, :W_], ALU.mult, ALU.add), r=rk + [('cv_acc', b)], w=[('cv_acc', b)])
                                accv = acc[b][:, :W_]
                                ak = ('cv_acc', b)
                            else:
                                stv = stT[:, f, :].rearrange("p (b j) -> p b j", j=2)
                                k.op('act', I('mul', ys[:, :], stv[:, :, 0], wc[:, f, 0:1]),
                                     r=['cv_st', 'cv_wc'], w=['cv_ys'])
                                k.op('dve', I('scalar_tensor_tensor', ys[:, :], stv[:, :, 1], wc[:, f, 1:2], ys[:, :],
                                               ALU.mult, ALU.add), r=['cv_st', 'cv_ys'], w=['cv_ys'])
                                k.op('dve', I('scalar_tensor_tensor', ys[:, :], cx[:, 2 + s0:2 + s0 + W_], wc[:, f, 2:3], ys[:, :],
                                               ALU.mult, ALU.add), r=[('cv_cx', n), 'cv_ys'], w=['cv_ys'])
                                sov = so[:, f, :].rearrange("p (b j) -> p b j", j=2)
                                k.op('act', I('copy', sov[:, :, 0], stv[:, :, 1]), r=['cv_st'], w=[('cv_so', f, 0)])
                                k.op('act', I('copy', sov[:, :, 1], cx[:, 2 + s0:2 + s0 + W_]), r=[('cv_cx', n)], w=[('cv_so', f, 1)])
                                accv = ys[:, :]
                                ak = 'cv_ys'
                            k.op('dve', I('tensor_tensor', zs[b][:, :W_], zs[b][:, :W_], accv, ALU.mult),
                                 r=[ak, ('cv_zs', b)], w=[('cv_zs', b)])
                            k.op('dve', I('tensor_tensor', yo[wb][:, s0:s0 + W_], zs[b][:, :W_], PS(pb, W_), ALU.mult),
                                 r=[('cv_zs', b), ('ps', pb)], w=[('cv_yo', wb, n)])
                        k.op('act', I('copy', halo[:, f, :], cx[:, TP:TP + 2]),
                             r=[('cv_cx', len(tl) - 1 - (1 if TW > TP else 0))], w=['cv_halo'])
                        k.dma(yTv[:, f, t0:t0 + TW], yo[wb][:, :TW], r=[('cv_yo', wb, n) for n in range(len(tl))])
                k.barrier()
                fm_to_rows(es, [halo[:, f, :] for f in range(16)], ['cv_halo'], 2, di['o_convp'], 'cp')
                fm_to_rows(es, [so[:, f, :] for f in range(16)], [('cv_so', f, j) for f in range(16) for j in range(2)],
                           NS * 2, di['o_convs'].rearrange("b j n -> (b j) n"), 'cq')
                k.barrier()
                k.flush()
            out_phase(layer, di['b_w_out'], 16)


        def pool_layer(layer=3):
            w_in = di['d_w_in']
            TP = T // 2
            with ExitStack() as es:
                wst.clear()
                scT = sb(es, 'pl_sc', [128, 16, 1])
                rows_to_fm(es, di['d_scale'].rearrange("(o n) -> o n", o=1), 1, 2048, scT, 'pl_sc', 'ps')
                stT = sb(es, 'pl_st', [128, 16, NS * 15])
                rows_to_fm(es, di['st_pool'].rearrange("b j n -> (b j) n"), NS * 15, 2048, stT, 'pl_st', 'pt')
                invc = sb(es, 'pl_invc', [128, 256])
                k.dma(invc[:], di['c_invcnt'], w=['pl_invc'])
                halo = sb(es, 'pl_halo', [128, 16, 15])
                k.op('dve', I('memset', halo[:], 0.0), w=['pl_halo'])
                so = sb(es, 'pl_so', [128, 16, NS * 15])
                wf = [sb(es, 'pl_wf%d' % i, [128, 8, 256], BF16) for i in range(2)]
                wg = sb(es, 'pl_wg', [128, 4, 512], BF16)
                xpb = sb(es, 'pl_xp', [128, 15 + TP + NS])
                sA = sb(es, 'pl_sA', [128, 15 + TP])
                sB = sb(es, 'pl_sB', [128, 15 + TP])
                rb = sb(es, 'pl_rb', [128, 4, TP + NS], BF16)
                zsb = sb(es, 'pl_zs', [128, 4, TP + NS], BF16)
                yo = [sb(es, 'pl_yo%d' % i, [128, TP + NS], BF16) for i in range(2)]
                t16 = sb(es, 'pl_t16', [128, 16])
                red = sb(es, 'pl_red', [128, NS])
                ytmp = [sb(es, 'pl_yt%d' % i, [128, 512]) for i in range(2)]
                hT = sb(es, 'pl_hT', [128, 8, TP + NS], BF16)
                nw = 0
                ny = 0

                def pl_load(n_, it):
                    for q_ in range(2):
                        load_w(es, wf[n_ % 2][:, :, q_ * 128:(q_ + 1) * 128], ('pl_wf', n_ % 2, q_),
                               w_in[:, q_ * 2048 + it * 128:q_ * 2048 + (it + 1) * 128], 8, 128, 'pl')

                plpre = Pre([f_ for hi_ in range(2) for f_ in range(16)], pl_load, 1)
                for hi, (t0, TW) in enumerate(HALVES):
                    norm_half(es, di['norm_g'][layer], t0, TW, hT, 'pl%d' % hi)
                    tl = [(s0, 512) for s0 in range(0, TP, 512)]
                    if TW > TP:
                        tl.append((TP, NS))
                    for g in range(4):
                        wdw = (2, 4, 8, 16)[g]
                        load_w(es, wg, 'pl_wg', di['d_w_grp'][g], 4, 512, 'pl')
                        for fi in range(4):
                            f = 4 * g + fi
                            wb = nw % 2
                            plpre.need(nw)
                            nw += 1
                            k.op('act', I('copy', xpb[:, 0:15], halo[:, f, :]), r=['pl_halo'], w=['pl_xp'])
                            for n, (s0, W_) in enumerate(tl):
                                px, pz = nps(), nps()
                                for q_, p in enumerate((px, pz)):
                                    for c in range(8):
                                        k.op('pe', I('matmul', PS(p, W_), wf[wb][:, c, q_ * 128:(q_ + 1) * 128],
                                                     hT[:, c, s0:s0 + W_], start=(c == 0), stop=(c == 7)),
                                             r=[('pl_wf', wb, q_)] + hkeys(s0), w=[('ps', p)])
                                k.op('act', I('copy', xpb[:, 15 + s0:15 + s0 + W_], PS(px, W_)), r=[('ps', px)], w=['pl_xp'])
                                k.op('act', I('activation', zsb[:, fi, s0:s0 + W_], PS(pz, W_), AF.Silu),
                                     r=[('ps', pz)], w=[('pl_zs', fi)])
                            cur, ck = xpb, 'pl_xp'
                            step, lo = 1, 0
                            pp = [(sA, 'pl_sA'), (sB, 'pl_sB')]
                            ip = 0
                            while step < wdw:
                                nxt, nk = pp[ip % 2]
                                ip += 1
                                lo2 = lo + step
                                k.op('dve', I('tensor_tensor', nxt[:, lo2:15 + TP], cur[:, lo2:15 + TP],
                                              cur[:, lo2 - step:15 + TP - step], ALU.add), r=[ck], w=[nk])
                                cur, ck, lo, step = nxt, nk, lo2, step * 2
                            k.op('dve', I('scalar_tensor_tensor', rb[:, fi, 0:TP], cur[:, 15:15 + TP], 1.0 / wdw,
                                          xpb[:, 15:15 + TP], ALU.mult, ALU.subtract), r=[ck, 'pl_xp'], w=[('pl_rb', fi)])
                            if hi == 0:
                                k.op('dve', I('tensor_tensor', t16[:], cur[:, 15:31], invc[:, f * 16:(f + 1) * 16], ALU.mult),
                                     r=[ck, 'pl_invc'], w=['pl_t16'])
                                k.op('dve', I('tensor_tensor', rb[:, fi, 0:16], t16[:], xpb[:, 15:31], ALU.subtract),
                                     r=['pl_t16', 'pl_xp'], w=[('pl_rb', fi)])
                            if TW > TP:
                                stv = stT[:, f, :].rearrange("p (b j) -> p b j", j=15)
                                xs_ = xpb[:, 15 + TP:15 + TP + NS]
                                k.op('dve', I('tensor_reduce', red[:], stv[:, :, 15 - (wdw - 1):15], AX.X, ALU.add),
                                     r=['pl_st'], w=['pl_red'])
                                k.op('dve', I('tensor_tensor', red[:], red[:], xs_, ALU.add), r=['pl_red', 'pl_xp'], w=['pl_red'])
                                k.op('dve', I('scalar_tensor_tensor', rb[:, fi, TP:TP + NS], red[:], 1.0 / wdw, xs_,
                                              ALU.mult, ALU.subtract), r=['pl_red', 'pl_xp'], w=[('pl_rb', fi)])
                                sov = so[:, f, :].rearrange("p (b j) -> p b j", j=15)
                                k.op('act', I('copy', sov[:, :, 0:14], stv[:, :, 1:15]), r=['pl_st'], w=[('pl_so', f, 0)])
                                k.op('act', I('copy', sov[:, :, 14], xs_), r=['pl_xp'], w=[('pl_so', f, 1)])
                            k.op('act', I('copy', halo[:, f, :], xpb[:, TP:TP + 15]), r=['pl_xp'], w=['pl_halo'])
                        for fo in range(4):
                            f = 4 * g + fo
                            yb = ny % 2
                            ny += 1
                            for n, (s0, W_) in enumerate(tl):
                                p = nps()
                                for c in range(4):
                                    k.op('pe', I('matmul', PS(p, W_), wg[:, c, fo * 128:(fo + 1) * 128], rb[:, c, s0:s0 + W_],
                                                 start=(c == 0), stop=(c == 3)),
                                         r=['pl_wg'] + [('pl_rb', c) for c in range(4)], w=[('ps', p)])
                                b = n % 2
                                k.op('act', I('mul', ytmp[b][:, :W_], PS(p, W_), scT[:, f, 0:1]), r=[('ps', p), 'pl_sc'], w=[('pl_yt', b)])
                                k.op('dve', I('tensor_tensor', yo[yb][:, s0:s0 + W_], ytmp[b][:, :W_], zsb[:, fo, s0:s0 + W_], ALU.mult),
                                     r=[('pl_yt', b), ('pl_zs', fo)], w=[('pl_yo', yb, n)])
                            k.dma(yTv[:, f, t0:t0 + TW], yo[yb][:, :TW], r=[('pl_yo', yb, n) for n in range(len(tl))])
                k.barrier()
                fm_to_rows(es, [halo[:, f, :] for f in range(16)], ['pl_halo'], 15, di['o_poolp'], 'pp')
                fm_to_rows(es, [so[:, f, :] for f in range(16)], [('pl_so', f, j) for f in range(16) for j in range(2)],
                           NS * 15, di['o_pools'].rearrange("b j n -> (b j) n"), 'pq')
                k.barrier()
                k.flush()
            out_phase(layer, di['d_w_out'], 16)


        def mlstm_layer(layer=0):
            w_in = di['a_w_in']
            TP = T // 2
            NU = TP // 128 + NS
            with ExitStack() as es:
                wst.clear()
                triu = sb(es, 'ml_triu', [128, 128]); k.dma(triu[:], di['c_triu'], w=['ml_triu'])
                mneg = sb(es, 'ml_mneg', [128, 128]); k.dma(mneg[:], di['c_maskneg'], w=['ml_mneg'])
                sell = sb(es, 'ml_sell', [128, 128]); k.dma(sell[:], di['c_sel_last'], w=['ml_sell'])
                bif = sb(es, 'ml_bif', [128, 16]); k.dma(bif[:], bcast_row(di['a_b_if'], 16), w=['ml_bif'])
                ng = sb(es, 'ml_ng', [128, 2048]); k.dma(ng[:], bcast_row(di['a_norm_g'], 2048), w=['ml_ng'])
                triub = sb(es, 'ml_triub', [128, 128], BF16); sellb = sb(es, 'ml_sellb', [128, 128], BF16)
                k.op('dve', I('tensor_copy', triub[:], triu[:]), r=['ml_triu'], w=['ml_triu'])
                k.op('dve', I('tensor_copy', sellb[:], sell[:]), r=['ml_sell'], w=['ml_sell'])
                hlA = sb(es, 'ml_hlA', [128, 128], BF16); hlB = sb(es, 'ml_hlB', [128, 128], BF16)

                def mm_hl(p, pv, lhsT, src, L, n, rkeys):
                    k.op('dve', I('tensor_copy', hlA[:L, :n], src), r=rkeys, w=['ml_hlA'])
                    k.op('dve', I('tensor_tensor', hlB[:L, :n], src, hlA[:L, :n], ALU.subtract), r=rkeys + ['ml_hlA'], w=['ml_hlB'])
                    k.op('pe', I('matmul', pv, lhsT, hlA[:L, :n], start=True, stop=False), r=['ml_hlA', 'ml_triu', 'ml_sell'], w=[('ps', p)])
                    k.op('pe', I('matmul', pv, lhsT, hlB[:L, :n], start=False, stop=True), r=['ml_hlB'], w=[('ps', p)])

                wgate = sb(es, 'ml_wgate', [128, 8, 16], BF16)
                load_w(es, wgate, 'ml_wgate', w_in[:, 8192:8208], 8, 16, 'ml')
                CT = sb(es, 'ml_CT', [128, 8, 257])
                CTb = sb(es, 'ml_CTb', [128, 257], BF16)
                mprev = sb(es, 'ml_mprev', [128, 8])
                k.op('dve', I('memset', CT[:], 0.0), w=[('ml_CT', h) for h in range(8)])
                k.op('dve', I('memset', mprev[:], 0.0), w=['ml_mprev'])
                col3 = sb(es, 'ml_col3', [128, NU, 24])
                ccol = sb(es, 'ml_ccol', [128, NU, 8])
                negm = sb(es, 'ml_negm', [128, NU, 8])
                expnegm = sb(es, 'ml_enm', [128, NU, 8])
                wcol = sb(es, 'ml_wcol', [128, NU, 8])
                bcs = sb(es, 'ml_bcs', [128, NU, 24])
                gs = sb(es, 'ml_gs', [128, 16]); lp = sb(es, 'ml_lp', [128, 8]); mxa = sb(es, 'ml_mx', [128, 8])
                inter = sb(es, 'ml_inter', [128, 8]); tmp8 = sb(es, 'ml_tmp8', [128, 8])
                from types import SimpleNamespace
                BS = []
                NBU = 4
                for par in range(NBU):
                    B = SimpleNamespace()
                    B.par = par
                    B.diagc = sb(es, 'ml_diagc', [128, 128]); B.logd = sb(es, 'ml_logd', [128, 128]); B.Dm = sb(es, 'ml_Dm', [128, 128])
                    B.Pm = sb(es, 'ml_P', [128, 128], BF16); B.PTs = sb(es, 'ml_PT', [128, 128], BF16)
                    B.ktok = sb(es, 'ml_ktok', [128, 128], BF16)
                    B.v1 = sb(es, 'ml_v1', [128, 257], BF16); B.wv = sb(es, 'ml_wv', [128, 257], BF16)
                    k.op('dve', I('memset', B.v1[:, 256:257], 1.0), w=[('ml_v1', par)])
                    B.og = sb(es, 'ml_og', [128, 256]); B.ez = sb(es, 'ml_ez', [128, 512]); B.zs = sb(es, 'ml_zs', [128, 256])
                    B.intra = sb(es, 'ml_intra', [128, 257]); B.nd = sb(es, 'ml_nd', [128, 257])
                    B.den = sb(es, 'ml_den', [128, 1]); B.ssq = sb(es, 'ml_ssq', [128, 1]); B.junk = BS[0].junk if BS else sb(es, 'ml_junk', [128, 256])
                    B.hs = sb(es, 'ml_hs', [128, 256]); B.yb = sb(es, 'ml_yb', [128, 256], BF16)
                    B.hlA = sb(es, 'ml_hlA2', [128, 128], BF16); B.hlB = sb(es, 'ml_hlB2', [128, 128], BF16)
                    BS.append(B)
                diagc = BS[0].diagc; logd = BS[0].logd
                qTb = sb(es, 'ml_qTb', [128, 512], BF16); kTb = sb(es, 'ml_kTb', [128, 512], BF16)
                yTs = sb(es, 'ml_yTs', [128, 2, TP + NS], BF16)
                ctmp = sb(es, 'ml_ctmp', [128, 2, 128])
                CTs = sb(es, 'ml_CTs', [128, NS, 257]); CTbs = [sb(es, 'ml_CTbs', [128, 257], BF16) for _ in range(NS)]
                ctmps = [sb(es, 'ml_ctmps', [128, 2, 128]) for _ in range(NS)]
                wh = [sb(es, 'ml_wh%d' % i, [128, 8, 1024], BF16) for i in range(2)]
                hT = sb(es, 'ml_hT', [128, 8, TP + NS], BF16)
                SC = 128.0 ** -0.5

                def logd_unit(u, h, L, B=None):
                    dg, ld, kd, kl = (diagc, logd, 'ml_diagc', 'ml_logd') if B is None else (B.diagc, B.logd, ('ml_diagc', B.par), ('ml_logd', B.par))
                    k.op('dve', I('tensor_scalar', dg[:L, :L], ident[:L, :L], ccol[:L, u, h:h + 1], None, ALU.mult),
                         r=[('col', u)], w=[kd])
                    p = nps()
                    if B is None:
                        mm_hl(p, PS(p, L)[:L, :], onesb[:L, :L], dg[:L, :L], L, L, [kd])
                    else:
                        k.op('dve', I('tensor_copy', B.hlA[:L, :L], dg[:L, :L]), r=[kd], w=[('ml_hlA', B.par)])
                        k.op('dve', I('tensor_tensor', B.hlB[:L, :L], dg[:L, :L], B.hlA[:L, :L], ALU.subtract), r=[kd, ('ml_hlA', B.par)], w=[('ml_hlB', B.par)])
                        k.op('pe', I('matmul', PS(p, L)[:L, :], onesb[:L, :L], B.hlA[:L, :L], start=True, stop=False), r=[('ml_hlA', B.par)], w=[('ps', p)])
                        k.op('pe', I('matmul', PS(p, L)[:L, :], onesb[:L, :L], B.hlB[:L, :L], start=False, stop=True), r=[('ml_hlB', B.par)], w=[('ps', p)])
                    k.op('dve', I('scalar_tensor_tensor', ld[:L, :L], PS(p, L)[:L, :], col3[:L, u, h:h + 1], mneg[:L, :L],
                                  ALU.add, ALU.add), r=[('ps', p), ('col', u), 'ml_mneg'], w=[kl])

                def gate_unit(u, c0, L):
                    p = nps()
                    for c in range(8):
                        k.op('pe', I('matmul', PS(p, 16)[:L, :], hT[:, c, c0:c0 + L], wgate[:, c, :], start=(c == 0), stop=(c == 7)),
                             r=['ml_wgate'] + hkeys(c0), w=[('ps', p)])
                    k.op('dve', I('tensor_tensor', gs[:L, :], PS(p, 16)[:L, :], bif[:L, :], ALU.add), r=[('ps', p), 'ml_bif'], w=['ml_gs'])
                    k.op('act', I('activation', lp[:L, :], gs[:L, 8:16], AF.Exp, scale=-1.0), r=['ml_gs'], w=['ml_lp'])
                    k.op('act', I('activation', lp[:L, :], lp[:L, :], AF.Ln, bias=onesf[:L, 0:1]), r=['ml_lp'], w=['ml_lp'])
                    p2 = nps()
                    mm_hl(p2, PS(p2, 8)[:L, :], triub[:L, :L], lp[:L, :], L, 8, ['ml_lp'])
                    k.op('act', I('mul', col3[:L, u, 0:8], PS(p2, 8)[:L, :], -1.0), r=[('ps', p2)], w=[('col', u)])
                    k.op('dve', I('tensor_tensor', ccol[:L, u, :], gs[:L, 0:8], PS(p2, 8)[:L, :], ALU.add),
                         r=[('ps', p2), 'ml_gs'], w=[('col', u)])
                    def gate_head(h, B, alt):
                        P_ = B.par
                        if alt:
                            dgc, hA, hB, lgd = B.Dm, B.Pm, B.PTs, B.og
                            kd, kA, kB, kl = ('ml_Dm', P_), ('ml_P', P_), ('ml_PT', P_), ('ml_og', P_)
                        else:
                            dgc, hA, hB, lgd = B.diagc, B.hlA, B.hlB, B.logd
                            kd, kA, kB, kl = ('ml_diagc', P_), ('ml_hlA', P_), ('ml_hlB', P_), ('ml_logd', P_)
                        k.op('act', I('mul', dgc[:L, :L], ident[:L, :L], ccol[:L, u, h:h + 1]), r=[('col', u)], w=[kd])
                        yield
                        k.op('act', I('copy', hA[:L, :L], dgc[:L, :L]), r=[kd], w=[kA])
                        yield
                        k.op('pool', I('tensor_tensor', hB[:L, :L], dgc[:L, :L], hA[:L, :L], ALU.subtract), r=[kd, kA], w=[kB])
                        yield
                        p = psalloc()
                        k.op('pe', I('matmul', PS(p, L)[:L, :], onesb[:L, :L], hA[:L, :L], start=True, stop=False), r=[kA], w=[('ps', p)])
                        k.op('pe', I('matmul', PS(p, L)[:L, :], onesb[:L, :L], hB[:L, :L], start=False, stop=True), r=[kB], w=[('ps', p)])
                        yield
                        k.op('dve', I('scalar_tensor_tensor', lgd[:L, :L], PS(p, L)[:L, :], col3[:L, u, h:h + 1], mneg[:L, :L],
                                      ALU.add, ALU.add), r=[('ps', p), ('col', u), 'ml_mneg'], w=[kl])
                        psfree(p)
                        yield
                        k.op('dve', I('tensor_reduce', mxa[:L, h:h + 1], lgd[:L, :L], AX.X, ALU.max), r=[kl], w=[('ml_mx', h)])

                    pipeline(iter([gate_head(h, BS[h % NBU], h >= NBU) for h in range(8)]), 8, ramp=3)
                    k.op('dve', I('tensor_tensor', inter[:L, :], col3[:L, u, 0:8], mprev[:L, :], ALU.add),
                         r=[('col', u), 'ml_mprev'], w=['ml_inter'])
                    k.op('dve', I('tensor_tensor', col3[:L, u, 8:16], inter[:L, :], mxa[:L, :], ALU.max),
                         r=['ml_inter'] + [('ml_mx', hh) for hh in range(8)], w=[('col', u)])
                    k.op('dve', I('tensor_tensor', tmp8[:L, :], inter[:L, :], col3[:L, u, 8:16], ALU.subtract),
                         r=['ml_inter', ('col', u)], w=['ml_tmp8'])
                    k.op('act', I('activation', col3[:L, u, 16:24], tmp8[:L, :], AF.Exp), r=['ml_tmp8'], w=[('col', u)])
                    k.op('act', I('mul', negm[:L, u, :], col3[:L, u, 8:16], -1.0), r=[('col', u)], w=[('col', u)])
                    k.op('act', I('activation', expnegm[:L, u, :], col3[:L, u, 8:16], AF.Exp, scale=-1.0), r=[('col', u)], w=[('col', u)])
                    p3 = nps()
                    lsel = sellb[:, :] if L == 128 else onesb[0:1, :]
                    mm_hl(p3, PS(p3, 24), lsel, col3[:L, u, :], L, 24, [('col', u)])
                    k.op('act', I('copy', bcs[:, u, :], PS(p3, 24)), r=[('ps', p3)], w=[('bcs', u)])
                    k.op('dve', I('tensor_tensor', tmp8[:L, :], ccol[:L, u, :], bcs[:L, u, 0:8], ALU.add),
                         r=[('col', u), ('bcs', u)], w=['ml_tmp8'])
                    k.op('dve', I('tensor_tensor', tmp8[:L, :], tmp8[:L, :], bcs[:L, u, 8:16], ALU.subtract),
                         r=['ml_tmp8', ('bcs', u)], w=['ml_tmp8'])
                    k.op('act', I('activation', wcol[:L, u, :], tmp8[:L, :], AF.Exp), r=['ml_tmp8'], w=[('col', u)])
                    k.op('dve', I('tensor_copy', mprev[:, :], bcs[:, u, 8:16]), r=[('bcs', u)], w=['ml_mprev'])

                def stage_a(h, u, c0, L, w_, wk, B):
                    P_ = B.par
                    K_ = lambda nm: (nm, P_)
                    dg, ld = B.diagc, B.logd
                    p1 = psalloc()
                    for c in range(8):
                        k.op('pe', I('matmul', PS(p1, 384)[:L, :], hT[:, c, c0:c0 + L], w_[:, c, 128:512], start=(c == 0), stop=(c == 7)),
                             r=[wk] + hkeys(c0), w=[('ps', p1)])
                    k.op('act', I('mul', dg[:L, :L], ident[:L, :L], ccol[:L, u, h:h + 1]), r=[('col', u)], w=[K_('ml_diagc')])
                    yield
                    k.op('act', I('copy', B.hlA[:L, :L], dg[:L, :L]), r=[K_('ml_diagc')], w=[K_('ml_hlA')])
                    k.op('act', I('mul', B.ktok[:L, :], PS(p1, 384)[:L, 0:128], SC), r=[('ps', p1)], w=[K_('ml_ktok')])
                    k.op('act', I('copy', B.v1[:L, 0:256], PS(p1, 384)[:L, 128:384]), r=[('ps', p1)], w=[K_('ml_v1')])
                    psfree(p1)
                    p2 = psalloc()
                    for c in range(8):
                        k.op('pe', I('matmul', PS(p2, 512)[:L, :], hT[:, c, c0:c0 + L], w_[:, c, 512:1024], start=(c == 0), stop=(c == 7)),
                             r=[wk] + hkeys(c0), w=[('ps', p2)])
                    yield
                    k.op('pool', I('tensor_tensor', B.hlB[:L, :L], dg[:L, :L], B.hlA[:L, :L], ALU.subtract), r=[K_('ml_diagc'), K_('ml_hlA')], w=[K_('ml_hlB')])
                    k.op('act', I('activation', B.ez[:L, :], PS(p2, 512)[:L, :], AF.Exp, scale=-1.0), r=[('ps', p2)], w=[K_('ml_ez')])
                    k.op('act', I('copy', B.zs[:L, :], PS(p2, 512)[:L, 256:512]), r=[('ps', p2)], w=[K_('ml_zs')])
                    psfree(p2)
                    k.op('dve', I('tensor_scalar', B.wv[:L, :], B.v1[:L, :], wcol[:L, u, h:h + 1], None, ALU.mult), r=[K_('ml_v1'), ('col', u)], w=[K_('ml_wv')])
                    yield
                    p = psalloc()
                    k.op('pe', I('matmul', PS(p, L)[:L, :], onesb[:L, :L], B.hlA[:L, :L], start=True, stop=False), r=[K_('ml_hlA')], w=[('ps', p)])
                    k.op('pe', I('matmul', PS(p, L)[:L, :], onesb[:L, :L], B.hlB[:L, :L], start=False, stop=True), r=[K_('ml_hlB')], w=[('ps', p)])
                    k.op('dve', I('tensor_scalar', B.ez[:L, :], B.ez[:L, :], 1.0, None, ALU.add), r=[K_('ml_ez')], w=[K_('ml_ez')])
                    yield
                    k.op('dve', I('scalar_tensor_tensor', ld[:L, :L], PS(p, L)[:L, :], col3[:L, u, h:h + 1], mneg[:L, :L],
                                  ALU.add, ALU.add), r=[('ps', p), ('col', u), 'ml_mneg'], w=[K_('ml_logd')])
                    psfree(p)
                    k.op('pool', I('tensor_tensor', B.ez[:L, 0:256], B.ez[:L, 0:256], B.ez[:L, 256:512], ALU.mult), r=[K_('ml_ez')], w=[K_('ml_ez')])
                    yield
                    k.op('act', I('activation', B.Dm[:L, :L], ld[:L, :L], AF.Exp, bias=negm[:L, u, h:h + 1]),
                         r=[K_('ml_logd'), ('col', u)], w=[K_('ml_Dm')])
                    ps_ = psalloc()
                    k.op('pe', I('matmul', PS(ps_, L)[:L, :], B.qT[:, :L], B.kT[:, :L], start=True, stop=True),
                         r=[('ml_qTb', B.qk), ('ml_kTb', B.qk)], w=[('ps', ps_)])
                    k.op('dve', I('reciprocal', B.ez[:L, 0:256], B.ez[:L, 0:256]), r=[K_('ml_ez')], w=[K_('ml_ez')])
                    yield
                    k.op('dve', I('tensor_tensor', B.Pm[:L, :L], PS(ps_, L)[:L, :], B.Dm[:L, :L], ALU.mult), r=[('ps', ps_), K_('ml_Dm')], w=[K_('ml_P')])
                    psfree(ps_)
                    k.op('dve', I('tensor_tensor', B.og[:L, :], B.zs[:L, :], B.ez[:L, 0:256], ALU.mult), r=[K_('ml_zs'), K_('ml_ez')], w=[K_('ml_og')])
                    yield
                    pb_ = P_ % 2
                    k.op('pe', I('transpose', PSB(pb_, L)[:L, :], B.Pm[:L, :L], identb[:L, :L]), r=[K_('ml_P')], w=[('psb', pb_)])
                    k.op('act', I('copy', B.PTs[:L, :L], PSB(pb_, L)[:L, :]), r=[('psb', pb_)], w=[K_('ml_PT')])
                    yield
                    pi = psalloc()
                    k.op('pe', I('matmul', PS(pi, 257)[:L, :], B.PTs[:L, :L], B.v1[:L, :], start=True, stop=True),
                         r=[K_('ml_PT'), K_('ml_v1')], w=[('ps', pi)])
                    yield
                    k.op('act', I('copy', B.intra[:L, :], PS(pi, 257)[:L, :]), r=[('ps', pi)], w=[K_('ml_intra')])
                    psfree(pi)

                def stage_c(h, u, c0, L, B, st=None):
                    P_ = B.par
                    K_ = lambda nm: (nm, P_)
                    CTv, CTbv, ck, bk = (CT[:, h, :], CTb, ('ml_CT', h), 'ml_CTb') if st is None else st
                    pj = nps()
                    k.op('pe', I('matmul', PS(pj, 257)[:L, :], B.qT[:, :L], CTbv[:, :], start=True, stop=True),
                         r=[('ml_qTb', B.qk), bk], w=[('ps', pj)])
                    pu = nps()
                    k.op('pe', I('matmul', PS(pu, 257), B.ktok[:L, :], B.wv[:L, :], start=True, stop=True), r=[K_('ml_ktok'), K_('ml_wv')], w=[('ps', pu)])
                    k.op('dve', I('scalar_tensor_tensor', CTv, CTv, bcs[:, u, 16 + h:17 + h], PS(pu, 257), ALU.mult, ALU.add),
                         r=[('ps', pu), ('bcs', u), ck], w=[ck])
                    if st is None:
                        k.op('act', I('copy', CTbv[:, :], CTv), r=[ck], w=[bk])
                    k.op('dve', I('scalar_tensor_tensor', B.nd[:L, :], PS(pj, 257)[:L, :], col3[:L, u, 16 + h:17 + h], B.intra[:L, :],
                                  ALU.mult, ALU.add), r=[('ps', pj), K_('ml_intra'), ('col', u)], w=[K_('ml_nd')])

                def stage_d(h, u, c0, L, B):
                    P_ = B.par
                    K_ = lambda nm: (nm, P_)
                    k.op('dve', I('scalar_tensor_tensor', B.den[:L, :], B.nd[:L, 256:257], -1.0, B.nd[:L, 256:257], ALU.mult, ALU.max),
                         r=[K_('ml_nd')], w=[K_('ml_den')])
                    k.op('dve', I('tensor_tensor', B.den[:L, :], B.den[:L, :], expnegm[:L, u, h:h + 1], ALU.max),
                         r=[K_('ml_den'), ('col', u)], w=[K_('ml_den')])
                    k.op('dve', I('reciprocal', B.den[:L, :], B.den[:L, :]), r=[K_('ml_den')], w=[K_('ml_den')])
                    k.op('dve', I('tensor_scalar', B.hs[:L, :], B.nd[:L, 0:256], B.den[:L, 0:1], None, ALU.mult), r=[K_('ml_nd'), K_('ml_den')], w=[K_('ml_hs')])
                    yield
                    k.op('act', I('activation', B.junk[:L, :], B.hs[:L, :], AF.Square, accum_out=B.ssq[:L, :]), r=[K_('ml_hs')], w=[K_('ml_ssq'), 'ml_junk'])
                    k.op('act', I('activation', B.ssq[:L, :], B.ssq[:L, :], AF.Ln, bias=epsc[:L, :], scale=1.0 / 256), r=[K_('ml_ssq')], w=[K_('ml_ssq')])
                    k.op('act', I('activation', B.ssq[:L, :], B.ssq[:L, :], AF.Exp, scale=-0.5), r=[K_('ml_ssq')], w=[K_('ml_ssq')])
                    yield
                    k.op('dve', I('scalar_tensor_tensor', B.hs[:L, :], B.hs[:L, :], B.ssq[:L, 0:1], ng[:L, h * 256:(h + 1) * 256],
                                  ALU.mult, ALU.mult), r=[K_('ml_hs'), K_('ml_ssq'), 'ml_ng'], w=[K_('ml_hs')])
                    yield
                    k.op('pool', I('tensor_tensor', B.yb[:L, :], B.hs[:L, :], B.og[:L, :], ALU.mult), r=[K_('ml_hs'), K_('ml_og')], w=[K_('ml_yb')])
                    yield
                    pb_ = P_ % 2
                    for vc in range(2):
                        k.op('pe', I('transpose', PSB(pb_, 256)[:, vc * 128:vc * 128 + L], B.yb[:L, vc * 128:(vc + 1) * 128], identb[:L, :L]),
                             r=[K_('ml_yb')], w=[('psb', pb_)])
                    for vc in range(2):
                        k.op('act' if vc else 'dve', I('copy' if vc else 'tensor_copy', yTs[:, vc, c0:c0 + L], PSB(pb_, 256)[:, vc * 128:vc * 128 + L]),
                             r=[('psb', pb_)], w=[('ml_yTs', u)])

                def head_setup(h, units, w_, wk, half):
                    bs = [(u, c0, L, BS[half * 2 + i]) for i, (u, c0, L) in enumerate(units)]
                    cb0 = units[0][1]
                    WB = sum(L for (_, _, L) in units)
                    o0 = half * 256
                    for (cc, dst, sc, nm) in ((0, qTb, None, ('ml_qTb', half)), (128, kTb, SC, ('ml_kTb', half))):
                        pq = nps()
                        for c in range(8):
                            k.op('pe', I('matmul', PS(pq, WB), w_[:, c, cc:cc + 128], hT[:, c, cb0:cb0 + WB], start=(c == 0), stop=(c == 7)),
                                 r=[wk] + hkeys(cb0) + hkeys(cb0 + WB - 1), w=[('ps', pq)])
                        if sc is None:
                            k.op('act', I('copy', dst[:, o0:o0 + WB], PS(pq, WB)), r=[('ps', pq)], w=[nm])
                        else:
                            k.op('act', I('mul', dst[:, o0:o0 + WB], PS(pq, WB), sc), r=[('ps', pq)], w=[nm])
                    off = o0
                    for (u, c0, L, B) in bs:
                        B.qT = qTb[:, off:off + 128]
                        B.kT = kTb[:, off:off + 128]
                        B.qk = half
                        off += L
                    return bs

                def batch_tail(h, bs):
                    for (u, c0, L, B) in bs:
                        stage_c(h, u, c0, L, B)
                        yield
                    gens = [stage_d(h, u, c0, L, B) for (u, c0, L, B) in bs]
                    while gens:
                        nxt = []
                        for g_ in gens:
                            try:
                                next(g_)
                                nxt.append(g_)
                            except StopIteration:
                                pass
                        gens = nxt
                        yield

                def head_run(h, unit_batches, w_, wk):
                    prev = None
                    for bi, units in enumerate(unit_batches):
                        bs = head_setup(h, units, w_, wk, bi % 2)
                        gens = [stage_a(h, u, c0, L, w_, wk, B) for (u, c0, L, B) in bs]
                        if prev is not None:
                            gens.append(prev)
                        lockstep(gens)
                        prev = batch_tail(h, bs)
                    if prev is not None:
                        lockstep([prev])

                def state_out(h, dC, dn):
                    for vc in range(2):
                        p = nps()
                        k.op('pe', I('transpose', PS(p, 128), CT[:, h, vc * 128:(vc + 1) * 128], ident[:]), r=[('ml_CT', h)], w=[('ps', p)])
                        k.op('act', I('copy', ctmp[:, vc, :], PS(p, 128)), r=[('ps', p)], w=[('ml_ctmp', vc)])
                    k.dma(dC.rearrange("(vc p) kk -> p vc kk", p=128), ctmp[:], r=[('ml_ctmp', 0), ('ml_ctmp', 1)])
                    k.dma(dn.rearrange("(p o) -> p o", o=1), CT[:, h, 256:257], r=[('ml_CT', h)])

                def state_in(h, sC, sn):
                    k.dma(ctmp[:], sC.rearrange("(vc p) kk -> p vc kk", p=128), w=[('ml_ctmp', 0), ('ml_ctmp', 1)])
                    for vc in range(2):
                        p = nps()
                        k.op('pe', I('transpose', PS(p, 128), ctmp[:, vc, :], ident[:]), r=[('ml_ctmp', vc)], w=[('ps', p)])
                        k.op('act', I('copy', CT[:, h, vc * 128:(vc + 1) * 128], PS(p, 128)), r=[('ps', p)], w=[('ml_CT', h)])
                    k.dma(CT[:, h, 256:257], sn.rearrange("(p o) -> p o", o=1), w=[('ml_CT', h)])
                    k.op('act', I('copy', CTb[:, :], CT[:, h, :]), r=[('ml_CT', h)], w=['ml_CTb'])

                def sample_run(h, w_, wk, npu_):
                    for j in range(NS):
                        k.groups[('ml_CTs', j)] = [(('ml_CTs', j), 0), (('ml_CTs', j), 1), (('ml_CTs', j), 'n')]
                    for j in range(NS):
                        ck, bk = ('ml_CTs', j), ('ml_CTbs', j)
                        k.dma(ctmps[j][:], di['st_C'][j, h].rearrange("(vc p) kk -> p vc kk", p=128), w=[('ml_ctmps', j)])
                        k.dma(CTs[:, j, 256:257], di['st_n'][j, h].rearrange("(p o) -> p o", o=1), w=[(ck, 'n')])
                    for j in range(NS):
                        ck, bk = ('ml_CTs', j), ('ml_CTbs', j)
                        for vc in range(2):
                            p = nps()
                            k.op('pe', I('transpose', PS(p, 128), ctmps[j][:, vc, :], ident[:]), r=[('ml_ctmps', j)], w=[('ps', p)])
                            k.op('act', I('copy', CTs[:, j, vc * 128:(vc + 1) * 128], PS(p, 128)), r=[('ps', p)], w=[(ck, vc)])
                        k.op('act', I('copy', CTbs[j][:, :], CTs[:, j, :]), r=[ck], w=[bk])
                    units = [(npu_ + j, TP + j, 1) for j in range(NS)]
                    bs = [(u, c0, L, BS[i]) for i, (u, c0, L) in enumerate(units)]
                    for (cc, dst, sc, nm) in ((0, qTb, None, ('ml_qTb', 0)), (128, kTb, SC, ('ml_kTb', 0))):
                        pq = nps()
                        for c in range(8):
                            k.op('pe', I('matmul', PS(pq, NS), w_[:, c, cc:cc + 128], hT[:, c, TP:TP + NS], start=(c == 0), stop=(c == 7)),
                                 r=[wk] + hkeys(TP), w=[('ps', pq)])
                        if sc is None:
                            k.op('act', I('copy', dst[:, 0:NS], PS(pq, NS)), r=[('ps', pq)], w=[nm])
                        else:
                            k.op('act', I('mul', dst[:, 0:NS], PS(pq, NS), sc), r=[('ps', pq)], w=[nm])
                    for i, (u, c0, L, B) in enumerate(bs):
                        B.qT = qTb[:, i:i + 128]
                        B.kT = kTb[:, i:i + 128]
                        B.qk = 0
                    lockstep([stage_a(h, u, c0, L, w_, wk, B) for (u, c0, L, B) in bs])
                    for j, (u, c0, L, B) in enumerate(bs):
                        stage_c(h, u, c0, L, B, st=(CTs[:, j, :], CTbs[j], ('ml_CTs', j), ('ml_CTbs', j)))
                    lockstep([stage_d(h, u, c0, L, B) for (u, c0, L, B) in bs])
                    for j in range(NS):
                        ck = ('ml_CTs', j)
                        for vc in range(2):
                            p = nps()
                            k.op('pe', I('transpose', PS(p, 128), CTs[:, j, vc * 128:(vc + 1) * 128], ident[:]), r=[ck], w=[('ps', p)])
                            k.op('act', I('copy', ctmps[j][:, vc, :], PS(p, 128)), r=[('ps', p)], w=[('ml_ctmps', j, vc)])
                        k.dma(di['o_Cs'][j, h].rearrange("(vc p) kk -> p vc kk", p=128), ctmps[j][:], r=[('ml_ctmps', j, 0), ('ml_ctmps', j, 1), ('ml_ctmps', j)])
                        k.dma(di['o_ns'][j, h].rearrange("(p o) -> p o", o=1), CTs[:, j, 256:257], r=[ck])

                nw = 0

                for wb_ in range(2):
                    k.groups[('ml_wh', wb_)] = [('ml_wh', wb_, i_) for i_ in range(5)]

                def ml_load(n_, h_):
                    for i_, (d0, s0_, nn_) in enumerate(((0, h_ * 128, 128), (128, 1024 + h_ * 128, 128), (256, 2048 + h_ * 256, 256),
                                                         (512, 4096 + h_ * 256, 256), (768, 6144 + h_ * 256, 256))):
                        load_w(es, wh[n_ % 2][:, :, d0:d0 + nn_], ('ml_wh', n_ % 2, i_), w_in[:, s0_:s0_ + nn_], 8, nn_, 'ml')

                mlpre = Pre([h_ for hi_ in range(2) for h_ in range(8)], ml_load, 1)
                for hi, (t0, TW) in enumerate(HALVES):
                    norm_half(es, di['norm_g'][layer], t0, TW, hT, 'ml%d' % hi, dbuf=False)
                    npu = TP // 128
                    for u in range(npu):
                        gate_unit(u, u * 128, 128)
                    NSS = 0 if os.environ.get('ML_NOSAMPLE') else NS
                    if hi == 1:
                        k.dma(di['o_mp'], mprev[0:1, :], r=['ml_mprev'])
                        for j in range(NSS):
                            k.dma(mprev[:, :], bass.AP(di['st_m'].tensor, di['st_m'][j].offset, [[0, 128], [1, 8]]), w=['ml_mprev'])
                            gate_unit(npu + j, TP + j, 1)
                            k.dma(di['o_ms'][j:j + 1, :], mprev[0:1, :], r=['ml_mprev'])
                    for h in range(8):
                        wb = nw % 2
                        mlpre.need(nw)
                        nw += 1
                        wk = ('ml_wh', wb)
                        k.op('act', I('copy', CTb[:, :], CT[:, h, :]), r=[('ml_CT', h)], w=['ml_CTb'])
                        head_run(h, [[(u, u * 128, 128) for u in range(u0, min(u0 + 2, npu))] for u0 in range(0, npu, 2)], wh[wb], wk)
                        if hi == 1:
                            state_out(h, di['o_Cp'][h], di['o_np'][h])
                            if NSS:
                                sample_run(h, wh[wb], wk, npu)
                        psmod[0] = 6
                        nuu = npu + (NSS if hi == 1 else 0)
                        for vc in range(2):
                            k.dma(yTv[:, h * 2 + vc, t0:t0 + TW], yTs[:, vc, :TW], r=[('ml_yTs', u) for u in range(nuu)])
                k.barrier()
                k.flush()
            out_phase(layer, di['a_w_out'], 16)


        def attn_layer(layer=2):
            w_in = di['c_w_in']
            TP = T // 2
            GR = ((128, 1), (512, 4), (2048, 16))
            qkT = dscr('qkT', [6, 1024, TA], BF16)
            vtok = dscr('vtok', [3, TA, 1024], BF16)
            zT = dscr('zT', [1024, TA], BF16)
            numS = dscr('numS', [3, T, 8 * 130])
            qs_tok = dscr('qs_tok', [3, 2, NS, 1024])
            vs_tok = dscr('vs_tok', [3, NS, 1024])
            os_tok = dscr('os_tok', [NS, 1024])
            ISQ = 128.0 ** -0.5
            with ExitStack() as es:
                wst.clear()
                permf = sb(es, 'at_permf', [128, 128]); k.dma(permf[:], di['c_ropeperm'], w=['at_perm'])
                permb = sb(es, 'at_permb', [128, 128], BF16)
                k.op('dve', I('tensor_copy', permb[:], permf[:]), r=['at_perm'], w=['at_permb'])
                rc = sb(es, 'at_rc', [128, TP + NS]); rs_ = sb(es, 'at_rs', [128, TP + NS])
                wq = [sb(es, 'at_wq%d' % i, [128, 8, 128], BF16) for i in range(3)]
                wvs = [sb(es, 'at_wv%d' % i, [128, 8, 512], BF16) for i in range(2)]
                xb16 = [sb(es, 'at_xb%d' % i, [128, 512], BF16) for i in range(5)]
                t1 = [sb(es, 'at_t1%d' % i, [128, 512]) for i in range(5)]
                res = [sb(es, 'at_res%d' % i, [128, 512]) for i in range(5)]
                stage = [sb(es, 'at_stage%d' % i, [128, TP + NS], BF16) for i in range(3)]
                ktile = [sb(es, 'at_ktile%d' % i, [128, 128]) for i in range(2)]
                vt32 = [sb(es, 'at_vt32%d' % i, [128, 512]) for i in range(3)]
                vt16 = [sb(es, 'at_vt16%d' % i, [128, 512], BF16) for i in range(3)]
                hT = sb(es, 'at_hT', [128, 8, TP + NS], BF16)
                nw = 0; nk = 0; nv = 0; nt3 = [0]

                def v_load(n_, it):
                    c0v = (3 * it[0] + 2) * 1024 + it[1] * 512
                    load_w(es, wvs[n_ % 2], ('at_wv', n_ % 2), w_in[:, c0v:c0v + 512], 8, 512, 'at')

                vpre = Pre([(g_, hb_) for hi_ in range(2) for g_ in range(len(GR)) for hb_ in range(2)], v_load, 1)
                nvw = 0
                for hi, (t0, TW) in enumerate(HALVES):
                    norm_half(es, di['norm_g'][layer], t0, TW, hT, 'at%d' % hi)
                    k.dma(rc[:, :TP], di['c_ropec'][:, t0:t0 + TP], w=['at_rc'])
                    k.dma(rs_[:, :TP], di['c_ropes'][:, t0:t0 + TP], w=['at_rs'])
                    if TW > TP:
                        for (dst_, src_, kk_) in ((rc, di['c_ropec'], 'at_rc'), (rs_, di['c_ropes'], 'at_rs')):
                            for jj in range(NS):
                                k.dma(dst_[:, TP + jj:TP + jj + 1], src_[:, T:T + 1], w=[kk_], allow_slow_non_contiguous=True)
                    tl = [(s0, 512) for s0 in range(0, TP, 512)]
                    if TW > TP:
                        tl.append((TP, NS))
                    for g, (win, dil) in enumerate(GR):
                        wn = str(win)
                        def qk_tile(n, s0, W_, b, wb, g, qk, h, win, wn, t0, last):
                            nonlocal nk
                            p = psalloc()
                            gemm_fm(p, wq[wb], ('at_wq', wb), 0, hT, s0, W_)
                            yield
                            k.op('act', I('copy', xb16[b][:, :W_], PS(p, W_)), r=[('ps', p)], w=[('at_xb', b)])
                            k.op('dve', I('tensor_tensor', t1[b][:, :W_], PS(p, W_), rc[:, s0:s0 + W_], ALU.mult),
                                 r=[('ps', p), 'at_rc'], w=[('at_t1', b)])
                            psfree(p)
                            yield
                            p2 = psalloc()
                            k.op('pe', I('matmul', PS(p2, W_), permb[:], xb16[b][:, :W_], start=True, stop=True),
                                 r=[('at_xb', b), 'at_permb'], w=[('ps', p2)])
                            yield
                            k.op('dve', I('tensor_tensor', res[b][:, :W_], PS(p2, W_), rs_[:, s0:s0 + W_], ALU.mult),
                                 r=[('ps', p2), 'at_rs'], w=[('at_res', b)])
                            psfree(p2)
                            yield
                            k.op('dve', I('tensor_tensor', res[b][:, :W_], res[b][:, :W_], t1[b][:, :W_], ALU.add),
                                 r=[('at_res', b), ('at_t1', b)], w=[('at_res', b)])
                            yield
                            k.op('act', I('copy', stage[wb][:, s0:s0 + W_], res[b][:, :W_]), r=[('at_res', b)], w=[('at_stage', wb, n)])
                            if W_ == 512 and qk == 1:
                                for j in range(4):
                                    tok0 = t0 + s0 + j * 128
                                    if tok0 >= T - min(win, T):
                                        kb_ = nk % 2; nk += 1
                                        p3 = nps()
                                        k.op('pe', I('transpose', PS(p3, 128), res[b][:, j * 128:(j + 1) * 128], ident[:]),
                                             r=[('at_res', b)], w=[('ps', p3)])
                                        k.op('act', I('copy', ktile[kb_][:, :], PS(p3, 128)), r=[('ps', p3)], w=[('at_ktile', kb_)])
                                        o0 = tok0 - (T - min(win, T))
                                        k.dma(di['o_kp' + wn][o0:o0 + 128, h * 128:(h + 1) * 128], ktile[kb_][:, :], r=[('at_ktile', kb_)])
                            if W_ == NS:
                                kb_ = nk % 2; nk += 1
                                p3 = nps()
                                k.op('pe', I('transpose', PS(p3, 128)[:NS, :], res[b][:, :NS], ident[:]), r=[('at_res', b)], w=[('ps', p3)])
                                k.op('act', I('copy', ktile[kb_][:NS, :], PS(p3, 128)[:NS, :]), r=[('ps', p3)], w=[('at_ktile', kb_)])
                                k.dma(qs_tok[g, qk, :, h * 128:(h + 1) * 128], ktile[kb_][:NS, :], r=[('at_ktile', kb_)])
                                if qk == 1:
                                    k.dma(di['o_ks' + wn][:, h * 128:(h + 1) * 128], ktile[kb_][:NS, :], r=[('at_ktile', kb_)])
                            if last:
                                k.dma(qkT[2 * g + qk, h * 128:(h + 1) * 128, t0:t0 + TW], stage[wb][:, :TW],
                                      r=[('at_stage', wb, n_) for n_ in range(len(tl))])

                        def qk_load(n_, it, g=g):
                            c0q = (3 * g + it[0]) * 1024 + it[1] * 128
                            load_w(es, wq[(nwb[0] + n_) % 3], ('at_wq', (nwb[0] + n_) % 3), w_in[:, c0q:c0q + 128], 8, 128, 'at')

                        vpre.need(nvw - 1)
                        nwb = [nw]
                        qkpre = Pre([(qk_, h_) for qk_ in range(2) for h_ in range(8)], qk_load, 1)

                        def qk_tiles(g=g, win=win, wn=wn, t0=t0):
                            nonlocal nw
                            gi = 0
                            for qk in range(2):
                                for h in range(8):
                                    wb = nw % 3; nw += 1
                                    qkpre.need(gi)
                                    gi += 1
                                    for n, (s0, W_) in enumerate(tl):
                                        b = nt3[0] % 5; nt3[0] += 1
                                        yield qk_tile(n, s0, W_, b, wb, g, qk, h, win, wn, t0, n == len(tl) - 1)

                        pipeline(qk_tiles(), 4)
                        for hb in range(2):
                            vpre.need(nvw)
                            wv_, wvk = wvs[nvw % 2], ('at_wv', nvw % 2)
                            nvw += 1
                            subs = [(j * 128, 128) for j in range(TP // 128)] + ([(TP, NS)] if TW > TP else [])
                            for (c0_, L) in subs:
                                b = nv % 3; nv += 1
                                p = nps()
                                for c in range(8):
                                    k.op('pe', I('matmul', PS(p, 512)[:L, :], hT[:, c, c0_:c0_ + L], wv_[:, c, :], start=(c == 0), stop=(c == 7)),
                                         r=[wvk] + hkeys(c0_), w=[('ps', p)])
                                k.op('act', I('copy', vt32[b][:L, :], PS(p, 512)[:L, :]), r=[('ps', p)], w=[('at_vt32', b)])
                                k.op('dve', I('tensor_copy', vt16[b][:L, :], PS(p, 512)[:L, :]), r=[('ps', p)], w=[('at_vt16', b)])
                                if L == 128:
                                    tok0 = t0 + c0_
                                    k.dma(vtok[g, tok0:tok0 + 128, hb * 512:(hb + 1) * 512], vt16[b][:, :], r=[('at_vt16', b)])
                                    if tok0 >= T - min(win, T):
                                        o0 = tok0 - (T - min(win, T))
                                        k.dma(di['o_vp' + wn][o0:o0 + 128, hb * 512:(hb + 1) * 512], vt32[b][:, :], r=[('at_vt32', b)])
                                else:
                                    k.dma(di['o_vs' + wn][:, hb * 512:(hb + 1) * 512], vt32[b][:NS, :], r=[('at_vt32', b)])
                                    k.dma(vs_tok[g, :, hb * 512:(hb + 1) * 512], vt32[b][:NS, :], r=[('at_vt32', b)])
                    for f in range(8):
                        wb = nw % 2; nw += 1
                        load_w(es, wq[wb], ('at_wq', wb), w_in[:, 9216 + f * 128:9216 + (f + 1) * 128], 8, 128, 'at')
                        for n, (s0, W_) in enumerate(tl):
                            p = nps()
                            gemm_fm(p, wq[wb], ('at_wq', wb), 0, hT, s0, W_)
                            k.op('act', I('activation', stage[wb][:, s0:s0 + W_], PS(p, W_), AF.Silu), r=[('ps', p)], w=[('at_stage', wb, n)])
                        k.dma(zT[f * 128:(f + 1) * 128, t0:t0 + TW], stage[wb][:, :TW], r=[('at_stage', wb, n) for n in range(len(tl))])
                k.barrier()
                k.flush()
            with ExitStack() as es:
                wst.clear()
                band = sb(es, 'ab_band', [128, 256]); k.dma(band[:], di['c_band'], w=['ab_band'])
                NB = 8
                qh = [sb(es, 'ab_qh%d' % i, [128, T], BF16) for i in range(2)]
                kh = [sb(es, 'ab_kh%d' % i, [128, T], BF16) for i in range(2)]
                sm = [sb(es, 'ab_sm%d' % i, [128, 256]) for i in range(NB)]
                Pm = [sb(es, 'ab_P%d' % i, [128, 256], BF16) for i in range(NB)]
                PT = [sb(es, 'ab_PT%d' % i, [128, 2, 128], BF16) for i in range(NB)]
                vt = [sb(es, 'ab_vt%d' % i, [128, 2, 128], BF16) for i in range(NB)]
                osb = [sb(es, 'ab_o%d' % i, [128, 130]) for i in range(NB)]
                nmx = [sb(es, 'ab_nmx%d' % i, [128, 1]) for i in range(NB)]
                nu = 0
                ABW = int(os.environ.get('ABW', '7'))
                heads_ = [(g, h) for g in range(len(GR)) for h in range(8)]

                def ab_load(i):
                    g, h = heads_[i]
                    hb = i % 2
                    k.dma(qh[hb][:, :], qkT[2 * g, h * 128:(h + 1) * 128, 0:T], w=[('ab_qh', hb)])
                    k.dma(kh[hb][:, :], qkT[2 * g + 1, h * 128:(h + 1) * 128, 0:T], w=[('ab_kh', hb)])

                def attn_unit(r, n, b, g, h, hb, dil, qv, kv, vg):
                    k0 = max(n - 1, 0) * 128
                    NK = 256 if n > 0 else 128
                    m0 = 0 if n > 0 else 128
                    k.dma(vt[b][:, 0:NK // 128, :], vg[r, k0:k0 + NK, :].rearrange("(j p) e -> p j e", p=128), w=[('ab_vt', b)])
                    p = psalloc()
                    k.op('pe', I('matmul', PS(p, NK), qv[:, r, n * 128:(n + 1) * 128], kv[:, r, k0:k0 + NK], start=True, stop=True),
                         r=[('ab_qh', hb), ('ab_kh', hb)], w=[('ps', p)])
                    yield
                    k.op('dve', I('scalar_tensor_tensor', sm[b][:, :NK], PS(p, NK), ISQ, band[:, m0:m0 + NK], ALU.mult, ALU.add),
                         r=[('ps', p), 'ab_band'], w=[('ab_sm', b)])
                    psfree(p)
                    yield
                    k.op('dve', I('tensor_reduce', osb[b][:, 128:129], sm[b][:, :NK], AX.X, ALU.max), r=[('ab_sm', b)], w=[('ab_o', b, 1)])
                    yield
                    k.op('pool', I('tensor_scalar', nmx[b][:, :], osb[b][:, 128:129], -1.0, None, ALU.mult), r=[('ab_o', b, 1)], w=[('ab_nmx', b)])
                    yield
                    k.op('act', I('activation', Pm[b][:, :NK], sm[b][:, :NK], AF.Exp, bias=nmx[b][:, :], accum_out=osb[b][:, 129:130]),
                         r=[('ab_sm', b), ('ab_nmx', b)], w=[('ab_P', b), ('ab_o', b, 2)])
                    yield
                    for j in range(NK // 128):
                        k.op('pe', I('transpose', PSB(j, 128), Pm[b][:, j * 128:(j + 1) * 128], identb[:]), r=[('ab_P', b)], w=[('psb', j)])
                        k.op('act' if j else 'dve', I('copy' if j else 'tensor_copy', PT[b][:, j, :], PSB(j, 128)), r=[('psb', j)], w=[('ab_PT', b, j)])
                    yield
                    po = psalloc()
                    for j in range(NK // 128):
                        k.op('pe', I('matmul', PS(po, 128), PT[b][:, j, :], vt[b][:, j, :], start=(j == 0), stop=(j == NK // 128 - 1)),
                             r=[('ab_PT', b, j), ('ab_vt', b)], w=[('ps', po)])
                    yield
                    k.op('act', I('copy', osb[b][:, 0:128], PS(po, 128)), r=[('ps', po)], w=[('ab_o', b, 0)])
                    psfree(po)
                    dst = numS[g].rearrange("(u d) f -> d u f", d=dil)[r, n * 128:(n + 1) * 128, h * 130:(h + 1) * 130]
                    k.dma(dst, osb[b][:, :], r=[('ab_o', b, 0), ('ab_o', b, 1), ('ab_o', b, 2)])

                def all_units():
                    nonlocal nu
                    ab_load(0)
                    for i, (g, h) in enumerate(heads_):
                        if i + 1 < len(heads_):
                            ab_load(i + 1)
                        win, dil = GR[g]
                        nb_ = (T // dil) // 128
                        hb = i % 2
                        qv = qh[hb][:, :].rearrange("p (u d) -> p d u", d=dil)
                        kv = kh[hb][:, :].rearrange("p (u d) -> p d u", d=dil)
                        vg = vtok[g, 0:T, h * 128:(h + 1) * 128].rearrange("(u d) e -> d u e", d=dil)
                        for r in range(dil):
                            for n in range(nb_):
                                b = nu % NB
                                nu += 1
                                yield attn_unit(r, n, b, g, h, hb, dil, qv, kv, vg)

                pipeline(all_units(), ABW, ramp=1)
                k.barrier()
                k.flush()
            with ExitStack() as es:
                wst.clear()
                NBC = 4
                A = [sb(es, 'ac_A%d' % i, [128, 3, 8, 130]) for i in range(NBC)]
                zt = [sb(es, 'ac_z%d' % i, [128, 8, 128], BF16) for i in range(NBC)]
                from types import SimpleNamespace as _NS
                MB = []
                for par in range(NBC):
                    m_ = _NS(par=par)
                    m_.M = sb(es, 'ac_M', [128, 8]); m_.wg = sb(es, 'ac_w', [128, 3, 8]); m_.den = sb(es, 'ac_den', [128, 8])
                    m_.tmp = sb(es, 'ac_tmp', [128, 3, 8]); m_.acc = sb(es, 'ac_acc', [128, 8, 128]); m_.ob = sb(es, 'ac_ob', [128, 8, 128], BF16)
                    MB.append(m_)
                yo = [sb(es, 'ac_yo%d' % i, [128, 8, 128], BF16) for i in range(NBC)]
                acc = MB[0].acc; ob = MB[0].ob

                def merge_g(Av, L, akeys, m_):
                    P_ = m_.par
                    K_ = lambda nm, *a: (nm, P_) + a
                    M, wg_, den, tmp, acc_, ob_ = m_.M, m_.wg, m_.den, m_.tmp, m_.acc, m_.ob
                    k.op('dve', I('tensor_tensor', M[:L, :], Av[:L, 0, :, 128], Av[:L, 1, :, 128], ALU.max), r=akeys, w=[K_('ac_M')])
                    yield
                    k.op('dve', I('tensor_tensor', M[:L, :], M[:L, :], Av[:L, 2, :, 128], ALU.max), r=akeys + [K_('ac_M')], w=[K_('ac_M')])
                    yield
                    for g in range(3):
                        k.op('dve', I('tensor_tensor', wg_[:L, g, :], Av[:L, g, :, 128], M[:L, :], ALU.subtract), r=akeys + [K_('ac_M')], w=[K_('ac_w', g)])
                    yield
                    k.op('act', I('activation', wg_[:L, :, :], wg_[:L, :, :], AF.Exp), r=[K_('ac_w', g) for g in range(3)], w=[K_('ac_w', g) for g in range(3)])
                    yield
                    for g in range(3):
                        k.op('dve', I('tensor_tensor', tmp[:L, g, :], wg_[:L, g, :], Av[:L, g, :, 129], ALU.mult), r=akeys + [K_('ac_w', g)], w=[K_('ac_tmp', g)])
                    yield
                    k.op('dve', I('tensor_tensor', den[:L, :], tmp[:L, 0, :], tmp[:L, 1, :], ALU.add), r=[K_('ac_tmp', 0), K_('ac_tmp', 1)], w=[K_('ac_den')])
                    yield
                    k.op('dve', I('tensor_tensor', den[:L, :], den[:L, :], tmp[:L, 2, :], ALU.add), r=[K_('ac_tmp', 2), K_('ac_den')], w=[K_('ac_den')])
                    yield
                    k.op('dve', I('reciprocal', den[:L, :], den[:L, :]), r=[K_('ac_den')], w=[K_('ac_den')])
                    yield
                    for h in range(8):
                        k.op('dve', I('tensor_scalar', acc_[:L, h, :], Av[:L, 0, h, 0:128], wg_[:L, 0, h:h + 1], None, ALU.mult),
                             r=akeys + [K_('ac_w', 0)], w=[K_('ac_acc', h)])
                    yield
                    for g in (1, 2):
                        for h in range(8):
                            k.op('dve', I('scalar_tensor_tensor', acc_[:L, h, :], Av[:L, g, h, 0:128], wg_[:L, g, h:h + 1], acc_[:L, h, :], ALU.mult, ALU.add),
                                 r=akeys + [K_('ac_w', g), K_('ac_acc', h)], w=[K_('ac_acc', h)])
                        yield
                    for h in range(8):
                        k.op('act', I('mul', ob_[:L, h, :], acc_[:L, h, :], den[:L, h:h + 1]),
                             r=[K_('ac_acc', h), K_('ac_den')], w=[K_('ac_ob', h)])

                def tile_g(tt, b):
                    m_ = MB[b]
                    for g in range(3):
                        k.dma(A[b][:, g, :, :], numS[g, tt * 128:(tt + 1) * 128, :].rearrange("t (h f) -> t h f", f=130), w=[('ac_A', b, g)])
                    k.dma(zt[b][:, :, :128], zT.rearrange("(c p) t -> p c t", p=128)[:, :, tt * 128:(tt + 1) * 128], w=[('ac_z', b)])
                    yield
                    yield from merge_g(A[b], 128, [('ac_A', b, g) for g in range(3)], m_)
                    yield
                    for h in range(8):
                        k.op('pe', I('transpose', PSB(h % 2, 128), m_.ob[:, h, :], identb[:, :]), r=[('ac_ob', b, h)], w=[('psb', h % 2)])
                        k.op('dve', I('tensor_tensor', yo[b][:, h, :], PSB(h % 2, 128), zt[b][:, h, :], ALU.mult),
                             r=[('psb', h % 2), ('ac_z', b)], w=[('ac_yo', b, h)])
                    k.dma(yTv[:, 0:8, tt * 128:(tt + 1) * 128], yo[b][:, :, :], r=[('ac_yo', b, h) for h in range(8)])

                pipeline((tile_g(tt, tt % NBC) for tt in range(T // 128)), NBC, ramp=1)

                def merge(Av, L, akeys):
                    for _ in merge_g(Av, L, akeys, MB[0]):
                        pass

                As = sb(es, 'as_A', [128, 3, 8, 130])
                Kc = sb(es, 'as_K', [128, 1024]); Vc = sb(es, 'as_V', [128, 1024])
                qb = sb(es, 'as_qb', [128, 1024]); kn = sb(es, 'as_kn', [128, 1024]); vn = sb(es, 'as_vn', [1, 1024])
                prod = sb(es, 'as_prod', [128, 1024])
                sc2 = sb(es, 'as_sc2', [128, 16])
                sall = sb(es, 'as_sall', [8, 130]); Ps = sb(es, 'as_P', [8, 130]); mxs = sb(es, 'as_mx', [8, 2]); nmxs = sb(es, 'as_nmx', [8, 1])
                PTk = sb(es, 'as_PTk', [128, 8]); PTn = sb(es, 'as_PTn', [1, 24])
                k.op('dve', I('memset', As[:], 0.0), w=[('as_A', j) for j in range(NS)])
                for j in range(NS):
                    for g, (win, dil) in enumerate(GR):
                        wn = str(win)
                        k.dma(Kc[:, :], di['ck' + wn][j].rearrange("(u d) f -> d u f", d=dil)[0, :, :], w=['as_K'])
                        k.dma(Vc[:, :], di['cv' + wn][j].rearrange("(u d) f -> d u f", d=dil)[0, :, :], w=['as_V'])
                        k.dma(qb[:, :], bass.AP(qs_tok.tensor, qs_tok[g, 0, j].offset, [[0, 128], [1, 1024]]), w=['as_qb'])
                        k.dma(kn[:, :], bass.AP(qs_tok.tensor, qs_tok[g, 1, j].offset, [[0, 128], [1, 1024]]), w=['as_kn'])
                        k.dma(vn[:, :], vs_tok[g, j:j + 1, :], w=['as_vn'])
                        k.op('dve', I('tensor_tensor', prod[:, :], Kc[:, :], qb[:, :], ALU.mult), r=['as_K', 'as_qb'], w=['as_prod'])
                        k.op('dve', I('tensor_reduce', sc2[:, 0:8], prod[:, :].rearrange("p (h e) -> p h e", e=128), AX.X, ALU.add), r=['as_prod'], w=['as_sc2'])
                        k.op('dve', I('tensor_tensor', prod[:, :], kn[:, :], qb[:, :], ALU.mult), r=['as_kn', 'as_qb', 'as_sc2'], w=['as_prod'])
                        k.op('dve', I('tensor_reduce', sc2[:, 8:16], prod[:, :].rearrange("p (h e) -> p h e", e=128), AX.X, ALU.add), r=['as_prod'], w=['as_sc2'])
                        p = nps()
                        k.op('pe', I('transpose', PS(p, 128)[:8, :], sc2[:, 0:8], ident[:]), r=['as_sc2'], w=[('ps', p)])
                        k.op('act', I('mul', sall[:, 0:128], PS(p, 128)[:8, :], ISQ), r=[('ps', p)], w=['as_sall'])
                        p = nps()
                        k.op('pe', I('transpose', PS(p, 128)[:8, :], sc2[:, 8:16], ident[:]), r=['as_sc2'], w=[('ps', p)])
                        k.op('act', I('mul', sall[:, 128:129], PS(p, 128)[:8, 0:1], ISQ), r=[('ps', p)], w=['as_sall'])
                        k.op('dve', I('tensor_reduce', mxs[:, 0:1], sall[:, 0:129], AX.X, ALU.max), r=['as_sall'], w=['as_mx'])
                        k.op('act', I('mul', nmxs[:, :], mxs[:, 0:1], -1.0), r=['as_mx'], w=['as_nmx'])
                        k.op('act', I('activation', Ps[:, 0:129], sall[:, 0:129], AF.Exp, bias=nmxs[:, :], accum_out=mxs[:, 1:2]),
                             r=['as_sall', 'as_nmx'], w=['as_P', 'as_mx'])
                        p = nps()
                        k.op('pe', I('transpose', PS(p, 8), Ps[:, 0:128], ident[:8, :8]), r=['as_P'], w=[('ps', p)])
                        k.op('act', I('copy', PTk[:, :], PS(p, 8)), r=[('ps', p)], w=['as_PTk'])
                        p = nps()
                        k.op('pe', I('transpose', PS(p, 8)[:1, :], Ps[:, 128:129], ident[:8, :8]), r=['as_P'], w=[('ps', p)])
                        k.op('act', I('copy', PTn[:, 0:8], PS(p, 8)[:1, :]), r=[('ps', p)], w=['as_PTn'])
                        p = nps()
                        k.op('pe', I('transpose', PS(p, 8)[:1, :], mxs[:, 0:1], ident[:8, :8]), r=['as_mx'], w=[('ps', p)])
                        k.op('act', I('copy', PTn[:, 8:16], PS(p, 8)[:1, :]), r=[('ps', p)], w=['as_PTn'])
                        p = nps()
                        k.op('pe', I('transpose', PS(p, 8)[:1, :], mxs[:, 1:2], ident[:8, :8]), r=['as_mx'], w=[('ps', p)])
                        k.op('act', I('copy', PTn[:, 16:24], PS(p, 8)[:1, :]), r=[('ps', p)], w=['as_PTn'])
                        k.op('dve', I('tensor_copy', As[j:j + 1, g, :, 128] if False else As[0:1, g, :, 128], PTn[:, 8:16]), r=['as_PTn'], w=[('as_A', j)])
                        k.op('dve', I('tensor_copy', As[0:1, g, :, 129], PTn[:, 16:24]), r=['as_PTn'], w=[('as_A', j)])
                        for h in range(8):
                            p = nps()
                            k.op('pe', I('matmul', PS(p, 128)[:1, :], PTk[:, h:h + 1], Vc[:, h * 128:(h + 1) * 128], start=True, stop=False),
                                 r=['as_PTk', 'as_V'], w=[('ps', p)])
                            k.op('pe', I('matmul', PS(p, 128)[:1, :], PTn[:, h:h + 1], vn[:, h * 128:(h + 1) * 128], start=False, stop=True),
                                 r=['as_PTn', 'as_vn'], w=[('ps', p)])
                            k.op('act', I('copy', As[0:1, g, h, 0:128], PS(p, 128)[:1, :]), r=[('ps', p)], w=[('as_A', j)])
                    merge(As, 1, [('as_A', j)])
                    k.op('act', I('copy', acc[0:1, :, :], ob[0:1, :, :]), r=[('ac_ob', 0, h) for h in range(8)], w=[('ac_acc', 0, h) for h in range(8)])
                    k.dma(os_tok[j:j + 1, :], acc[0:1, :, :].rearrange("p h e -> p (h e)"), r=[('ac_acc', 0, h) for h in range(8)])
                k.barrier()
                osT = sb(es, 'as_osT', [128, 8, NS])
                rows_to_fm(es, os_tok, NS, 1024, osT, 'as_osT', 'ao')
                k.dma(zt[0][:, :, :NS], zT.rearrange("(c p) t -> p c t", p=128)[:, :, T:T + NS], w=[('ac_z', 0)])
                k.op('dve', I('tensor_tensor', yo[0][:, :, :NS], osT[:, :, :], zt[0][:, :, :NS], ALU.mult), r=['as_osT', ('ac_z', 0)], w=[('ac_yo', 0, 0)])
                k.dma(yTv[:, 0:8, T:T + NS], yo[0][:, :, :NS], r=[('ac_yo', 0, 0)])
                k.barrier()
                k.flush()
            out_phase(layer, di['c_w_out'], 8)

        MIXERS = {0: mlstm_layer, 1: conv_layer, 2: attn_layer, 3: pool_layer}
        for li in layers:
            MIXERS[li]()

        with ExitStack() as es:
            wst.clear()
            hf = [sb(es, 'hf%d' % i, [128, 8, 512]) for i in range(2)]
            yo = [sb(es, 'yo%d' % i, [128, D]) for i in range(2)]
            n = 0
            ftl = [(s0, min(512, TA - s0)) for s0 in list(range(0, T, 512)) + [T]]
            for fn_, (s0, W_) in enumerate(ftl):
                norm_tile_f32(es, di['final_g'], ftl, fn_, hf)
                for j0 in range(0, W_, 128):
                    L = min(128, W_ - j0)
                    b = n % 2
                    n += 1
                    for c in range(8):
                        p = nps()
                        k.op('pe', I('transpose', PS(p, 128)[:L, :], hf[fn_ % 2][:, c, j0:j0 + L], ident[:]),
                             r=[('hf', fn_ % 2, c)], w=[('ps', p)])
                        k.op('act' if c % 2 else 'dve',
                             I('copy' if c % 2 else 'tensor_copy', yo[b][:L, c * 128:(c + 1) * 128], PS(p, 128)[:L, :]),
                             r=[('ps', p)], w=[('yo', b, c)])
                    dst = di['y_prompt'][s0 + j0:s0 + j0 + L, :] if s0 < T else di['y_sample']
                    k.dma(dst, yo[b][:L, :], r=[('yo', b, c) for c in range(8)])
            if dbg:
                k.barrier()
                for c in range(8):
                    k.dma(di['dbg_xT'][c * 128:(c + 1) * 128, :], xT[c * 128:(c + 1) * 128, :])
            k.barrier()
            k.flush()
    return nc


_CACHE = {}


def _prep_inputs(inp, T):
    cst = host_consts(T)
    shared = {}
    for k_ in W_SHAPES:
        a = np.asarray(inp[k_], dtype=np.float32)
        if k_ in ('norm_g', 'pe_w', 'pg_w', 'final_g'):
            shared[k_] = np.ascontiguousarray(a)
        else:
            shared[k_] = np.ascontiguousarray(a[0])
    for k_ in CONST_SHAPES:
        shared['c_' + k_] = cst[k_]
    shared['c_ropec'] = cst['ropec']
    shared['c_ropes'] = cst['ropes']
    maps = []
    for c in range(8):
        b = c % 4
        sl = slice(NS * c, NS * c + NS)
        m = dict(shared)
        m['x_prompt'] = np.ascontiguousarray(inp['x_prompt'][b])
        m['x_sample'] = np.ascontiguousarray(inp['x_sample'][sl, 0])
        m['p_prompt'] = np.ascontiguousarray(inp['p_prompt'][:, b])
        m['p_sample'] = np.ascontiguousarray(inp['p_sample'][:, sl, 0])
        m['st_C'] = np.ascontiguousarray(inp['state_mlstm_C'][0, sl])
        m['st_n'] = np.ascontiguousarray(inp['state_mlstm_n'][0, sl])
        m['st_m'] = np.ascontiguousarray(inp['state_mlstm_m'][0, sl])
        m['st_conv'] = np.ascontiguousarray(inp['state_conv'][0, sl])
        m['st_pool'] = np.ascontiguousarray(inp['state_pool'][0, sl])
        for wn in ('128', '512', '2048'):
            m['ck' + wn] = np.ascontiguousarray(inp['cache_k_w' + wn][0, sl]).reshape(NS, -1, 1024)
            m['cv' + wn] = np.ascontiguousarray(inp['cache_v_w' + wn][0, sl]).reshape(NS, -1, 1024)
        maps.append(m)
    return maps


def _gather(res, T):
    R = res.results
    B = 4

    def P(name):
        return np.stack([np.asarray(R[c][name]) for c in range(B)])

    def S(name):
        return np.concatenate([np.asarray(R[c][name]) for c in range(8)], axis=0)

    outs = [P('y_prompt'), S('y_sample')[:, None, :],
            P('o_Cp')[None], S('o_Cs')[None],
            P('o_np')[None], S('o_ns')[None],
            P('o_mp')[:, 0][None], S('o_ms')[None],
            P('o_convp')[None], S('o_convs')[None]]
    for wn in ('128', '512', '2048'):
        nk = min(int(wn), T)
        outs += [P('o_kp' + wn).reshape(1, B, nk, 8, 128), S('o_ks' + wn).reshape(1, 32, 1, 8, 128),
                 P('o_vp' + wn).reshape(1, B, nk, 8, 128), S('o_vs' + wn).reshape(1, 32, 1, 8, 128)]
    outs += [P('o_poolp')[None], S('o_pools')[None]]
    return tuple(np.ascontiguousarray(o, dtype=np.float32) for o in outs)


def kernel(**inp):
    T = int(np.asarray(inp['x_prompt']).shape[1])
    if T not in _CACHE:
        _CACHE[T] = build(T)
    nc = _CACHE[T]
    inp = {k_: np.asarray(v) for k_, v in inp.items()}
    maps = _prep_inputs(inp, T)
    res = run_bass_kernel_spmd(nc, maps, core_ids=list(range(8)))
    return _gather(res, T)
```

```python
import os
import numpy as np
from contextlib import ExitStack
import concourse.bass as bass
import concourse.mybir as mybir
from concourse.bass_utils import run_bass_kernel_spmd

F32 = mybir.dt.float32
BF16 = mybir.dt.bfloat16
AF = mybir.ActivationFunctionType
ALU = mybir.AluOpType
AX = mybir.AxisListType

ENGS = ('pe', 'act', 'dve', 'pool', 'sp')
NDMA = 32
D = 1024
NS = 4
EPS = 1e-6


def I(name, *a, **kw):
    return lambda e: getattr(e, name)(*a, **kw)


PHASE_LOG = []


class Pre:
    def __init__(self, items, load, depth):
        self.items, self.load, self.depth, self.n = items, load, depth, 0

    def need(self, i):
        while self.n <= min(i + self.depth, len(self.items) - 1):
            self.load(self.n, self.items[self.n])
            self.n += 1


def pipeline(gen_iter, width, ramp=None):
    active = []
    it = iter(gen_iter)
    done = False
    while True:
        started = 0
        while len(active) < width and not done and (ramp is None or started < ramp):
            try:
                active.append(next(it))
                started += 1
            except StopIteration:
                done = True
        if not active:
            break
        nxt = []
        for g in active:
            try:
                next(g)
                nxt.append(g)
            except StopIteration:
                pass
        active = nxt


def lockstep(gens):
    gens = list(gens)
    while gens:
        nxt = []
        for g in gens:
            try:
                next(g)
                nxt.append(g)
            except StopIteration:
                pass
        gens = nxt


class Trk:
    def __init__(self, nc, sems, dsems, nsw=8):
        self.nc = nc
        self.eng = {'pe': nc.tensor, 'act': nc.scalar, 'dve': nc.vector, 'pool': nc.gpsimd, 'sp': nc.sync}
        self.sems = dict(sems)
        for i, s in enumerate(dsems):
            self.sems['d%d' % i] = s
        self.cnt = {e: 0 for e in ENGS}
        self.dtot = [0] * len(dsems)
        self.nhw = len(dsems) - nsw
        self.rr = 0
        self.rr_sw = 0
        self.known = {e: {} for e in ENGS}
        self.streams = {e: [] for e in ENGS}
        self.last_w = {}
        self.readers = {}
        self.groups = {}

    def _exp(self, keys):
        out = []
        for k in keys:
            out.extend(self.groups.get(k, (k,)))
        return out

    def _deps(self, en, r, w):
        r = self._exp(r)
        w = self._exp(w)
        deps = []
        for k in r:
            t = self.last_w.get(k)
            if t:
                deps.append(t)
            if isinstance(k, tuple) and k[0] in ('ps', 'psb'):
                deps.extend(t2 for t2 in self.readers.get(k, ()) if t2[0] != en)
        for k in w:
            t = self.last_w.get(k)
            if t:
                deps.append(t)
            deps.extend(self.readers.get(k, ()))
        waits = []
        kn = self.known[en]
        for (s, v) in deps:
            if s == 'pe' and en == 'pe':
                continue
            if kn.get(s, 0) >= v:
                continue
            kn[s] = v
            waits.append((s, v))
        return waits

    def _commit(self, tok, r, w):
        r = self._exp(r)
        w = self._exp(w)
        for k in r:
            self.readers.setdefault(k, []).append(tok)
        for k in w:
            self.last_w[k] = tok
            self.readers[k] = []

    def op(self, en, fn, r=(), w=()):
        waits = self._deps(en, r, w)
        self.cnt[en] += 1
        tok = (en, self.cnt[en])
        self.streams[en].append((waits, fn, en, 1))
        self._commit(tok, r, w)

    def dma(self, out, in_, r=(), w=(), q='sp', **kw):
        waits = self._deps(q, r, w)
        if q == 'pool':
            i = self.nhw + self.rr_sw
            self.rr_sw = (self.rr_sw + 1) % (len(self.dtot) - self.nhw)
        else:
            i = self.rr
            self.rr = (self.rr + 1) % self.nhw
        s = 'd%d' % i
        if self.known[q].get(s, 0) < self.dtot[i]:
            self.known[q][s] = self.dtot[i]
            waits.append((s, self.dtot[i]))
        self.dtot[i] += 16
        tok = (s, self.dtot[i])
        self.streams[q].append((waits, I('dma_start', out=out, in_=in_, **kw), s, 16))
        self._commit(tok, r, w)

    def barrier(self):
        allt = [(e, self.cnt[e]) for e in ENGS if self.cnt[e] > 0]
        allt += [('d%d' % i, v) for i, v in enumerate(self.dtot) if v > 0]
        for en in ENGS:
            waits = []
            for (s, v) in allt:
                if self.known[en].get(s, 0) < v:
                    self.known[en][s] = v
                    waits.append((s, v))
            if waits:
                self.streams[en].append((waits, None, None, 0))
        self.last_w = {}
        self.readers = {}

    def flush(self):
        PHASE_LOG.append(dict(self.cnt))
        nc = self.nc
        streams = self.streams
        self.streams = {e: [] for e in ENGS}
        sems = self.sems

        def replay(en):
            def f(e):
                for (waits, fn, s, inc) in streams[en]:
                    for (ws, wv) in waits:
                        e.wait_ge(sems[ws], wv)
                    if fn is not None:
                        fn(e).then_inc(sems[s], inc)
            return f

        with nc.Block() as block:
            block.tensor(replay('pe'))
            block.scalar(replay('act'))
            block.vector(replay('dve'))
            block.gpsimd(replay('pool'))
            block.sync(replay('sp'))


def host_consts(T):
    c = {}
    c['ident'] = np.eye(128, dtype=np.float32)
    i = np.arange(128)
    c['triu'] = (i[:, None] <= i[None, :]).astype(np.float32)
    c['maskneg'] = np.where(i[None, :] <= i[:, None], 0.0, -1e30).astype(np.float32)
    sel = np.zeros((128, 128), np.float32)
    sel[127, :] = 1.0
    c['sel_last'] = sel
    qi = i[:, None]
    kj = np.arange(256)[None, :]
    dist = qi + 128 - kj
    band = (dist >= 0) & (dist <= 128)
    c['band'] = np.where(band, 0.0, -1e30).astype(np.float32)
    c['band0'] = np.where(band & (kj >= 128), 0.0, -1e30).astype(np.float32)
    half = 16
    inv = (500000.0 ** (-np.arange(half, dtype=np.float32) / half)).astype(np.float32)
    pos = np.concatenate([np.arange(T), np.array([8192])]).astype(np.float32)
    ang = (pos[None, :] * inv[:, None]).astype(np.float32)
    cosT = np.ones((128, T + 1), np.float32)
    sinT = np.zeros((128, T + 1), np.float32)
    cosT[0:16] = np.cos(ang)
    cosT[16:32] = np.cos(ang)
    sinT[0:16] = -np.sin(ang)
    sinT[16:32] = np.sin(ang)
    c['ropec'] = cosT
    c['ropes'] = sinT
    pm = np.zeros((128, 128), np.float32)
    for p in range(16):
        pm[p + 16, p] = 1.0
        pm[p, p + 16] = 1.0
    c['ropeperm'] = pm
    ic = np.zeros((128, 16, 16), np.float32)
    for ch in range(16):
        wdw = (2, 4, 8, 16)[ch // 4]
        for t in range(16):
            ic[:, ch, t] = 1.0 / min(wdw, t + 1)
    c['invcnt'] = ic.reshape(128, 256)
    return c


CONST_SHAPES = {'ident': [128, 128], 'triu': [128, 128], 'maskneg': [128, 128], 'sel_last': [128, 128],
                'band': [128, 256], 'band0': [128, 256], 'ropeperm': [128, 128], 'invcnt': [128, 256]}

W_SHAPES = {
    'norm_g': [4, 1024], 'pe_w': [4, 256, 1024], 'pg_w': [4, 1024, 1024], 'final_g': [1024],
    'a_w_in': [1024, 8208], 'a_b_if': [16], 'a_norm_g': [2048], 'a_w_out': [2048, 1024],
    'b_w_in': [1024, 8192], 'b_conv_w': [3, 2048], 'b_w_out': [2048, 1024],
    'c_w_in': [1024, 10240], 'c_w_out': [1024, 1024],
    'd_w_in': [1024, 4096], 'd_w_grp': [4, 512, 512], 'd_scale': [2048], 'd_w_out': [2048, 1024],
}


def build(T=4096, layers=(0, 1, 2, 3), dbg=False):
    TA = T + NS
    NCH = T // 128
    nc = bass.Bass("TRN2", target_bir_lowering=False)
    di = {}

    def din(name, shape, dt=F32):
        di[name] = nc.dram_tensor(name, list(shape), dt, kind="ExternalInput").ap()
        return di[name]

    def dout(name, shape, dt=F32):
        di[name] = nc.dram_tensor(name, list(shape), dt, kind="ExternalOutput").ap()
        return di[name]

    def dscr(name, shape, dt=F32):
        di[name] = nc.dram_tensor(name, list(shape), dt, kind="Internal").ap()
        return di[name]

    din('x_prompt', [T, D]); din('x_sample', [NS, D])
    din('p_prompt', [4, T, 256]); din('p_sample', [4, NS, 256])
    din('st_C', [NS, 8, 256, 128]); din('st_n', [NS, 8, 128]); din('st_m', [NS, 8])
    din('st_conv', [NS, 2, 2048]); din('st_pool', [NS, 15, 2048])
    for wn, nb in (('128', 128), ('512', 512), ('2048', 2048)):
        din('ck' + wn, [NS, nb, 1024]); din('cv' + wn, [NS, nb, 1024])
    for k_, s_ in W_SHAPES.items():
        din(k_, s_)
    for k_, s_ in CONST_SHAPES.items():
        din('c_' + k_, s_)
    din('c_ropec', [128, T + 1]); din('c_ropes', [128, T + 1])

    dout('y_prompt', [T, D]); dout('y_sample', [NS, D])
    dout('o_Cp', [8, 256, 128]); dout('o_Cs', [NS, 8, 256, 128])
    dout('o_np', [8, 128]); dout('o_ns', [NS, 8, 128])
    dout('o_mp', [1, 8]); dout('o_ms', [NS, 8])
    dout('o_convp', [2, 2048]); dout('o_convs', [NS, 2, 2048])
    for wn, nb in (('128', 128), ('512', 512), ('2048', 2048)):
        nk = min(nb, T)
        dout('o_kp' + wn, [nk, 1024]); dout('o_ks' + wn, [NS, 1024])
        dout('o_vp' + wn, [nk, 1024]); dout('o_vs' + wn, [NS, 1024])
    dout('o_poolp', [15, 2048]); dout('o_pools', [NS, 15, 2048])
    if dbg:
        dout('dbg_xT', [D, TA])

    xT = dscr('xT', [D, TA])
    yT = dscr('yT', [2048, TA], BF16)

    with ExitStack() as top:
        sems = {e: top.enter_context(nc.semaphore('s_' + e)) for e in ENGS}
        dsems = [top.enter_context(nc.semaphore('sd%d' % i)) for i in range(NDMA)]
        k = Trk(nc, sems, dsems)

        uid = [0]

        def sb(es, name, shape, dt=F32):
            uid[0] += 1
            return es.enter_context(nc.sbuf_tensor('%s_%d' % (name, uid[0]), list(shape), dt))

        ident = sb(top, 'ident', [128, 128]); identb = sb(top, 'identb', [128, 128], BF16)
        onesb = sb(top, 'onesb', [128, 128], BF16); onesf = sb(top, 'onesf', [128, 128])
        epsc = sb(top, 'epsc', [128, 1])
        psF = top.enter_context(nc.psum_tensor('psF', [128, 6 * 512], F32))
        psB = top.enter_context(nc.psum_tensor('psB', [128, 2 * 1024], BF16))

        def PS(i, n=512):
            return psF[:, i * 512:i * 512 + n]

        def PSB(i, n=128):
            return psB[:, i * 1024:i * 1024 + n]

        k.dma(ident[:], di['c_ident'], w=['ident'])
        k.op('dve', I('tensor_copy', identb[:], ident[:]), r=['ident'], w=['identb'])
        k.op('dve', I('memset', onesb[:], 1.0), w=['onesb'])
        k.op('dve', I('memset', onesf[:], 1.0), w=['onesf'])
        k.op('dve', I('memset', epsc[:], EPS), w=['epsc'])
        k.barrier()

        rrp = [0]

        psmod = [6]

        pslive = set()

        def nps():
            for _ in range(psmod[0]):
                rrp[0] = (rrp[0] + 1) % psmod[0]
                if rrp[0] not in pslive:
                    return rrp[0]
            raise RuntimeError('no free PSUM bank')

        def psalloc():
            p = nps()
            pslive.add(p)
            return p

        def psfree(p):
            pslive.discard(p)

        with ExitStack() as es:
            xin = [sb(es, 'xin%d' % i, [128, D]) for i in range(4)]
            xo = [sb(es, 'xo%d' % i, [128, 8, 128]) for i in range(4)]
            tiles = [(t * 128, 128, di['x_prompt'][t * 128:(t + 1) * 128, :]) for t in range(NCH)]
            tiles.append((T, NS, di['x_sample']))

            def p0_load(n):
                k.dma(xin[n % 4][:tiles[n][1], :], tiles[n][2], w=[('xin', n % 4)])

            p0_load(0)
            p0_load(1)
            for n, (t0, L, src) in enumerate(tiles):
                b = n % 4
                if n + 2 < len(tiles):
                    p0_load(n + 2)
                for c in range(8):
                    p = nps()
                    k.op('pe', I('transpose', PS(p, L), xin[b][:L, c * 128:(c + 1) * 128], ident[:L, :L]),
                         r=[('xin', b)], w=[('ps', p)])
                    k.op('act' if c % 2 else 'dve',
                         I('copy' if c % 2 else 'tensor_copy', xo[b][:, c, :L], PS(p, L)),
                         r=[('ps', p)], w=[('xo', b, c)])
                k.dma(xT.rearrange("(c p) t -> p c t", p=128)[:, :, t0:t0 + L], xo[b][:, :, :L],
                      r=[('xo', b, c) for c in range(8)])
            k.barrier()
            k.flush()

        def bcast_row(ap1d, n):
            return bass.AP(ap1d.tensor, ap1d.offset, [[0, 128], [1, n]])

        def norm_half(es, g_ap, t0, TW, hT, tag, dbuf=True):
            nb_ = 2 if dbuf else 1
            if '_norm' not in wst:
                gcol = sb(es, 'gcol' + tag, [128, 8])
                k.dma(gcol[:], g_ap.rearrange("(c p) -> p c", p=128), w=['gcol'], allow_slow_non_contiguous=True)
                xt = [sb(es, 'nx%s%d' % (tag, i), [128, 8, 512]) for i in range(nb_)]
                sq = [sb(es, 'nsq%s%d' % (tag, i), [128, 8, 512], BF16) for i in range(nb_)]
                rs = [sb(es, 'nrs%s%d' % (tag, i), [128, 512]) for i in range(nb_)]
                wst['_norm'] = (gcol, xt, sq, rs)
            gcol, xt, sq, rs = wst['_norm']
            ntl = [(s0, min(512, TW - s0)) for s0 in range(0, TW, 512)]

            def n_load(n):
                s0, W_ = ntl[n]
                b = n % nb_
                k.dma(xt[b][:, :, :W_], xT.rearrange("(c p) t -> p c t", p=128)[:, :, t0 + s0:t0 + s0 + W_],
                      w=[('nx', b)])

            n_load(0)
            for n, (s0, W_) in enumerate(ntl):
                b = n % nb_
                if nb_ == 2 and n + 1 < len(ntl):
                    n_load(n + 1)
                k.op('act', I('activation', sq[b][:, :, :W_], xt[b][:, :, :W_], AF.Square),
                     r=[('nx', b)], w=[('nsq', b)])
                p = nps()
                for c in range(8):
                    k.op('pe', I('matmul', PS(p, W_), onesb[:], sq[b][:, c, :W_], start=(c == 0), stop=(c == 7)),
                         r=[('nsq', b), 'onesb'], w=[('ps', p)])
                k.op('act', I('activation', rs[b][:, :W_], PS(p, W_), AF.Sqrt, bias=epsc[:], scale=1.0 / D),
                     r=[('ps', p), 'epsc'], w=[('nrs', b)])
                k.op('dve', I('reciprocal', rs[b][:, :W_], rs[b][:, :W_]), r=[('nrs', b)], w=[('nrs', b)])
                for c in range(8):
                    k.op('dve',
                         I('scalar_tensor_tensor', hT[:, c, s0:s0 + W_], xt[b][:, c, :W_], gcol[:, c:c + 1],
                           rs[b][:, :W_], ALU.mult, ALU.mult),
                         r=[('nx', b), ('nrs', b), 'gcol'], w=[('hT', c, s0 // 512)])
                if nb_ == 1 and n + 1 < len(ntl):
                    n_load(n + 1)

        fin_state = {}

        def norm_tile_f32(es, g_ap, tl_, n, hf):
            if 'g' not in fin_state:
                fin_state['g'] = sb(es, 'fgcol', [128, 8])
                k.dma(fin_state['g'][:], g_ap.rearrange("(c p) -> p c", p=128), w=['fgcol'], allow_slow_non_contiguous=True)
                fin_state['xt'] = [sb(es, 'fnx%d' % i, [128, 8, 512]) for i in range(2)]
                fin_state['sq'] = [sb(es, 'fnsq%d' % i, [128, 8, 512], BF16) for i in range(2)]
                fin_state['rs'] = [sb(es, 'fnrs%d' % i, [128, 512]) for i in range(2)]

            def f_load(m):
                t0_, Wm = tl_[m]
                k.dma(fin_state['xt'][m % 2][:, :, :Wm], xT.rearrange("(c p) t -> p c t", p=128)[:, :, t0_:t0_ + Wm],
                      w=[('fnx', m % 2)])

            if n == 0:
                f_load(0)
            if n + 1 < len(tl_):
                f_load(n + 1)
            b = n % 2
            t0, W_ = tl_[n]
            gcol, xt, sq, rs = fin_state['g'], fin_state['xt'][b], fin_state['sq'][b], fin_state['rs'][b]
            k.op('act', I('activation', sq[:, :, :W_], xt[:, :, :W_], AF.Square), r=[('fnx', b)], w=[('fnsq', b)])
            p = nps()
            for c in range(8):
                k.op('pe', I('matmul', PS(p, W_), onesb[:], sq[:, c, :W_], start=(c == 0), stop=(c == 7)),
                     r=[('fnsq', b), 'onesb'], w=[('ps', p)])
            k.op('act', I('activation', rs[:, :W_], PS(p, W_), AF.Sqrt, bias=epsc[:], scale=1.0 / D),
                 r=[('ps', p), 'epsc'], w=[('fnrs', b)])
            k.op('dve', I('reciprocal', rs[:, :W_], rs[:, :W_]), r=[('fnrs', b)], w=[('fnrs', b)])
            for c in range(8):
                k.op('dve',
                     I('scalar_tensor_tensor', hf[b][:, c, :W_], xt[:, c, :W_], gcol[:, c:c + 1],
                       rs[:, :W_], ALU.mult, ALU.mult),
                     r=[('fnx', b), ('fnrs', b), 'fgcol'], w=[('hf', b, c)])

        wst = {}

        def load_w(es, dst, dst_key, wsrc, kc, ncols, tag):
            parts = [(k0, min(8, kc - k0)) for k0 in range(0, kc, 8)]
            if len(parts) > 1:
                k.groups[dst_key] = [(dst_key, 'part', i) for i in range(len(parts))]
            for i, (k0, kw) in enumerate(parts):
                k.dma(dst[:, k0:k0 + kw, 0:ncols],
                      wsrc[k0 * 128:(k0 + kw) * 128, :].rearrange("(c p) n -> p c n", p=128),
                      w=[(dst_key, 'part', i) if len(parts) > 1 else dst_key], q='pool')

        def out_phase(layer, w_out_ap, KY):
            with ExitStack() as es:
                wst.clear()
                wo = sb(es, 'wo', [128, KY, D], BF16)
                wg = sb(es, 'wg', [128, 8, D], BF16)
                wp = sb(es, 'wp', [128, 2, D], BF16)
                load_w(es, wo, 'wo', w_out_ap, KY, D, 'o')
                load_w(es, wg, 'wg', di['pg_w'][layer], 8, D, 'o')
                load_w(es, wp, 'wp', di['pe_w'][layer], 2, D, 'o')
                yt = [sb(es, 'oy%d' % i, [128, KY, 512], BF16) for i in range(2)]
                xt = [sb(es, 'ox%d' % i, [128, 8, 512]) for i in range(2)]
                xb = [sb(es, 'oxb%d' % i, [128, 8, 512], BF16) for i in range(2)]
                pt = [sb(es, 'op%d' % i, [128, 4, 256]) for i in range(2)]
                pT = [sb(es, 'opT%d' % i, [128, 2, 512], BF16) for i in range(2)]
                gt = [sb(es, 'og%d' % i, [128, 512]) for i in range(2)]
                tl = [(s0, 512) for s0 in range(0, T, 512)] + [(T, NS)]
                def o_loads(n):
                    s0, W_ = tl[n]
                    b = n % 2
                    k.dma(yt[b][:, :, :W_], yT.rearrange("(c p) t -> p c t", p=128)[:, 0:KY, s0:s0 + W_],
                          w=[('oy', b)])
                    k.dma(xt[b][:, :, :W_], xT.rearrange("(c p) t -> p c t", p=128)[:, :, s0:s0 + W_],
                          w=[('ox', b)])
                    if W_ == 512:
                        k.dma(pt[b][:], di['p_prompt'][layer, s0:s0 + 512, :].rearrange("(j p) n -> p j n", p=128),
                              w=[('op', b)])
                    else:
                        k.dma(pt[b][:NS, 0, :], di['p_sample'][layer], w=[('op', b)])

                o_loads(0)
                for n, (s0, W_) in enumerate(tl):
                    b = n % 2
                    if n + 1 < len(tl):
                        o_loads(n + 1)
                    if W_ == 512:
                        subs = [(j, 128) for j in range(4)]
                    else:
                        subs = [(0, NS)]
                    for (j, L) in subs:
                        for c in range(2):
                            p = nps()
                            k.op('pe', I('transpose', PS(p, L), pt[b][:L, j, c * 128:(c + 1) * 128], ident[:L, :L]),
                                 r=[('op', b)], w=[('ps', p)])
                            k.op('act', I('copy', pT[b][:, c, j * 128:j * 128 + L], PS(p, L)),
                                 r=[('ps', p)], w=[('opT', b, j, c)])
                    pTr = [('opT', b, j, c) for (j, L) in subs for c in range(2)]
                    for dc in range(8):
                        p = nps()
                        for c in range(KY):
                            k.op('pe', I('matmul', PS(p, W_), wo[:, c, dc * 128:(dc + 1) * 128], yt[b][:, c, :W_],
                                         start=(c == 0), stop=(c == KY - 1)),
                                 r=['wo', ('oy', b)], w=[('ps', p)])
                        k.op('dve', I('tensor_tensor', xt[b][:, dc, :W_], xt[b][:, dc, :W_], PS(p, W_), ALU.add),
                             r=[('ps', p), ('ox', b)], w=[('ox', b, dc)])
                        k.op('act', I('copy', xb[b][:, dc, :W_], xt[b][:, dc, :W_]),
                             r=[('ox', b, dc)], w=[('oxb', b, dc)])
                    for dc in range(8):
                        p = nps()
                        for c in range(8):
                            k.op('pe', I('matmul', PS(p, W_), wg[:, c, dc * 128:(dc + 1) * 128], xb[b][:, c, :W_],
                                         start=(c == 0), stop=(c == 7)),
                                 r=['wg'] + [('oxb', b, cc) for cc in range(8)], w=[('ps', p)])
                        k.op('act', I('activation', gt[b][:, :W_], PS(p, W_), AF.Sigmoid),
                             r=[('ps', p)], w=[('og', b)])
                        p2 = nps()
                        for c in range(2):
                            k.op('pe', I('matmul', PS(p2, W_), wp[:, c, dc * 128:(dc + 1) * 128], pT[b][:, c, :W_],
                                         start=(c == 0), stop=(c == 1)),
                                 r=['wp'] + pTr, w=[('ps', p2)])
                        k.op('dve', I('tensor_tensor', gt[b][:, :W_], gt[b][:, :W_], PS(p2, W_), ALU.mult),
                             r=[('ps', p2), ('og', b)], w=[('og', b)])
                        k.op('pool', I('tensor_tensor', xt[b][:, dc, :W_], xt[b][:, dc, :W_], gt[b][:, :W_], ALU.add),
                             r=[('og', b), ('ox', b, dc), ('oxb', b, dc)], w=[('ox', b, dc)])
                    k.dma(xT.rearrange("(c p) t -> p c t", p=128)[:, :, s0:s0 + W_], xt[b][:, :, :W_],
                          r=[('ox', b, dc) for dc in range(8)] + [('ox', b)])
                k.barrier()
                k.flush()


        xTv = xT.rearrange("(c p) t -> p c t", p=128)
        yTv = yT.rearrange("(c p) t -> p c t", p=128)
        HALVES = [(0, T // 2), (T // 2, T // 2 + NS)]

        def hkeys(s0):
            return [('hT', c, s0 // 512) for c in range(8)]

        def gemm_fm(p, wt, wkey, col0, hT, s0, W_, kc=8, M=128):
            for c in range(kc):
                k.op('pe', I('matmul', PS(p, W_)[:M, :], wt[:, c, col0:col0 + M], hT[:, c, s0:s0 + W_],
                             start=(c == 0), stop=(c == kc - 1)),
                     r=[wkey] + hkeys(s0), w=[('ps', p)])

        def rows_to_fm(es, src, R, ncols, dst, dkey, tag):
            if '_rowtmp' not in wst:
                wst['_rowtmp'] = sb(es, 'rowtmp' + tag, [64, 2048])
            tmp = wst['_rowtmp']
            k.dma(tmp[:R, :ncols], src, w=['rowtmp'])
            for c in range(ncols // 128):
                p = nps()
                k.op('pe', I('transpose', PS(p, R), tmp[:R, c * 128:(c + 1) * 128], ident[:R, :R]),
                     r=['rowtmp'], w=[('ps', p)])
                k.op('dve', I('tensor_copy', dst[:, c, 0:R], PS(p, R)), r=[('ps', p)], w=[dkey])

        def fm_to_rows(es, srcs, skeys, R, dst, tag):
            if '_rowtmp' not in wst:
                wst['_rowtmp'] = sb(es, 'rowtmp' + tag, [64, 2048])
            tmp = wst['_rowtmp']
            for c, a in enumerate(srcs):
                p = nps()
                k.op('pe', I('transpose', PS(p, 128)[:R, :], a, ident[:]), r=skeys, w=[('ps', p)])
                k.op('dve', I('tensor_copy', tmp[:R, c * 128:(c + 1) * 128], PS(p, 128)[:R, :]),
                     r=[('ps', p)], w=['rowtmp'])
            k.dma(dst, tmp[:R, :len(srcs) * 128], r=['rowtmp'])

        def conv_layer(layer=1):
            w_in = di['b_w_in']
            with ExitStack() as es:
                wst.clear()
                wc = sb(es, 'cv_wc', [128, 16, 3])
                rows_to_fm(es, di['b_conv_w'], 3, 2048, wc, 'cv_wc', 'cw')
                stT = sb(es, 'cv_st', [128, 16, NS * 2])
                rows_to_fm(es, di['st_conv'].rearrange("b j n -> (b j) n"), NS * 2, 2048, stT, 'cv_st', 'cs')
                halo = sb(es, 'cv_halo', [128, 16, 2])
                k.op('dve', I('memset', halo[:], 0.0), w=['cv_halo'])
                so = sb(es, 'cv_so', [128, 16, NS * 2])
                wf = [sb(es, 'cv_wf%d' % i, [128, 8, 512], BF16) for i in range(2)]
                cx = sb(es, 'cv_cx', [128, 2 + T // 2 + NS])
                yo = [sb(es, 'cv_yo%d' % i, [128, T // 2 + NS], BF16) for i in range(2)]
                cgs = [sb(es, 'cv_cg%d' % i, [128, 512]) for i in range(2)]
                zs = [sb(es, 'cv_zs%d' % i, [128, 512]) for i in range(2)]
                acc = [sb(es, 'cv_acc%d' % i, [128, 512]) for i in range(2)]
                ys = sb(es, 'cv_ys', [128, NS])
                hT = sb(es, 'cv_hT', [128, 8, T // 2 + NS], BF16)
                nw = 0

                def cv_load(n_, it):
                    for q_ in range(4):
                        load_w(es, wf[n_ % 2][:, :, q_ * 128:(q_ + 1) * 128], ('cv_wf', n_ % 2, q_),
                               w_in[:, q_ * 2048 + it[1] * 128:q_ * 2048 + (it[1] + 1) * 128], 8, 128, 'cv')

                cvpre = Pre([(hi_, f_) for hi_ in range(2) for f_ in range(16)], cv_load, 1)
                for hi, (t0, TW) in enumerate(HALVES):
                    norm_half(es, di['norm_g'][layer], t0, TW, hT, 'cv%d' % hi)
                    TP = T // 2
                    for f in range(16):
                        wb = nw % 2
                        cvpre.need(nw)
                        nw += 1
                        wkeys = [('cv_wf', wb, q_) for q_ in range(4)]
                        k.op('act', I('copy', cx[:, 0:2], halo[:, f, :]), r=['cv_halo'], w=[('cv_cx', 'h')])
                        tl = [(s0, 512) for s0 in range(0, TP, 512)]
                        if TW > TP:
                            tl.append((TP, NS))
                        for n, (s0, W_) in enumerate(tl):
                            b = n % 2
                            pc, px, pz, pb = nps(), nps(), nps(), nps()
                            for q_, p in ((1, pc), (2, px), (3, pz), (0, pb)):
                                for c in range(8):
                                    k.op('pe', I('matmul', PS(p, W_), wf[wb][:, c, q_ * 128:(q_ + 1) * 128],
                                                 hT[:, c, s0:s0 + W_], start=(c == 0), stop=(c == 7)),
                                         r=[('cv_wf', wb, q_)] + hkeys(s0), w=[('ps', p)])
                            k.op('act', I('copy', cgs[b][:, :W_], PS(pc, W_)), r=[('ps', pc)], w=[('cv_cg', b)])
                            k.op('dve', I('tensor_tensor', cx[:, 2 + s0:2 + s0 + W_], cgs[b][:, :W_], PS(px, W_), ALU.mult),
                                 r=[('cv_cg', b), ('ps', px)], w=[('cv_cx', n)])
                            k.op('act', I('activation', zs[b][:, :W_], PS(pz, W_), AF.Silu), r=[('ps', pz)], w=[('cv_zs', b)])
                            if W_ == 512:
                                rk = [('cv_cx', n), ('cv_cx', n - 1) if n > 0 else ('cv_cx', 'h')]
                                k.op('act', I('mul', acc[b][:, :W_], cx[:, s0:s0 + W_], wc[:, f, 0:1]),
                                     r=rk + ['cv_wc'], w=[('cv_acc', b)])
                                k.op('dve', I('scalar_tensor_tensor', acc[b][:, :W_], cx[:, 1 + s0:1 + s0 + W_], wc[:, f, 1:2],
                                               acc[b][:, :W_], ALU.mult, ALU.add), r=rk + [('cv_acc', b)], w=[('cv_acc', b)])
                                k.op('dve', I('scalar_tensor_tensor', acc[b][:, :W_], cx[:, 2 + s0:2 + s0 + W_], wc[:, f, 2:3],
                                               acc[b][:, :W_], ALU.mult, ALU.add), r=rk + [('cv_acc', b)], w=[('cv_acc', b)])
                                accv = acc[b][:, :W_]
                                ak = ('cv_acc', b)
                            else:
                                stv = stT[:, f, :].rearrange("p (b j) -> p b j", j=2)
                                k.op('act', I('mul', ys[:, :], stv[:, :, 0], wc[:, f, 0:1]),
                                     r=['cv_st', 'cv_wc'], w=['cv_ys'])
                                k.op('dve', I('scalar_tensor_tensor', ys[:, :], stv[:, :, 1], wc[:, f, 1:2], ys[:, :],
                                               ALU.mult, ALU.add), r=['cv_st', 'cv_ys'], w=['cv_ys'])
                                k.op('dve', I('scalar_tensor_tensor', ys[:, :], cx[:, 2 + s0:2 + s0 + W_], wc[:, f, 2:3], ys[:, :],
                                               ALU.mult, ALU.add), r=[('cv_cx', n), 'cv_ys'], w=['cv_ys'])
                                sov = so[:, f, :].rearrange("p (b j) -> p b j", j=2)
                                k.op('act', I('copy', sov[:, :, 0], stv[:, :, 1]), r=['cv_st'], w=[('cv_so', f, 0)])
                                k.op('act', I('copy', sov[:, :, 1], cx[:, 2 + s0:2 + s0 + W_]), r=[('cv_cx', n)], w=[('cv_so', f, 1)])
                                accv = ys[:, :]
                                ak = 'cv_ys'
                            k.op('dve', I('tensor_tensor', zs[b][:, :W_], zs[b][:, :W_], accv, ALU.mult),
                                 r=[ak, ('cv_zs', b)], w=[('cv_zs', b)])
                            k.op('dve', I('tensor_tensor', yo[wb][:, s0:s0 + W_], zs[b][:, :W_], PS(pb, W_), ALU.mult),
                                 r=[('cv_zs', b), ('ps', pb)], w=[('cv_yo', wb, n)])
                        k.op('act', I('copy', halo[:, f, :], cx[:, TP:TP + 2]),
                             r=[('cv_cx', len(tl) - 1 - (1 if TW > TP else 0))], w=['cv_halo'])
                        k.dma(yTv[:, f, t0:t0 + TW], yo[wb][:, :TW], r=[('cv_yo', wb, n) for n in range(len(tl))])
                k.barrier()
                fm_to_rows(es, [halo[:, f, :] for f in range(16)], ['cv_halo'], 2, di['o_convp'], 'cp')
                fm_to_rows(es, [so[:, f, :] for f in range(16)], [('cv_so', f, j) for f in range(16) for j in range(2)],
                           NS * 2, di['o_convs'].rearrange("b j n -> (b j) n"), 'cq')
                k.barrier()
                k.flush()
            out_phase(layer, di['b_w_out'], 16)


        def pool_layer(layer=3):
            w_in = di['d_w_in']
            TP = T // 2
            with ExitStack() as es:
                wst.clear()
                scT = sb(es, 'pl_sc', [128, 16, 1])
                rows_to_fm(es, di['d_scale'].rearrange("(o n) -> o n", o=1), 1, 2048, scT, 'pl_sc', 'ps')
                stT = sb(es, 'pl_st', [128, 16, NS * 15])
                rows_to_fm(es, di['st_pool'].rearrange("b j n -> (b j) n"), NS * 15, 2048, stT, 'pl_st', 'pt')
                invc = sb(es, 'pl_invc', [128, 256])
                k.dma(invc[:], di['c_invcnt'], w=['pl_invc'])
                halo = sb(es, 'pl_halo', [128, 16, 15])
                k.op('dve', I('memset', halo[:], 0.0), w=['pl_halo'])
                so = sb(es, 'pl_so', [128, 16, NS * 15])
                wf = [sb(es, 'pl_wf%d' % i, [128, 8, 256], BF16) for i in range(2)]
                wg = sb(es, 'pl_wg', [128, 4, 512], BF16)
                xpb = sb(es, 'pl_xp', [128, 15 + TP + NS])
                sA = sb(es, 'pl_sA', [128, 15 + TP])
                sB = sb(es, 'pl_sB', [128, 15 + TP])
                rb = sb(es, 'pl_rb', [128, 4, TP + NS], BF16)
                zsb = sb(es, 'pl_zs', [128, 4, TP + NS], BF16)
                yo = [sb(es, 'pl_yo%d' % i, [128, TP + NS], BF16) for i in range(2)]
                t16 = sb(es, 'pl_t16', [128, 16])
                red = sb(es, 'pl_red', [128, NS])
                ytmp = [sb(es, 'pl_yt%d' % i, [128, 512]) for i in range(2)]
                hT = sb(es, 'pl_hT', [128, 8, TP + NS], BF16)
                nw = 0
                ny = 0

                def pl_load(n_, it):
                    for q_ in range(2):
                        load_w(es, wf[n_ % 2][:, :, q_ * 128:(q_ + 1) * 128], ('pl_wf', n_ % 2, q_),
                               w_in[:, q_ * 2048 + it * 128:q_ * 2048 + (it + 1) * 128], 8, 128, 'pl')

                plpre = Pre([f_ for hi_ in range(2) for f_ in range(16)], pl_load, 1)
                for hi, (t0, TW) in enumerate(HALVES):
                    norm_half(es, di['norm_g'][layer], t0, TW, hT, 'pl%d' % hi)
                    tl = [(s0, 512) for s0 in range(0, TP, 512)]
                    if TW > TP:
                        tl.append((TP, NS))
                    for g in range(4):
                        wdw = (2, 4, 8, 16)[g]
                        load_w(es, wg, 'pl_wg', di['d_w_grp'][g], 4, 512, 'pl')
                        for fi in range(4):
                            f = 4 * g + fi
                            wb = nw % 2
                            plpre.need(nw)
                            nw += 1
                            k.op('act', I('copy', xpb[:, 0:15], halo[:, f, :]), r=['pl_halo'], w=['pl_xp'])
                            for n, (s0, W_) in enumerate(tl):
                                px, pz = nps(), nps()
                                for q_, p in enumerate((px, pz)):
                                    for c in range(8):
                                        k.op('pe', I('matmul', PS(p, W_), wf[wb][:, c, q_ * 128:(q_ + 1) * 128],
                                                     hT[:, c, s0:s0 + W_], start=(c == 0), stop=(c == 7)),
                                             r=[('pl_wf', wb, q_)] + hkeys(s0), w=[('ps', p)])
                                k.op('act', I('copy', xpb[:, 15 + s0:15 + s0 + W_], PS(px, W_)), r=[('ps', px)], w=['pl_xp'])
                                k.op('act', I('activation', zsb[:, fi, s0:s0 + W_], PS(pz, W_), AF.Silu),
                                     r=[('ps', pz)], w=[('pl_zs', fi)])
                            cur, ck = xpb, 'pl_xp'
                            step, lo = 1, 0
                            pp = [(sA, 'pl_sA'), (sB, 'pl_sB')]
                            ip = 0
                            while step < wdw:
                                nxt, nk = pp[ip % 2]
                                ip += 1
                                lo2 = lo + step
                                k.op('dve', I('tensor_tensor', nxt[:, lo2:15 + TP], cur[:, lo2:15 + TP],
                                              cur[:, lo2 - step:15 + TP - step], ALU.add), r=[ck], w=[nk])
                                cur, ck, lo, step = nxt, nk, lo2, step * 2
                            k.op('dve', I('scalar_tensor_tensor', rb[:, fi, 0:TP], cur[:, 15:15 + TP], 1.0 / wdw,
                                          xpb[:, 15:15 + TP], ALU.mult, ALU.subtract), r=[ck, 'pl_xp'], w=[('pl_rb', fi)])
                            if hi == 0:
                                k.op('dve', I('tensor_tensor', t16[:], cur[:, 15:31], invc[:, f * 16:(f + 1) * 16], ALU.mult),
                                     r=[ck, 'pl_invc'], w=['pl_t16'])
                                k.op('dve', I('tensor_tensor', rb[:, fi, 0:16], t16[:], xpb[:, 15:31], ALU.subtract),
                                     r=['pl_t16', 'pl_xp'], w=[('pl_rb', fi)])
                            if TW > TP:
                                stv = stT[:, f, :].rearrange("p (b j) -> p b j", j=15)
                                xs_ = xpb[:, 15 + TP:15 + TP + NS]
                                k.op('dve', I('tensor_reduce', red[:], stv[:, :, 15 - (wdw - 1):15], AX.X, ALU.add),
                                     r=['pl_st'], w=['pl_red'])
                                k.op('dve', I('tensor_tensor', red[:], red[:], xs_, ALU.add), r=['pl_red', 'pl_xp'], w=['pl_red'])
                                k.op('dve', I('scalar_tensor_tensor', rb[:, fi, TP:TP + NS], red[:], 1.0 / wdw, xs_,
                                              ALU.mult, ALU.subtract), r=['pl_red', 'pl_xp'], w=[('pl_rb', fi)])
                                sov = so[:, f, :].rearrange("p (b j) -> p b j", j=15)
                                k.op('act', I('copy', sov[:, :, 0:14], stv[:, :, 1:15]), r=['pl_st'], w=[('pl_so', f, 0)])
                                k.op('act', I('copy', sov[:, :, 14], xs_), r=['pl_xp'], w=[('pl_so', f, 1)])
                            k.op('act', I('copy', halo[:, f, :], xpb[:, TP:TP + 15]), r=['pl_xp'], w=['pl_halo'])
                        for fo in range(4):
                            f = 4 * g + fo
                            yb = ny % 2
                            ny += 1
                            for n, (s0, W_) in enumerate(tl):
                                p = nps()
                                for c in range(4):
                                    k.op('pe', I('matmul', PS(p, W_), wg[:, c, fo * 128:(fo + 1) * 128], rb[:, c, s0:s0 + W_],
                                                 start=(c == 0), stop=(c == 3)),
                                         r=['pl_wg'] + [('pl_rb', c) for c in range(4)], w=[('ps', p)])
                                b = n % 2
                                k.op('act', I('mul', ytmp[b][:, :W_], PS(p, W_), scT[:, f, 0:1]), r=[('ps', p), 'pl_sc'], w=[('pl_yt', b)])
                                k.op('dve', I('tensor_tensor', yo[yb][:, s0:s0 + W_], ytmp[b][:, :W_], zsb[:, fo, s0:s0 + W_], ALU.mult),
                                     r=[('pl_yt', b), ('pl_zs', fo)], w=[('pl_yo', yb, n)])
                            k.dma(yTv[:, f, t0:t0 + TW], yo[yb][:, :TW], r=[('pl_yo', yb, n) for n in range(len(tl))])
                k.barrier()
                fm_to_rows(es, [halo[:, f, :] for f in range(16)], ['pl_halo'], 15, di['o_poolp'], 'pp')
                fm_to_rows(es, [so[:, f, :] for f in range(16)], [('pl_so', f, j) for f in range(16) for j in range(2)],
                           NS * 15, di['o_pools'].rearrange("b j n -> (b j) n"), 'pq')
                k.barrier()
                k.flush()
            out_phase(layer, di['d_w_out'], 16)


        def mlstm_layer(layer=0):
            w_in = di['a_w_in']
            TP = T // 2
            NU = TP // 128 + NS
            with ExitStack() as es:
                wst.clear()
                triu = sb(es, 'ml_triu', [128, 128]); k.dma(triu[:], di['c_triu'], w=['ml_triu'])
                mneg = sb(es, 'ml_mneg', [128, 128]); k.dma(mneg[:], di['c_maskneg'], w=['ml_mneg'])
                sell = sb(es, 'ml_sell', [128, 128]); k.dma(sell[:], di['c_sel_last'], w=['ml_sell'])
                bif = sb(es, 'ml_bif', [128, 16]); k.dma(bif[:], bcast_row(di['a_b_if'], 16), w=['ml_bif'])
                ng = sb(es, 'ml_ng', [128, 2048]); k.dma(ng[:], bcast_row(di['a_norm_g'], 2048), w=['ml_ng'])
                triub = sb(es, 'ml_triub', [128, 128], BF16); sellb = sb(es, 'ml_sellb', [128, 128], BF16)
                k.op('dve', I('tensor_copy', triub[:], triu[:]), r=['ml_triu'], w=['ml_triu'])
                k.op('dve', I('tensor_copy', sellb[:], sell[:]), r=['ml_sell'], w=['ml_sell'])
                hlA = sb(es, 'ml_hlA', [128, 128], BF16); hlB = sb(es, 'ml_hlB', [128, 128], BF16)

                def mm_hl(p, pv, lhsT, src, L, n, rkeys):
                    k.op('dve', I('tensor_copy', hlA[:L, :n], src), r=rkeys, w=['ml_hlA'])
                    k.op('dve', I('tensor_tensor', hlB[:L, :n], src, hlA[:L, :n], ALU.subtract), r=rkeys + ['ml_hlA'], w=['ml_hlB'])
                    k.op('pe', I('matmul', pv, lhsT, hlA[:L, :n], start=True, stop=False), r=['ml_hlA', 'ml_triu', 'ml_sell'], w=[('ps', p)])
                    k.op('pe', I('matmul', pv, lhsT, hlB[:L, :n], start=False, stop=True), r=['ml_hlB'], w=[('ps', p)])

                wgate = sb(es, 'ml_wgate', [128, 8, 16], BF16)
                load_w(es, wgate, 'ml_wgate', w_in[:, 8192:8208], 8, 16, 'ml')
                CT = sb(es, 'ml_CT', [128, 8, 257])
                CTb = sb(es, 'ml_CTb', [128, 257], BF16)
                mprev = sb(es, 'ml_mprev', [128, 8])
                k.op('dve', I('memset', CT[:], 0.0), w=[('ml_CT', h) for h in range(8)])
                k.op('dve', I('memset', mprev[:], 0.0), w=['ml_mprev'])
                col3 = sb(es, 'ml_col3', [128, NU, 24])
                ccol = sb(es, 'ml_ccol', [128, NU, 8])
                negm = sb(es, 'ml_negm', [128, NU, 8])
                expnegm = sb(es, 'ml_enm', [128, NU, 8])
                wcol = sb(es, 'ml_wcol', [128, NU, 8])
                bcs = sb(es, 'ml_bcs', [128, NU, 24])
                gs = sb(es, 'ml_gs', [128, 16]); lp = sb(es, 'ml_lp', [128, 8]); mxa = sb(es, 'ml_mx', [128, 8])
                inter = sb(es, 'ml_inter', [128, 8]); tmp8 = sb(es, 'ml_tmp8', [128, 8])
                from types import SimpleNamespace
                BS = []
                NBU = 4
                for par in range(NBU):
                    B = SimpleNamespace()
                    B.par = par
                    B.diagc = sb(es, 'ml_diagc', [128, 128]); B.logd = sb(es, 'ml_logd', [128, 128]); B.Dm = sb(es, 'ml_Dm', [128, 128])
                    B.Pm = sb(es, 'ml_P', [128, 128], BF16); B.PTs = sb(es, 'ml_PT', [128, 128], BF16)
                    B.ktok = sb(es, 'ml_ktok', [128, 128], BF16)
                    B.v1 = sb(es, 'ml_v1', [128, 257], BF16); B.wv = sb(es, 'ml_wv', [128, 257], BF16)
                    k.op('dve', I('memset', B.v1[:, 256:257], 1.0), w=[('ml_v1', par)])
                    B.og = sb(es, 'ml_og', [128, 256]); B.ez = sb(es, 'ml_ez', [128, 512]); B.zs = sb(es, 'ml_zs', [128, 256])
                    B.intra = sb(es, 'ml_intra', [128, 257]); B.nd = sb(es, 'ml_nd', [128, 257])
                    B.den = sb(es, 'ml_den', [128, 1]); B.ssq = sb(es, 'ml_ssq', [128, 1]); B.junk = BS[0].junk if BS else sb(es, 'ml_junk', [128, 256])
                    B.hs = sb(es, 'ml_hs', [128, 256]); B.yb = sb(es, 'ml_yb', [128, 256], BF16)
                    B.hlA = sb(es, 'ml_hlA2', [128, 128], BF16); B.hlB = sb(es, 'ml_hlB2', [128, 128], BF16)
                    BS.append(B)
                diagc = BS[0].diagc; logd = BS[0].logd
                qTb = sb(es, 'ml_qTb', [128, 512], BF16); kTb = sb(es, 'ml_kTb', [128, 512], BF16)
                yTs = sb(es, 'ml_yTs', [128, 2, TP + NS], BF16)
                ctmp = sb(es, 'ml_ctmp', [128, 2, 128])
                CTs = sb(es, 'ml_CTs', [128, NS, 257]); CTbs = [sb(es, 'ml_CTbs', [128, 257], BF16) for _ in range(NS)]
                ctmps = [sb(es, 'ml_ctmps', [128, 2, 128]) for _ in range(NS)]
                wh = [sb(es, 'ml_wh%d' % i, [128, 8, 1024], BF16) for i in range(2)]
                hT = sb(es, 'ml_hT', [128, 8, TP + NS], BF16)
                SC = 128.0 ** -0.5

                def logd_unit(u, h, L, B=None):
                    dg, ld, kd, kl = (diagc, logd, 'ml_diagc', 'ml_logd') if B is None else (B.diagc, B.logd, ('ml_diagc', B.par), ('ml_logd', B.par))
                    k.op('dve', I('tensor_scalar', dg[:L, :L], ident[:L, :L], ccol[:L, u, h:h + 1], None, ALU.mult),
                         r=[('col', u)], w=[kd])
                    p = nps()
                    if B is None:
                        mm_hl(p, PS(p, L)[:L, :], onesb[:L, :L], dg[:L, :L], L, L, [kd])
                    else:
                        k.op('dve', I('tensor_copy', B.hlA[:L, :L], dg[:L, :L]), r=[kd], w=[('ml_hlA', B.par)])
                        k.op('dve', I('tensor_tensor', B.hlB[:L, :L], dg[:L, :L], B.hlA[:L, :L], ALU.subtract), r=[kd, ('ml_hlA', B.par)], w=[('ml_hlB', B.par)])
                        k.op('pe', I('matmul', PS(p, L)[:L, :], onesb[:L, :L], B.hlA[:L, :L], start=True, stop=False), r=[('ml_hlA', B.par)], w=[('ps', p)])
                        k.op('pe', I('matmul', PS(p, L)[:L, :], onesb[:L, :L], B.hlB[:L, :L], start=False, stop=True), r=[('ml_hlB', B.par)], w=[('ps', p)])
                    k.op('dve', I('scalar_tensor_tensor', ld[:L, :L], PS(p, L)[:L, :], col3[:L, u, h:h + 1], mneg[:L, :L],
                                  ALU.add, ALU.add), r=[('ps', p), ('col', u), 'ml_mneg'], w=[kl])

                def gate_unit(u, c0, L):
                    p = nps()
                    for c in range(8):
                        k.op('pe', I('matmul', PS(p, 16)[:L, :], hT[:, c, c0:c0 + L], wgate[:, c, :], start=(c == 0), stop=(c == 7)),
                             r=['ml_wgate'] + hkeys(c0), w=[('ps', p)])
                    k.op('dve', I('tensor_tensor', gs[:L, :], PS(p, 16)[:L, :], bif[:L, :], ALU.add), r=[('ps', p), 'ml_bif'], w=['ml_gs'])
                    k.op('act', I('activation', lp[:L, :], gs[:L, 8:16], AF.Exp, scale=-1.0), r=['ml_gs'], w=['ml_lp'])
                    k.op('act', I('activation', lp[:L, :], lp[:L, :], AF.Ln, bias=onesf[:L, 0:1]), r=['ml_lp'], w=['ml_lp'])
                    p2 = nps()
                    mm_hl(p2, PS(p2, 8)[:L, :], triub[:L, :L], lp[:L, :], L, 8, ['ml_lp'])
                    k.op('act', I('mul', col3[:L, u, 0:8], PS(p2, 8)[:L, :], -1.0), r=[('ps', p2)], w=[('col', u)])
                    k.op('dve', I('tensor_tensor', ccol[:L, u, :], gs[:L, 0:8], PS(p2, 8)[:L, :], ALU.add),
                         r=[('ps', p2), 'ml_gs'], w=[('col', u)])
                    def gate_head(h, B):
                        P_ = B.par
                        K_ = lambda nm: (nm, P_)
                        k.op('act', I('mul', B.diagc[:L, :L], ident[:L, :L], ccol[:L, u, h:h + 1]), r=[('col', u)], w=[K_('ml_diagc')])
                        yield
                        k.op('act', I('copy', B.hlA[:L, :L], B.diagc[:L, :L]), r=[K_('ml_diagc')], w=[K_('ml_hlA')])
                        yield
                        k.op('pool', I('tensor_tensor', B.hlB[:L, :L], B.diagc[:L, :L], B.hlA[:L, :L], ALU.subtract), r=[K_('ml_diagc'), K_('ml_hlA')], w=[K_('ml_hlB')])
                        yield
                        p = psalloc()
                        k.op('pe', I('matmul', PS(p, L)[:L, :], onesb[:L, :L], B.hlA[:L, :L], start=True, stop=False), r=[K_('ml_hlA')], w=[('ps', p)])
                        k.op('pe', I('matmul', PS(p, L)[:L, :], onesb[:L, :L], B.hlB[:L, :L], start=False, stop=True), r=[K_('ml_hlB')], w=[('ps', p)])
                        yield
                        k.op('dve', I('scalar_tensor_tensor', B.logd[:L, :L], PS(p, L)[:L, :], col3[:L, u, h:h + 1], mneg[:L, :L],
                                      ALU.add, ALU.add), r=[('ps', p), ('col', u), 'ml_mneg'], w=[K_('ml_logd')])
                        psfree(p)
                        yield
                        k.op('dve', I('tensor_reduce', mxa[:L, h:h + 1], B.logd[:L, :L], AX.X, ALU.max), r=[K_('ml_logd')], w=[('ml_mx', h)])

                    for h0 in range(0, 8, NBU):
                        lockstep([gate_head(h, BS[h - h0]) for h in range(h0, min(h0 + NBU, 8))])
                    k.op('dve', I('tensor_tensor', inter[:L, :], col3[:L, u, 0:8], mprev[:L, :], ALU.add),
                         r=[('col', u), 'ml_mprev'], w=['ml_inter'])
                    k.op('dve', I('tensor_tensor', col3[:L, u, 8:16], inter[:L, :], mxa[:L, :], ALU.max),
                         r=['ml_inter'] + [('ml_mx', hh) for hh in range(8)], w=[('col', u)])
                    k.op('dve', I('tensor_tensor', tmp8[:L, :], inter[:L, :], col3[:L, u, 8:16], ALU.subtract),
                         r=['ml_inter', ('col', u)], w=['ml_tmp8'])
                    k.op('act', I('activation', col3[:L, u, 16:24], tmp8[:L, :], AF.Exp), r=['ml_tmp8'], w=[('col', u)])
                    k.op('act', I('mul', negm[:L, u, :], col3[:L, u, 8:16], -1.0), r=[('col', u)], w=[('col', u)])
                    k.op('act', I('activation', expnegm[:L, u, :], col3[:L, u, 8:16], AF.Exp, scale=-1.0), r=[('col', u)], w=[('col', u)])
                    p3 = nps()
                    lsel = sellb[:, :] if L == 128 else onesb[0:1, :]
                    mm_hl(p3, PS(p3, 24), lsel, col3[:L, u, :], L, 24, [('col', u)])
                    k.op('act', I('copy', bcs[:, u, :], PS(p3, 24)), r=[('ps', p3)], w=[('bcs', u)])
                    k.op('dve', I('tensor_tensor', tmp8[:L, :], ccol[:L, u, :], bcs[:L, u, 0:8], ALU.add),
                         r=[('col', u), ('bcs', u)], w=['ml_tmp8'])
                    k.op('dve', I('tensor_tensor', tmp8[:L, :], tmp8[:L, :], bcs[:L, u, 8:16], ALU.subtract),
                         r=['ml_tmp8', ('bcs', u)], w=['ml_tmp8'])
                    k.op('act', I('activation', wcol[:L, u, :], tmp8[:L, :], AF.Exp), r=['ml_tmp8'], w=[('col', u)])
                    k.op('dve', I('tensor_copy', mprev[:, :], bcs[:, u, 8:16]), r=[('bcs', u)], w=['ml_mprev'])

                def stage_a(h, u, c0, L, w_, wk, B):
                    P_ = B.par
                    K_ = lambda nm: (nm, P_)
                    dg, ld = B.diagc, B.logd
                    p1 = psalloc()
                    for c in range(8):
                        k.op('pe', I('matmul', PS(p1, 384)[:L, :], hT[:, c, c0:c0 + L], w_[:, c, 128:512], start=(c == 0), stop=(c == 7)),
                             r=[wk] + hkeys(c0), w=[('ps', p1)])
                    k.op('act', I('mul', dg[:L, :L], ident[:L, :L], ccol[:L, u, h:h + 1]), r=[('col', u)], w=[K_('ml_diagc')])
                    yield
                    k.op('act', I('copy', B.hlA[:L, :L], dg[:L, :L]), r=[K_('ml_diagc')], w=[K_('ml_hlA')])
                    k.op('act', I('mul', B.ktok[:L, :], PS(p1, 384)[:L, 0:128], SC), r=[('ps', p1)], w=[K_('ml_ktok')])
                    k.op('act', I('copy', B.v1[:L, 0:256], PS(p1, 384)[:L, 128:384]), r=[('ps', p1)], w=[K_('ml_v1')])
                    psfree(p1)
                    p2 = psalloc()
                    for c in range(8):
                        k.op('pe', I('matmul', PS(p2, 512)[:L, :], hT[:, c, c0:c0 + L], w_[:, c, 512:1024], start=(c == 0), stop=(c == 7)),
                             r=[wk] + hkeys(c0), w=[('ps', p2)])
                    yield
                    k.op('pool', I('tensor_tensor', B.hlB[:L, :L], dg[:L, :L], B.hlA[:L, :L], ALU.subtract), r=[K_('ml_diagc'), K_('ml_hlA')], w=[K_('ml_hlB')])
                    k.op('act', I('activation', B.ez[:L, :], PS(p2, 512)[:L, :], AF.Exp, scale=-1.0), r=[('ps', p2)], w=[K_('ml_ez')])
                    k.op('act', I('copy', B.zs[:L, :], PS(p2, 512)[:L, 256:512]), r=[('ps', p2)], w=[K_('ml_zs')])
                    psfree(p2)
                    k.op('dve', I('tensor_scalar', B.wv[:L, :], B.v1[:L, :], wcol[:L, u, h:h + 1], None, ALU.mult), r=[K_('ml_v1'), ('col', u)], w=[K_('ml_wv')])
                    yield
                    p = psalloc()
                    k.op('pe', I('matmul', PS(p, L)[:L, :], onesb[:L, :L], B.hlA[:L, :L], start=True, stop=False), r=[K_('ml_hlA')], w=[('ps', p)])
                    k.op('pe', I('matmul', PS(p, L)[:L, :], onesb[:L, :L], B.hlB[:L, :L], start=False, stop=True), r=[K_('ml_hlB')], w=[('ps', p)])
                    k.op('dve', I('tensor_scalar', B.ez[:L, :], B.ez[:L, :], 1.0, None, ALU.add), r=[K_('ml_ez')], w=[K_('ml_ez')])
                    yield
                    k.op('dve', I('scalar_tensor_tensor', ld[:L, :L], PS(p, L)[:L, :], col3[:L, u, h:h + 1], mneg[:L, :L],
                                  ALU.add, ALU.add), r=[('ps', p), ('col', u), 'ml_mneg'], w=[K_('ml_logd')])
                    psfree(p)
                    k.op('pool', I('tensor_tensor', B.ez[:L, 0:256], B.ez[:L, 0:256], B.ez[:L, 256:512], ALU.mult), r=[K_('ml_ez')], w=[K_('ml_ez')])
                    yield
                    k.op('act', I('activation', B.Dm[:L, :L], ld[:L, :L], AF.Exp, bias=negm[:L, u, h:h + 1]),
                         r=[K_('ml_logd'), ('col', u)], w=[K_('ml_Dm')])
                    ps_ = psalloc()
                    k.op('pe', I('matmul', PS(ps_, L)[:L, :], B.qT[:, :L], B.kT[:, :L], start=True, stop=True),
                         r=[('ml_qTb', B.qk), ('ml_kTb', B.qk)], w=[('ps', ps_)])
                    k.op('dve', I('reciprocal', B.ez[:L, 0:256], B.ez[:L, 0:256]), r=[K_('ml_ez')], w=[K_('ml_ez')])
                    yield
                    k.op('dve', I('tensor_tensor', B.Pm[:L, :L], PS(ps_, L)[:L, :], B.Dm[:L, :L], ALU.mult), r=[('ps', ps_), K_('ml_Dm')], w=[K_('ml_P')])
                    psfree(ps_)
                    k.op('dve', I('tensor_tensor', B.og[:L, :], B.zs[:L, :], B.ez[:L, 0:256], ALU.mult), r=[K_('ml_zs'), K_('ml_ez')], w=[K_('ml_og')])
                    yield
                    pb_ = P_ % 2
                    k.op('pe', I('transpose', PSB(pb_, L)[:L, :], B.Pm[:L, :L], identb[:L, :L]), r=[K_('ml_P')], w=[('psb', pb_)])
                    k.op('act', I('copy', B.PTs[:L, :L], PSB(pb_, L)[:L, :]), r=[('psb', pb_)], w=[K_('ml_PT')])
                    yield
                    pi = psalloc()
                    k.op('pe', I('matmul', PS(pi, 257)[:L, :], B.PTs[:L, :L], B.v1[:L, :], start=True, stop=True),
                         r=[K_('ml_PT'), K_('ml_v1')], w=[('ps', pi)])
                    yield
                    k.op('act', I('copy', B.intra[:L, :], PS(pi, 257)[:L, :]), r=[('ps', pi)], w=[K_('ml_intra')])
                    psfree(pi)

                def stage_c(h, u, c0, L, B, st=None):
                    P_ = B.par
                    K_ = lambda nm: (nm, P_)
                    CTv, CTbv, ck, bk = (CT[:, h, :], CTb, ('ml_CT', h), 'ml_CTb') if st is None else st
                    pj = nps()
                    k.op('pe', I('matmul', PS(pj, 257)[:L, :], B.qT[:, :L], CTbv[:, :], start=True, stop=True),
                         r=[('ml_qTb', B.qk), bk], w=[('ps', pj)])
                    pu = nps()
                    k.op('pe', I('matmul', PS(pu, 257), B.ktok[:L, :], B.wv[:L, :], start=True, stop=True), r=[K_('ml_ktok'), K_('ml_wv')], w=[('ps', pu)])
                    k.op('dve', I('scalar_tensor_tensor', CTv, CTv, bcs[:, u, 16 + h:17 + h], PS(pu, 257), ALU.mult, ALU.add),
                         r=[('ps', pu), ('bcs', u), ck], w=[ck])
                    if st is None:
                        k.op('act', I('copy', CTbv[:, :], CTv), r=[ck], w=[bk])
                    k.op('dve', I('scalar_tensor_tensor', B.nd[:L, :], PS(pj, 257)[:L, :], col3[:L, u, 16 + h:17 + h], B.intra[:L, :],
                                  ALU.mult, ALU.add), r=[('ps', pj), K_('ml_intra'), ('col', u)], w=[K_('ml_nd')])

                def stage_d(h, u, c0, L, B):
                    P_ = B.par
                    K_ = lambda nm: (nm, P_)
                    k.op('dve', I('scalar_tensor_tensor', B.den[:L, :], B.nd[:L, 256:257], -1.0, B.nd[:L, 256:257], ALU.mult, ALU.max),
                         r=[K_('ml_nd')], w=[K_('ml_den')])
                    k.op('dve', I('tensor_tensor', B.den[:L, :], B.den[:L, :], expnegm[:L, u, h:h + 1], ALU.max),
                         r=[K_('ml_den'), ('col', u)], w=[K_('ml_den')])
                    k.op('dve', I('reciprocal', B.den[:L, :], B.den[:L, :]), r=[K_('ml_den')], w=[K_('ml_den')])
                    k.op('dve', I('tensor_scalar', B.hs[:L, :], B.nd[:L, 0:256], B.den[:L, 0:1], None, ALU.mult), r=[K_('ml_nd'), K_('ml_den')], w=[K_('ml_hs')])
                    yield
                    k.op('act', I('activation', B.junk[:L, :], B.hs[:L, :], AF.Square, accum_out=B.ssq[:L, :]), r=[K_('ml_hs')], w=[K_('ml_ssq'), 'ml_junk'])
                    k.op('act', I('activation', B.ssq[:L, :], B.ssq[:L, :], AF.Ln, bias=epsc[:L, :], scale=1.0 / 256), r=[K_('ml_ssq')], w=[K_('ml_ssq')])
                    k.op('act', I('activation', B.ssq[:L, :], B.ssq[:L, :], AF.Exp, scale=-0.5), r=[K_('ml_ssq')], w=[K_('ml_ssq')])
                    yield
                    k.op('dve', I('scalar_tensor_tensor', B.hs[:L, :], B.hs[:L, :], B.ssq[:L, 0:1], ng[:L, h * 256:(h + 1) * 256],
                                  ALU.mult, ALU.mult), r=[K_('ml_hs'), K_('ml_ssq'), 'ml_ng'], w=[K_('ml_hs')])
                    yield
                    k.op('pool', I('tensor_tensor', B.yb[:L, :], B.hs[:L, :], B.og[:L, :], ALU.mult), r=[K_('ml_hs'), K_('ml_og')], w=[K_('ml_yb')])
                    yield
                    pb_ = P_ % 2
                    for vc in range(2):
                        k.op('pe', I('transpose', PSB(pb_, 256)[:, vc * 128:vc * 128 + L], B.yb[:L, vc * 128:(vc + 1) * 128], identb[:L, :L]),
                             r=[K_('ml_yb')], w=[('psb', pb_)])
                    for vc in range(2):
                        k.op('act' if vc else 'dve', I('copy' if vc else 'tensor_copy', yTs[:, vc, c0:c0 + L], PSB(pb_, 256)[:, vc * 128:vc * 128 + L]),
                             r=[('psb', pb_)], w=[('ml_yTs', u)])

                def head_setup(h, units, w_, wk, half):
                    bs = [(u, c0, L, BS[half * 2 + i]) for i, (u, c0, L) in enumerate(units)]
                    cb0 = units[0][1]
                    WB = sum(L for (_, _, L) in units)
                    o0 = half * 256
                    for (cc, dst, sc, nm) in ((0, qTb, None, ('ml_qTb', half)), (128, kTb, SC, ('ml_kTb', half))):
                        pq = nps()
                        for c in range(8):
                            k.op('pe', I('matmul', PS(pq, WB), w_[:, c, cc:cc + 128], hT[:, c, cb0:cb0 + WB], start=(c == 0), stop=(c == 7)),
                                 r=[wk] + hkeys(cb0) + hkeys(cb0 + WB - 1), w=[('ps', pq)])
                        if sc is None:
                            k.op('act', I('copy', dst[:, o0:o0 + WB], PS(pq, WB)), r=[('ps', pq)], w=[nm])
                        else:
                            k.op('act', I('mul', dst[:, o0:o0 + WB], PS(pq, WB), sc), r=[('ps', pq)], w=[nm])
                    off = o0
                    for (u, c0, L, B) in bs:
                        B.qT = qTb[:, off:off + 128]
                        B.kT = kTb[:, off:off + 128]
                        B.qk = half
                        off += L
                    return bs

                def batch_tail(h, bs):
                    for (u, c0, L, B) in bs:
                        stage_c(h, u, c0, L, B)
                        yield
                    gens = [stage_d(h, u, c0, L, B) for (u, c0, L, B) in bs]
                    while gens:
                        nxt = []
                        for g_ in gens:
                            try:
                                next(g_)
                                nxt.append(g_)
                            except StopIteration:
                                pass
                        gens = nxt
                        yield

                def head_run(h, unit_batches, w_, wk):
                    prev = None
                    for bi, units in enumerate(unit_batches):
                        bs = head_setup(h, units, w_, wk, bi % 2)
                        gens = [stage_a(h, u, c0, L, w_, wk, B) for (u, c0, L, B) in bs]
                        if prev is not None:
                            gens.append(prev)
                        lockstep(gens)
                        prev = batch_tail(h, bs)
                    if prev is not None:
                        lockstep([prev])

                def state_out(h, dC, dn):
                    for vc in range(2):
                        p = nps()
                        k.op('pe', I('transpose', PS(p, 128), CT[:, h, vc * 128:(vc + 1) * 128], ident[:]), r=[('ml_CT', h)], w=[('ps', p)])
                        k.op('act', I('copy', ctmp[:, vc, :], PS(p, 128)), r=[('ps', p)], w=[('ml_ctmp', vc)])
                    k.dma(dC.rearrange("(vc p) kk -> p vc kk", p=128), ctmp[:], r=[('ml_ctmp', 0), ('ml_ctmp', 1)])
                    k.dma(dn.rearrange("(p o) -> p o", o=1), CT[:, h, 256:257], r=[('ml_CT', h)])

                def state_in(h, sC, sn):
                    k.dma(ctmp[:], sC.rearrange("(vc p) kk -> p vc kk", p=128), w=[('ml_ctmp', 0), ('ml_ctmp', 1)])
                    for vc in range(2):
                        p = nps()
                        k.op('pe', I('transpose', PS(p, 128), ctmp[:, vc, :], ident[:]), r=[('ml_ctmp', vc)], w=[('ps', p)])
                        k.op('act', I('copy', CT[:, h, vc * 128:(vc + 1) * 128], PS(p, 128)), r=[('ps', p)], w=[('ml_CT', h)])
                    k.dma(CT[:, h, 256:257], sn.rearrange("(p o) -> p o", o=1), w=[('ml_CT', h)])
                    k.op('act', I('copy', CTb[:, :], CT[:, h, :]), r=[('ml_CT', h)], w=['ml_CTb'])

                def sample_run(h, w_, wk, npu_):
                    for j in range(NS):
                        k.groups[('ml_CTs', j)] = [(('ml_CTs', j), 0), (('ml_CTs', j), 1), (('ml_CTs', j), 'n')]
                    for j in range(NS):
                        ck, bk = ('ml_CTs', j), ('ml_CTbs', j)
                        k.dma(ctmps[j][:], di['st_C'][j, h].rearrange("(vc p) kk -> p vc kk", p=128), w=[('ml_ctmps', j)])
                        k.dma(CTs[:, j, 256:257], di['st_n'][j, h].rearrange("(p o) -> p o", o=1), w=[(ck, 'n')])
                    for j in range(NS):
                        ck, bk = ('ml_CTs', j), ('ml_CTbs', j)
                        for vc in range(2):
                            p = nps()
                            k.op('pe', I('transpose', PS(p, 128), ctmps[j][:, vc, :], ident[:]), r=[('ml_ctmps', j)], w=[('ps', p)])
                            k.op('act', I('copy', CTs[:, j, vc * 128:(vc + 1) * 128], PS(p, 128)), r=[('ps', p)], w=[(ck, vc)])
                        k.op('act', I('copy', CTbs[j][:, :], CTs[:, j, :]), r=[ck], w=[bk])
                    units = [(npu_ + j, TP + j, 1) for j in range(NS)]
                    bs = [(u, c0, L, BS[i]) for i, (u, c0, L) in enumerate(units)]
                    for (cc, dst, sc, nm) in ((0, qTb, None, ('ml_qTb', 0)), (128, kTb, SC, ('ml_kTb', 0))):
                        pq = nps()
                        for c in range(8):
                            k.op('pe', I('matmul', PS(pq, NS), w_[:, c, cc:cc + 128], hT[:, c, TP:TP + NS], start=(c == 0), stop=(c == 7)),
                                 r=[wk] + hkeys(TP), w=[('ps', pq)])
                        if sc is None:
                            k.op('act', I('copy', dst[:, 0:NS], PS(pq, NS)), r=[('ps', pq)], w=[nm])
                        else:
                            k.op('act', I('mul', dst[:, 0:NS], PS(pq, NS), sc), r=[('ps', pq)], w=[nm])
                    for i, (u, c0, L, B) in enumerate(bs):
                        B.qT = qTb[:, i:i + 128]
                        B.kT = kTb[:, i:i + 128]
                        B.qk = 0
                    lockstep([stage_a(h, u, c0, L, w_, wk, B) for (u, c0, L, B) in bs])
                    for j, (u, c0, L, B) in enumerate(bs):
                        stage_c(h, u, c0, L, B, st=(CTs[:, j, :], CTbs[j], ('ml_CTs', j), ('ml_CTbs', j)))
                    lockstep([stage_d(h, u, c0, L, B) for (u, c0, L, B) in bs])
                    for j in range(NS):
                        ck = ('ml_CTs', j)
                        for vc in range(2):
                            p = nps()
                            k.op('pe', I('transpose', PS(p, 128), CTs[:, j, vc * 128:(vc + 1) * 128], ident[:]), r=[ck], w=[('ps', p)])
                            k.op('act', I('copy', ctmps[j][:, vc, :], PS(p, 128)), r=[('ps', p)], w=[('ml_ctmps', j, vc)])
                        k.dma(di['o_Cs'][j, h].rearrange("(vc p) kk -> p vc kk", p=128), ctmps[j][:], r=[('ml_ctmps', j, 0), ('ml_ctmps', j, 1), ('ml_ctmps', j)])
                        k.dma(di['o_ns'][j, h].rearrange("(p o) -> p o", o=1), CTs[:, j, 256:257], r=[ck])

                nw = 0

                for wb_ in range(2):
                    k.groups[('ml_wh', wb_)] = [('ml_wh', wb_, i_) for i_ in range(5)]

                def ml_load(n_, h_):
                    for i_, (d0, s0_, nn_) in enumerate(((0, h_ * 128, 128), (128, 1024 + h_ * 128, 128), (256, 2048 + h_ * 256, 256),
                                                         (512, 4096 + h_ * 256, 256), (768, 6144 + h_ * 256, 256))):
                        load_w(es, wh[n_ % 2][:, :, d0:d0 + nn_], ('ml_wh', n_ % 2, i_), w_in[:, s0_:s0_ + nn_], 8, nn_, 'ml')

                mlpre = Pre([h_ for hi_ in range(2) for h_ in range(8)], ml_load, 1)
                for hi, (t0, TW) in enumerate(HALVES):
                    norm_half(es, di['norm_g'][layer], t0, TW, hT, 'ml%d' % hi, dbuf=False)
                    npu = TP // 128
                    for u in range(npu):
                        gate_unit(u, u * 128, 128)
                    NSS = 0 if os.environ.get('ML_NOSAMPLE') else NS
                    if hi == 1:
                        k.dma(di['o_mp'], mprev[0:1, :], r=['ml_mprev'])
                        for j in range(NSS):
                            k.dma(mprev[:, :], bass.AP(di['st_m'].tensor, di['st_m'][j].offset, [[0, 128], [1, 8]]), w=['ml_mprev'])
                            gate_unit(npu + j, TP + j, 1)
                            k.dma(di['o_ms'][j:j + 1, :], mprev[0:1, :], r=['ml_mprev'])
                    for h in range(8):
                        wb = nw % 2
                        mlpre.need(nw)
                        nw += 1
                        wk = ('ml_wh', wb)
                        k.op('act', I('copy', CTb[:, :], CT[:, h, :]), r=[('ml_CT', h)], w=['ml_CTb'])
                        head_run(h, [[(u, u * 128, 128) for u in range(u0, min(u0 + 2, npu))] for u0 in range(0, npu, 2)], wh[wb], wk)
                        if hi == 1:
                            state_out(h, di['o_Cp'][h], di['o_np'][h])
                            if NSS:
                                sample_run(h, wh[wb], wk, npu)
                        psmod[0] = 6
                        nuu = npu + (NSS if hi == 1 else 0)
                        for vc in range(2):
                            k.dma(yTv[:, h * 2 + vc, t0:t0 + TW], yTs[:, vc, :TW], r=[('ml_yTs', u) for u in range(nuu)])
                k.barrier()
                k.flush()
            out_phase(layer, di['a_w_out'], 16)


        def attn_layer(layer=2):
            w_in = di['c_w_in']
            TP = T // 2
            GR = ((128, 1), (512, 4), (2048, 16))
            qkT = dscr('qkT', [6, 1024, TA], BF16)
            vtok = dscr('vtok', [3, TA, 1024], BF16)
            zT = dscr('zT', [1024, TA], BF16)
            numS = dscr('numS', [3, T, 8 * 130])
            qs_tok = dscr('qs_tok', [3, 2, NS, 1024])
            vs_tok = dscr('vs_tok', [3, NS, 1024])
            os_tok = dscr('os_tok', [NS, 1024])
            ISQ = 128.0 ** -0.5
            with ExitStack() as es:
                wst.clear()
                permf = sb(es, 'at_permf', [128, 128]); k.dma(permf[:], di['c_ropeperm'], w=['at_perm'])
                permb = sb(es, 'at_permb', [128, 128], BF16)
                k.op('dve', I('tensor_copy', permb[:], permf[:]), r=['at_perm'], w=['at_permb'])
                rc = sb(es, 'at_rc', [128, TP + NS]); rs_ = sb(es, 'at_rs', [128, TP + NS])
                wq = [sb(es, 'at_wq%d' % i, [128, 8, 128], BF16) for i in range(3)]
                wvs = [sb(es, 'at_wv%d' % i, [128, 8, 512], BF16) for i in range(2)]
                xb16 = [sb(es, 'at_xb%d' % i, [128, 512], BF16) for i in range(5)]
                t1 = [sb(es, 'at_t1%d' % i, [128, 512]) for i in range(5)]
                res = [sb(es, 'at_res%d' % i, [128, 512]) for i in range(5)]
                stage = [sb(es, 'at_stage%d' % i, [128, TP + NS], BF16) for i in range(3)]
                ktile = [sb(es, 'at_ktile%d' % i, [128, 128]) for i in range(2)]
                vt32 = [sb(es, 'at_vt32%d' % i, [128, 512]) for i in range(3)]
                vt16 = [sb(es, 'at_vt16%d' % i, [128, 512], BF16) for i in range(3)]
                hT = sb(es, 'at_hT', [128, 8, TP + NS], BF16)
                nw = 0; nk = 0; nv = 0; nt3 = [0]

                def v_load(n_, it):
                    c0v = (3 * it[0] + 2) * 1024 + it[1] * 512
                    load_w(es, wvs[n_ % 2], ('at_wv', n_ % 2), w_in[:, c0v:c0v + 512], 8, 512, 'at')

                vpre = Pre([(g_, hb_) for hi_ in range(2) for g_ in range(len(GR)) for hb_ in range(2)], v_load, 1)
                nvw = 0
                for hi, (t0, TW) in enumerate(HALVES):
                    norm_half(es, di['norm_g'][layer], t0, TW, hT, 'at%d' % hi)
                    k.dma(rc[:, :TP], di['c_ropec'][:, t0:t0 + TP], w=['at_rc'])
                    k.dma(rs_[:, :TP], di['c_ropes'][:, t0:t0 + TP], w=['at_rs'])
                    if TW > TP:
                        for (dst_, src_, kk_) in ((rc, di['c_ropec'], 'at_rc'), (rs_, di['c_ropes'], 'at_rs')):
                            for jj in range(NS):
                                k.dma(dst_[:, TP + jj:TP + jj + 1], src_[:, T:T + 1], w=[kk_], allow_slow_non_contiguous=True)
                    tl = [(s0, 512) for s0 in range(0, TP, 512)]
                    if TW > TP:
                        tl.append((TP, NS))
                    for g, (win, dil) in enumerate(GR):
                        wn = str(win)
                        def qk_tile(n, s0, W_, b, wb, g, qk, h, win, wn, t0, last):
                            nonlocal nk
                            p = psalloc()
                            gemm_fm(p, wq[wb], ('at_wq', wb), 0, hT, s0, W_)
                            yield
                            k.op('act', I('copy', xb16[b][:, :W_], PS(p, W_)), r=[('ps', p)], w=[('at_xb', b)])
                            k.op('dve', I('tensor_tensor', t1[b][:, :W_], PS(p, W_), rc[:, s0:s0 + W_], ALU.mult),
                                 r=[('ps', p), 'at_rc'], w=[('at_t1', b)])
                            psfree(p)
                            yield
                            p2 = psalloc()
                            k.op('pe', I('matmul', PS(p2, W_), permb[:], xb16[b][:, :W_], start=True, stop=True),
                                 r=[('at_xb', b), 'at_permb'], w=[('ps', p2)])
                            yield
                            k.op('dve', I('tensor_tensor', res[b][:, :W_], PS(p2, W_), rs_[:, s0:s0 + W_], ALU.mult),
                                 r=[('ps', p2), 'at_rs'], w=[('at_res', b)])
                            psfree(p2)
                            yield
                            k.op('dve', I('tensor_tensor', res[b][:, :W_], res[b][:, :W_], t1[b][:, :W_], ALU.add),
                                 r=[('at_res', b), ('at_t1', b)], w=[('at_res', b)])
                            yield
                            k.op('act', I('copy', stage[wb][:, s0:s0 + W_], res[b][:, :W_]), r=[('at_res', b)], w=[('at_stage', wb, n)])
                            if W_ == 512 and qk == 1:
                                for j in range(4):
                                    tok0 = t0 + s0 + j * 128
                                    if tok0 >= T - min(win, T):
                                        kb_ = nk % 2; nk += 1
                                        p3 = nps()
                                        k.op('pe', I('transpose', PS(p3, 128), res[b][:, j * 128:(j + 1) * 128], ident[:]),
                                             r=[('at_res', b)], w=[('ps', p3)])
                                        k.op('act', I('copy', ktile[kb_][:, :], PS(p3, 128)), r=[('ps', p3)], w=[('at_ktile', kb_)])
                                        o0 = tok0 - (T - min(win, T))
                                        k.dma(di['o_kp' + wn][o0:o0 + 128, h * 128:(h + 1) * 128], ktile[kb_][:, :], r=[('at_ktile', kb_)])
                            if W_ == NS:
                                kb_ = nk % 2; nk += 1
                                p3 = nps()
                                k.op('pe', I('transpose', PS(p3, 128)[:NS, :], res[b][:, :NS], ident[:]), r=[('at_res', b)], w=[('ps', p3)])
                                k.op('act', I('copy', ktile[kb_][:NS, :], PS(p3, 128)[:NS, :]), r=[('ps', p3)], w=[('at_ktile', kb_)])
                                k.dma(qs_tok[g, qk, :, h * 128:(h + 1) * 128], ktile[kb_][:NS, :], r=[('at_ktile', kb_)])
                                if qk == 1:
                                    k.dma(di['o_ks' + wn][:, h * 128:(h + 1) * 128], ktile[kb_][:NS, :], r=[('at_ktile', kb_)])
                            if last:
                                k.dma(qkT[2 * g + qk, h * 128:(h + 1) * 128, t0:t0 + TW], stage[wb][:, :TW],
                                      r=[('at_stage', wb, n_) for n_ in range(len(tl))])

                        def qk_load(n_, it, g=g):
                            c0q = (3 * g + it[0]) * 1024 + it[1] * 128
                            load_w(es, wq[(nwb[0] + n_) % 3], ('at_wq', (nwb[0] + n_) % 3), w_in[:, c0q:c0q + 128], 8, 128, 'at')

                        vpre.need(nvw - 1)
                        nwb = [nw]
                        qkpre = Pre([(qk_, h_) for qk_ in range(2) for h_ in range(8)], qk_load, 1)

                        def qk_tiles(g=g, win=win, wn=wn, t0=t0):
                            nonlocal nw
                            gi = 0
                            for qk in range(2):
                                for h in range(8):
                                    wb = nw % 3; nw += 1
                                    qkpre.need(gi)
                                    gi += 1
                                    for n, (s0, W_) in enumerate(tl):
                                        b = nt3[0] % 5; nt3[0] += 1
                                        yield qk_tile(n, s0, W_, b, wb, g, qk, h, win, wn, t0, n == len(tl) - 1)

                        pipeline(qk_tiles(), 5, ramp=1)
                        for hb in range(2):
                            vpre.need(nvw)
                            wv_, wvk = wvs[nvw % 2], ('at_wv', nvw % 2)
                            nvw += 1
                            subs = [(j * 128, 128) for j in range(TP // 128)] + ([(TP, NS)] if TW > TP else [])
                            for (c0_, L) in subs:
                                b = nv % 3; nv += 1
                                p = nps()
                                for c in range(8):
                                    k.op('pe', I('matmul', PS(p, 512)[:L, :], hT[:, c, c0_:c0_ + L], wv_[:, c, :], start=(c == 0), stop=(c == 7)),
                                         r=[wvk] + hkeys(c0_), w=[('ps', p)])
                                k.op('act', I('copy', vt32[b][:L, :], PS(p, 512)[:L, :]), r=[('ps', p)], w=[('at_vt32', b)])
                                k.op('dve', I('tensor_copy', vt16[b][:L, :], PS(p, 512)[:L, :]), r=[('ps', p)], w=[('at_vt16', b)])
                                if L == 128:
                                    tok0 = t0 + c0_
                                    k.dma(vtok[g, tok0:tok0 + 128, hb * 512:(hb + 1) * 512], vt16[b][:, :], r=[('at_vt16', b)])
                                    if tok0 >= T - min(win, T):
                                        o0 = tok0 - (T - min(win, T))
                                        k.dma(di['o_vp' + wn][o0:o0 + 128, hb * 512:(hb + 1) * 512], vt32[b][:, :], r=[('at_vt32', b)])
                                else:
                                    k.dma(di['o_vs' + wn][:, hb * 512:(hb + 1) * 512], vt32[b][:NS, :], r=[('at_vt32', b)])
                                    k.dma(vs_tok[g, :, hb * 512:(hb + 1) * 512], vt32[b][:NS, :], r=[('at_vt32', b)])
                    for f in range(8):
                        wb = nw % 2; nw += 1
                        load_w(es, wq[wb], ('at_wq', wb), w_in[:, 9216 + f * 128:9216 + (f + 1) * 128], 8, 128, 'at')
                        for n, (s0, W_) in enumerate(tl):
                            p = nps()
                            gemm_fm(p, wq[wb], ('at_wq', wb), 0, hT, s0, W_)
                            k.op('act', I('activation', stage[wb][:, s0:s0 + W_], PS(p, W_), AF.Silu), r=[('ps', p)], w=[('at_stage', wb, n)])
                        k.dma(zT[f * 128:(f + 1) * 128, t0:t0 + TW], stage[wb][:, :TW], r=[('at_stage', wb, n) for n in range(len(tl))])
                k.barrier()
                k.flush()
            with ExitStack() as es:
                wst.clear()
                band = sb(es, 'ab_band', [128, 256]); k.dma(band[:], di['c_band'], w=['ab_band'])
                NB = 8
                qh = [sb(es, 'ab_qh%d' % i, [128, T], BF16) for i in range(2)]
                kh = [sb(es, 'ab_kh%d' % i, [128, T], BF16) for i in range(2)]
                sm = [sb(es, 'ab_sm%d' % i, [128, 256]) for i in range(NB)]
                Pm = [sb(es, 'ab_P%d' % i, [128, 256], BF16) for i in range(NB)]
                PT = [sb(es, 'ab_PT%d' % i, [128, 2, 128], BF16) for i in range(NB)]
                vt = [sb(es, 'ab_vt%d' % i, [128, 2, 128], BF16) for i in range(NB)]
                osb = [sb(es, 'ab_o%d' % i, [128, 130]) for i in range(NB)]
                nmx = [sb(es, 'ab_nmx%d' % i, [128, 1]) for i in range(NB)]
                nu = 0
                ABW = int(os.environ.get('ABW', '8'))
                heads_ = [(g, h) for g in range(len(GR)) for h in range(8)]

                def ab_load(i):
                    g, h = heads_[i]
                    hb = i % 2
                    k.dma(qh[hb][:, :], qkT[2 * g, h * 128:(h + 1) * 128, 0:T], w=[('ab_qh', hb)])
                    k.dma(kh[hb][:, :], qkT[2 * g + 1, h * 128:(h + 1) * 128, 0:T], w=[('ab_kh', hb)])

                def attn_unit(r, n, b, g, h, hb, dil, qv, kv, vg):
                    k0 = max(n - 1, 0) * 128
                    NK = 256 if n > 0 else 128
                    m0 = 0 if n > 0 else 128
                    k.dma(vt[b][:, 0:NK // 128, :], vg[r, k0:k0 + NK, :].rearrange("(j p) e -> p j e", p=128), w=[('ab_vt', b)])
                    p = psalloc()
                    k.op('pe', I('matmul', PS(p, NK), qv[:, r, n * 128:(n + 1) * 128], kv[:, r, k0:k0 + NK], start=True, stop=True),
                         r=[('ab_qh', hb), ('ab_kh', hb)], w=[('ps', p)])
                    yield
                    k.op('dve', I('scalar_tensor_tensor', sm[b][:, :NK], PS(p, NK), ISQ, band[:, m0:m0 + NK], ALU.mult, ALU.add),
                         r=[('ps', p), 'ab_band'], w=[('ab_sm', b)])
                    psfree(p)
                    yield
                    k.op('dve', I('tensor_reduce', osb[b][:, 128:129], sm[b][:, :NK], AX.X, ALU.max), r=[('ab_sm', b)], w=[('ab_o', b, 1)])
                    yield
                    k.op('pool', I('tensor_scalar', nmx[b][:, :], osb[b][:, 128:129], -1.0, None, ALU.mult), r=[('ab_o', b, 1)], w=[('ab_nmx', b)])
                    yield
                    k.op('act', I('activation', Pm[b][:, :NK], sm[b][:, :NK], AF.Exp, bias=nmx[b][:, :], accum_out=osb[b][:, 129:130]),
                         r=[('ab_sm', b), ('ab_nmx', b)], w=[('ab_P', b), ('ab_o', b, 2)])
                    yield
                    for j in range(NK // 128):
                        k.op('pe', I('transpose', PSB(j, 128), Pm[b][:, j * 128:(j + 1) * 128], identb[:]), r=[('ab_P', b)], w=[('psb', j)])
                        k.op('act' if j else 'dve', I('copy' if j else 'tensor_copy', PT[b][:, j, :], PSB(j, 128)), r=[('psb', j)], w=[('ab_PT', b, j)])
                    yield
                    po = psalloc()
                    for j in range(NK // 128):
                        k.op('pe', I('matmul', PS(po, 128), PT[b][:, j, :], vt[b][:, j, :], start=(j == 0), stop=(j == NK // 128 - 1)),
                             r=[('ab_PT', b, j), ('ab_vt', b)], w=[('ps', po)])
                    yield
                    k.op('act', I('copy', osb[b][:, 0:128], PS(po, 128)), r=[('ps', po)], w=[('ab_o', b, 0)])
                    psfree(po)
                    dst = numS[g].rearrange("(u d) f -> d u f", d=dil)[r, n * 128:(n + 1) * 128, h * 130:(h + 1) * 130]
                    k.dma(dst, osb[b][:, :], r=[('ab_o', b, 0), ('ab_o', b, 1), ('ab_o', b, 2)])

                def all_units():
                    nonlocal nu
                    ab_load(0)
                    for i, (g, h) in enumerate(heads_):
                        if i + 1 < len(heads_):
                            ab_load(i + 1)
                        win, dil = GR[g]
                        nb_ = (T // dil) // 128
                        hb = i % 2
                        qv = qh[hb][:, :].rearrange("p (u d) -> p d u", d=dil)
                        kv = kh[hb][:, :].rearrange("p (u d) -> p d u", d=dil)
                        vg = vtok[g, 0:T, h * 128:(h + 1) * 128].rearrange("(u d) e -> d u e", d=dil)
                        for r in range(dil):
                            for n in range(nb_):
                                b = nu % NB
                                nu += 1
                                yield attn_unit(r, n, b, g, h, hb, dil, qv, kv, vg)

                pipeline(all_units(), ABW, ramp=1)
                k.barrier()
                k.flush()
            with ExitStack() as es:
                wst.clear()
                NBC = 4
                A = [sb(es, 'ac_A%d' % i, [128, 3, 8, 130]) for i in range(NBC)]
                zt = [sb(es, 'ac_z%d' % i, [128, 8, 128], BF16) for i in range(NBC)]
                from types import SimpleNamespace as _NS
                MB = []
                for par in range(NBC):
                    m_ = _NS(par=par)
                    m_.M = sb(es, 'ac_M', [128, 8]); m_.wg = sb(es, 'ac_w', [128, 3, 8]); m_.den = sb(es, 'ac_den', [128, 8])
                    m_.tmp = sb(es, 'ac_tmp', [128, 3, 8]); m_.acc = sb(es, 'ac_acc', [128, 8, 128]); m_.ob = sb(es, 'ac_ob', [128, 8, 128], BF16)
                    MB.append(m_)
                yo = [sb(es, 'ac_yo%d' % i, [128, 8, 128], BF16) for i in range(NBC)]
                acc = MB[0].acc; ob = MB[0].ob

                def merge_g(Av, L, akeys, m_):
                    P_ = m_.par
                    K_ = lambda nm, *a: (nm, P_) + a
                    M, wg_, den, tmp, acc_, ob_ = m_.M, m_.wg, m_.den, m_.tmp, m_.acc, m_.ob
                    k.op('dve', I('tensor_tensor', M[:L, :], Av[:L, 0, :, 128], Av[:L, 1, :, 128], ALU.max), r=akeys, w=[K_('ac_M')])
                    yield
                    k.op('dve', I('tensor_tensor', M[:L, :], M[:L, :], Av[:L, 2, :, 128], ALU.max), r=akeys + [K_('ac_M')], w=[K_('ac_M')])
                    yield
                    for g in range(3):
                        k.op('dve', I('tensor_tensor', wg_[:L, g, :], Av[:L, g, :, 128], M[:L, :], ALU.subtract), r=akeys + [K_('ac_M')], w=[K_('ac_w', g)])
                    yield
                    k.op('act', I('activation', wg_[:L, :, :], wg_[:L, :, :], AF.Exp), r=[K_('ac_w', g) for g in range(3)], w=[K_('ac_w', g) for g in range(3)])
                    yield
                    for g in range(3):
                        k.op('dve', I('tensor_tensor', tmp[:L, g, :], wg_[:L, g, :], Av[:L, g, :, 129], ALU.mult), r=akeys + [K_('ac_w', g)], w=[K_('ac_tmp', g)])
                    yield
                    k.op('dve', I('tensor_tensor', den[:L, :], tmp[:L, 0, :], tmp[:L, 1, :], ALU.add), r=[K_('ac_tmp', 0), K_('ac_tmp', 1)], w=[K_('ac_den')])
                    yield
                    k.op('dve', I('tensor_tensor', den[:L, :], den[:L, :], tmp[:L, 2, :], ALU.add), r=[K_('ac_tmp', 2), K_('ac_den')], w=[K_('ac_den')])
                    yield
                    k.op('dve', I('reciprocal', den[:L, :], den[:L, :]), r=[K_('ac_den')], w=[K_('ac_den')])
                    yield
                    for h in range(8):
                        k.op('dve', I('tensor_scalar', acc_[:L, h, :], Av[:L, 0, h, 0:128], wg_[:L, 0, h:h + 1], None, ALU.mult),
                             r=akeys + [K_('ac_w', 0)], w=[K_('ac_acc', h)])
                    yield
                    for g in (1, 2):
                        for h in range(8):
                            k.op('dve', I('scalar_tensor_tensor', acc_[:L, h, :], Av[:L, g, h, 0:128], wg_[:L, g, h:h + 1], acc_[:L, h, :], ALU.mult, ALU.add),
                                 r=akeys + [K_('ac_w', g), K_('ac_acc', h)], w=[K_('ac_acc', h)])
                        yield
                    for h in range(8):
                        k.op('act', I('mul', ob_[:L, h, :], acc_[:L, h, :], den[:L, h:h + 1]),
                             r=[K_('ac_acc', h), K_('ac_den')], w=[K_('ac_ob', h)])

                def tile_g(tt, b):
                    m_ = MB[b]
                    for g in range(3):
                        k.dma(A[b][:, g, :, :], numS[g, tt * 128:(tt + 1) * 128, :].rearrange("t (h f) -> t h f", f=130), w=[('ac_A', b, g)])
                    k.dma(zt[b][:, :, :128], zT.rearrange("(c p) t -> p c t", p=128)[:, :, tt * 128:(tt + 1) * 128], w=[('ac_z', b)])
                    yield
                    yield from merge_g(A[b], 128, [('ac_A', b, g) for g in range(3)], m_)
                    yield
                    for h in range(8):
                        k.op('pe', I('transpose', PSB(h % 2, 128), m_.ob[:, h, :], identb[:, :]), r=[('ac_ob', b, h)], w=[('psb', h % 2)])
                        k.op('dve', I('tensor_tensor', yo[b][:, h, :], PSB(h % 2, 128), zt[b][:, h, :], ALU.mult),
                             r=[('psb', h % 2), ('ac_z', b)], w=[('ac_yo', b, h)])
                    k.dma(yTv[:, 0:8, tt * 128:(tt + 1) * 128], yo[b][:, :, :], r=[('ac_yo', b, h) for h in range(8)])

                pipeline((tile_g(tt, tt % NBC) for tt in range(T // 128)), NBC, ramp=1)

                def merge(Av, L, akeys):
                    for _ in merge_g(Av, L, akeys, MB[0]):
                        pass

                As = sb(es, 'as_A', [128, 3, 8, 130])
                Kc = sb(es, 'as_K', [128, 1024]); Vc = sb(es, 'as_V', [128, 1024])
                qb = sb(es, 'as_qb', [128, 1024]); kn = sb(es, 'as_kn', [128, 1024]); vn = sb(es, 'as_vn', [1, 1024])
                prod = sb(es, 'as_prod', [128, 1024])
                sc2 = sb(es, 'as_sc2', [128, 16])
                sall = sb(es, 'as_sall', [8, 130]); Ps = sb(es, 'as_P', [8, 130]); mxs = sb(es, 'as_mx', [8, 2]); nmxs = sb(es, 'as_nmx', [8, 1])
                PTk = sb(es, 'as_PTk', [128, 8]); PTn = sb(es, 'as_PTn', [1, 24])
                k.op('dve', I('memset', As[:], 0.0), w=[('as_A', j) for j in range(NS)])
                for j in range(NS):
                    for g, (win, dil) in enumerate(GR):
                        wn = str(win)
                        k.dma(Kc[:, :], di['ck' + wn][j].rearrange("(u d) f -> d u f", d=dil)[0, :, :], w=['as_K'])
                        k.dma(Vc[:, :], di['cv' + wn][j].rearrange("(u d) f -> d u f", d=dil)[0, :, :], w=['as_V'])
                        k.dma(qb[:, :], bass.AP(qs_tok.tensor, qs_tok[g, 0, j].offset, [[0, 128], [1, 1024]]), w=['as_qb'])
                        k.dma(kn[:, :], bass.AP(qs_tok.tensor, qs_tok[g, 1, j].offset, [[0, 128], [1, 1024]]), w=['as_kn'])
                        k.dma(vn[:, :], vs_tok[g, j:j + 1, :], w=['as_vn'])
                        k.op('dve', I('tensor_tensor', prod[:, :], Kc[:, :], qb[:, :], ALU.mult), r=['as_K', 'as_qb'], w=['as_prod'])
                        k.op('dve', I('tensor_reduce', sc2[:, 0:8], prod[:, :].rearrange("p (h e) -> p h e", e=128), AX.X, ALU.add), r=['as_prod'], w=['as_sc2'])
                        k.op('dve', I('tensor_tensor', prod[:, :], kn[:, :], qb[:, :], ALU.mult), r=['as_kn', 'as_qb', 'as_sc2'], w=['as_prod'])
                        k.op('dve', I('tensor_reduce', sc2[:, 8:16], prod[:, :].rearrange("p (h e) -> p h e", e=128), AX.X, ALU.add), r=['as_prod'], w=['as_sc2'])
                        p = nps()
                        k.op('pe', I('transpose', PS(p, 128)[:8, :], sc2[:, 0:8], ident[:]), r=['as_sc2'], w=[('ps', p)])
                        k.op('act', I('mul', sall[:, 0:128], PS(p, 128)[:8, :], ISQ), r=[('ps', p)], w=['as_sall'])
                        p = nps()
                        k.op('pe', I('transpose', PS(p, 128)[:8, :], sc2[:, 8:16], ident[:]), r=['as_sc2'], w=[('ps', p)])
                        k.op('act', I('mul', sall[:, 128:129], PS(p, 128)[:8, 0:1], ISQ), r=[('ps', p)], w=['as_sall'])
                        k.op('dve', I('tensor_reduce', mxs[:, 0:1], sall[:, 0:129], AX.X, ALU.max), r=['as_sall'], w=['as_mx'])
                        k.op('act', I('mul', nmxs[:, :], mxs[:, 0:1], -1.0), r=['as_mx'], w=['as_nmx'])
                        k.op('act', I('activation', Ps[:, 0:129], sall[:, 0:129], AF.Exp, bias=nmxs[:, :], accum_out=mxs[:, 1:2]),
                             r=['as_sall', 'as_nmx'], w=['as_P', 'as_mx'])
                        p = nps()
                        k.op('pe', I('transpose', PS(p, 8), Ps[:, 0:128], ident[:8, :8]), r=['as_P'], w=[('ps', p)])
                        k.op('act', I('copy', PTk[:, :], PS(p, 8)), r=[('ps', p)], w=['as_PTk'])
                        p = nps()
                        k.op('pe', I('transpose', PS(p, 8)[:1, :], Ps[:, 128:129], ident[:8, :8]), r=['as_P'], w=[('ps', p)])
                        k.op('act', I('copy', PTn[:, 0:8], PS(p, 8)[:1, :]), r=[('ps', p)], w=['as_PTn'])
                        p = nps()
                        k.op('pe', I('transpose', PS(p, 8)[:1, :], mxs[:, 0:1], ident[:8, :8]), r=['as_mx'], w=[('ps', p)])
                        k.op('act', I('copy', PTn[:, 8:16], PS(p, 8)[:1, :]), r=[('ps', p)], w=['as_PTn'])
                        p = nps()
                        k.op('pe', I('transpose', PS(p, 8)[:1, :], mxs[:, 1:2], ident[:8, :8]), r=['as_mx'], w=[('ps', p)])
                        k.op('act', I('copy', PTn[:, 16:24], PS(p, 8)[:1, :]), r=[('ps', p)], w=['as_PTn'])
                        k.op('dve', I('tensor_copy', As[j:j + 1, g, :, 128] if False else As[0:1, g, :, 128], PTn[:, 8:16]), r=['as_PTn'], w=[('as_A', j)])
                        k.op('dve', I('tensor_copy', As[0:1, g, :, 129], PTn[:, 16:24]), r=['as_PTn'], w=[('as_A', j)])
                        for h in range(8):
                            p = nps()
                            k.op('pe', I('matmul', PS(p, 128)[:1, :], PTk[:, h:h + 1], Vc[:, h * 128:(h + 1) * 128], start=True, stop=False),
                                 r=['as_PTk', 'as_V'], w=[('ps', p)])
                            k.op('pe', I('matmul', PS(p, 128)[:1, :], PTn[:, h:h + 1], vn[:, h * 128:(h + 1) * 128], start=False, stop=True),
                                 r=['as_PTn', 'as_vn'], w=[('ps', p)])
                            k.op('act', I('copy', As[0:1, g, h, 0:128], PS(p, 128)[:1, :]), r=[('ps', p)], w=[('as_A', j)])
                    merge(As, 1, [('as_A', j)])
                    k.op('act', I('copy', acc[0:1, :, :], ob[0:1, :, :]), r=[('ac_ob', 0, h) for h in range(8)], w=[('ac_acc', 0, h) for h in range(8)])
                    k.dma(os_tok[j:j + 1, :], acc[0:1, :, :].rearrange("p h e -> p (h e)"), r=[('ac_acc', 0, h) for h in range(8)])
                k.barrier()
                osT = sb(es, 'as_osT', [128, 8, NS])
                rows_to_fm(es, os_tok, NS, 1024, osT, 'as_osT', 'ao')
                k.dma(zt[0][:, :, :NS], zT.rearrange("(c p) t -> p c t", p=128)[:, :, T:T + NS], w=[('ac_z', 0)])
                k.op('dve', I('tensor_tensor', yo[0][:, :, :NS], osT[:, :, :], zt[0][:, :, :NS], ALU.mult), r=['as_osT', ('ac_z', 0)], w=[('ac_yo', 0, 0)])
                k.dma(yTv[:, 0:8, T:T + NS], yo[0][:, :, :NS], r=[('ac_yo', 0, 0)])
                k.barrier()
                k.flush()
            out_phase(layer, di['c_w_out'], 8)

        MIXERS = {0: mlstm_layer, 1: conv_layer, 2: attn_layer, 3: pool_layer}
        for li in layers:
            MIXERS[li]()

        with ExitStack() as es:
            wst.clear()
            hf = [sb(es, 'hf%d' % i, [128, 8, 512]) for i in range(2)]
            yo = [sb(es, 'yo%d' % i, [128, D]) for i in range(2)]
            n = 0
            ftl = [(s0, min(512, TA - s0)) for s0 in list(range(0, T, 512)) + [T]]
            for fn_, (s0, W_) in enumerate(ftl):
                norm_tile_f32(es, di['final_g'], ftl, fn_, hf)
                for j0 in range(0, W_, 128):
                    L = min(128, W_ - j0)
                    b = n % 2
                    n += 1
                    for c in range(8):
                        p = nps()
                        k.op('pe', I('transpose', PS(p, 128)[:L, :], hf[fn_ % 2][:, c, j0:j0 + L], ident[:]),
                             r=[('hf', fn_ % 2, c)], w=[('ps', p)])
                        k.op('act' if c % 2 else 'dve',
                             I('copy' if c % 2 else 'tensor_copy', yo[b][:L, c * 128:(c + 1) * 128], PS(p, 128)[:L, :]),
                             r=[('ps', p)], w=[('yo', b, c)])
                    dst = di['y_prompt'][s0 + j0:s0 + j0 + L, :] if s0 < T else di['y_sample']
                    k.dma(dst, yo[b][:L, :], r=[('yo', b, c) for c in range(8)])
            if dbg:
                k.barrier()
                for c in range(8):
                    k.dma(di['dbg_xT'][c * 128:(c + 1) * 128, :], xT[c * 128:(c + 1) * 128, :])
            k.barrier()
            k.flush()
    return nc


_CACHE = {}


def _prep_inputs(inp, T):
    cst = host_consts(T)
    shared = {}
    for k_ in W_SHAPES:
        a = np.asarray(inp[k_], dtype=np.float32)
        if k_ in ('norm_g', 'pe_w', 'pg_w', 'final_g'):
            shared[k_] = np.ascontiguousarray(a)
        else:
            shared[k_] = np.ascontiguousarray(a[0])
    for k_ in CONST_SHAPES:
        shared['c_' + k_] = cst[k_]
    shared['c_ropec'] = cst['ropec']
    shared['c_ropes'] = cst['ropes']
    maps = []
    for c in range(8):
        b = c % 4
        sl = slice(NS * c, NS * c + NS)
        m = dict(shared)
        m['x_prompt'] = np.ascontiguousarray(inp['x_prompt'][b])
        m['x_sample'] = np.ascontiguousarray(inp['x_sample'][sl, 0])
        m['p_prompt'] = np.ascontiguousarray(inp['p_prompt'][:, b])
        m['p_sample'] = np.ascontiguousarray(inp['p_sample'][:, sl, 0])
        m['st_C'] = np.ascontiguousarray(inp['state_mlstm_C'][0, sl])
        m['st_n'] = np.ascontiguousarray(inp['state_mlstm_n'][0, sl])
        m['st_m'] = np.ascontiguousarray(inp['state_mlstm_m'][0, sl])
        m['st_conv'] = np.ascontiguousarray(inp['state_conv'][0, sl])
        m['st_pool'] = np.ascontiguousarray(inp['state_pool'][0, sl])
        for wn in ('128', '512', '2048'):
            m['ck' + wn] = np.ascontiguousarray(inp['cache_k_w' + wn][0, sl]).reshape(NS, -1, 1024)
            m['cv' + wn] = np.ascontiguousarray(inp['cache_v_w' + wn][0, sl]).reshape(NS, -1, 1024)
        maps.append(m)
    return maps


def _gather(res, T):
    R = res.results
    B = 4

    def P(name):
        return np.stack([np.asarray(R[c][name]) for c in range(B)])

    def S(name):
        return np.concatenate([np.asarray(R[c][name]) for c in range(8)], axis=0)

    outs = [P('y_prompt'), S('y_sample')[:, None, :],
            P('o_Cp')[None], S('o_Cs')[None],
            P('o_np')[None], S('o_ns')[None],
            P('o_mp')[:, 0][None], S('o_ms')[None],
            P('o_convp')[None], S('o_convs')[None]]
    for wn in ('128', '512', '2048'):
        nk = min(int(wn), T)
        outs += [P('o_kp' + wn).reshape(1, B, nk, 8, 128), S('o_ks' + wn).reshape(1, 32, 1, 8, 128),
                 P('o_vp' + wn).reshape(1, B, nk, 8, 128), S('o_vs' + wn).reshape(1, 32, 1, 8, 128)]
    outs += [P('o_poolp')[None], S('o_pools')[None]]
    return tuple(np.ascontiguousarray(o, dtype=np.float32) for o in outs)


def kernel(**inp):
    T = int(np.asarray(inp['x_prompt']).shape[1])
    if T not in _CACHE:
        _CACHE[T] = build(T)
    nc = _CACHE[T]
    inp = {k_: np.asarray(v) for k_, v in inp.items()}
    maps = _prep_inputs(inp, T)
    res = run_bass_kernel_spmd(nc, maps, core_ids=list(range(8)))
    return _gather(res, T)
```
